# Optimizing a Trainium2 kernel written in Bass

```python
import math
import jax, jax.numpy as jnp
from jax import lax
import numpy as np

D_MODEL = 1024
BATCH = 4
SEQ = 4096
DEPTH = 4

GRID_W = 64
CTX_LEN = 256

D_MIX = D_MODEL
N_MIXERS = 4
D_GROUP = D_MIX // N_MIXERS
CONV_W = 7

SSD_HEADS = 4
SSD_HEAD_DIM = D_GROUP // SSD_HEADS
SSD_GROUPS = 2
SSD_STATE = 64
SSD_CHUNK = 128
SSD_CONV_DIM = D_GROUP + 2 * SSD_GROUPS * SSD_STATE
DT_MIN = 1e-3
DT_MAX = 1e-1

POOL_WINDOWS = (2, 4, 8, 16)
POOL_GROUP = D_GROUP // len(POOL_WINDOWS)

GDN_HEADS = 4
GDN_HEAD_DIM = D_GROUP // GDN_HEADS
GDN_CHUNK = 64

GMLP_GROUPS = 4
GMLP_CHUNK = 128
GMLP_GROUP_DIM = D_GROUP // GMLP_GROUPS

D_FF = 2816
FFN_CONV_W = 3

DN_ALPHA = (2 * DEPTH) ** 0.25
DN_BETA = (8 * DEPTH) ** -0.25
LN_EPS = 1e-6
RMS_EPS = 1e-6

IN_SPLITS = (D_GROUP,
             SSD_CONV_DIM,
             2 * SSD_HEADS,
             D_GROUP,
             3 * D_GROUP,
             D_GROUP,
             2 * GDN_HEADS,
             2 * GDN_HEADS,
             2 * D_GROUP)
N_IN = sum(IN_SPLITS)

kernel_name = "hymba_style_ssd_pool_deltanet_gmlp_dit_block"


def split_last(t, sizes):
    idx = np.cumsum(sizes)[:-1].tolist()
    return jnp.split(t, idx, axis=-1)


def layer_norm(x, w=None, b=None):
    xf = x.astype(jnp.float32)
    mu = jnp.mean(xf, -1, keepdims=True)
    var = jnp.mean(jnp.square(xf - mu), -1, keepdims=True)
    y = (xf - mu) * lax.rsqrt(var + LN_EPS)
    if w is not None:
        y = y * w.astype(jnp.float32) + b.astype(jnp.float32)
    return y.astype(x.dtype)


def rms_norm(x, w):
    xf = x.astype(jnp.float32)
    y = xf * lax.rsqrt(jnp.mean(xf * xf, -1, keepdims=True) + RMS_EPS) * w.astype(jnp.float32)
    return y.astype(x.dtype)


def l2_normalize(x):
    xf = x.astype(jnp.float32)
    return xf * lax.rsqrt(jnp.sum(xf * xf, -1, keepdims=True) + 1e-6)


def adaln(x, shift, scale):
    return layer_norm(x) * (1 + scale) + shift


def dwconv(x, w, b=None):
    k, ch = w.shape
    y = lax.conv_general_dilated(x, w[:, None, :].astype(x.dtype), window_strides=(1,),
                                 padding=[(k // 2, k // 2)], dimension_numbers=('NWC', 'WIO', 'NWC'),
                                 feature_group_count=ch)
    return y if b is None else y + b


def flip(t):
    return jnp.flip(t, axis=1)


def ssd_scan(x, dt, a, bm, cm, h0, return_y):
    bsz, seq, nh, hp = x.shape
    q = SSD_CHUNK
    nc = seq // q
    x = x.reshape(bsz, nc, q, nh, hp)
    dt = dt.reshape(bsz, nc, q, nh)
    bm = bm.reshape(bsz, nc, q, nh, -1)
    cm = cm.reshape(bsz, nc, q, nh, -1)
    cum = jnp.cumsum(dt * a, axis=2)
    last = cum[:, :, -1]
    xdt = x * dt[..., None]
    w_end = jnp.exp(last[:, :, None, :] - cum)
    chunk_states = jnp.einsum('bcqh,bcqhn,bcqhp->bchpn', w_end, bm, xdt)

    def step(h, inp):
        st, dec = inp
        return h * jnp.exp(dec)[..., None, None] + st, h

    h_final, h_starts = lax.scan(step, h0, (jnp.moveaxis(chunk_states, 1, 0), jnp.moveaxis(last, 1, 0)))
    if not return_y:
        return None, h_final
    h_starts = jnp.moveaxis(h_starts, 0, 1)
    pos = jnp.arange(q)
    tri = pos[:, None] >= pos[None, :]
    seg = jnp.where(tri[None, None, :, :, None], cum[:, :, :, None, :] - cum[:, :, None, :, :], -jnp.inf)
    scores = jnp.einsum('bcihn,bcjhn->bcijh', cm, bm) * jnp.exp(seg)
    y_intra = jnp.einsum('bcijh,bcjhp->bcihp', scores, xdt)
    y_inter = jnp.einsum('bcihn,bchpn->bcihp', cm, h_starts) * jnp.exp(cum)[..., None]
    return (y_intra + y_inter).reshape(bsz, seq, nh, hp), h_final


def ssd_mixer(z, xbc, dt_raw, conv_w, conv_b, dt_bias, a_log, d_skip, norm_w, h0_f, h0_b, return_y):
    bsz, seq, _ = xbc.shape
    xbc = jax.nn.silu(dwconv(xbc, conv_w, conv_b)).astype(jnp.float32)
    xs, bm, cm = split_last(xbc, (D_GROUP, SSD_GROUPS * SSD_STATE, SSD_GROUPS * SSD_STATE))
    xs = xs.reshape(bsz, seq, SSD_HEADS, SSD_HEAD_DIM)
    rep = SSD_HEADS // SSD_GROUPS
    bm = jnp.repeat(bm.reshape(bsz, seq, SSD_GROUPS, SSD_STATE), rep, axis=2)
    cm = jnp.repeat(cm.reshape(bsz, seq, SSD_GROUPS, SSD_STATE), rep, axis=2)
    dt = jax.nn.softplus(dt_raw.astype(jnp.float32).reshape(bsz, seq, 2, SSD_HEADS)
                         + dt_bias.astype(jnp.float32))
    a = -jnp.exp(a_log.astype(jnp.float32))
    y_f, h_f = ssd_scan(xs, dt[:, :, 0], a[0], bm, cm, h0_f, return_y)
    y_b, h_b = ssd_scan(flip(xs), flip(dt[:, :, 1]), a[1], flip(bm), flip(cm), h0_b, return_y)
    if not return_y:
        return None, h_f, h_b
    y = y_f + flip(y_b) + xs * d_skip.astype(jnp.float32)[:, None]
    y = y.reshape(bsz, seq, D_GROUP) * jax.nn.silu(z.astype(jnp.float32))
    return rms_norm(y, norm_w), h_f, h_b


def multiscale_pool(x):
    seq = x.shape[-2]
    xf = x.astype(jnp.float32)
    csum = jnp.concatenate([jnp.zeros_like(xf[..., :1, :]), jnp.cumsum(xf, axis=-2)], axis=-2)
    pos = jnp.arange(seq)
    outs = []
    for g, w in enumerate(POOL_WINDOWS):
        lo = jnp.clip(pos - w // 2, 0, seq)
        hi = jnp.clip(pos + w - w // 2, 0, seq)
        cg = csum[..., g * POOL_GROUP:(g + 1) * POOL_GROUP]
        s = jnp.take(cg, hi, axis=-2) - jnp.take(cg, lo, axis=-2)
        outs.append(s / (hi - lo).astype(jnp.float32)[:, None])
    return jnp.concatenate(outs, axis=-1) - xf


def pool_mixer(p_in, pool_w, pool_scale, on_grid):
    bsz, seq, ch = p_in.shape
    if on_grid:
        rows = seq // GRID_W
        pooled = multiscale_pool(p_in.reshape(bsz, rows, GRID_W, ch)).reshape(bsz, seq, ch)
    else:
        pooled = multiscale_pool(p_in)
    pg = pooled.reshape(bsz, seq, len(POOL_WINDOWS), POOL_GROUP)
    y = jnp.einsum('blgc,gcd->blgd', pg, pool_w.astype(jnp.float32)).reshape(bsz, seq, ch)
    return y * pool_scale.astype(jnp.float32)


def gated_delta_chunked(q, k, v, g, beta, s0, return_o):
    bsz, seq, nh, dk = q.shape
    dv = v.shape[-1]
    cl = GDN_CHUNK
    nc = seq // cl

    def chunks(t):
        return t.reshape(bsz, nc, cl, nh, -1).transpose(0, 3, 1, 2, 4)

    q = chunks(q * dk ** -0.5)
    k = chunks(k)
    v = chunks(v)
    g = g.reshape(bsz, nc, cl, nh).transpose(0, 3, 1, 2)
    beta = beta.reshape(bsz, nc, cl, nh).transpose(0, 3, 1, 2)
    gc = jnp.cumsum(g, axis=-1)
    pos = jnp.arange(cl)
    lower = pos[:, None] >= pos[None, :]
    strict = pos[:, None] > pos[None, :]
    decay = jnp.exp(jnp.where(lower, gc[..., :, None] - gc[..., None, :], -jnp.inf))
    kb = k * beta[..., None]
    a_mat = jnp.where(strict, jnp.einsum('bhcid,bhcjd->bhcij', kb, k) * decay, 0.0)
    t_mat = a_mat + jnp.eye(cl, dtype=jnp.float32)
    rhs = jnp.concatenate([v * beta[..., None], kb * jnp.exp(gc)[..., None]], axis=-1)
    sol = lax.linalg.triangular_solve(t_mat, rhs, left_side=True, lower=True, unit_diagonal=True)
    u, w = sol[..., :dv], sol[..., dv:]
    k_tail = k * jnp.exp(gc[..., -1:] - gc)[..., None]
    g_last = gc[..., -1]

    def step(s, inp):
        u_c, w_c, kt_c, gl_c = inp
        v_new = u_c - jnp.einsum('bhik,bhkv->bhiv', w_c, s)
        s_next = s * jnp.exp(gl_c)[..., None, None] + jnp.einsum('bhik,bhiv->bhkv', kt_c, v_new)
        return s_next, (v_new, s)

    s_final, (v_new, s_starts) = lax.scan(
        step, s0, (jnp.moveaxis(u, 2, 0), jnp.moveaxis(w, 2, 0), jnp.moveaxis(k_tail, 2, 0), jnp.moveaxis(g_last, 2, 0)))
    if not return_o:
        return None, s_final
    v_new = jnp.moveaxis(v_new, 0, 2)
    s_starts = jnp.moveaxis(s_starts, 0, 2)
    attn = jnp.where(lower, jnp.einsum('bhcid,bhcjd->bhcij', q, k) * decay, 0.0)
    o = (jnp.einsum('bhcij,bhcjv->bhciv', attn, v_new)
         + jnp.einsum('bhcik,bhckv->bhciv', q * jnp.exp(gc)[..., None], s_starts))
    return o.transpose(0, 2, 3, 1, 4).reshape(bsz, seq, nh, dv), s_final


def gdn_mixer(qkv, gate, a_raw, b_raw, conv_w, dt_bias, a_log, norm_w, s0_f, s0_b, return_y):
    bsz, seq, _ = qkv.shape
    qkv = jax.nn.silu(dwconv(qkv, conv_w)).astype(jnp.float32)
    q, k, v = jnp.split(qkv, 3, axis=-1)
    shp = (bsz, seq, GDN_HEADS, GDN_HEAD_DIM)
    q = l2_normalize(q.reshape(shp))
    k = l2_normalize(k.reshape(shp))
    v = v.reshape(shp)
    g = -jnp.exp(a_log.astype(jnp.float32)) * jax.nn.softplus(
        a_raw.astype(jnp.float32).reshape(bsz, seq, 2, GDN_HEADS) + dt_bias.astype(jnp.float32))
    beta = jax.nn.sigmoid(b_raw.astype(jnp.float32).reshape(bsz, seq, 2, GDN_HEADS))
    o_f, s_f = gated_delta_chunked(q, k, v, g[:, :, 0], beta[:, :, 0], s0_f, return_y)
    o_b, s_b = gated_delta_chunked(flip(q), flip(k), flip(v), flip(g[:, :, 1]), flip(beta[:, :, 1]), s0_b, return_y)
    if not return_y:
        return None, s_f, s_b
    o = rms_norm(o_f + flip(o_b), norm_w) * jax.nn.silu(gate.astype(jnp.float32).reshape(shp))
    return o.reshape(bsz, seq, D_GROUP), s_f, s_b


def gmlp_mixer(uv, ln_w, ln_b, ws, bs):
    u, v = jnp.split(jax.nn.gelu(uv), 2, axis=-1)
    v = layer_norm(v, ln_w, ln_b)
    bsz, seq, _ = v.shape
    nc = seq // GMLP_CHUNK
    vc = v.reshape(bsz, nc, GMLP_CHUNK, GMLP_GROUPS, GMLP_GROUP_DIM)
    vs = jnp.einsum('gij,bcjgd->bcigd', ws, vc) + bs.T[None, None, :, :, None]
    return u * vs.reshape(bsz, seq, D_GROUP)


def token_mixers(h, lp, states0, on_grid, return_y):
    proj = h @ lp["w_in"]
    (ssd_z, ssd_xbc, ssd_dt, pool_in, gdn_qkv, gdn_gate, gdn_a, gdn_b, gmlp_uv) = split_last(proj, IN_SPLITS)
    ssd_h0f, ssd_h0b, gdn_s0f, gdn_s0b = states0
    y_ssd, ssd_hf, ssd_hb = ssd_mixer(ssd_z, ssd_xbc, ssd_dt, lp["ssd_conv_w"], lp["ssd_conv_b"], lp["ssd_dt_bias"],
                                      lp["ssd_a_log"], lp["ssd_d"], lp["ssd_norm_w"], ssd_h0f, ssd_h0b, return_y)
    y_gdn, gdn_sf, gdn_sb = gdn_mixer(gdn_qkv, gdn_gate, gdn_a, gdn_b, lp["gdn_conv_w"], lp["gdn_dt_bias"],
                                      lp["gdn_a_log"], lp["gdn_norm_w"], gdn_s0f, gdn_s0b, return_y)
    states = (ssd_hf, ssd_hb, gdn_sf, gdn_sb)
    if not return_y:
        return None, states
    y_pool = pool_mixer(pool_in, lp["pool_w"], lp["pool_scale"], on_grid)
    y_gmlp = gmlp_mixer(gmlp_uv, lp["gmlp_ln_w"], lp["gmlp_ln_b"], lp["gmlp_ws"], lp["gmlp_bs"])
    y = jnp.concatenate([y_ssd.astype(h.dtype), y_pool.astype(h.dtype), y_gdn.astype(h.dtype),
                         y_gmlp.astype(h.dtype)], axis=-1)
    return y, states


def conv_ffn(h, w_up, conv_w, conv_b, w_down):
    a, b = jnp.split(h @ w_up, 2, axis=-1)
    return (jax.nn.silu(dwconv(a, conv_w, conv_b)) * b) @ w_down


def stream_layer(x, mod, lp, states0, on_grid):
    sh1, sc1, g1, sh2, sc2, g2 = jnp.split(mod, 6, axis=-1)
    y, states = token_mixers(adaln(x, sh1, sc1), lp, states0, on_grid, True)
    x = layer_norm(DN_ALPHA * x + g1 * (y @ lp["w_out"]), lp["ln1_w"], lp["ln1_b"])
    f = conv_ffn(adaln(x, sh2, sc2), lp["ffn_up"], lp["ffn_conv_w"], lp["ffn_conv_b"], lp["ffn_down"])
    x = layer_norm(DN_ALPHA * x + g2 * f, lp["ln2_w"], lp["ln2_b"])
    return x, states


def zero_states(bsz):
    hs = jnp.zeros((bsz, SSD_HEADS, SSD_HEAD_DIM, SSD_STATE), jnp.float32)
    sg = jnp.zeros((bsz, GDN_HEADS, GDN_HEAD_DIM, GDN_HEAD_DIM), jnp.float32)
    return (hs, hs, sg, sg)


def setup_inputs(seed: int = 0) -> dict:
    key = jax.random.key(seed)
    ks = iter(jax.random.split(key, 48))
    L, D = DEPTH, D_MODEL

    def nrm(shape, s):
        return jax.random.normal(next(ks), shape, jnp.float32) * s

    def gain(shape):
        return 1.0 + nrm(shape, 0.02)

    def dt_bias_init(shape):
        u = jax.random.uniform(next(ks), shape, jnp.float32)
        dt = jnp.exp(u * (math.log(DT_MAX) - math.log(DT_MIN)) + math.log(DT_MIN))
        return dt + jnp.log(-jnp.expm1(-dt))

    def a_log_init(shape):
        return jnp.log(jax.random.uniform(next(ks), shape, jnp.float32, 1.0, 16.0))

    return {
        "x": nrm((BATCH, SEQ, D), 1.0),
        "c": nrm((BATCH, D), 1.0),
        "ctx": nrm((BATCH, CTX_LEN, D), 1.0),
        "c_ctx": nrm((D,), 1.0),
        "w_mod": nrm((L, D, 6 * D), 0.5 * D ** -0.5),
        "b_mod": nrm((L, 6 * D), 0.02),
        "w_in": nrm((L, D, N_IN), D ** -0.5),
        "ssd_conv_w": nrm((L, CONV_W, SSD_CONV_DIM), CONV_W ** -0.5),
        "ssd_conv_b": nrm((L, SSD_CONV_DIM), 0.02),
        "ssd_dt_bias": dt_bias_init((L, 2, SSD_HEADS)),
        "ssd_a_log": a_log_init((L, 2, SSD_HEADS)),
        "ssd_d": gain((L, SSD_HEADS)),
        "ssd_norm_w": gain((L, D_GROUP)),
        "pool_w": nrm((L, len(POOL_WINDOWS), POOL_GROUP, POOL_GROUP), POOL_GROUP ** -0.5),
        "pool_scale": gain((L, D_GROUP)),
        "gdn_conv_w": nrm((L, CONV_W, 3 * D_GROUP), CONV_W ** -0.5),
        "gdn_dt_bias": dt_bias_init((L, 2, GDN_HEADS)),
        "gdn_a_log": a_log_init((L, 2, GDN_HEADS)),
        "gdn_norm_w": gain((L, GDN_HEAD_DIM)),
        "gmlp_ln_w": gain((L, D_GROUP)),
        "gmlp_ln_b": nrm((L, D_GROUP), 0.02),
        "gmlp_ws": nrm((L, GMLP_GROUPS, GMLP_CHUNK, GMLP_CHUNK), GMLP_CHUNK ** -0.5),
        "gmlp_bs": gain((L, GMLP_GROUPS, GMLP_CHUNK)),
        "w_out": nrm((L, D_MIX, D), DN_BETA * D_MIX ** -0.5),
        "ln1_w": gain((L, D)),
        "ln1_b": nrm((L, D), 0.02),
        "ffn_up": nrm((L, D, 2 * D_FF), D ** -0.5),
        "ffn_conv_w": nrm((L, FFN_CONV_W, D_FF), FFN_CONV_W ** -0.5),
        "ffn_conv_b": nrm((L, D_FF), 0.02),
        "ffn_down": nrm((L, D_FF, D), DN_BETA * D_FF ** -0.5),
        "ln2_w": gain((L, D)),
        "ln2_b": nrm((L, D), 0.02),
    }


def reference(x, c, ctx, c_ctx, w_mod, b_mod, w_in, ssd_conv_w, ssd_conv_b, ssd_dt_bias, ssd_a_log, ssd_d,
              ssd_norm_w, pool_w, pool_scale, gdn_conv_w, gdn_dt_bias, gdn_a_log, gdn_norm_w, gmlp_ln_w, gmlp_ln_b,
              gmlp_ws, gmlp_bs, w_out, ln1_w, ln1_b, ffn_up, ffn_conv_w, ffn_conv_b, ffn_down, ln2_w, ln2_b):
    lat = x
    cx = ctx
    silu_c = jax.nn.silu(c)[:, None, :]
    silu_cc = jax.nn.silu(c_ctx)[None, None, :]
    for l in range(DEPTH):
        lp = {
            "w_in": w_in[l], "ssd_conv_w": ssd_conv_w[l], "ssd_conv_b": ssd_conv_b[l],
            "ssd_dt_bias": ssd_dt_bias[l], "ssd_a_log": ssd_a_log[l], "ssd_d": ssd_d[l],
            "ssd_norm_w": ssd_norm_w[l], "pool_w": pool_w[l], "pool_scale": pool_scale[l],
            "gdn_conv_w": gdn_conv_w[l], "gdn_dt_bias": gdn_dt_bias[l], "gdn_a_log": gdn_a_log[l],
            "gdn_norm_w": gdn_norm_w[l], "gmlp_ln_w": gmlp_ln_w[l], "gmlp_ln_b": gmlp_ln_b[l],
            "gmlp_ws": gmlp_ws[l], "gmlp_bs": gmlp_bs[l], "w_out": w_out[l], "ln1_w": ln1_w[l],
            "ln1_b": ln1_b[l], "ffn_up": ffn_up[l], "ffn_conv_w": ffn_conv_w[l], "ffn_conv_b": ffn_conv_b[l],
            "ffn_down": ffn_down[l], "ln2_w": ln2_w[l], "ln2_b": ln2_b[l],
        }
        mod_lat = silu_c @ w_mod[l] + b_mod[l]
        mod_ctx = silu_cc @ w_mod[l] + b_mod[l]
        init = zero_states(cx.shape[0])
        if l < DEPTH - 1:
            cx_next, ctx_states = stream_layer(cx, mod_ctx, lp, init, False)
        else:
            sh1, sc1 = jnp.split(mod_ctx, 6, axis=-1)[:2]
            _, ctx_states = token_mixers(adaln(cx, sh1, sc1), lp, init, False, False)
            cx_next = cx
        lat, _ = stream_layer(lat, mod_lat, lp, ctx_states, True)
        cx = cx_next
    return lat
```

```python
import numpy as np
import concourse.bass as bass
import concourse.mybir as mybir
from concourse.bass_utils import run_bass_kernel_spmd

F32 = mybir.dt.float32
BF16 = mybir.dt.bfloat16
ALU = mybir.AluOpType
AF = mybir.ActivationFunctionType
AX = mybir.AxisListType

import os
SSD_STOP = int(os.environ.get('SSD_STOP', '99'))
GDN_ROUNDS = int(os.environ.get('GDN_ROUNDS', '5'))
SEG = 16000
DSEG = 1000
DMAK = 8


class Buf:
    __slots__ = ("w", "r", "name")

    def __init__(self, name=""):
        self.w = None
        self.r = {}
        self.name = name


class Prog:
    ENGS = ["pe", "act", "dve", "pool", "sp"]

    def __init__(self, nc):
        self.nc = nc
        self.ops = {e: [] for e in self.ENGS}
        self.count = {e: 0 for e in self.ENGS}
        self.sems = {}
        self.waited = {e: {} for e in self.ENGS}
        self.dma_n = {e: 0 for e in self.ENGS}
        self.last = {}

    def _sem(self, key):
        if key not in self.sems:
            self.sems[key] = self.nc.alloc_semaphore("s_" + "_".join(map(str, key)))
        return self.sems[key]

    def _need(self, eng, waits, tok):
        if tok is None:
            return
        key, val = tok
        if key[0] == "c" and key[1] == "pe" and eng == "pe":
            return
        if self.waited[eng].get(key, 0) >= val:
            return
        if waits.get(key, 0) < val:
            waits[key] = val

    def op(self, eng, fn, reads=(), writes=(), dma=False, extra=()):
        waits = {}
        for b in reads:
            self._need(eng, waits, b.w)
        for b in writes:
            self._need(eng, waits, b.w)
            for k, v in b.r.items():
                self._need(eng, waits, (k, v))
        for t in extra:
            self._need(eng, waits, t)
        if dma:
            n = self.dma_n[eng]
            self.dma_n[eng] += 1
            s, r = n % DMAK, n // DMAK
            if r >= 1:
                pk = ("d", eng, s, (r - 1) // DSEG)
                self._need(eng, waits, (pk, 16 * (((r - 1) % DSEG) + 1)))
            tok = (("d", eng, s, r // DSEG), 16 * ((r % DSEG) + 1))
            inc = 16
        elif fn is None:
            tok = None
            inc = 0
        else:
            n = self.count[eng]
            self.count[eng] += 1
            tok = (("c", eng, n // SEG), (n % SEG) + 1)
            inc = 1
        for k, v in waits.items():
            self.waited[eng][k] = v
            self._sem(k)
        if tok is not None:
            self._sem(tok[0])
            self.last[tok[0]] = tok[1]
        self.ops[eng].append((list(waits.items()), fn, tok, inc))
        if tok is not None:
            for b in reads:
                b.r[tok[0]] = tok[1]
            for b in writes:
                b.w = tok
                b.r = {}
        return tok

    def dma(self, out, in_, reads=(), writes=(), q="sp", **kw):
        return self.op(q, lambda e: e.dma_start(out=out, in_=in_, **kw), reads, writes, dma=True)

    def barrier(self):
        toks = list(self.last.items())
        for e in self.ENGS:
            self.op(e, None, extra=toks)

    def emit(self):
        nc = self.nc
        with nc.Block() as block:
            def run(name):
                def body(e):
                    for waits, fn, tok, inc in self.ops[name]:
                        for k, v in waits:
                            e.wait_ge(self.sems[k], v)
                        if fn is not None:
                            ins = fn(e)
                            ins.then_inc(self.sems[tok[0]], inc)
                return body
            block.tensor(run("pe"))
            block.scalar(run("act"))
            block.vector(run("dve"))
            block.gpsimd(run("pool"))
            block.sync(run("sp"))


class Rec:
    def __init__(self):
        self.calls = []

    def op(self, eng, fn, reads=(), writes=(), dma=False, extra=()):
        self.calls.append((eng, fn, tuple(reads), tuple(writes), dma, tuple(extra)))

    def dma(self, out, in_, reads=(), writes=(), q="sp", **kw):
        self.op(q, lambda e: e.dma_start(out=out, in_=in_, **kw), reads, writes, dma=True)


def merge(P, recs):
    idx = [0] * len(recs)
    live = True
    while live:
        live = False
        for k, r in enumerate(recs):
            if idx[k] < len(r.calls):
                P.op(*r.calls[idx[k]])
                idx[k] += 1
                live = True


class T:
    def __init__(self, h, name=""):
        self.h = h
        self.b = Buf(name)

    def __getitem__(self, k):
        return self.h[k]


def _dtsize(dt):
    return 2 if dt == BF16 else 4


class Arena:
    def __init__(self, nc, base=16512, limit=229344):
        self.nc, self.base, self.limit, self.top, self.n = nc, base, limit, base, 0

    def alloc(self, name, shape, dt=F32):
        el = 1
        for s in shape[1:]:
            el *= s
        size = (el * _dtsize(dt) + 63) // 64 * 64
        off = self.top
        self.top += size
        assert self.top <= self.limit, (name, self.top)
        self.n += 1
        return T(self.nc.alloc_sbuf_tensor_at(f"{name}{self.n}", list(shape), dt, offset=off), name)

    def mark(self):
        return self.top

    def reset(self, m):
        self.top = m


D = 1024
LC = 256
LL = 4096
TALL = LC + LL
DEPTH = 4
NIN = 2584
DFF = 2816
DN_ALPHA = (2 * DEPTH) ** 0.25
LN_EPS = 1e-6
O_Z, O_XBC, O_DT, O_POOL, O_QKV, O_GATE, O_A, O_B, O_UV = 0, 256, 768, 776, 1032, 1800, 2056, 2064, 2072
FM_GROUPS = [(O_XBC, 512), (O_POOL, 256), (O_QKV, 768), (O_UV, 512)]
NFM = 2048
NTM = 536
C_SSDCW, C_SSDCB, C_GDNCW, C_FFNCW, C_FFNCB, C_PSCALE, C_BMOD = 0, 28, 32, 74, 140, 162, 164
NCOL = 164 + 48
R_SSDNW, R_GDNNW, R_GLNW, R_GLNB, R_LN1W, R_LN1B, R_LN2W, R_LN2B = 0, 256, 320, 576, 832, 1856, 2880, 3904
R_SSDD, R_GBS, R_SDTB, R_SALOG, R_GDTB, R_GALOG, R_BG1, R_BG2 = 4928, 5184, 5696, 5704, 5712, 5720, 5728, 6752
NROW = 7776
K_ID, K_ONES, K_LF, K_LB, K_NMF, K_NMB, K_LF2, K_LB2, K_NI2F, K_NI2B, K_NS2F, K_NS2B, K_BO2, K_SEL0, K_SEL1 = range(15)
NCONST = 15
NEG = -30000.0


def make_consts():
    k = np.arange(128)[:, None]
    m = np.arange(128)[None, :]
    same = (k // 64) == (m // 64)
    c = np.zeros((128, NCONST, 128), np.float32)
    c[:, K_ID] = (k == m)
    c[:, K_ONES] = 1.0
    c[:, K_LF] = (k <= m)
    c[:, K_LB] = (k >= m)
    c[:, K_NMF] = np.where(m >= k, 0.0, NEG)
    c[:, K_NMB] = np.where(m <= k, 0.0, NEG)
    c[:, K_LF2] = (k <= m) & same
    c[:, K_LB2] = (k >= m) & same
    c[:, K_NI2F] = np.where((m >= k) & same, 0.0, NEG)
    c[:, K_NI2B] = np.where((m <= k) & same, 0.0, NEG)
    c[:, K_NS2F] = np.where((k > m) & same, 0.0, NEG)
    c[:, K_NS2B] = np.where((k < m) & same, 0.0, NEG)
    c[:, K_BO2] = same
    c[:, K_SEL0] = (k < 64) * np.ones_like(m)
    c[:, K_SEL1] = (k >= 64) * np.ones_like(m)
    return c


class Ctx:
    pass


def build(nlayers=DEPTH, stop_after=None, dbg=False, mixers=None, only=None):
    nc = bass.Bass("TRN2", target_bir_lowering=False)
    G = Ctx()
    G.nc = nc
    P = Prog(nc)
    G.P = P

    def din(name, shape):
        return nc.dram_tensor(name, list(shape), F32, kind="ExternalInput")

    G.x_in = din("x_in", [LL, D])
    G.ctx_in = din("ctx_in", [LC, D])
    G.crep = din("crep", [128, 2, 8, 128])
    G.consts_d = din("consts", [128, NCONST, 128])
    G.w_mod = din("w_mod", [DEPTH, D, 6 * D])
    G.w_in = din("w_in", [DEPTH, D, NIN])
    G.w_out = din("w_out", [DEPTH, D, D])
    G.ffn_up = din("ffn_up", [DEPTH, D, 2 * DFF])
    G.ffn_down = din("ffn_down", [DEPTH, DFF, D])
    G.colp = din("colp", [DEPTH, 128, NCOL])
    G.rowp = din("rowp", [DEPTH, 1, NROW])
    G.poolw = din("poolw", [DEPTH, 128, 2, 128])
    G.gws = din("gws", [DEPTH, 128, 4, 128])
    G.pinv_g = din("pinv_g", [128, 2, 64])
    G.pinv_c = din("pinv_c", [128, 2, 256])
    G.out = T(nc.dram_tensor("out", [LL, D], F32, kind="ExternalOutput"), "out")
    G.xs = [T(nc.dram_tensor(f"xs{i}", [TALL, D], F32), f"xs{i}") for i in range(2)]
    G.x1 = T(nc.dram_tensor("x1s", [TALL, D], F32), "x1s")
    G.projF = T(nc.dram_tensor("projF", [NFM, TALL], F32), "projF")
    G.projT = T(nc.dram_tensor("projT", [TALL, NTM], F32), "projT")
    G.convF = T(nc.dram_tensor("convF", [1280, TALL], F32), "convF")
    G.yT = T(nc.dram_tensor("yT", [D, TALL], BF16), "yT")
    G.hid = T(nc.dram_tensor("hid", [DFF, TALL], BF16), "hid")
    G.dbg = {}
    if dbg:
        G.dbg["projF"] = T(nc.dram_tensor("d_projF", [NFM, TALL], F32, kind="ExternalOutput"))
        G.dbg["projT"] = T(nc.dram_tensor("d_projT", [TALL, NTM], F32, kind="ExternalOutput"))
        G.dbg["mod"] = T(nc.dram_tensor("d_mod", [128, 64], F32, kind="ExternalOutput"))
        G.dbg["gb"] = T(nc.dram_tensor("d_gb", [128, 4096], F32, kind="ExternalOutput"))
        G.dbg["yT"] = T(nc.dram_tensor("d_yT", [D, TALL], BF16, kind="ExternalOutput"))
        G.dbg["convF"] = T(nc.dram_tensor("d_convF", [1280, TALL], F32, kind="ExternalOutput"))
        G.dbg["x1"] = T(nc.dram_tensor("d_x1", [TALL, D], F32, kind="ExternalOutput"))
        G.dbg["x2"] = T(nc.dram_tensor("d_x2", [TALL, D], F32, kind="ExternalOutput"))
        G.dbg["st"] = T(nc.dram_tensor("d_st", [64, 2 * 4 * 64], F32, kind="ExternalOutput"))

    A = Arena(nc)
    G.A = A
    G.psum = [T(nc.alloc_psum_tensor(f"ps{i}", [128, 512], F32), f"ps{i}") for i in range(8)]
    G.consts = A.alloc("consts", [128, NCONST, 128])
    G.csil = A.alloc("csil", [128, 2, 8, 128])
    G.colp_sb = A.alloc("colp", [128, NCOL])
    G.rowp_sb = A.alloc("rowp", [128, NROW])
    G.modc = A.alloc("modc", [128, 4, 8, 2])
    G.gb = A.alloc("gb", [128, 2, 2, 1024])
    G.eps = A.alloc("eps", [128, 1])
    G.sst = [A.alloc("sst", [64, 4, 64]) for _ in range(2)]
    G.gst = [A.alloc("gst", [64, 4, 64]) for _ in range(2)]
    if mixers is not None:
        G.mixers = mixers
    G.dumps_on = dbg
    G.dumps = {}

    P.dma(G.consts[:], G.consts_d[:, :, :], writes=[G.consts.b])
    P.dma(G.csil[:], G.crep[:, :, :, :], writes=[G.csil.b])
    P.op("act", lambda e: e.activation(G.csil[:], G.csil[:], AF.Silu), [G.csil.b], [G.csil.b])
    P.op("dve", lambda e: e.memset(G.eps[:], LN_EPS), [], [G.eps.b])

    streams = [("ctx", 0, LC, 1), ("lat", LC, LL, 0)]
    if only is not None:
        streams = [st_ for st_ in streams if st_[0] in only]
    for l in range(nlayers):
        G.l = l
        xin = G.xs[l % 2]
        mod_phase(G, l)
        if stop_after == "mod":
            break
        phase_a(G, l, streams)
        if stop_after == "A":
            break
        prep_pass(G, l, streams)
        mix = G.mixers if hasattr(G, "mixers") else ("pool", "gmlp", "ssd", "gdn")
        for d_ in range(2):
            P.op("dve", lambda e, d_=d_: e.memset(G.sst[d_][:], 0.0), [], [G.sst[d_].b])
            P.op("dve", lambda e, d_=d_: e.memset(G.gst[d_][:], 0.0), [], [G.gst[d_].b])
        for st_ in streams:
            if "pool" in mix:
                pool_mixer(G, l, st_)
            if "gmlp" in mix:
                gmlp_mixer(G, l, st_)
            if "ssd" in mix:
                ssd_mixer(G, l, st_)
            if "gdn" in mix:
                gdn_mixer(G, l, st_)
        if stop_after == "mix":
            break
        last = (l == DEPTH - 1)
        cd_streams = [st_ for st_ in streams if not (last and st_[0] == "ctx")]
        mC = A.mark()
        hT2 = A.alloc("hT2", [128, 8, TALL], BF16)
        phase_c(G, l, cd_streams, hT2)
        phase_d1(G, l, cd_streams, hT2)
        A.reset(mC)
        phase_d2(G, l, cd_streams, last)

    if dbg:
        P.barrier()
        m0 = A.mark()
        tlo = min(st_[1] for st_ in streams)
        thi = max(st_[1] + st_[2] for st_ in streams)
        TW = thi - tlo
        tmp = A.alloc("dbgtmp", [128, TALL])
        tb = A.alloc("dbgtb", [128, TALL], BF16)
        P.dma(G.dbg["mod"][:, :], G.modc[:].rearrange("p a k s -> p (a k s)"), reads=[G.modc.b], writes=[G.dbg["mod"].b], q="pool")
        P.dma(G.dbg["gb"][:, :], G.gb[:].rearrange("p s g d -> p (s g d)"), reads=[G.gb.b], writes=[G.dbg["gb"].b], q="pool")
        if stop_after == "A":
            for r in range(NFM // 128):
                P.dma(tmp[:, 0:TW], G.projF[r * 128:(r + 1) * 128, tlo:thi], reads=[G.projF.b], writes=[tmp.b])
                P.dma(G.dbg["projF"][r * 128:(r + 1) * 128, tlo:thi], tmp[:, 0:TW], reads=[tmp.b], writes=[G.dbg["projF"].b], q="pool")
            for r in range(tlo // 128, thi // 128):
                P.dma(tmp[:, 0:NTM], G.projT[r * 128:(r + 1) * 128, :], reads=[G.projT.b], writes=[tmp.b])
                P.dma(G.dbg["projT"][r * 128:(r + 1) * 128, :], tmp[:, 0:NTM], reads=[tmp.b], writes=[G.dbg["projT"].b], q="pool")
        if stop_after is None:
            for r in range(tlo // 128, thi // 128):
                P.dma(tmp[:, 0:D], G.x1[r * 128:(r + 1) * 128, :], reads=[G.x1.b], writes=[tmp.b])
                P.dma(G.dbg["x1"][r * 128:(r + 1) * 128, :], tmp[:, 0:D], reads=[tmp.b], writes=[G.dbg["x1"].b], q="pool")
                P.dma(tmp[:, 0:D], G.xs[nlayers % 2][r * 128:(r + 1) * 128, :], reads=[G.xs[nlayers % 2].b], writes=[tmp.b])
                P.dma(G.dbg["x2"][r * 128:(r + 1) * 128, :], tmp[:, 0:D], reads=[tmp.b], writes=[G.dbg["x2"].b], q="pool")
        if stop_after == "mix":
            for d_ in range(2):
                P.dma(G.dbg["st"][:, d_ * 256:(d_ + 1) * 256], G.sst[d_][:].rearrange("p h d -> p (h d)"), reads=[G.sst[d_].b], writes=[G.dbg["st"].b], q="pool")
            rows = dict(ssd=(0, 2), pool=(2, 4), gdn=(4, 6), gmlp=(6, 8))
            for mname in (G.mixers if hasattr(G, "mixers") else rows.keys()):
                for r in range(*rows[mname]):
                    P.dma(tb[:, 0:TW], G.yT[r * 128:(r + 1) * 128, tlo:thi], reads=[G.yT.b], writes=[tb.b])
                    P.dma(G.dbg["yT"][r * 128:(r + 1) * 128, tlo:thi], tb[:, 0:TW], reads=[tb.b], writes=[G.dbg["yT"].b], q="pool")
            for r in range(1280 // 128):
                P.dma(tmp[:, 0:TW], G.convF[r * 128:(r + 1) * 128, tlo:thi], reads=[G.convF.b], writes=[tmp.b])
                P.dma(G.dbg["convF"][r * 128:(r + 1) * 128, tlo:thi], tmp[:, 0:TW], reads=[tmp.b], writes=[G.dbg["convF"].b], q="pool")
        P.op("pool", None, reads=[v.b for v in G.dbg.values()])
        A.reset(m0)
    P.op("pool", None, reads=[G.out.b])
    P.emit()
    return nc


def mod_phase(G, l):
    nc, P, A = G.nc, G.P, G.A
    m0 = A.mark()
    P.dma(G.colp_sb[:], G.colp[l, :, :], writes=[G.colp_sb.b])
    P.dma(G.rowp_sb[:], G.rowp[l, 0:1, :].partition_broadcast(128), writes=[G.rowp_sb.b])
    wm = [A.alloc("wm", [128, 8, 512]) for _ in range(2)]
    pc = G.psum[0]
    pg = [G.psum[1], G.psum[2]]
    wsrc = G.w_mod[l, :, :].rearrange("(k p) c -> p k c", p=128)
    colvec = {0: 0, 1: 1, 3: 2, 4: 3}
    for n in range(12):
        w = wm[n % 2]
        P.dma(w[:], wsrc[:, :, n * 512:(n + 1) * 512], writes=[w.b])
        vec, half = n // 2, n % 2
        if vec in colvec:
            a = colvec[vec]
            for j in range(4):
                kc = half * 4 + j

                def f(e, w=w, j=j, a=a, kc=kc):
                    for k in range(8):
                        ins = e.matmul(pc[:, (a * 8 + kc) * 2:(a * 8 + kc) * 2 + 2], w[:, k, j * 128:(j + 1) * 128],
                                       G.csil[:, :, k, 0], start=(k == 0), stop=(k == 7))
                    return ins
                P.op("pe", f, [w.b, G.csil.b], [pc.b])
        else:
            gi = 0 if vec == 2 else 1
            boff = R_BG1 if gi == 0 else R_BG2
            for s in range(2):
                def f(e, w=w, s=s):
                    for k in range(8):
                        ins = e.matmul(pg[s][:, :], G.csil[:, s, k, :], w[:, k, :], start=(k == 0), stop=(k == 7))
                    return ins
                P.op("pe", f, [w.b, G.csil.b], [pg[s].b])
                P.op("dve", lambda e, s=s, gi=gi, half=half, boff=boff: e.tensor_tensor(
                    G.gb[:, s, gi, half * 512:(half + 1) * 512], pg[s][:, :],
                    G.rowp_sb[:, boff + half * 512: boff + (half + 1) * 512], ALU.add),
                    [pg[s].b, G.rowp_sb.b], [G.gb.b])
    bm = G.colp_sb[:, C_BMOD:C_BMOD + 48].rearrange("p (v k) -> p v k", v=6)
    for vec, a in colvec.items():
        for s in range(2):
            P.op("dve", lambda e, vec=vec, a=a, s=s: e.tensor_tensor(
                G.modc[:, a, :, s], pc[:, a * 16:(a + 1) * 16].rearrange("p (k s) -> p k s", s=2)[:, :, s],
                bm[:, vec, :], ALU.add), [pc.b, G.colp_sb.b], [G.modc.b])
    for a in (1, 3):
        P.op("dve", lambda e, a=a: e.tensor_scalar_add(G.modc[:, a, :, :], G.modc[:, a, :, :], 1.0), [G.modc.b], [G.modc.b])
    P.barrier()
    A.reset(m0)


def ln_stats(G, xt, np_, st, mv, rstd):
    P = G.P
    def f(e):
        e.bn_stats(st[0:np_, 0, :], xt[0:np_, 0:512])
        return e.bn_stats(st[0:np_, 1, :], xt[0:np_, 512:1024])
    P.op("dve", f, [xt.b], [st.b])
    P.op("dve", lambda e: e.bn_aggr(mv[0:np_, :], st[0:np_, :, :].rearrange("p a b -> p (a b)")), [st.b], [mv.b])
    P.op("act", lambda e: e.activation(rstd[0:np_, :], mv[0:np_, 1:2], AF.Sqrt, bias=G.eps[0:np_, :]), [mv.b, G.eps.b], [rstd.b])
    P.op("dve", lambda e: e.reciprocal(rstd[0:np_, :], rstd[0:np_, :]), [rstd.b], [rstd.b])


def load_weight_bf16(G, dst, src_rows, ncols, stage, nk):
    P = G.P
    for k in range(nk):
        s = stage[k % len(stage)]
        P.dma(s[:, 0:ncols], src_rows(k), writes=[s.b])
        P.op("pool", lambda e, s=s, k=k: e.tensor_copy(dst[:, k, 0:ncols], s[:, 0:ncols]), [s.b], [dst.b])


def phase_a(G, l, streams):
    nc, P, A = G.nc, G.P, G.A
    m0 = A.mark()
    wi = A.alloc("wi", [128, 8, NIN], BF16)
    stage = [A.alloc("wstage", [128, NIN]) for _ in range(2)]
    load_weight_bf16(G, wi, lambda k: G.w_in[l, k * 128:(k + 1) * 128, :], NIN, stage, 8)
    xt = [A.alloc("xt", [128, D]) for _ in range(2)]
    xn = [A.alloc("xn", [128, D]) for _ in range(4)]
    st = [A.alloc("st", [128, 2, 6]) for _ in range(2)]
    mv = [A.alloc("mv", [128, 2]) for _ in range(2)]
    rstd = [A.alloc("rstd", [128, 1]) for _ in range(2)]
    hT = [A.alloc("hT", [128, 8, 512], BF16) for _ in range(2)]
    oF = [A.alloc("oF", [128, 512]) for _ in range(3)]
    oT = [A.alloc("oT", [128, NTM]) for _ in range(2)]
    pT = [G.psum[0], G.psum[1]]
    pF = [G.psum[2], G.psum[3]]
    pTa = [G.psum[4], G.psum[5]]
    pTb = [G.psum[6], G.psum[7]]
    ident = G.consts[:, K_ID, :]
    cnt = dict(t=0, b=0, f=0, o=0)
    xsrc = G.xs[l % 2]
    for (name, tok0, L, s) in streams:
        bs = 512 if L >= 512 else L
        for b0 in range(0, L, bs):
            nt = bs // 128
            W = bs
            h = hT[cnt["b"] % 2]
            cnt["b"] += 1
            for m in range(nt):
                t = xt[cnt["t"] % 2]
                q = cnt["t"] % 2
                cnt["t"] += 1
                r0 = b0 + m * 128
                if l == 0:
                    src = (G.ctx_in if name == "ctx" else G.x_in)[r0:r0 + 128, :]
                    P.dma(t[:], src, writes=[t.b])
                else:
                    P.dma(t[:], xsrc[tok0 + r0: tok0 + r0 + 128, :], reads=[xsrc.b], writes=[t.b])
                ln_stats(G, t, 128, st[q], mv[q], rstd[q])
                P.op("dve", lambda e, t=t, q=q, m=m: e.tensor_scalar(xn[m][:], t[:], mv[q][:, 0:1], rstd[q][:, 0:1],
                                                                    ALU.subtract, ALU.mult), [t.b, mv[q].b, rstd[q].b], [xn[m].b])
            for k in range(8):
                p = pT[k % 2]

                def f(e, p=p, k=k, nt=nt):
                    for m in range(nt):
                        ins = e.transpose(p[:, m * 128:(m + 1) * 128], xn[m][:, k * 128:(k + 1) * 128], ident)
                    return ins
                P.op("pe", f, [xn[m].b for m in range(nt)] + [G.consts.b], [p.b])
                P.op("act", lambda e, p=p, k=k, h=h, W=W, s=s: e.activation(
                    h[:, k, 0:W], p[:, 0:W], AF.Identity, bias=G.modc[:, 0, k, s:s + 1], scale=G.modc[:, 1, k, s:s + 1]),
                    [p.b, G.modc.b], [h.b])
            row = 0
            for (c0, n) in FM_GROUPS:
                for j in range(n // 128):
                    p = pF[cnt["f"] % 2]
                    o = oF[cnt["f"] % 3]
                    ev = "act" if cnt["f"] % 2 == 0 else "dve"
                    cnt["f"] += 1
                    cc = c0 + j * 128

                    def f(e, p=p, cc=cc, h=h, W=W):
                        for k in range(8):
                            ins = e.matmul(p[:, 0:W], wi[:, k, cc:cc + 128], h[:, k, 0:W], start=(k == 0), stop=(k == 7))
                        return ins
                    P.op("pe", f, [wi.b, h.b], [p.b])
                    if ev == "act":
                        P.op("act", lambda e, p=p, o=o, W=W: e.activation(o[:, 0:W], p[:, 0:W], AF.Copy), [p.b], [o.b])
                    else:
                        P.op("dve", lambda e, p=p, o=o, W=W: e.tensor_copy(o[:, 0:W], p[:, 0:W]), [p.b], [o.b])
                    P.dma(G.projF[row:row + 128, tok0 + b0: tok0 + b0 + W], o[:, 0:W], reads=[o.b], writes=[G.projF.b], q="pool")
                    row += 128
            for m in range(nt):
                pa = pTa[cnt["o"] % 2]
                pb = pTb[cnt["o"] % 2]
                o = oT[cnt["o"] % 2]
                cnt["o"] += 1

                def f(e, pa=pa, pb=pb, h=h, m=m):
                    for k in range(8):
                        lw = h[:, k, m * 128:(m + 1) * 128]
                        e.matmul(pa[:, 0:256], lw, wi[:, k, O_Z:O_Z + 256], start=(k == 0), stop=(k == 7))
                        e.matmul(pb[:, 0:272], lw, wi[:, k, O_GATE:O_GATE + 272], start=(k == 0), stop=(k == 7))
                    for k in range(8):
                        lw = h[:, k, m * 128:(m + 1) * 128]
                        ins = e.matmul(pa[:, 256:264], lw, wi[:, k, O_DT:O_DT + 8], start=(k == 0), stop=(k == 7))
                    return ins
                P.op("pe", f, [wi.b, h.b], [pa.b, pb.b])
                P.op("act", lambda e, pa=pa, o=o: e.activation(o[:, 0:264], pa[:, 0:264], AF.Copy), [pa.b], [o.b])
                P.op("dve", lambda e, pb=pb, o=o: e.tensor_copy(o[:, 264:536], pb[:, 0:272]), [pb.b], [o.b])
                r0 = tok0 + b0 + m * 128
                P.dma(G.projT[r0:r0 + 128, :], o[:], reads=[o.b], writes=[G.projT.b], q="pool")
    P.barrier()
    A.reset(m0)


def dbgdump(G, name, t, ap, shape, dt=F32, P=None):
    if not getattr(G, "dumps_on", False) or name in G.dumps:
        return
    o = T(G.nc.dram_tensor("dd_" + name, list(shape), dt, kind="ExternalOutput"))
    G.dumps[name] = o
    nd = len(shape)
    P = P or G.P
    P.dma(o[tuple(slice(None) for _ in range(nd))], ap, reads=[t.b], writes=[o.b], q="pool")
    P.op("pool", None, reads=[o.b])


def bc_ap(ap, dims):
    return bass.AP(ap.tensor, ap.offset, [list(ap.ap[0])] + [list(d) for d in dims])


def prep_pass(G, l, streams):
    P, A = G.P, G.A
    m0 = A.mark()
    xin = [A.alloc("cin", [128, LL + 6]) for _ in range(2)]
    acc = [A.alloc("cacc", [128, LL]) for _ in range(2)]
    sq = [A.alloc("csq", [128, 512]) for _ in range(2)]
    rn = [A.alloc("crn", [128, 512]) for _ in range(2)]
    ps = [G.psum[0], G.psum[1]]
    bo2 = G.consts[:, K_BO2, :]
    chunks = []
    for j in range(4):
        chunks.append((j * 128, j * 128, C_SSDCW + j * 7, C_SSDCB + j, "p"))
    for j in range(6):
        chunks.append((768 + j * 128, 512 + j * 128, C_GDNCW + j * 7, None, "q" if j < 2 else ("k" if j < 4 else "p")))
    cnt = 0
    sc = 0
    for (name, tok0, L, s) in streams:
        for (src, dst, wo, bo, kind) in chunks:
            t = xin[cnt % 2]
            a = acc[cnt % 2]
            cnt += 1
            w = G.colp_sb
            P.op("pool", lambda e, t=t: e.memset(t[:, 0:3], 0.0), [], [t.b])
            P.op("pool", lambda e, t=t, L=L: e.memset(t[:, L + 3:L + 6], 0.0), [], [t.b])
            P.dma(t[:, 3:L + 3], G.projF[src:src + 128, tok0:tok0 + L], reads=[G.projF.b], writes=[t.b])
            P.op("dve", lambda e, t=t, a=a, L=L, wo=wo: e.tensor_scalar(a[:, 0:L], t[:, 0:L], w[:, wo:wo + 1], None, ALU.mult),
                 [t.b, w.b], [a.b])
            for tap in range(1, 7):
                P.op("dve", lambda e, t=t, a=a, L=L, wo=wo, tap=tap: e.scalar_tensor_tensor(
                    a[:, 0:L], t[:, tap:tap + L], w[:, wo + tap:wo + tap + 1], a[:, 0:L], ALU.mult, ALU.add),
                    [t.b, w.b, a.b], [a.b])
            if bo is not None:
                P.op("act", lambda e, a=a, L=L, bo=bo: e.activation(a[:, 0:L], a[:, 0:L], AF.Silu, bias=w[:, bo:bo + 1]),
                     [a.b, w.b], [a.b])
            else:
                P.op("act", lambda e, a=a, L=L: e.activation(a[:, 0:L], a[:, 0:L], AF.Silu), [a.b], [a.b])
            if kind in ("q", "k"):
                for sl in range(0, L, 512):
                    Wd = min(512, L - sl)
                    q_, r_, p_ = sq[sc % 2], rn[sc % 2], ps[sc % 2]
                    sc += 1
                    P.op("act", lambda e, a=a, q_=q_, sl=sl, Wd=Wd: e.activation(q_[:, 0:Wd], a[:, sl:sl + Wd], AF.Square), [a.b], [q_.b])
                    P.op("pe", lambda e, q_=q_, p_=p_, Wd=Wd: e.matmul(p_[:, 0:Wd], bo2, q_[:, 0:Wd], start=True, stop=True),
                         [q_.b, G.consts.b], [p_.b])
                    P.op("act", lambda e, r_=r_, p_=p_, Wd=Wd: e.activation(r_[:, 0:Wd], p_[:, 0:Wd], AF.Sqrt, bias=G.eps[:, :]),
                         [p_.b, G.eps.b], [r_.b])
                    P.op("dve", lambda e, r_=r_, Wd=Wd: e.reciprocal(r_[:, 0:Wd], r_[:, 0:Wd]), [r_.b], [r_.b])
                    if kind == "q":
                        P.op("dve", lambda e, a=a, r_=r_, sl=sl, Wd=Wd: e.scalar_tensor_tensor(
                            a[:, sl:sl + Wd], a[:, sl:sl + Wd], 0.125, r_[:, 0:Wd], ALU.mult, ALU.mult), [a.b, r_.b], [a.b])
                    else:
                        P.op("dve", lambda e, a=a, r_=r_, sl=sl, Wd=Wd: e.tensor_tensor(
                            a[:, sl:sl + Wd], a[:, sl:sl + Wd], r_[:, 0:Wd], ALU.mult), [a.b, r_.b], [a.b])
            P.dma(G.convF[dst:dst + 128, tok0:tok0 + L], a[:, 0:L], reads=[a.b], writes=[G.convF.b], q="pool")
    P.barrier()
    A.reset(m0)


def pool_mixer(G, l, stream):
    P, A = G.P, G.A
    name, tok0, L, s = stream
    m0 = A.mark()
    RW = 64 if name == "lat" else L
    NR = L // RW
    PW = RW + 16
    F = NR * PW
    xp = A.alloc("pxp", [128, NR, PW])
    ca = A.alloc("pca", [128, NR, PW])
    cb = A.alloc("pcb", [128, NR, PW])
    tmp = A.alloc("ptmp", [128, NR, RW])
    pooled = A.alloc("ppool", [128, NR, RW], BF16)
    pinv = A.alloc("pinv", [128, 2, RW])
    pwf = A.alloc("pwf", [128, 2, 128])
    pwb = A.alloc("pwb", [128, 2, 128], BF16)
    yo = [A.alloc("pyo", [128, 512], BF16) for _ in range(2)]
    ps = [G.psum[2], G.psum[3]]
    P.dma(pinv[:], (G.pinv_g if name == "lat" else G.pinv_c)[:, :, :], writes=[pinv.b])
    P.dma(pwf[:], G.poolw[l, :, :, :], writes=[pwf.b])
    P.op("pool", lambda e: e.tensor_copy(pwb[:], pwf[:]), [pwf.b], [pwb.b])
    fl = lambda t: t[:].rearrange("p r w -> p (r w)")
    cnt = 0
    for jc in range(2):
        P.op("pool", lambda e: e.memset(xp[:], 0.0), [], [xp.b])
        P.dma(xp[:, :, 8:8 + RW], G.projF[512 + jc * 128:512 + (jc + 1) * 128, tok0:tok0 + L].rearrange("p (r w) -> p r w", w=RW),
              reads=[G.projF.b], writes=[xp.b])
        xf, af, bf = fl(xp), fl(ca), fl(cb)
        P.op("dve", lambda e: e.tensor_tensor(af[:, 0:F - 1], xf[:, 0:F - 1], xf[:, 1:F], ALU.add), [xp.b], [ca.b])
        if jc == 0:
            P.op("dve", lambda e: e.tensor_tensor(bf[64:128, 0:F - 3], af[64:128, 0:F - 3], af[64:128, 2:F - 1], ALU.add), [ca.b], [cb.b])
            srcs = [(0, 64, 2, ca), (64, 128, 4, cb)]
        else:
            P.op("dve", lambda e: e.tensor_tensor(bf[:, 0:F - 3], af[:, 0:F - 3], af[:, 2:F - 1], ALU.add), [ca.b], [cb.b])
            P.op("dve", lambda e: e.tensor_tensor(af[:, 0:F - 7], bf[:, 0:F - 7], bf[:, 4:F - 3], ALU.add), [cb.b, ca.b], [ca.b])
            P.op("dve", lambda e: e.tensor_tensor(bf[64:128, 0:F - 15], af[64:128, 0:F - 15], af[64:128, 8:F - 7], ALU.add), [ca.b, cb.b], [cb.b])
            srcs = [(0, 64, 8, ca), (64, 128, 16, cb)]
        for (lo, hi, w, cw) in srcs:
            o = 8 - w // 2
            P.op("dve", lambda e, lo=lo, hi=hi, cw=cw, o=o, jc=jc: e.tensor_tensor(
                tmp[lo:hi, :, :], cw[lo:hi, :, o:o + RW], bc_ap(pinv[lo:hi, jc, :], [[0, NR], [1, RW]]), ALU.mult),
                [cw.b, pinv.b], [tmp.b])
            P.op("dve", lambda e, lo=lo, hi=hi: e.tensor_tensor(pooled[lo:hi, :, :], tmp[lo:hi, :, :], xp[lo:hi, :, 8:8 + RW], ALU.subtract),
                 [tmp.b, xp.b], [pooled.b])
        pf = pooled[:].rearrange("p r w -> p (r w)")
        for sl in range(0, L, 512):
            Wd = min(512, L - sl)
            p_, y_ = ps[cnt % 2], yo[cnt % 2]
            cnt += 1
            P.op("pe", lambda e, p_=p_, sl=sl, Wd=Wd, jc=jc: e.matmul(p_[:, 0:Wd], pwb[:, jc, :], pf[:, sl:sl + Wd], start=True, stop=True),
                 [pwb.b, pooled.b], [p_.b])
            P.op("act", lambda e, p_=p_, y_=y_, Wd=Wd, jc=jc: e.activation(
                y_[:, 0:Wd], p_[:, 0:Wd], AF.Copy, scale=G.colp_sb[:, C_PSCALE + jc:C_PSCALE + jc + 1]), [p_.b, G.colp_sb.b], [y_.b])
            P.dma(G.yT[256 + jc * 128:256 + (jc + 1) * 128, tok0 + sl:tok0 + sl + Wd], y_[:, 0:Wd], reads=[y_.b], writes=[G.yT.b], q="pool")
    P.barrier()
    A.reset(m0)


def gmlp_mixer(G, l, stream):
    P, A = G.P, G.A
    name, tok0, L, s = stream
    m0 = A.mark()
    NB = 2
    uv = [A.alloc("guv", [128, 4, 128]) for _ in range(NB)]
    gt = [A.alloc("ggt", [128, 4, 128]) for _ in range(NB)]
    vt = [A.alloc("gvt", [128, 256]) for _ in range(NB)]
    vb = [A.alloc("gvb", [128, 256], BF16) for _ in range(NB)]
    st = [A.alloc("gst_", [128, 6]) for _ in range(NB)]
    mv = [A.alloc("gmv", [128, 2]) for _ in range(NB)]
    rs = [A.alloc("grs", [128, 1]) for _ in range(NB)]
    tm = [A.alloc("gtm", [128, 2, 128]) for _ in range(NB)]
    yo = [A.alloc("gyo", [128, 2, 128], BF16) for _ in range(NB)]
    wsf = A.alloc("gwsf", [128, 4, 128])
    wsb = A.alloc("gwsb", [128, 4, 128], BF16)
    P.dma(wsf[:], G.gws[l, :, :, :], writes=[wsf.b])
    P.op("pool", lambda e: e.tensor_copy(wsb[:], wsf[:]), [wsf.b], [wsb.b])
    pT = [G.psum[4], G.psum[5]]
    pq = [G.psum[6], G.psum[7]]
    ident = G.consts[:, K_ID, :]
    rp = G.rowp_sb
    for ti in range(L // 128):
        i = ti % NB
        u, g, v, vbb, p1, p2, tt, y = uv[i], gt[i], vt[i], vb[i], pT[i], pq[i], tm[i], yo[i]
        t0 = tok0 + ti * 128
        P.dma(u[:], G.projF[1536:2048, t0:t0 + 128].rearrange("(j p) t -> p j t", p=128), reads=[G.projF.b], writes=[u.b])
        uf = u[:].rearrange("p j t -> p (j t)")
        gf = g[:].rearrange("p j t -> p (j t)")
        P.op("dve", lambda e, uf=uf, gf=gf: e.tensor_tensor(gf, uf, uf, ALU.mult), [u.b], [g.b])
        P.op("dve", lambda e, gf=gf: e.tensor_scalar(gf, gf, 0.044715, 1.0, ALU.mult, ALU.add), [g.b], [g.b])
        P.op("dve", lambda e, uf=uf, gf=gf: e.tensor_tensor(gf, gf, uf, ALU.mult), [g.b, u.b], [g.b])
        P.op("act", lambda e, gf=gf: e.activation(gf, gf, AF.Sigmoid, scale=1.5957691216057308), [g.b], [g.b])
        P.op("dve", lambda e, uf=uf, gf=gf: e.tensor_tensor(gf, gf, uf, ALU.mult), [g.b, u.b], [g.b])

        def f(e, g=g, p1=p1):
            e.transpose(p1[:, 0:128], g[:, 2, :], ident)
            return e.transpose(p1[:, 128:256], g[:, 3, :], ident)
        P.op("pe", f, [g.b, G.consts.b], [p1.b])
        P.op("act", lambda e, v=v, p1=p1: e.activation(v[:], p1[:, 0:256], AF.Copy), [p1.b], [v.b])
        P.op("dve", lambda e, v=v, i=i: e.bn_stats(st[i][:], v[:]), [v.b], [st[i].b])
        P.op("dve", lambda e, i=i: e.bn_aggr(mv[i][:], st[i][:]), [st[i].b], [mv[i].b])
        P.op("act", lambda e, i=i: e.activation(rs[i][:], mv[i][:, 1:2], AF.Sqrt, bias=G.eps[:, :]), [mv[i].b, G.eps.b], [rs[i].b])
        P.op("dve", lambda e, i=i: e.reciprocal(rs[i][:], rs[i][:]), [rs[i].b], [rs[i].b])
        P.op("dve", lambda e, v=v, i=i: e.tensor_scalar(v[:], v[:], mv[i][:, 0:1], rs[i][:, 0:1], ALU.subtract, ALU.mult),
             [v.b, mv[i].b, rs[i].b], [v.b])
        P.op("dve", lambda e, v=v: e.tensor_tensor(v[:], v[:], rp[:, R_GLNW:R_GLNW + 256], ALU.mult), [v.b, rp.b], [v.b])
        P.op("dve", lambda e, v=v, vbb=vbb: e.tensor_tensor(vbb[:], v[:], rp[:, R_GLNB:R_GLNB + 256], ALU.add), [v.b, rp.b], [vbb.b])

        def f2(e, vbb=vbb, p2=p2):
            e.matmul(p2[:, 0:256], vbb[:, 0:128], wsb[:, 0:2, :].rearrange("p g i -> p (g i)"), start=True, stop=True)
            return e.matmul(p2[:, 256:512], vbb[:, 128:256], wsb[:, 2:4, :].rearrange("p g i -> p (g i)"), start=True, stop=True)
        P.op("pe", f2, [vbb.b, wsb.b], [p2.b])
        for q in range(2):
            for hh in range(2):
                gg = 2 * q + hh
                lo, hi = hh * 64, (hh + 1) * 64
                c0 = q * 256 + hh * 128
                P.op("dve", lambda e, lo=lo, hi=hi, c0=c0, q=q, gg=gg, tt=tt, p2=p2: e.tensor_tensor(
                    tt[lo:hi, q, :], p2[lo:hi, c0:c0 + 128], rp[lo:hi, R_GBS + gg * 128:R_GBS + (gg + 1) * 128], ALU.add),
                    [p2.b, rp.b], [tt.b])
                P.op("dve", lambda e, lo=lo, hi=hi, q=q, tt=tt, y=y, g=g: e.tensor_tensor(
                    y[lo:hi, q, :], tt[lo:hi, q, :], g[lo:hi, q, :], ALU.mult), [tt.b, g.b], [y.b])
        P.dma(G.yT[768:1024, t0:t0 + 128].rearrange("(j p) t -> p j t", p=128), y[:], reads=[y.b], writes=[G.yT.b], q="pool")
    P.barrier()
    A.reset(m0)


def softplus_small(P, out, x, tmp, reads, extra_w=()):
    P.op("dve", lambda e: e.scalar_tensor_tensor(tmp, x, -1.0, x, ALU.mult, ALU.max), reads, [extra_w[0]])
    P.op("act", lambda e: e.activation(tmp, tmp, AF.Exp, scale=-1.0), [extra_w[0]], [extra_w[0]])
    P.op("act", lambda e: e.activation(tmp, tmp, AF.Ln, bias=1.0), [extra_w[0]], [extra_w[0]])
    P.op("dve", lambda e: e.scalar_tensor_tensor(out, x, 0.0, tmp, ALU.max, ALU.add), reads + [extra_w[0]], [extra_w[1]])


def ssd_mixer(G, l, stream):
    P, A = G.P, G.A
    name, tok0, L, s = stream
    NT = L // 128
    m0 = A.mark()
    rp = G.rowp_sb
    C = G.consts
    ident = C[:, K_ID, :]
    ones = C[:, K_ONES, :]
    nega = A.alloc("snega", [128, 8])
    negones = A.alloc("snegones", [128, 128])
    yacc = A.alloc("syacc", [128, NT, 256])
    P.op("act", lambda e: e.activation(nega[:], rp[:, R_SALOG:R_SALOG + 8], AF.Exp), [rp.b], [nega.b])
    P.op("dve", lambda e: e.tensor_scalar(nega[:], nega[:], -1.0, None, ALU.mult), [nega.b], [nega.b])
    P.op("dve", lambda e: e.memset(negones[:], -1.0), [], [negones.b])
    NB = 2
    def mk(nm, shape, dt=F32):
        return [A.alloc(nm, shape, dt) for _ in range(NB)]
    zt, cx, dtt, dtm, dta = mk("szt", [128, 264]) + mk("szt", [128, 264]), mk("scx", [128, 2, 128]) + mk("scx", [128, 2, 128]), mk("sdt", [128, 8]), mk("sdtm", [128, 8]), mk("sdta", [128, 8])
    bc4 = mk("sbc4", [64, 4, 128]) + mk("sbc4", [64, 4, 128])
    xtm, btm, cbb = mk("sxtm", [128, 256]), mk("sbtm", [128, 128], BF16), mk("scbb", [64, 4, 128], BF16)
    ML, Dm, Wm = mk("sML", [128, 4, 128]), mk("sDm", [128, 4, 128]), mk("sWm", [128, 4, 128], BF16)
    scl, dtw = mk("sscl", [128, 12]), mk("sdtw", [128, 8])
    xdt, xdw = mk("sxdt", [128, 256], BF16), mk("sxdw", [128, 256], BF16)
    yt, yg, ss, yo = mk("syt", [128, 256]), mk("syg", [128, 256]), mk("sss", [128, 2]), mk("syo", [128, 2, 128], BF16)
    ps = G.psum
    def body(R, d, ti, i, li, second):
        Ltri = C[:, K_LF if d == 0 else K_LB, :]
        nmask = C[:, K_NMF if d == 0 else K_NMB, :]
        t0 = tok0 + ti * 128
        z_, c_, dt_, dm_, da_, x_, b_, cb_ = zt[li], cx[li], dtt[i], dtm[i], dta[i], xtm[i], btm[i], cbb[i]
        bc_ = bc4[li]
        ml_, D_, W_, sc_, dw_, xd_, xw_ = ML[i], Dm[i], Wm[i], scl[i], dtw[i], xdt[i], xdw[i]
        pA, pB, pC, pD = ps[(4 * i) % 8], ps[(4 * i + 1) % 8], ps[(4 * i + 2) % 8], ps[(4 * i + 3) % 8]
        R.dma(z_[:], G.projT[t0:t0 + 128, 0:264], reads=[G.projT.b], writes=[z_.b])
        R.dma(c_[:], G.convF[0:256, t0:t0 + 128].rearrange("(j p) t -> p j t", p=128), reads=[G.convF.b], writes=[c_.b])
        R.dma(bc_[:], G.convF[256:512, t0:t0 + 128].rearrange("(j p) t -> p j t", p=64), reads=[G.convF.b], writes=[bc_.b])
        R.op("dve", lambda e, z_=z_, dt_=dt_: e.tensor_tensor(dt_[:], z_[:, 256:264], rp[:, R_SDTB:R_SDTB + 8], ALU.add), [z_.b, rp.b], [dt_.b])
        softplus_small(R, dt_[:], dt_[:], dm_[:], [dt_.b], (dm_.b, dt_.b))
        R.op("dve", lambda e, dt_=dt_, da_=da_: e.tensor_tensor(da_[:], dt_[:], nega[:], ALU.mult), [dt_.b, nega.b], [da_.b])
        if SSD_STOP <= 1:
            return
        def f(e, c_=c_, pA=pA, bc_=bc_):
            e.transpose(pA[:, 0:128], c_[:, 0, :], ident)
            e.transpose(pA[:, 128:256], c_[:, 1, :], ident)
            e.transpose(pA[:, 256:320], bc_[:, 0, :], ident[0:64, 0:64])
            return e.transpose(pA[:, 320:384], bc_[:, 1, :], ident[0:64, 0:64])
        R.op("pe", f, [c_.b, bc_.b, C.b], [pA.b])
        R.op("act", lambda e, x_=x_, pA=pA: e.activation(x_[:], pA[:, 0:256], AF.Copy), [pA.b], [x_.b])
        R.op("act", lambda e, b_=b_, pA=pA: e.activation(b_[:], pA[:, 256:384], AF.Copy), [pA.b], [b_.b])
        R.op("pool", lambda e, cb_=cb_, bc_=bc_: e.tensor_copy(cb_[:], bc_[:]), [bc_.b], [cb_.b])
        if SSD_STOP <= 2:
            return
        def f(e, cb_=cb_, pB=pB):
            e.matmul(pB[:, 0:128], cb_[:, 0, :], cb_[:, 2, :], start=True, stop=True)
            return e.matmul(pB[:, 128:256], cb_[:, 1, :], cb_[:, 3, :], start=True, stop=True)
        R.op("pe", f, [cb_.b], [pB.b])
        if SSD_STOP <= 3:
            return
        R.op("dve", lambda e, ml_=ml_, da_=da_, Ltri=Ltri: e.tensor_tensor(
            ml_[:], bc_ap(Ltri, [[0, 4], [1, 128]]), bc_ap(da_[:, d * 4:d * 4 + 4], [[1, 4], [0, 128]]), ALU.mult), [da_.b, C.b], [ml_.b])
        def f(e, ml_=ml_, pC=pC, nmask=nmask):
            for h in range(4):
                e.matmul(pC[:, h * 128:(h + 1) * 128], ones, ml_[:, h, :], start=True, stop=False)
                e.matmul(pC[:, h * 128:(h + 1) * 128], ml_[:, h, :], negones[:], start=False, stop=False)
                ins = e.matmul(pC[:, h * 128:(h + 1) * 128], ident, nmask, start=False, stop=True)
            return ins
        R.op("pe", f, [ml_.b, C.b, negones.b], [pC.b])
        R.op("act", lambda e, D_=D_, pC=pC: e.activation(D_[:].rearrange("p h i -> p (h i)"), pC[:, :], AF.Exp), [pC.b], [D_.b])
        if SSD_STOP <= 4:
            return
        def f(e, da_=da_, pD=pD, Ltri=Ltri):
            e.matmul(pD[:, 0:4], Ltri, da_[:, d * 4:d * 4 + 4], start=True, stop=True)
            return e.matmul(pD[:, 4:8], ones, da_[:, d * 4:d * 4 + 4], start=True, stop=True)
        R.op("pe", f, [da_.b, C.b], [pD.b])
        R.op("dve", lambda e, sc_=sc_, pD=pD: e.tensor_copy(sc_[:, 0:8], pD[:, 0:8]), [pD.b], [sc_.b])
        R.op("dve", lambda e, sc_=sc_: e.tensor_copy(sc_[:, 8:12], sc_[:, 4:8]), [sc_.b], [sc_.b])
        R.op("dve", lambda e, sc_=sc_: e.tensor_tensor(sc_[:, 4:8], sc_[:, 4:8], sc_[:, 0:4], ALU.subtract), [sc_.b], [sc_.b])
        R.op("act", lambda e, sc_=sc_: e.activation(sc_[:], sc_[:], AF.Exp), [sc_.b], [sc_.b])
        if SSD_STOP <= 5:
            return
        R.op("dve", lambda e, W_=W_, D_=D_, pB=pB: e.tensor_tensor(
            W_[:].rearrange("p (g r) i -> p g r i", g=2), bc_ap(pB[:, 0:256], [[128, 2], [0, 2], [1, 128]]),
            D_[:].rearrange("p (g r) i -> p g r i", g=2), ALU.mult), [pB.b, D_.b], [W_.b])
        if SSD_STOP <= 6:
            return
        R.op("dve", lambda e, dw_=dw_, dt_=dt_, sc_=sc_: e.tensor_tensor(dw_[:, 0:4], dt_[:, d * 4:d * 4 + 4], sc_[:, 4:8], ALU.mult),
             [dt_.b, sc_.b], [dw_.b])
        R.op("dve", lambda e, xd_=xd_, x_=x_, dt_=dt_: e.tensor_tensor(
            xd_[:].rearrange("p (h q) -> p h q", h=4), x_[:].rearrange("p (h q) -> p h q", h=4),
            bc_ap(dt_[:, d * 4:d * 4 + 4], [[1, 4], [0, 64]]), ALU.mult), [x_.b, dt_.b], [xd_.b])
        R.op("dve", lambda e, xw_=xw_, x_=x_, dw_=dw_: e.tensor_tensor(
            xw_[:].rearrange("p (h q) -> p h q", h=4), x_[:].rearrange("p (h q) -> p h q", h=4),
            bc_ap(dw_[:, 0:4], [[1, 4], [0, 64]]), ALU.mult), [x_.b, dw_.b], [xw_.b])
        if SSD_STOP <= 7:
            return
        def f(e, W_=W_, xd_=xd_, pA=pA):
            for h in range(4):
                ins = e.matmul(pA[:, h * 64:(h + 1) * 64], W_[:, h, :], xd_[:, h * 64:(h + 1) * 64], start=True, stop=True)
            return ins
        R.op("pe", f, [W_.b, xd_.b], [pA.b])
        def f(e, bc_=bc_, pB=pB):
            for h in range(4):
                g = h // 2
                ins = e.matmul(pB[:, 256 + h * 64:256 + (h + 1) * 64], bc_[:, 2 + g, :],
                               G.sst[d][:, h, :], start=True, stop=True)
            return ins
        R.op("pe", f, [bc_.b, G.sst[d].b], [pB.b])
        def f(e, b_=b_, xw_=xw_, pD=pD):
            for h in range(4):
                g = h // 2
                ins = e.matmul(pD[0:64, 64 + h * 64:64 + (h + 1) * 64], b_[:, g * 64:(g + 1) * 64], xw_[:, h * 64:(h + 1) * 64], start=True, stop=True)
            return ins
        R.op("pe", f, [b_.b, xw_.b], [pD.b])
        if SSD_STOP <= 8:
            return
        for h in range(4):
            lo, hi = 0, 64
            R.op("dve", lambda e, lo=lo, hi=hi, h=h, sc_=sc_, pD=pD: e.scalar_tensor_tensor(
                G.sst[d][lo:hi, h, :], G.sst[d][lo:hi, h, :], sc_[lo:hi, 8 + h:9 + h], pD[lo:hi, 64 + h * 64:64 + (h + 1) * 64],
                ALU.mult, ALU.add), [G.sst[d].b, sc_.b, pD.b], [G.sst[d].b])
        if SSD_STOP <= 9:
            return
        y_ = yt[i]
        R.op("dve", lambda e, y_=y_, pB=pB, sc_=sc_: e.tensor_tensor(
            y_[:].rearrange("p (h q) -> p h q", h=4), pB[:, 256:512].rearrange("p (h q) -> p h q", h=4),
            bc_ap(sc_[:, 0:4], [[1, 4], [0, 64]]), ALU.mult), [pB.b, sc_.b], [y_.b])
        R.op("dve", lambda e, y_=y_, pA=pA: e.tensor_tensor(y_[:], y_[:], pA[:, 0:256], ALU.add), [y_.b, pA.b], [y_.b])
        if name == "ctx" and ti == 0:
            dbgdump(G, f"dt{d}", dt_, dt_[:], [128, 8], P=R)
            dbgdump(G, f"da{d}", da_, da_[:], [128, 8], P=R)
            dbgdump(G, f"x{d}", x_, x_[:], [128, 256], P=R)
            dbgdump(G, f"D{d}", D_, D_[:].rearrange("p h i -> p (h i)"), [128, 512], P=R)
            dbgdump(G, f"W{d}", W_, W_[:].rearrange("p h i -> p (h i)"), [128, 512], BF16, P=R)
            dbgdump(G, f"sc{d}", sc_, sc_[:], [128, 12], P=R)
            dbgdump(G, f"xd{d}", xd_, xd_[:], [128, 256], BF16, P=R)
            dbgdump(G, f"y{d}", y_, y_[:], [128, 256], P=R)
        if not second:
            R.op("dve", lambda e, x_=x_, ti=ti: e.tensor_tensor(yacc[:, ti, :], x_[:], rp[:, R_SSDD:R_SSDD + 256], ALU.mult),
                 [x_.b, rp.b], [yacc.b])
            R.op("dve", lambda e, y_=y_, ti=ti: e.tensor_tensor(yacc[:, ti, :], yacc[:, ti, :], y_[:], ALU.add), [yacc.b, y_.b], [yacc.b])
        else:
            g_, s_, o_ = yg[i], ss[i], yo[i]
            R.op("dve", lambda e, y_=y_, ti=ti: e.tensor_tensor(y_[:], y_[:], yacc[:, ti, :], ALU.add), [yacc.b, y_.b], [y_.b])
            R.op("act", lambda e, g_=g_, z_=z_: e.activation(g_[:], z_[:, 0:256], AF.Silu), [z_.b], [g_.b])
            R.op("dve", lambda e, g_=g_, y_=y_: e.tensor_tensor(g_[:], g_[:], y_[:], ALU.mult), [g_.b, y_.b], [g_.b])
            R.op("act", lambda e, g_=g_, y_=y_, s_=s_: e.activation(y_[:], g_[:], AF.Square, accum_out=s_[:, 0:1]), [g_.b], [y_.b, s_.b])
            R.op("act", lambda e, s_=s_: e.activation(s_[:, 1:2], s_[:, 0:1], AF.Sqrt, bias=G.eps[:, :], scale=1.0 / 256.0), [s_.b, G.eps.b], [s_.b])
            R.op("dve", lambda e, s_=s_: e.reciprocal(s_[:, 1:2], s_[:, 1:2]), [s_.b], [s_.b])
            R.op("dve", lambda e, g_=g_, s_=s_: e.scalar_tensor_tensor(
                g_[:], g_[:], s_[:, 1:2], rp[:, R_SSDNW:R_SSDNW + 256], ALU.mult, ALU.mult), [g_.b, s_.b, rp.b], [g_.b])
            def f(e, g_=g_, pC=pC):
                e.transpose(pC[:, 0:128], g_[:, 0:128], ident)
                return e.transpose(pC[:, 128:256], g_[:, 128:256], ident)
            R.op("pe", f, [g_.b, C.b], [pC.b])
            R.op("act", lambda e, o_=o_, pC=pC: e.activation(o_[:].rearrange("p j t -> p (j t)"), pC[:, 0:256], AF.Copy), [pC.b], [o_.b])
            R.dma(G.yT[0:256, t0:t0 + 128].rearrange("(j p) t -> p j t", p=128), o_[:], reads=[o_.b], writes=[G.yT.b], q="pool")

    cnt = 0
    for st_ in range(NT):
        recs = []
        for d in range(2):
            ti = st_ if d == 0 else NT - 1 - st_
            second = (ti >= NT // 2) if d == 0 else (ti < NT // 2)
            recs.append(Rec())
            body(recs[-1], d, ti, d, 2 * d + st_ % 2, second)
        merge(P, recs)
    P.barrier()
    A.reset(m0)


def gdn_mixer(G, l, stream):
    P, A = G.P, G.A
    name, tok0, L, s = stream
    NT = L // 128
    m0 = A.mark()
    rp, C = G.rowp_sb, G.consts
    ident, ones = C[:, K_ID, :], C[:, K_ONES, :]
    id64 = C[0:64, K_ID, 0:64]
    negag = A.alloc("gnegag", [128, 8])
    negones = A.alloc("gnegones", [128, 128])
    oacc = A.alloc("goacc", [128, NT, 256])
    P.op("act", lambda e: e.activation(negag[:], rp[:, R_GALOG:R_GALOG + 8], AF.Exp), [rp.b], [negag.b])
    P.op("dve", lambda e: e.tensor_scalar(negag[:], negag[:], -1.0, None, ALU.mult), [negag.b], [negag.b])
    P.op("dve", lambda e: e.memset(negones[:], -1.0), [], [negones.b])
    NB = 2

    def mk(nm, shape, dt=F32):
        return [A.alloc(nm, shape, dt) for _ in range(NB)]
    pt, fm, gx, kv = mk("gpt", [128, 272]) + mk("gpt", [128, 272]), mk("gfm", [64, 12, 128]), mk("ggx", [128, 24]), mk("gkv", [128, 512])
    sc, esc, ML = mk("gsc", [128, 16]), mk("gesc", [128, 16]), mk("gML", [128, 4, 128])
    decT, decS, A0, Q0, aT = mk("gdecT", [128, 4, 128]), mk("gdecS", [128, 4, 128]), mk("gA0", [128, 4, 128]), mk("gQ0", [128, 4, 128]), mk("gaT", [128, 4, 128])
    Pb, Qb, MTb = [mk("gPb", [128, 4, 128]) for _ in range(2)], [mk("gQb", [128, 4, 128]) for _ in range(2)], [mk("gMT", [128, 4, 128]) for _ in range(3)]
    Rv, Rk, bek, usb, wsb = mk("gRv", [128, 4, 64]), mk("gRk", [128, 4, 64]), mk("gbek", [128, 4]), mk("gusb", [128, 256]), mk("gwsb", [64, 512])
    ektc, kt = [mk("gektc", [128, 4]) for _ in range(2)], [mk("gkt", [128, 4, 64]) for _ in range(2)]
    vnew, oint, ot, sq, ss, yo = mk("gvnew", [128, 256]), mk("goint", [128, 256]), mk("got", [128, 256]), mk("gsq", [128, 256]), mk("gss", [128, 8]), mk("gyo", [128, 2, 128], BF16)
    for i in range(NB):
        P.op("dve", lambda e, i=i: e.memset(vnew[i][:], 0.0), [], [vnew[i].b])
    v4 = lambda t: t[:].rearrange("p (h q) -> p h q", h=4)

    def body(R, d, ti, i, li, second):
        g0_, g1_, g2_, g3_ = G.psum[4 * d:4 * d + 4]
        b = [g0_, g2_, g3_, g2_, g3_, g2_, g3_, g1_]
        bM = g0_
        t0 = tok0 + ti * 128
        Ltri2 = C[:, K_LF2 if d == 0 else K_LB2, :]
        niT = C[:, K_NI2F if d == 0 else K_NI2B, :]
        nsS = C[:, K_NS2F if d == 0 else K_NS2B, :]
        selc = [C[:, K_SEL0, 0:1], C[:, K_SEL1, 0:1]]
        pt_, f_, gx_, kv_, sc_, es_, ml_ = pt[li], fm[i], gx[i], kv[i], sc[i], esc[i], ML[i]
        dT_, dS_, a0_, q0_, at_ = decT[i], decS[i], A0[i], Q0[i], aT[i]
        R.dma(pt_[:], G.projT[t0:t0 + 128, 264:536], reads=[G.projT.b], writes=[pt_.b])
        R.dma(f_[:], G.convF[512:1280, t0:t0 + 128].rearrange("(j p) t -> p j t", p=64), reads=[G.convF.b], writes=[f_.b])
        R.op("dve", lambda e: e.tensor_tensor(gx_[:, 0:8], pt_[:, 256:264], rp[:, R_GDTB:R_GDTB + 8], ALU.add), [pt_.b, rp.b], [gx_.b])
        softplus_small(R, gx_[:, 0:8], gx_[:, 0:8], gx_[:, 16:24], [gx_.b], (gx_.b, gx_.b))
        R.op("dve", lambda e: e.tensor_tensor(gx_[:, 0:8], gx_[:, 0:8], negag[:], ALU.mult), [gx_.b, negag.b], [gx_.b])
        R.op("act", lambda e: e.activation(gx_[:, 8:16], pt_[:, 264:272], AF.Sigmoid), [pt_.b], [gx_.b])
        gd = gx_[:, d * 4:d * 4 + 4]
        bd = gx_[:, 8 + d * 4:8 + d * 4 + 4]
        def f(e):
            for h in range(4):
                e.transpose(b[0][:, h * 64:(h + 1) * 64], f_[:, 4 + h, :], id64)
            for h in range(4):
                ins = e.transpose(b[0][:, 256 + h * 64:256 + (h + 1) * 64], f_[:, 8 + h, :], id64)
            return ins
        R.op("pe", f, [f_.b, C.b], [b[0].b])
        R.op("act", lambda e: e.activation(kv_[:], b[0][:, :], AF.Copy), [b[0].b], [kv_.b])
        def f(e):
            e.matmul(b[7][:, 0:4], Ltri2, gd, start=True, stop=True)
            e.matmul(b[7][:, 4:8], C[:, K_BO2, :], gd, start=True, stop=True)
            e.matmul(b[7][:, 8:12], C[:, K_SEL0, :], gd, start=True, stop=True)
            return e.matmul(b[7][:, 12:16], C[:, K_SEL1, :], gd, start=True, stop=True)
        R.op("pe", f, [gx_.b, C.b], [b[7].b])
        R.op("dve", lambda e: e.tensor_copy(sc_[:], b[7][:, 0:16]), [b[7].b], [sc_.b])
        R.op("dve", lambda e: e.tensor_tensor(sc_[:, 4:8], sc_[:, 4:8], sc_[:, 0:4], ALU.subtract), [sc_.b], [sc_.b])
        R.op("act", lambda e: e.activation(es_[:], sc_[:], AF.Exp), [sc_.b], [es_.b])
        R.op("dve", lambda e: e.tensor_tensor(ml_[:], bc_ap(Ltri2, [[0, 4], [1, 128]]), bc_ap(gd, [[1, 4], [0, 128]]), ALU.mult),
             [gx_.b, C.b], [ml_.b])
        def f(e):
            for h in range(4):
                o = b[1][:, h * 128:(h + 1) * 128]
                e.matmul(o, ones, ml_[:, h, :], start=True, stop=False)
                e.matmul(o, ml_[:, h, :], negones[:], start=False, stop=False)
                ins = e.matmul(o, ident, niT, start=False, stop=True)
            return ins
        R.op("pe", f, [ml_.b, C.b, negones.b], [b[1].b])
        def f(e):
            for h in range(4):
                o = b[2][:, h * 128:(h + 1) * 128]
                e.matmul(o, ml_[:, h, :], ones, start=True, stop=False)
                e.matmul(o, negones[:], ml_[:, h, :], start=False, stop=False)
                ins = e.matmul(o, ident, nsS, start=False, stop=True)
            return ins
        R.op("pe", f, [ml_.b, C.b, negones.b], [b[2].b])
        fl = lambda t: t[:].rearrange("p h i -> p (h i)")
        R.op("act", lambda e: e.activation(fl(dT_), b[1][:, :], AF.Exp), [b[1].b], [dT_.b])
        R.op("act", lambda e: e.activation(fl(dS_), b[2][:, :], AF.Exp), [b[2].b], [dS_.b])
        def f(e):
            for h in range(4):
                ins = e.matmul(b[3][:, h * 128:(h + 1) * 128], f_[:, 4 + h, :], f_[:, 4 + h, :], start=True, stop=True)
            return ins
        R.op("pe", f, [f_.b], [b[3].b])
        def f(e):
            for h in range(4):
                ins = e.matmul(b[4][:, h * 128:(h + 1) * 128], f_[:, 4 + h, :], f_[:, h, :], start=True, stop=True)
            return ins
        R.op("pe", f, [f_.b], [b[4].b])
        R.op("dve", lambda e: e.tensor_tensor(fl(a0_), b[3][:, :], fl(dS_), ALU.mult), [b[3].b, dS_.b], [a0_.b])
        R.op("dve", lambda e: e.tensor_tensor(a0_[:], a0_[:], bc_ap(bd, [[1, 4], [0, 128]]), ALU.mult), [a0_.b, gx_.b], [a0_.b])
        R.op("dve", lambda e: e.tensor_tensor(fl(at_), b[4][:, :], fl(dT_), ALU.mult), [b[4].b, dT_.b], [at_.b])
        def f(e):
            for h in range(4):
                ins = e.transpose(b[5][:, h * 128:(h + 1) * 128], a0_[:, h, :], ident)
            return ins
        R.op("pe", f, [a0_.b, C.b], [b[5].b])
        R.op("act", lambda e: e.activation(fl(q0_), b[5][:, :], AF.Copy), [b[5].b], [q0_.b])
        mt = MTb[0][i]
        R.op("dve", lambda e, mt=mt: e.tensor_tensor(mt[:], bc_ap(ident, [[0, 4], [1, 128]]), q0_[:], ALU.subtract), [q0_.b, C.b], [mt.b])
        if name == "ctx" and ti == (0 if d == 0 else 1):
            dbgdump(G, f"gA{d}", a0_, fl(a0_), [128, 512], P=R)
            dbgdump(G, f"gQ{d}", q0_, fl(q0_), [128, 512], P=R)
            dbgdump(G, f"gM0{d}", mt, fl(mt), [128, 512], P=R)
            dbgdump(G, f"gdS{d}", dS_, fl(dS_), [128, 512], P=R)
            dbgdump(G, f"ggx{d}", gx_, gx_[:], [128, 24], P=R)
            dbgdump(G, f"gkv{d}", kv_, kv_[:], [128, 512], P=R)
        Pc, Qc = a0_, q0_
        for m in range(GDN_ROUNDS):
            Pn, Qn, mtn = Pb[m % 2][i], Qb[m % 2][i], MTb[(m + 1) % 3][i]
            def f(e, Pc=Pc, Qc=Qc):
                for h in range(4):
                    ins = e.matmul(b[3][:, h * 128:(h + 1) * 128], Qc[:, h, :], Pc[:, h, :], start=True, stop=True)
                return ins
            R.op("pe", f, [Pc.b, Qc.b], [b[3].b])
            def f(e, Pc=Pc, Qc=Qc):
                for h in range(4):
                    ins = e.matmul(b[4][:, h * 128:(h + 1) * 128], Pc[:, h, :], Qc[:, h, :], start=True, stop=True)
                return ins
            R.op("pe", f, [Pc.b, Qc.b], [b[4].b])
            R.op("act", lambda e, Pn=Pn: e.activation(fl(Pn), b[3][:, :], AF.Copy), [b[3].b], [Pn.b])
            R.op("dve", lambda e, Qn=Qn: e.tensor_copy(fl(Qn), b[4][:, :]), [b[4].b], [Qn.b])
            def f(e, Pn=Pn, mt=mt):
                for h in range(4):
                    ins = e.matmul(bM[:, h * 128:(h + 1) * 128], Pn[:, h, :], mt[:, h, :], start=True, stop=True)
                return ins
            R.op("pe", f, [Pn.b, mt.b], [bM.b])
            R.op("dve", lambda e, mt=mt, mtn=mtn: e.tensor_tensor(fl(mtn), fl(mt), bM[:, :], ALU.add), [mt.b, bM.b], [mtn.b])
            Pc, Qc, mt = Pn, Qn, mtn
        rv_, rk_, bk_, u_, w_ = Rv[i], Rk[i], bek[i], usb[i], wsb[i]
        R.op("dve", lambda e: e.tensor_tensor(rv_[:], v4(kv_)[:, 4:8, :] if False else kv_[:, 256:512].rearrange("p (h q) -> p h q", h=4),
                                              bc_ap(bd, [[1, 4], [0, 64]]), ALU.mult), [kv_.b, gx_.b], [rv_.b])
        R.op("dve", lambda e: e.tensor_tensor(bk_[:], bd, es_[:, 0:4], ALU.mult), [gx_.b, es_.b], [bk_.b])
        R.op("dve", lambda e: e.tensor_tensor(rk_[:], kv_[:, 0:256].rearrange("p (h q) -> p h q", h=4),
                                              bc_ap(bk_[:, 0:4], [[1, 4], [0, 64]]), ALU.mult), [kv_.b, bk_.b], [rk_.b])
        def f(e):
            for h in range(4):
                ins = e.matmul(b[7][:, 64 + h * 64:64 + (h + 1) * 64], mt[:, h, :], rv_[:, h, :], start=True, stop=True)
            return ins
        R.op("pe", f, [mt.b, rv_.b], [b[7].b])
        def f(e):
            for h in range(4):
                ins = e.matmul(b[0][0:64, h * 128:(h + 1) * 128], rk_[:, h, :], mt[:, h, :], start=True, stop=True)
            return ins
        R.op("pe", f, [mt.b, rk_.b], [b[0].b])
        R.op("act", lambda e: e.activation(u_[:], b[7][:, 64:320], AF.Copy), [b[7].b], [u_.b])
        R.op("dve", lambda e: e.tensor_copy(w_[:], b[0][0:64, :]), [b[0].b], [w_.b])
        for c in range(2):
            ek, k_ = ektc[c][i], kt[c][i]
            R.op("dve", lambda e, ek=ek, c=c: e.tensor_scalar(ek[:], es_[:, 4:8], selc[c], None, ALU.mult), [es_.b, C.b], [ek.b])
            R.op("dve", lambda e, ek=ek, k_=k_: e.tensor_tensor(k_[:], kv_[:, 0:256].rearrange("p (h q) -> p h q", h=4),
                                                               bc_ap(ek[:, 0:4], [[1, 4], [0, 64]]), ALU.mult), [kv_.b, ek.b], [k_.b])
        vn_, oi_ = vnew[i], oint[i]
        for c in ((0, 1) if d == 0 else (1, 0)):
            lo, hi = c * 64, (c + 1) * 64
            k_ = kt[c][i]
            def f(e):
                for h in range(4):
                    e.matmul(b[5][:, h * 64:(h + 1) * 64], w_[:, h * 128:(h + 1) * 128], G.gst[d][:, h, :], start=True, stop=True)
                for h in range(4):
                    ins = e.matmul(b[5][:, 256 + h * 64:256 + (h + 1) * 64], f_[:, h, :], G.gst[d][:, h, :], start=True, stop=True)
                return ins
            R.op("pe", f, [w_.b, f_.b, G.gst[d].b], [b[5].b])
            R.op("dve", lambda e, lo=lo, hi=hi: e.tensor_tensor(vn_[lo:hi, :], u_[lo:hi, :], b[5][lo:hi, 0:256], ALU.subtract), [u_.b, b[5].b], [vn_.b])
            R.op("dve", lambda e, lo=lo, hi=hi: e.tensor_tensor(oi_[lo:hi, :].rearrange("p (h q) -> p h q", h=4),
                                                                b[5][lo:hi, 256:512].rearrange("p (h q) -> p h q", h=4),
                                                                bc_ap(es_[lo:hi, 0:4], [[1, 4], [0, 64]]), ALU.mult), [b[5].b, es_.b], [oi_.b])
            def f(e, k_=k_):
                for h in range(4):
                    ins = e.matmul(b[6][0:64, h * 64:(h + 1) * 64], k_[:, h, :], vn_[:, h * 64:(h + 1) * 64], start=True, stop=True)
                return ins
            R.op("pe", f, [k_.b, vn_.b], [b[6].b])
            for h in range(4):
                R.op("dve", lambda e, h=h, c=c: e.scalar_tensor_tensor(
                    G.gst[d][:, h, :], G.gst[d][:, h, :], es_[0:64, 8 + 4 * c + h:9 + 4 * c + h], b[6][0:64, h * 64:(h + 1) * 64],
                    ALU.mult, ALU.add), [G.gst[d].b, es_.b, b[6].b], [G.gst[d].b])
        def f(e):
            for h in range(4):
                ins = e.matmul(b[7][:, 64 + h * 64:64 + (h + 1) * 64], at_[:, h, :], vn_[:, h * 64:(h + 1) * 64], start=True, stop=True)
            return ins
        R.op("pe", f, [at_.b, vn_.b], [b[7].b])
        if name == "ctx" and ti == (0 if d == 0 else 1):
            dbgdump(G, f"gMT{d}", mt, fl(mt), [128, 512], P=R)
            dbgdump(G, f"gu{d}", u_, u_[:], [128, 256], P=R)
            dbgdump(G, f"gw{d}", w_, w_[:], [64, 512], P=R)
            dbgdump(G, f"gvn{d}", vn_, vn_[:], [128, 256], P=R)
            dbgdump(G, f"gaT{d}", at_, fl(at_), [128, 512], P=R)
        if not second:
            R.op("dve", lambda e: e.tensor_tensor(oacc[:, ti, :], b[7][:, 64:320], oi_[:], ALU.add), [b[7].b, oi_.b], [oacc.b])
            return
        o_, q_, s_, y_ = ot[i], sq[i], ss[i], yo[i]
        R.op("dve", lambda e: e.tensor_tensor(o_[:], b[7][:, 64:320], oi_[:], ALU.add), [b[7].b, oi_.b], [o_.b])
        R.op("dve", lambda e: e.tensor_tensor(o_[:], o_[:], oacc[:, ti, :], ALU.add), [o_.b, oacc.b], [o_.b])
        R.op("dve", lambda e: e.tensor_tensor(q_[:], o_[:], o_[:], ALU.mult), [o_.b], [q_.b])
        R.op("dve", lambda e: e.reduce_sum(s_[:, 0:4], q_[:].rearrange("p (h q) -> p h q", h=4), AX.X), [q_.b], [s_.b])
        R.op("act", lambda e: e.activation(s_[:, 4:8], s_[:, 0:4], AF.Sqrt, bias=G.eps[:, :], scale=1.0 / 64.0), [s_.b, G.eps.b], [s_.b])
        R.op("dve", lambda e: e.reciprocal(s_[:, 4:8], s_[:, 4:8]), [s_.b], [s_.b])
        R.op("dve", lambda e: e.tensor_tensor(v4(o_), v4(o_), bc_ap(s_[:, 4:8], [[1, 4], [0, 64]]), ALU.mult), [o_.b, s_.b], [o_.b])
        R.op("dve", lambda e: e.tensor_tensor(v4(o_), v4(o_), bc_ap(rp[:, R_GDNNW:R_GDNNW + 64], [[0, 4], [1, 64]]), ALU.mult), [o_.b, rp.b], [o_.b])
        R.op("act", lambda e: e.activation(q_[:], pt_[:, 0:256], AF.Silu), [pt_.b], [q_.b])
        R.op("dve", lambda e: e.tensor_tensor(o_[:], o_[:], q_[:], ALU.mult), [o_.b, q_.b], [o_.b])
        def f(e):
            e.transpose(b[2][:, 0:128], o_[:, 0:128], ident)
            return e.transpose(b[2][:, 128:256], o_[:, 128:256], ident)
        R.op("pe", f, [o_.b, C.b], [b[2].b])
        R.op("act", lambda e: e.activation(y_[:].rearrange("p j t -> p (j t)"), b[2][:, 0:256], AF.Copy), [b[2].b], [y_.b])
        R.dma(G.yT[512:768, t0:t0 + 128].rearrange("(j p) t -> p j t", p=128), y_[:], reads=[y_.b], writes=[G.yT.b], q="pool")

    cnt = 0
    for st_ in range(NT):
        recs = []
        for d in range(2):
            ti = st_ if d == 0 else NT - 1 - st_
            second = (ti >= NT // 2) if d == 0 else (ti < NT // 2)
            recs.append(Rec())
            body(recs[-1], d, ti, d, 2 * d + st_ % 2, second)
        merge(P, recs)
    P.barrier()
    A.reset(m0)


def x_rows(G, l, name, tok0, r0, n):
    if l == 0:
        return (G.ctx_in if name == "ctx" else G.x_in)[r0:r0 + n, :], None
    t = G.xs[l % 2]
    return t[tok0 + r0:tok0 + r0 + n, :], t.b


def phase_c(G, l, streams, hT2):
    P, A = G.P, G.A
    m0 = A.mark()
    rp = G.rowp_sb
    wo = A.alloc("wo", [128, 8, D], BF16)
    xt = [A.alloc("cxt", [128, D]) for _ in range(2)]
    load_weight_bf16(G, wo, lambda k: G.w_out[l, k * 128:(k + 1) * 128, :], D, xt, 8)
    yb = [A.alloc("cyb", [128, 8, 512], BF16) for _ in range(2)]
    t1 = [A.alloc("ct1", [128, D]) for _ in range(2)]
    xn = [A.alloc("cxn", [128, D]) for _ in range(4)]
    st = [A.alloc("cst", [128, 2, 6]) for _ in range(2)]
    mv = [A.alloc("cmv", [128, 2]) for _ in range(2)]
    rstd = [A.alloc("crstd", [128, 1]) for _ in range(2)]
    ident = G.consts[:, K_ID, :]
    po = [[G.psum[0], G.psum[1]], [G.psum[2], G.psum[3]]]
    pT = [G.psum[4], G.psum[5]]
    cb = 0
    ct = 0
    for (name, tok0, L, s) in streams:
        bs = min(512, L)
        for b0 in range(0, L, bs):
            nt = bs // 128
            y_ = yb[cb % 2]
            cb += 1
            P.dma(y_[:, :, 0:bs], G.yT[:, tok0 + b0:tok0 + b0 + bs].rearrange("(k p) t -> p k t", p=128), reads=[G.yT.b], writes=[y_.b])
            for m in range(nt):
                q = ct % 2
                ct += 1
                x_, t_, pp = xt[q], t1[q], po[q]
                r0 = b0 + m * 128
                src, sb_ = x_rows(G, l, name, tok0, r0, 128)
                P.dma(x_[:], src, reads=[sb_] if sb_ is not None else [], writes=[x_.b])
                for half in range(2):
                    def f(e, half=half, y_=y_, m=m, pp=pp):
                        for k in range(8):
                            ins = e.matmul(pp[half][:, :], y_[:, k, m * 128:(m + 1) * 128], wo[:, k, half * 512:(half + 1) * 512],
                                           start=(k == 0), stop=(k == 7))
                        return ins
                    P.op("pe", f, [y_.b, wo.b], [pp[half].b])
                    P.op("dve", lambda e, half=half, t_=t_, pp=pp, s=s: e.tensor_tensor(
                        t_[:, half * 512:(half + 1) * 512], pp[half][:, :], G.gb[:, s, 0, half * 512:(half + 1) * 512], ALU.mult),
                        [pp[half].b, G.gb.b], [t_.b])
                P.op("dve", lambda e, t_=t_, x_=x_: e.scalar_tensor_tensor(t_[:], x_[:], DN_ALPHA, t_[:], ALU.mult, ALU.add), [x_.b, t_.b], [t_.b])
                ln_stats(G, t_, 128, st[q], mv[q], rstd[q])
                P.op("dve", lambda e, t_=t_, q=q: e.tensor_scalar(t_[:], t_[:], mv[q][:, 0:1], rstd[q][:, 0:1], ALU.subtract, ALU.mult),
                     [t_.b, mv[q].b, rstd[q].b], [t_.b])
                P.op("dve", lambda e, t_=t_: e.tensor_tensor(t_[:], t_[:], rp[:, R_LN1W:R_LN1W + D], ALU.mult), [t_.b, rp.b], [t_.b])
                P.op("dve", lambda e, t_=t_: e.tensor_tensor(t_[:], t_[:], rp[:, R_LN1B:R_LN1B + D], ALU.add), [t_.b, rp.b], [t_.b])
                P.dma(G.x1[tok0 + r0:tok0 + r0 + 128, :], t_[:], reads=[t_.b], writes=[G.x1.b], q="pool")
                ln_stats(G, t_, 128, st[q], mv[q], rstd[q])
                P.op("dve", lambda e, t_=t_, q=q, m=m: e.tensor_scalar(xn[m][:], t_[:], mv[q][:, 0:1], rstd[q][:, 0:1], ALU.subtract, ALU.mult),
                     [t_.b, mv[q].b, rstd[q].b], [xn[m].b])
            for k in range(8):
                p = pT[k % 2]
                def f(e, p=p, k=k, nt=nt):
                    for m in range(nt):
                        ins = e.transpose(p[:, m * 128:(m + 1) * 128], xn[m][:, k * 128:(k + 1) * 128], ident)
                    return ins
                P.op("pe", f, [xn[m].b for m in range(nt)] + [G.consts.b], [p.b])
                c0 = tok0 + b0
                P.op("act", lambda e, p=p, k=k, bs=bs, s=s, c0=c0: e.activation(
                    hT2[:, k, c0:c0 + bs], p[:, 0:bs], AF.Identity, bias=G.modc[:, 2, k, s:s + 1], scale=G.modc[:, 3, k, s:s + 1]),
                    [p.b, G.modc.b], [hT2.b])
    P.barrier()
    A.reset(m0)


def phase_d1(G, l, streams, hT2):
    P, A = G.P, G.A
    m0 = A.mark()
    cp = G.colp_sb
    wst = [A.alloc("dwst", [128, 8, 256]) for _ in range(2)]
    wab = [A.alloc("dwab", [128, 8, 256], BF16) for _ in range(2)]
    asb = [A.alloc("dasb", [128, 514]) for _ in range(2)]
    acc = [A.alloc("dacc", [128, 512]) for _ in range(2)]
    hc = [A.alloc("dhc", [128, 512], BF16) for _ in range(3)]
    up = G.ffn_up[l, :, :].rearrange("(k p) c -> p k c", p=128)
    pa = [[G.psum[0], G.psum[1], G.psum[2]], [G.psum[3], G.psum[4], G.psum[5]]]
    cnt = 0
    for c in range(DFF // 128):
        w_, wb_ = wst[c % 2], wab[c % 2]
        P.dma(w_[:, :, 0:128], up[:, :, c * 128:(c + 1) * 128], writes=[w_.b])
        P.dma(w_[:, :, 128:256], up[:, :, DFF + c * 128:DFF + (c + 1) * 128], writes=[w_.b])
        P.op("pool", lambda e, w_=w_, wb_=wb_: e.tensor_copy(wb_[:], w_[:]), [w_.b], [wb_.b])
        for (name, tok0, L, s) in streams:
            bs = min(512, L)
            for b0 in range(0, L, bs):
                i = cnt % 2
                cnt += 1
                a_, ac_, h_ = asb[i], acc[i], hc[cnt % 3]
                p0, p1, pb = pa[i]
                t0 = tok0 + b0
                lo = max(t0 - 1, tok0)
                hi = min(t0 + bs + 1, tok0 + L)
                jlo, jhi = lo - (t0 - 1), hi - (t0 - 1)
                half = (bs + 2) // 2
                segs = [(jlo, half + 1), (half - 1, jhi)] if bs == 512 else [(jlo, jhi)]
                for si, (j0, j1) in enumerate(segs):
                    pp = p0 if si == 0 else p1
                    def f(e, pp=pp, j0=j0, j1=j1, wb_=wb_, t0=t0):
                        for k in range(8):
                            ins = e.matmul(pp[:, 0:j1 - j0], wb_[:, k, 0:128], hT2[:, k, t0 - 1 + j0:t0 - 1 + j1], start=(k == 0), stop=(k == 7))
                        return ins
                    P.op("pe", f, [wb_.b, hT2.b], [pp.b])
                def f(e, pb=pb, wb_=wb_, t0=t0, bs=bs):
                    for k in range(8):
                        ins = e.matmul(pb[:, 0:bs], wb_[:, k, 128:256], hT2[:, k, t0:t0 + bs], start=(k == 0), stop=(k == 7))
                    return ins
                P.op("pe", f, [wb_.b, hT2.b], [pb.b])
                if jlo > 0:
                    P.op("pool", lambda e, a_=a_: e.memset(a_[:, 0:1], 0.0), [], [a_.b])
                if jhi < bs + 2:
                    P.op("pool", lambda e, a_=a_, bs=bs: e.memset(a_[:, bs + 1:bs + 2], 0.0), [], [a_.b])
                if len(segs) == 2:
                    (a0, a1), (b0_, b1_) = segs
                    P.op("act", lambda e, a_=a_, p0=p0, a0=a0, a1=a1: e.activation(a_[:, a0:a1], p0[:, 0:a1 - a0], AF.Copy), [p0.b], [a_.b])
                    P.op("act", lambda e, a_=a_, p1=p1, a1=a1, b0_=b0_, b1_=b1_: e.activation(
                        a_[:, a1:b1_], p1[:, a1 - b0_:b1_ - b0_], AF.Copy), [p1.b], [a_.b])
                else:
                    (a0, a1), = segs
                    P.op("act", lambda e, a_=a_, p0=p0, a0=a0, a1=a1: e.activation(a_[:, a0:a1], p0[:, 0:a1 - a0], AF.Copy), [p0.b], [a_.b])
                wo_ = C_FFNCW + c * 3
                P.op("dve", lambda e, a_=a_, ac_=ac_, bs=bs, wo_=wo_: e.tensor_scalar(ac_[:, 0:bs], a_[:, 0:bs], cp[:, wo_:wo_ + 1], None, ALU.mult),
                     [a_.b, cp.b], [ac_.b])
                for tap in (1, 2):
                    P.op("dve", lambda e, a_=a_, ac_=ac_, bs=bs, wo_=wo_, tap=tap: e.scalar_tensor_tensor(
                        ac_[:, 0:bs], a_[:, tap:tap + bs], cp[:, wo_ + tap:wo_ + tap + 1], ac_[:, 0:bs], ALU.mult, ALU.add), [a_.b, cp.b, ac_.b], [ac_.b])
                P.op("act", lambda e, ac_=ac_, bs=bs, c=c: e.activation(ac_[:, 0:bs], ac_[:, 0:bs], AF.Silu, bias=cp[:, C_FFNCB + c:C_FFNCB + c + 1]),
                     [ac_.b, cp.b], [ac_.b])
                P.op("dve", lambda e, ac_=ac_, h_=h_, pb=pb, bs=bs: e.tensor_tensor(h_[:, 0:bs], ac_[:, 0:bs], pb[:, 0:bs], ALU.mult), [ac_.b, pb.b], [h_.b])
                P.dma(G.hid[c * 128:(c + 1) * 128, t0:t0 + bs], h_[:, 0:bs], reads=[h_.b], writes=[G.hid.b], q="pool")
    P.barrier()
    A.reset(m0)


def phase_d2(G, l, streams, last):
    P, A = G.P, G.A
    m0 = A.mark()
    rp = G.rowp_sb
    NC_ = DFF // 128
    wd = A.alloc("wd", [128, NC_, D], BF16)
    stage = [A.alloc("wdstage", [128, D]) for _ in range(2)]
    load_weight_bf16(G, wd, lambda k: G.ffn_down[l, k * 128:(k + 1) * 128, :], D, stage, NC_)
    hb = [A.alloc("ehb", [128, NC_, 512], BF16) for _ in range(2)]
    xt = [A.alloc("ext", [128, D]) for _ in range(2)]
    t1 = [A.alloc("et1", [128, D]) for _ in range(2)]
    st = [A.alloc("est", [128, 2, 6]) for _ in range(2)]
    mv = [A.alloc("emv", [128, 2]) for _ in range(2)]
    rstd = [A.alloc("erstd", [128, 1]) for _ in range(2)]
    po = [[G.psum[0], G.psum[1]], [G.psum[2], G.psum[3]]]
    xnext = G.xs[(l + 1) % 2]
    cb = 0
    ct = 0
    for (name, tok0, L, s) in streams:
        bs = min(512, L)
        for b0 in range(0, L, bs):
            nt = bs // 128
            h_ = hb[cb % 2]
            cb += 1
            P.dma(h_[:, :, 0:bs], G.hid[:, tok0 + b0:tok0 + b0 + bs].rearrange("(c p) t -> p c t", p=128), reads=[G.hid.b], writes=[h_.b])
            for m in range(nt):
                q = ct % 2
                ct += 1
                x_, t_, pp = xt[q], t1[q], po[q]
                r0 = tok0 + b0 + m * 128
                P.dma(x_[:], G.x1[r0:r0 + 128, :], reads=[G.x1.b], writes=[x_.b])
                for half in range(2):
                    def f(e, half=half, h_=h_, m=m, pp=pp):
                        for c in range(NC_):
                            ins = e.matmul(pp[half][:, :], h_[:, c, m * 128:(m + 1) * 128], wd[:, c, half * 512:(half + 1) * 512],
                                           start=(c == 0), stop=(c == NC_ - 1))
                        return ins
                    P.op("pe", f, [h_.b, wd.b], [pp[half].b])
                    P.op("dve", lambda e, half=half, t_=t_, pp=pp, s=s: e.tensor_tensor(
                        t_[:, half * 512:(half + 1) * 512], pp[half][:, :], G.gb[:, s, 1, half * 512:(half + 1) * 512], ALU.mult),
                        [pp[half].b, G.gb.b], [t_.b])
                P.op("dve", lambda e, t_=t_, x_=x_: e.scalar_tensor_tensor(t_[:], x_[:], DN_ALPHA, t_[:], ALU.mult, ALU.add), [x_.b, t_.b], [t_.b])
                ln_stats(G, t_, 128, st[q], mv[q], rstd[q])
                P.op("dve", lambda e, t_=t_, q=q: e.tensor_scalar(t_[:], t_[:], mv[q][:, 0:1], rstd[q][:, 0:1], ALU.subtract, ALU.mult),
                     [t_.b, mv[q].b, rstd[q].b], [t_.b])
                P.op("dve", lambda e, t_=t_: e.tensor_tensor(t_[:], t_[:], rp[:, R_LN2W:R_LN2W + D], ALU.mult), [t_.b, rp.b], [t_.b])
                P.op("dve", lambda e, t_=t_: e.tensor_tensor(t_[:], t_[:], rp[:, R_LN2B:R_LN2B + D], ALU.add), [t_.b, rp.b], [t_.b])
                if last:
                    rr = b0 + m * 128
                    P.dma(G.out[rr:rr + 128, :], t_[:], reads=[t_.b], writes=[G.out.b], q="pool")
                else:
                    P.dma(xnext[r0:r0 + 128, :], t_[:], reads=[t_.b], writes=[xnext.b], q="pool")
    P.barrier()
    A.reset(m0)


def _col(v, nchunk):
    return np.ascontiguousarray(v.reshape(nchunk, 128).T)


def prep_inputs(inputs):
    f = lambda a: np.ascontiguousarray(np.asarray(a, dtype=np.float32))
    I = {k: f(v) for k, v in inputs.items()}
    colp = np.zeros((DEPTH, 128, NCOL), np.float32)
    rowp = np.zeros((DEPTH, 1, NROW), np.float32)
    poolw = np.zeros((DEPTH, 128, 2, 128), np.float32)
    gws = np.zeros((DEPTH, 128, 4, 128), np.float32)
    for l in range(DEPTH):
        cw = I["ssd_conv_w"][l]
        colp[l, :, C_SSDCW:C_SSDCW + 28] = cw.T.reshape(4, 128, 7).transpose(1, 0, 2).reshape(128, 28)
        colp[l, :, C_SSDCB:C_SSDCB + 4] = _col(I["ssd_conv_b"][l], 4)
        gw = I["gdn_conv_w"][l]
        colp[l, :, C_GDNCW:C_GDNCW + 42] = gw.T.reshape(6, 128, 7).transpose(1, 0, 2).reshape(128, 42)
        fw = I["ffn_conv_w"][l]
        colp[l, :, C_FFNCW:C_FFNCW + 66] = fw.T.reshape(22, 128, 3).transpose(1, 0, 2).reshape(128, 66)
        colp[l, :, C_FFNCB:C_FFNCB + 22] = _col(I["ffn_conv_b"][l], 22)
        colp[l, :, C_PSCALE:C_PSCALE + 2] = _col(I["pool_scale"][l], 2)
        colp[l, :, C_BMOD:C_BMOD + 48] = I["b_mod"][l].reshape(6, 8, 128).transpose(2, 0, 1).reshape(128, 48)
        r = rowp[l, 0]
        r[R_SSDNW:R_SSDNW + 256] = I["ssd_norm_w"][l]
        r[R_GDNNW:R_GDNNW + 64] = I["gdn_norm_w"][l]
        r[R_GLNW:R_GLNW + 256] = I["gmlp_ln_w"][l]
        r[R_GLNB:R_GLNB + 256] = I["gmlp_ln_b"][l]
        r[R_LN1W:R_LN1W + 1024] = I["ln1_w"][l]
        r[R_LN1B:R_LN1B + 1024] = I["ln1_b"][l]
        r[R_LN2W:R_LN2W + 1024] = I["ln2_w"][l]
        r[R_LN2B:R_LN2B + 1024] = I["ln2_b"][l]
        r[R_SSDD:R_SSDD + 256] = np.repeat(I["ssd_d"][l], 64)
        r[R_GBS:R_GBS + 512] = I["gmlp_bs"][l].reshape(-1)
        r[R_SDTB:R_SDTB + 8] = I["ssd_dt_bias"][l].reshape(-1)
        r[R_SALOG:R_SALOG + 8] = I["ssd_a_log"][l].reshape(-1)
        r[R_GDTB:R_GDTB + 8] = I["gdn_dt_bias"][l].reshape(-1)
        r[R_GALOG:R_GALOG + 8] = I["gdn_a_log"][l].reshape(-1)
        r[R_BG1:R_BG1 + 1024] = I["b_mod"][l][2048:3072]
        r[R_BG2:R_BG2 + 1024] = I["b_mod"][l][5120:6144]
        pw = I["pool_w"][l]
        for g in range(4):
            j, h = g // 2, g % 2
            poolw[l, h * 64:(h + 1) * 64, j, h * 64:(h + 1) * 64] = pw[g]
        gws[l] = I["gmlp_ws"][l].transpose(2, 0, 1)
    consts = make_consts()

    def pinv(RW):
        o = np.zeros((128, 2, RW), np.float32)
        pos = np.arange(RW)
        for jc in range(2):
            for hh in range(2):
                w = (2, 4, 8, 16)[2 * jc + hh]
                lo = np.clip(pos - w // 2, 0, RW)
                hi = np.clip(pos + w - w // 2, 0, RW)
                o[hh * 64:(hh + 1) * 64, jc, :] = 1.0 / (hi - lo).astype(np.float32)
        return o
    shared = dict(consts=consts, w_mod=I["w_mod"], w_in=I["w_in"], w_out=I["w_out"], ffn_up=I["ffn_up"],
                  ffn_down=I["ffn_down"], colp=colp, rowp=rowp, poolw=poolw, gws=gws,
                  pinv_g=pinv(64), pinv_c=pinv(256))
    maps = []
    for core in range(8):
        b = core % 4
        crep = np.zeros((128, 2, 8, 128), np.float32)
        crep[:, 0] = np.repeat(_col(I["c"][b], 8)[:, :, None], 128, axis=2)
        crep[:, 1] = np.repeat(_col(I["c_ctx"], 8)[:, :, None], 128, axis=2)
        m = dict(shared)
        m.update(x_in=I["x"][b], ctx_in=I["ctx"][b], crep=crep)
        maps.append(m)
    return maps


_NC_CACHE = {}


def kernel(**inputs):
    maps = prep_inputs(inputs)
    if "nc" not in _NC_CACHE:
        _NC_CACHE["nc"] = build()
    res = run_bass_kernel_spmd(_NC_CACHE["nc"], maps, core_ids=list(range(8)))
    out = np.stack([np.asarray(res.results[b]["out"], dtype=np.float32) for b in range(4)], axis=0)
    return out
```

```python
import numpy as np
import concourse.bass as bass
import concourse.mybir as mybir
from concourse.bass_utils import run_bass_kernel_spmd

F32 = mybir.dt.float32
BF16 = mybir.dt.bfloat16
ALU = mybir.AluOpType
AF = mybir.ActivationFunctionType
AX = mybir.AxisListType

import os
SSD_STOP = int(os.environ.get('SSD_STOP', '99'))
GDN_ROUNDS = int(os.environ.get('GDN_ROUNDS', '5'))
SEG = 16000
DSEG = 1000
DMAK = 8


class Buf:
    __slots__ = ("w", "r", "name")

    def __init__(self, name=""):
        self.w = None
        self.r = {}
        self.name = name


class Prog:
    ENGS = ["pe", "act", "dve", "pool", "sp"]

    def __init__(self, nc):
        self.nc = nc
        self.ops = {e: [] for e in self.ENGS}
        self.count = {e: 0 for e in self.ENGS}
        self.sems = {}
        self.waited = {e: {} for e in self.ENGS}
        self.dma_n = {e: 0 for e in self.ENGS}
        self.last = {}

    def _sem(self, key):
        if key not in self.sems:
            self.sems[key] = self.nc.alloc_semaphore("s_" + "_".join(map(str, key)))
        return self.sems[key]

    def _need(self, eng, waits, tok):
        if tok is None:
            return
        key, val = tok
        if key[0] == "c" and key[1] == "pe" and eng == "pe":
            return
        if self.waited[eng].get(key, 0) >= val:
            return
        if waits.get(key, 0) < val:
            waits[key] = val

    def op(self, eng, fn, reads=(), writes=(), dma=False, extra=()):
        waits = {}
        for b in reads:
            self._need(eng, waits, b.w)
        for b in writes:
            self._need(eng, waits, b.w)
            for k, v in b.r.items():
                self._need(eng, waits, (k, v))
        for t in extra:
            self._need(eng, waits, t)
        if dma:
            n = self.dma_n[eng]
            self.dma_n[eng] += 1
            s, r = n % DMAK, n // DMAK
            if r >= 1:
                pk = ("d", eng, s, (r - 1) // DSEG)
                self._need(eng, waits, (pk, 16 * (((r - 1) % DSEG) + 1)))
            tok = (("d", eng, s, r // DSEG), 16 * ((r % DSEG) + 1))
            inc = 16
        elif fn is None:
            tok = None
            inc = 0
        else:
            n = self.count[eng]
            self.count[eng] += 1
            tok = (("c", eng, n // SEG), (n % SEG) + 1)
            inc = 1
        for k, v in waits.items():
            self.waited[eng][k] = v
            self._sem(k)
        if tok is not None:
            self._sem(tok[0])
            self.last[tok[0]] = tok[1]
        self.ops[eng].append((list(waits.items()), fn, tok, inc))
        if tok is not None:
            for b in reads:
                b.r[tok[0]] = tok[1]
            for b in writes:
                b.w = tok
                b.r = {}
        return tok

    def dma(self, out, in_, reads=(), writes=(), q="sp", **kw):
        return self.op(q, lambda e: e.dma_start(out=out, in_=in_, **kw), reads, writes, dma=True)

    def barrier(self):
        toks = list(self.last.items())
        for e in self.ENGS:
            self.op(e, None, extra=toks)

    def emit(self):
        nc = self.nc
        with nc.Block() as block:
            def run(name):
                def body(e):
                    for waits, fn, tok, inc in self.ops[name]:
                        for k, v in waits:
                            e.wait_ge(self.sems[k], v)
                        if fn is not None:
                            ins = fn(e)
                            ins.then_inc(self.sems[tok[0]], inc)
                return body
            block.tensor(run("pe"))
            block.scalar(run("act"))
            block.vector(run("dve"))
            block.gpsimd(run("pool"))
            block.sync(run("sp"))


class Rec:
    def __init__(self):
        self.calls = []

    def op(self, eng, fn, reads=(), writes=(), dma=False, extra=()):
        self.calls.append((eng, fn, tuple(reads), tuple(writes), dma, tuple(extra)))

    def dma(self, out, in_, reads=(), writes=(), q="sp", **kw):
        self.op(q, lambda e: e.dma_start(out=out, in_=in_, **kw), reads, writes, dma=True)


def merge(P, recs):
    idx = [0] * len(recs)
    live = True
    while live:
        live = False
        for k, r in enumerate(recs):
            if idx[k] < len(r.calls):
                P.op(*r.calls[idx[k]])
                idx[k] += 1
                live = True


class T:
    def __init__(self, h, name=""):
        self.h = h
        self.b = Buf(name)

    def __getitem__(self, k):
        return self.h[k]


def _dtsize(dt):
    return 2 if dt == BF16 else 4


class Arena:
    def __init__(self, nc, base=16512, limit=229344):
        self.nc, self.base, self.limit, self.top, self.n = nc, base, limit, base, 0

    def alloc(self, name, shape, dt=F32):
        el = 1
        for s in shape[1:]:
            el *= s
        size = (el * _dtsize(dt) + 63) // 64 * 64
        off = self.top
        self.top += size
        assert self.top <= self.limit, (name, self.top)
        self.n += 1
        return T(self.nc.alloc_sbuf_tensor_at(f"{name}{self.n}", list(shape), dt, offset=off), name)

    def mark(self):
        return self.top

    def reset(self, m):
        self.top = m


D = 1024
LC = 256
LL = 4096
TALL = LC + LL
DEPTH = 4
NIN = 2584
DFF = 2816
DN_ALPHA = (2 * DEPTH) ** 0.25
LN_EPS = 1e-6
O_Z, O_XBC, O_DT, O_POOL, O_QKV, O_GATE, O_A, O_B, O_UV = 0, 256, 768, 776, 1032, 1800, 2056, 2064, 2072
FM_GROUPS = [(O_XBC, 512), (O_POOL, 256), (O_QKV, 768), (O_UV, 512)]
NFM = 2048
NTM = 536
C_SSDCW, C_SSDCB, C_GDNCW, C_FFNCW, C_FFNCB, C_PSCALE, C_BMOD = 0, 28, 32, 74, 140, 162, 164
NCOL = 164 + 48
R_SSDNW, R_GDNNW, R_GLNW, R_GLNB, R_LN1W, R_LN1B, R_LN2W, R_LN2B = 0, 256, 320, 576, 832, 1856, 2880, 3904
R_SSDD, R_GBS, R_SDTB, R_SALOG, R_GDTB, R_GALOG, R_BG1, R_BG2 = 4928, 5184, 5696, 5704, 5712, 5720, 5728, 6752
NROW = 7776
K_ID, K_ONES, K_LF, K_LB, K_NMF, K_NMB, K_LF2, K_LB2, K_NI2F, K_NI2B, K_NS2F, K_NS2B, K_BO2, K_SEL0, K_SEL1 = range(15)
NCONST = 15
NEG = -30000.0


def make_consts():
    k = np.arange(128)[:, None]
    m = np.arange(128)[None, :]
    same = (k // 64) == (m // 64)
    c = np.zeros((128, NCONST, 128), np.float32)
    c[:, K_ID] = (k == m)
    c[:, K_ONES] = 1.0
    c[:, K_LF] = (k <= m)
    c[:, K_LB] = (k >= m)
    c[:, K_NMF] = np.where(m >= k, 0.0, NEG)
    c[:, K_NMB] = np.where(m <= k, 0.0, NEG)
    c[:, K_LF2] = (k <= m) & same
    c[:, K_LB2] = (k >= m) & same
    c[:, K_NI2F] = np.where((m >= k) & same, 0.0, NEG)
    c[:, K_NI2B] = np.where((m <= k) & same, 0.0, NEG)
    c[:, K_NS2F] = np.where((k > m) & same, 0.0, NEG)
    c[:, K_NS2B] = np.where((k < m) & same, 0.0, NEG)
    c[:, K_BO2] = same
    c[:, K_SEL0] = (k < 64) * np.ones_like(m)
    c[:, K_SEL1] = (k >= 64) * np.ones_like(m)
    return c


class Ctx:
    pass


def build(nlayers=DEPTH, stop_after=None, dbg=False, mixers=None, only=None):
    nc = bass.Bass("TRN2", target_bir_lowering=False)
    G = Ctx()
    G.nc = nc
    P = Prog(nc)
    G.P = P

    def din(name, shape):
        return nc.dram_tensor(name, list(shape), F32, kind="ExternalInput")

    G.x_in = din("x_in", [LL, D])
    G.ctx_in = din("ctx_in", [LC, D])
    G.crep = din("crep", [128, 2, 8, 128])
    G.consts_d = din("consts", [128, NCONST, 128])
    G.w_mod = din("w_mod", [DEPTH, D, 6 * D])
    G.w_in = din("w_in", [DEPTH, D, NIN])
    G.w_out = din("w_out", [DEPTH, D, D])
    G.ffn_up = din("ffn_up", [DEPTH, D, 2 * DFF])
    G.ffn_down = din("ffn_down", [DEPTH, DFF, D])
    G.colp = din("colp", [DEPTH, 128, NCOL])
    G.rowp = din("rowp", [DEPTH, 1, NROW])
    G.poolw = din("poolw", [DEPTH, 128, 2, 128])
    G.gws = din("gws", [DEPTH, 128, 4, 128])
    G.pinv_g = din("pinv_g", [128, 2, 64])
    G.pinv_c = din("pinv_c", [128, 2, 256])
    G.out = T(nc.dram_tensor("out", [LL, D], F32, kind="ExternalOutput"), "out")
    G.xs = [T(nc.dram_tensor(f"xs{i}", [TALL, D], F32), f"xs{i}") for i in range(2)]
    G.x1 = T(nc.dram_tensor("x1s", [TALL, D], F32), "x1s")
    G.projF = T(nc.dram_tensor("projF", [NFM, TALL], F32), "projF")
    G.projT = T(nc.dram_tensor("projT", [TALL, NTM], F32), "projT")
    G.convF = T(nc.dram_tensor("convF", [1280, TALL], F32), "convF")
    G.yT = T(nc.dram_tensor("yT", [D, TALL], BF16), "yT")
    G.hid = T(nc.dram_tensor("hid", [DFF, TALL], BF16), "hid")
    G.dbg = {}
    if dbg:
        G.dbg["projF"] = T(nc.dram_tensor("d_projF", [NFM, TALL], F32, kind="ExternalOutput"))
        G.dbg["projT"] = T(nc.dram_tensor("d_projT", [TALL, NTM], F32, kind="ExternalOutput"))
        G.dbg["mod"] = T(nc.dram_tensor("d_mod", [128, 64], F32, kind="ExternalOutput"))
        G.dbg["gb"] = T(nc.dram_tensor("d_gb", [128, 4096], F32, kind="ExternalOutput"))
        G.dbg["yT"] = T(nc.dram_tensor("d_yT", [D, TALL], BF16, kind="ExternalOutput"))
        G.dbg["convF"] = T(nc.dram_tensor("d_convF", [1280, TALL], F32, kind="ExternalOutput"))
        G.dbg["x1"] = T(nc.dram_tensor("d_x1", [TALL, D], F32, kind="ExternalOutput"))
        G.dbg["x2"] = T(nc.dram_tensor("d_x2", [TALL, D], F32, kind="ExternalOutput"))
        G.dbg["st"] = T(nc.dram_tensor("d_st", [64, 2 * 4 * 64], F32, kind="ExternalOutput"))

    A = Arena(nc)
    G.A = A
    G.psum = [T(nc.alloc_psum_tensor(f"ps{i}", [128, 512], F32), f"ps{i}") for i in range(8)]
    G.consts = A.alloc("consts", [128, NCONST, 128])
    G.csil = A.alloc("csil", [128, 2, 8, 128])
    G.colp_sb = A.alloc("colp", [128, NCOL])
    G.rowp_sb = A.alloc("rowp", [128, NROW])
    G.modc = A.alloc("modc", [128, 4, 8, 2])
    G.gb = A.alloc("gb", [128, 2, 2, 1024])
    G.eps = A.alloc("eps", [128, 1])
    G.sst = [A.alloc("sst", [64, 4, 64]) for _ in range(2)]
    G.gst = [A.alloc("gst", [64, 4, 64]) for _ in range(2)]
    if mixers is not None:
        G.mixers = mixers
    G.dumps_on = dbg
    G.dumps = {}

    P.dma(G.consts[:], G.consts_d[:, :, :], writes=[G.consts.b])
    P.dma(G.csil[:], G.crep[:, :, :, :], writes=[G.csil.b])
    P.op("act", lambda e: e.activation(G.csil[:], G.csil[:], AF.Silu), [G.csil.b], [G.csil.b])
    P.op("dve", lambda e: e.memset(G.eps[:], LN_EPS), [], [G.eps.b])

    streams = [("ctx", 0, LC, 1), ("lat", LC, LL, 0)]
    if only is not None:
        streams = [st_ for st_ in streams if st_[0] in only]
    for l in range(nlayers):
        G.l = l
        xin = G.xs[l % 2]
        mod_phase(G, l)
        if stop_after == "mod":
            break
        phase_a(G, l, streams)
        if stop_after == "A":
            break
        prep_pass(G, l, streams)
        mix = G.mixers if hasattr(G, "mixers") else ("pool", "gmlp", "ssd", "gdn")
        for d_ in range(2):
            P.op("dve", lambda e, d_=d_: e.memset(G.sst[d_][:], 0.0), [], [G.sst[d_].b])
            P.op("dve", lambda e, d_=d_: e.memset(G.gst[d_][:], 0.0), [], [G.gst[d_].b])
        for st_ in streams:
            if "pool" in mix:
                pool_mixer(G, l, st_)
            if "gmlp" in mix and "ssd" in mix:
                mg = A.mark()
                gt_ = gmlp_setup(G, l)
                ssd_mixer(G, l, st_, co_tile=gt_)
                A.reset(mg)
            else:
                if "gmlp" in mix:
                    gmlp_mixer(G, l, st_)
                if "ssd" in mix:
                    ssd_mixer(G, l, st_)
            if "gdn" in mix:
                gdn_mixer(G, l, st_)
        if stop_after == "mix":
            break
        last = (l == DEPTH - 1)
        cd_streams = [st_ for st_ in streams if not (last and st_[0] == "ctx")]
        mC = A.mark()
        hT2 = A.alloc("hT2", [128, 8, TALL], BF16)
        phase_c(G, l, cd_streams, hT2)
        phase_d1(G, l, cd_streams, hT2)
        A.reset(mC)
        phase_d2(G, l, cd_streams, last)

    if dbg:
        P.barrier()
        m0 = A.mark()
        tlo = min(st_[1] for st_ in streams)
        thi = max(st_[1] + st_[2] for st_ in streams)
        TW = thi - tlo
        tmp = A.alloc("dbgtmp", [128, TALL])
        tb = A.alloc("dbgtb", [128, TALL], BF16)
        P.dma(G.dbg["mod"][:, :], G.modc[:].rearrange("p a k s -> p (a k s)"), reads=[G.modc.b], writes=[G.dbg["mod"].b], q="pool")
        P.dma(G.dbg["gb"][:, :], G.gb[:].rearrange("p s g d -> p (s g d)"), reads=[G.gb.b], writes=[G.dbg["gb"].b], q="pool")
        if stop_after == "A":
            for r in range(NFM // 128):
                P.dma(tmp[:, 0:TW], G.projF[r * 128:(r + 1) * 128, tlo:thi], reads=[G.projF.b], writes=[tmp.b])
                P.dma(G.dbg["projF"][r * 128:(r + 1) * 128, tlo:thi], tmp[:, 0:TW], reads=[tmp.b], writes=[G.dbg["projF"].b], q="pool")
            for r in range(tlo // 128, thi // 128):
                P.dma(tmp[:, 0:NTM], G.projT[r * 128:(r + 1) * 128, :], reads=[G.projT.b], writes=[tmp.b])
                P.dma(G.dbg["projT"][r * 128:(r + 1) * 128, :], tmp[:, 0:NTM], reads=[tmp.b], writes=[G.dbg["projT"].b], q="pool")
        if stop_after is None:
            for r in range(tlo // 128, thi // 128):
                P.dma(tmp[:, 0:D], G.x1[r * 128:(r + 1) * 128, :], reads=[G.x1.b], writes=[tmp.b])
                P.dma(G.dbg["x1"][r * 128:(r + 1) * 128, :], tmp[:, 0:D], reads=[tmp.b], writes=[G.dbg["x1"].b], q="pool")
                P.dma(tmp[:, 0:D], G.xs[nlayers % 2][r * 128:(r + 1) * 128, :], reads=[G.xs[nlayers % 2].b], writes=[tmp.b])
                P.dma(G.dbg["x2"][r * 128:(r + 1) * 128, :], tmp[:, 0:D], reads=[tmp.b], writes=[G.dbg["x2"].b], q="pool")
        if stop_after == "mix":
            for d_ in range(2):
                P.dma(G.dbg["st"][:, d_ * 256:(d_ + 1) * 256], G.sst[d_][:].rearrange("p h d -> p (h d)"), reads=[G.sst[d_].b], writes=[G.dbg["st"].b], q="pool")
            rows = dict(ssd=(0, 2), pool=(2, 4), gdn=(4, 6), gmlp=(6, 8))
            for mname in (G.mixers if hasattr(G, "mixers") else rows.keys()):
                for r in range(*rows[mname]):
                    P.dma(tb[:, 0:TW], G.yT[r * 128:(r + 1) * 128, tlo:thi], reads=[G.yT.b], writes=[tb.b])
                    P.dma(G.dbg["yT"][r * 128:(r + 1) * 128, tlo:thi], tb[:, 0:TW], reads=[tb.b], writes=[G.dbg["yT"].b], q="pool")
            for r in range(1280 // 128):
                P.dma(tmp[:, 0:TW], G.convF[r * 128:(r + 1) * 128, tlo:thi], reads=[G.convF.b], writes=[tmp.b])
                P.dma(G.dbg["convF"][r * 128:(r + 1) * 128, tlo:thi], tmp[:, 0:TW], reads=[tmp.b], writes=[G.dbg["convF"].b], q="pool")
        P.op("pool", None, reads=[v.b for v in G.dbg.values()])
        A.reset(m0)
    P.op("pool", None, reads=[G.out.b])
    P.emit()
    return nc


def mod_phase(G, l):
    nc, P, A = G.nc, G.P, G.A
    m0 = A.mark()
    P.dma(G.colp_sb[:], G.colp[l, :, :], writes=[G.colp_sb.b])
    P.dma(G.rowp_sb[:], G.rowp[l, 0:1, :].partition_broadcast(128), writes=[G.rowp_sb.b])
    wm = [A.alloc("wm", [128, 8, 512]) for _ in range(2)]
    pc = G.psum[0]
    pg = [G.psum[1], G.psum[2]]
    wsrc = G.w_mod[l, :, :].rearrange("(k p) c -> p k c", p=128)
    colvec = {0: 0, 1: 1, 3: 2, 4: 3}
    for n in range(12):
        w = wm[n % 2]
        P.dma(w[:], wsrc[:, :, n * 512:(n + 1) * 512], writes=[w.b])
        vec, half = n // 2, n % 2
        if vec in colvec:
            a = colvec[vec]
            for j in range(4):
                kc = half * 4 + j

                def f(e, w=w, j=j, a=a, kc=kc):
                    for k in range(8):
                        ins = e.matmul(pc[:, (a * 8 + kc) * 2:(a * 8 + kc) * 2 + 2], w[:, k, j * 128:(j + 1) * 128],
                                       G.csil[:, :, k, 0], start=(k == 0), stop=(k == 7))
                    return ins
                P.op("pe", f, [w.b, G.csil.b], [pc.b])
        else:
            gi = 0 if vec == 2 else 1
            boff = R_BG1 if gi == 0 else R_BG2
            for s in range(2):
                def f(e, w=w, s=s):
                    for k in range(8):
                        ins = e.matmul(pg[s][:, :], G.csil[:, s, k, :], w[:, k, :], start=(k == 0), stop=(k == 7))
                    return ins
                P.op("pe", f, [w.b, G.csil.b], [pg[s].b])
                P.op("dve", lambda e, s=s, gi=gi, half=half, boff=boff: e.tensor_tensor(
                    G.gb[:, s, gi, half * 512:(half + 1) * 512], pg[s][:, :],
                    G.rowp_sb[:, boff + half * 512: boff + (half + 1) * 512], ALU.add),
                    [pg[s].b, G.rowp_sb.b], [G.gb.b])
    bm = G.colp_sb[:, C_BMOD:C_BMOD + 48].rearrange("p (v k) -> p v k", v=6)
    for vec, a in colvec.items():
        for s in range(2):
            P.op("dve", lambda e, vec=vec, a=a, s=s: e.tensor_tensor(
                G.modc[:, a, :, s], pc[:, a * 16:(a + 1) * 16].rearrange("p (k s) -> p k s", s=2)[:, :, s],
                bm[:, vec, :], ALU.add), [pc.b, G.colp_sb.b], [G.modc.b])
    for a in (1, 3):
        P.op("dve", lambda e, a=a: e.tensor_scalar_add(G.modc[:, a, :, :], G.modc[:, a, :, :], 1.0), [G.modc.b], [G.modc.b])
    P.barrier()
    A.reset(m0)


def ln_stats(G, xt, np_, st, mv, rstd):
    P = G.P
    def f(e):
        e.bn_stats(st[0:np_, 0, :], xt[0:np_, 0:512])
        return e.bn_stats(st[0:np_, 1, :], xt[0:np_, 512:1024])
    P.op("dve", f, [xt.b], [st.b])
    P.op("dve", lambda e: e.bn_aggr(mv[0:np_, :], st[0:np_, :, :].rearrange("p a b -> p (a b)")), [st.b], [mv.b])
    P.op("act", lambda e: e.activation(rstd[0:np_, :], mv[0:np_, 1:2], AF.Sqrt, bias=G.eps[0:np_, :]), [mv.b, G.eps.b], [rstd.b])
    P.op("dve", lambda e: e.reciprocal(rstd[0:np_, :], rstd[0:np_, :]), [rstd.b], [rstd.b])


def load_weight_bf16(G, dst, src_rows, ncols, stage, nk):
    P = G.P
    for k in range(nk):
        s = stage[k % len(stage)]
        P.dma(s[:, 0:ncols], src_rows(k), writes=[s.b])
        P.op("pool", lambda e, s=s, k=k: e.tensor_copy(dst[:, k, 0:ncols], s[:, 0:ncols]), [s.b], [dst.b])


def phase_a(G, l, streams):
    nc, P, A = G.nc, G.P, G.A
    m0 = A.mark()
    wi = A.alloc("wi", [128, 8, NIN], BF16)
    stage = [A.alloc("wstage", [128, NIN]) for _ in range(2)]
    load_weight_bf16(G, wi, lambda k: G.w_in[l, k * 128:(k + 1) * 128, :], NIN, stage, 8)
    xt = [A.alloc("xt", [128, D]) for _ in range(2)]
    xn = [A.alloc("xn", [128, D]) for _ in range(4)]
    st = [A.alloc("st", [128, 2, 6]) for _ in range(2)]
    mv = [A.alloc("mv", [128, 2]) for _ in range(2)]
    rstd = [A.alloc("rstd", [128, 1]) for _ in range(2)]
    hT = [A.alloc("hT", [128, 8, 512], BF16) for _ in range(2)]
    oF = [A.alloc("oF", [128, 512]) for _ in range(3)]
    oT = [A.alloc("oT", [128, NTM]) for _ in range(2)]
    pT = [G.psum[0], G.psum[1]]
    pF = [G.psum[2], G.psum[3]]
    pTa = [G.psum[4], G.psum[5]]
    pTb = [G.psum[6], G.psum[7]]
    ident = G.consts[:, K_ID, :]
    cnt = dict(t=0, b=0, f=0, o=0)
    xsrc = G.xs[l % 2]
    for (name, tok0, L, s) in streams:
        bs = 512 if L >= 512 else L
        for b0 in range(0, L, bs):
            nt = bs // 128
            W = bs
            h = hT[cnt["b"] % 2]
            cnt["b"] += 1
            for m in range(nt):
                t = xt[cnt["t"] % 2]
                q = cnt["t"] % 2
                cnt["t"] += 1
                r0 = b0 + m * 128
                if l == 0:
                    src = (G.ctx_in if name == "ctx" else G.x_in)[r0:r0 + 128, :]
                    P.dma(t[:], src, writes=[t.b])
                else:
                    P.dma(t[:], xsrc[tok0 + r0: tok0 + r0 + 128, :], reads=[xsrc.b], writes=[t.b])
                ln_stats(G, t, 128, st[q], mv[q], rstd[q])
                P.op("dve", lambda e, t=t, q=q, m=m: e.tensor_scalar(xn[m][:], t[:], mv[q][:, 0:1], rstd[q][:, 0:1],
                                                                    ALU.subtract, ALU.mult), [t.b, mv[q].b, rstd[q].b], [xn[m].b])
            for k in range(8):
                p = pT[k % 2]

                def f(e, p=p, k=k, nt=nt):
                    for m in range(nt):
                        ins = e.transpose(p[:, m * 128:(m + 1) * 128], xn[m][:, k * 128:(k + 1) * 128], ident)
                    return ins
                P.op("pe", f, [xn[m].b for m in range(nt)] + [G.consts.b], [p.b])
                P.op("act", lambda e, p=p, k=k, h=h, W=W, s=s: e.activation(
                    h[:, k, 0:W], p[:, 0:W], AF.Identity, bias=G.modc[:, 0, k, s:s + 1], scale=G.modc[:, 1, k, s:s + 1]),
                    [p.b, G.modc.b], [h.b])
            row = 0
            for (c0, n) in FM_GROUPS:
                for j in range(n // 128):
                    p = pF[cnt["f"] % 2]
                    o = oF[cnt["f"] % 3]
                    ev = "act" if cnt["f"] % 2 == 0 else "dve"
                    cnt["f"] += 1
                    cc = c0 + j * 128

                    def f(e, p=p, cc=cc, h=h, W=W):
                        for k in range(8):
                            ins = e.matmul(p[:, 0:W], wi[:, k, cc:cc + 128], h[:, k, 0:W], start=(k == 0), stop=(k == 7))
                        return ins
                    P.op("pe", f, [wi.b, h.b], [p.b])
                    if ev == "act":
                        P.op("act", lambda e, p=p, o=o, W=W: e.activation(o[:, 0:W], p[:, 0:W], AF.Copy), [p.b], [o.b])
                    else:
                        P.op("dve", lambda e, p=p, o=o, W=W: e.tensor_copy(o[:, 0:W], p[:, 0:W]), [p.b], [o.b])
                    P.dma(G.projF[row:row + 128, tok0 + b0: tok0 + b0 + W], o[:, 0:W], reads=[o.b], writes=[G.projF.b], q="pool")
                    row += 128
            for m in range(nt):
                pa = pTa[cnt["o"] % 2]
                pb = pTb[cnt["o"] % 2]
                o = oT[cnt["o"] % 2]
                cnt["o"] += 1

                def f(e, pa=pa, pb=pb, h=h, m=m):
                    for k in range(8):
                        lw = h[:, k, m * 128:(m + 1) * 128]
                        e.matmul(pa[:, 0:256], lw, wi[:, k, O_Z:O_Z + 256], start=(k == 0), stop=(k == 7))
                        e.matmul(pb[:, 0:272], lw, wi[:, k, O_GATE:O_GATE + 272], start=(k == 0), stop=(k == 7))
                    for k in range(8):
                        lw = h[:, k, m * 128:(m + 1) * 128]
                        ins = e.matmul(pa[:, 256:264], lw, wi[:, k, O_DT:O_DT + 8], start=(k == 0), stop=(k == 7))
                    return ins
                P.op("pe", f, [wi.b, h.b], [pa.b, pb.b])
                P.op("act", lambda e, pa=pa, o=o: e.activation(o[:, 0:264], pa[:, 0:264], AF.Copy), [pa.b], [o.b])
                P.op("dve", lambda e, pb=pb, o=o: e.tensor_copy(o[:, 264:536], pb[:, 0:272]), [pb.b], [o.b])
                r0 = tok0 + b0 + m * 128
                P.dma(G.projT[r0:r0 + 128, :], o[:], reads=[o.b], writes=[G.projT.b], q="pool")
    P.barrier()
    A.reset(m0)


def dbgdump(G, name, t, ap, shape, dt=F32, P=None):
    if not getattr(G, "dumps_on", False) or name in G.dumps:
        return
    o = T(G.nc.dram_tensor("dd_" + name, list(shape), dt, kind="ExternalOutput"))
    G.dumps[name] = o
    nd = len(shape)
    P = P or G.P
    P.dma(o[tuple(slice(None) for _ in range(nd))], ap, reads=[t.b], writes=[o.b], q="pool")
    P.op("pool", None, reads=[o.b])


def bc_ap(ap, dims):
    return bass.AP(ap.tensor, ap.offset, [list(ap.ap[0])] + [list(d) for d in dims])


def prep_pass(G, l, streams):
    P, A = G.P, G.A
    m0 = A.mark()
    xin = [A.alloc("cin", [128, LL + 6]) for _ in range(2)]
    acc = [A.alloc("cacc", [128, LL]) for _ in range(2)]
    sq = [A.alloc("csq", [128, 512]) for _ in range(2)]
    rn = [A.alloc("crn", [128, 512]) for _ in range(2)]
    ps = [G.psum[0], G.psum[1]]
    bo2 = G.consts[:, K_BO2, :]
    chunks = []
    for j in range(4):
        chunks.append((j * 128, j * 128, C_SSDCW + j * 7, C_SSDCB + j, "p"))
    for j in range(6):
        chunks.append((768 + j * 128, 512 + j * 128, C_GDNCW + j * 7, None, "q" if j < 2 else ("k" if j < 4 else "p")))
    cnt = 0
    sc = 0
    for (name, tok0, L, s) in streams:
        for (src, dst, wo, bo, kind) in chunks:
            t = xin[cnt % 2]
            a = acc[cnt % 2]
            cnt += 1
            w = G.colp_sb
            P.op("pool", lambda e, t=t: e.memset(t[:, 0:3], 0.0), [], [t.b])
            P.op("pool", lambda e, t=t, L=L: e.memset(t[:, L + 3:L + 6], 0.0), [], [t.b])
            P.dma(t[:, 3:L + 3], G.projF[src:src + 128, tok0:tok0 + L], reads=[G.projF.b], writes=[t.b])
            P.op("dve", lambda e, t=t, a=a, L=L, wo=wo: e.tensor_scalar(a[:, 0:L], t[:, 0:L], w[:, wo:wo + 1], None, ALU.mult),
                 [t.b, w.b], [a.b])
            for tap in range(1, 7):
                P.op("dve", lambda e, t=t, a=a, L=L, wo=wo, tap=tap: e.scalar_tensor_tensor(
                    a[:, 0:L], t[:, tap:tap + L], w[:, wo + tap:wo + tap + 1], a[:, 0:L], ALU.mult, ALU.add),
                    [t.b, w.b, a.b], [a.b])
            if bo is not None:
                P.op("act", lambda e, a=a, L=L, bo=bo: e.activation(a[:, 0:L], a[:, 0:L], AF.Silu, bias=w[:, bo:bo + 1]),
                     [a.b, w.b], [a.b])
            else:
                P.op("act", lambda e, a=a, L=L: e.activation(a[:, 0:L], a[:, 0:L], AF.Silu), [a.b], [a.b])
            if kind in ("q", "k"):
                for sl in range(0, L, 512):
                    Wd = min(512, L - sl)
                    q_, r_, p_ = sq[sc % 2], rn[sc % 2], ps[sc % 2]
                    sc += 1
                    P.op("act", lambda e, a=a, q_=q_, sl=sl, Wd=Wd: e.activation(q_[:, 0:Wd], a[:, sl:sl + Wd], AF.Square), [a.b], [q_.b])
                    P.op("pe", lambda e, q_=q_, p_=p_, Wd=Wd: e.matmul(p_[:, 0:Wd], bo2, q_[:, 0:Wd], start=True, stop=True),
                         [q_.b, G.consts.b], [p_.b])
                    P.op("act", lambda e, r_=r_, p_=p_, Wd=Wd: e.activation(r_[:, 0:Wd], p_[:, 0:Wd], AF.Sqrt, bias=G.eps[:, :]),
                         [p_.b, G.eps.b], [r_.b])
                    P.op("dve", lambda e, r_=r_, Wd=Wd: e.reciprocal(r_[:, 0:Wd], r_[:, 0:Wd]), [r_.b], [r_.b])
                    if kind == "q":
                        P.op("dve", lambda e, a=a, r_=r_, sl=sl, Wd=Wd: e.scalar_tensor_tensor(
                            a[:, sl:sl + Wd], a[:, sl:sl + Wd], 0.125, r_[:, 0:Wd], ALU.mult, ALU.mult), [a.b, r_.b], [a.b])
                    else:
                        P.op("dve", lambda e, a=a, r_=r_, sl=sl, Wd=Wd: e.tensor_tensor(
                            a[:, sl:sl + Wd], a[:, sl:sl + Wd], r_[:, 0:Wd], ALU.mult), [a.b, r_.b], [a.b])
            P.dma(G.convF[dst:dst + 128, tok0:tok0 + L], a[:, 0:L], reads=[a.b], writes=[G.convF.b], q="pool")
    P.barrier()
    A.reset(m0)


def pool_mixer(G, l, stream):
    P, A = G.P, G.A
    name, tok0, L, s = stream
    m0 = A.mark()
    RW = 64 if name == "lat" else L
    NR = L // RW
    PW = RW + 16
    F = NR * PW
    xp = A.alloc("pxp", [128, NR, PW])
    ca = A.alloc("pca", [128, NR, PW])
    cb = A.alloc("pcb", [128, NR, PW])
    tmp = A.alloc("ptmp", [128, NR, RW])
    pooled = A.alloc("ppool", [128, NR, RW], BF16)
    pinv = A.alloc("pinv", [128, 2, RW])
    pwf = A.alloc("pwf", [128, 2, 128])
    pwb = A.alloc("pwb", [128, 2, 128], BF16)
    yo = [A.alloc("pyo", [128, 512], BF16) for _ in range(2)]
    ps = [G.psum[2], G.psum[3]]
    P.dma(pinv[:], (G.pinv_g if name == "lat" else G.pinv_c)[:, :, :], writes=[pinv.b])
    P.dma(pwf[:], G.poolw[l, :, :, :], writes=[pwf.b])
    P.op("pool", lambda e: e.tensor_copy(pwb[:], pwf[:]), [pwf.b], [pwb.b])
    fl = lambda t: t[:].rearrange("p r w -> p (r w)")
    cnt = 0
    for jc in range(2):
        P.op("pool", lambda e: e.memset(xp[:], 0.0), [], [xp.b])
        P.dma(xp[:, :, 8:8 + RW], G.projF[512 + jc * 128:512 + (jc + 1) * 128, tok0:tok0 + L].rearrange("p (r w) -> p r w", w=RW),
              reads=[G.projF.b], writes=[xp.b])
        xf, af, bf = fl(xp), fl(ca), fl(cb)
        P.op("dve", lambda e: e.tensor_tensor(af[:, 0:F - 1], xf[:, 0:F - 1], xf[:, 1:F], ALU.add), [xp.b], [ca.b])
        if jc == 0:
            P.op("dve", lambda e: e.tensor_tensor(bf[64:128, 0:F - 3], af[64:128, 0:F - 3], af[64:128, 2:F - 1], ALU.add), [ca.b], [cb.b])
            srcs = [(0, 64, 2, ca), (64, 128, 4, cb)]
        else:
            P.op("dve", lambda e: e.tensor_tensor(bf[:, 0:F - 3], af[:, 0:F - 3], af[:, 2:F - 1], ALU.add), [ca.b], [cb.b])
            P.op("dve", lambda e: e.tensor_tensor(af[:, 0:F - 7], bf[:, 0:F - 7], bf[:, 4:F - 3], ALU.add), [cb.b, ca.b], [ca.b])
            P.op("dve", lambda e: e.tensor_tensor(bf[64:128, 0:F - 15], af[64:128, 0:F - 15], af[64:128, 8:F - 7], ALU.add), [ca.b, cb.b], [cb.b])
            srcs = [(0, 64, 8, ca), (64, 128, 16, cb)]
        for (lo, hi, w, cw) in srcs:
            o = 8 - w // 2
            P.op("dve", lambda e, lo=lo, hi=hi, cw=cw, o=o, jc=jc: e.tensor_tensor(
                tmp[lo:hi, :, :], cw[lo:hi, :, o:o + RW], bc_ap(pinv[lo:hi, jc, :], [[0, NR], [1, RW]]), ALU.mult),
                [cw.b, pinv.b], [tmp.b])
            P.op("dve", lambda e, lo=lo, hi=hi: e.tensor_tensor(pooled[lo:hi, :, :], tmp[lo:hi, :, :], xp[lo:hi, :, 8:8 + RW], ALU.subtract),
                 [tmp.b, xp.b], [pooled.b])
        pf = pooled[:].rearrange("p r w -> p (r w)")
        for sl in range(0, L, 512):
            Wd = min(512, L - sl)
            p_, y_ = ps[cnt % 2], yo[cnt % 2]
            cnt += 1
            P.op("pe", lambda e, p_=p_, sl=sl, Wd=Wd, jc=jc: e.matmul(p_[:, 0:Wd], pwb[:, jc, :], pf[:, sl:sl + Wd], start=True, stop=True),
                 [pwb.b, pooled.b], [p_.b])
            P.op("act", lambda e, p_=p_, y_=y_, Wd=Wd, jc=jc: e.activation(
                y_[:, 0:Wd], p_[:, 0:Wd], AF.Copy, scale=G.colp_sb[:, C_PSCALE + jc:C_PSCALE + jc + 1]), [p_.b, G.colp_sb.b], [y_.b])
            P.dma(G.yT[256 + jc * 128:256 + (jc + 1) * 128, tok0 + sl:tok0 + sl + Wd], y_[:, 0:Wd], reads=[y_.b], writes=[G.yT.b], q="pool")
    P.barrier()
    A.reset(m0)


def gmlp_setup(G, l):
    P, A = G.P, G.A
    NB = 2
    uv = [A.alloc("guv", [128, 4, 128]) for _ in range(NB)]
    gt = [A.alloc("ggt", [128, 4, 128]) for _ in range(NB)]
    vt = [A.alloc("gvt", [128, 256]) for _ in range(NB)]
    vb = [A.alloc("gvb", [128, 256], BF16) for _ in range(NB)]
    st = [A.alloc("gst_", [128, 6]) for _ in range(NB)]
    mv = [A.alloc("gmv", [128, 2]) for _ in range(NB)]
    rs = [A.alloc("grs", [128, 1]) for _ in range(NB)]
    tm = [A.alloc("gtm", [128, 2, 128]) for _ in range(NB)]
    yo = [A.alloc("gyo", [128, 2, 128], BF16) for _ in range(NB)]
    wsf = A.alloc("gwsf", [128, 4, 128])
    wsb = A.alloc("gwsb", [128, 4, 128], BF16)
    P.dma(wsf[:], G.gws[l, :, :, :], writes=[wsf.b])
    P.op("pool", lambda e: e.tensor_copy(wsb[:], wsf[:]), [wsf.b], [wsb.b])
    pT = [G.psum[4], G.psum[5]]
    pq = [G.psum[6], G.psum[7]]
    ident = G.consts[:, K_ID, :]
    rp = G.rowp_sb
    def tile(R, stream, ti):
        name, tok0, L, s = stream
        i = ti % NB
        u, g, v, vbb, p1, p2, tt, y = uv[i], gt[i], vt[i], vb[i], pT[i], pq[i], tm[i], yo[i]
        t0 = tok0 + ti * 128
        R.dma(u[:], G.projF[1536:2048, t0:t0 + 128].rearrange("(j p) t -> p j t", p=128), reads=[G.projF.b], writes=[u.b])
        uf = u[:].rearrange("p j t -> p (j t)")
        gf = g[:].rearrange("p j t -> p (j t)")
        R.op("dve", lambda e, uf=uf, gf=gf: e.tensor_tensor(gf, uf, uf, ALU.mult), [u.b], [g.b])
        R.op("dve", lambda e, gf=gf: e.tensor_scalar(gf, gf, 0.044715, 1.0, ALU.mult, ALU.add), [g.b], [g.b])
        R.op("dve", lambda e, uf=uf, gf=gf: e.tensor_tensor(gf, gf, uf, ALU.mult), [g.b, u.b], [g.b])
        R.op("act", lambda e, gf=gf: e.activation(gf, gf, AF.Sigmoid, scale=1.5957691216057308), [g.b], [g.b])
        R.op("dve", lambda e, uf=uf, gf=gf: e.tensor_tensor(gf, gf, uf, ALU.mult), [g.b, u.b], [g.b])

        def f(e, g=g, p1=p1):
            e.transpose(p1[:, 0:128], g[:, 2, :], ident)
            return e.transpose(p1[:, 128:256], g[:, 3, :], ident)
        R.op("pe", f, [g.b, G.consts.b], [p1.b])
        R.op("act", lambda e, v=v, p1=p1: e.activation(v[:], p1[:, 0:256], AF.Copy), [p1.b], [v.b])
        R.op("dve", lambda e, v=v, i=i: e.bn_stats(st[i][:], v[:]), [v.b], [st[i].b])
        R.op("dve", lambda e, i=i: e.bn_aggr(mv[i][:], st[i][:]), [st[i].b], [mv[i].b])
        R.op("act", lambda e, i=i: e.activation(rs[i][:], mv[i][:, 1:2], AF.Sqrt, bias=G.eps[:, :]), [mv[i].b, G.eps.b], [rs[i].b])
        R.op("dve", lambda e, i=i: e.reciprocal(rs[i][:], rs[i][:]), [rs[i].b], [rs[i].b])
        R.op("dve", lambda e, v=v, i=i: e.tensor_scalar(v[:], v[:], mv[i][:, 0:1], rs[i][:, 0:1], ALU.subtract, ALU.mult),
             [v.b, mv[i].b, rs[i].b], [v.b])
        R.op("dve", lambda e, v=v: e.tensor_tensor(v[:], v[:], rp[:, R_GLNW:R_GLNW + 256], ALU.mult), [v.b, rp.b], [v.b])
        R.op("dve", lambda e, v=v, vbb=vbb: e.tensor_tensor(vbb[:], v[:], rp[:, R_GLNB:R_GLNB + 256], ALU.add), [v.b, rp.b], [vbb.b])

        def f2(e, vbb=vbb, p2=p2):
            e.matmul(p2[:, 0:256], vbb[:, 0:128], wsb[:, 0:2, :].rearrange("p g i -> p (g i)"), start=True, stop=True)
            return e.matmul(p2[:, 256:512], vbb[:, 128:256], wsb[:, 2:4, :].rearrange("p g i -> p (g i)"), start=True, stop=True)
        R.op("pe", f2, [vbb.b, wsb.b], [p2.b])
        for q in range(2):
            for hh in range(2):
                gg = 2 * q + hh
                lo, hi = hh * 64, (hh + 1) * 64
                c0 = q * 256 + hh * 128
                R.op("dve", lambda e, lo=lo, hi=hi, c0=c0, q=q, gg=gg, tt=tt, p2=p2: e.tensor_tensor(
                    tt[lo:hi, q, :], p2[lo:hi, c0:c0 + 128], rp[lo:hi, R_GBS + gg * 128:R_GBS + (gg + 1) * 128], ALU.add),
                    [p2.b, rp.b], [tt.b])
                R.op("dve", lambda e, lo=lo, hi=hi, q=q, tt=tt, y=y, g=g: e.tensor_tensor(
                    y[lo:hi, q, :], tt[lo:hi, q, :], g[lo:hi, q, :], ALU.mult), [tt.b, g.b], [y.b])
        R.dma(G.yT[768:1024, t0:t0 + 128].rearrange("(j p) t -> p j t", p=128), y[:], reads=[y.b], writes=[G.yT.b], q="pool")
    return tile


def gmlp_mixer(G, l, stream):
    P, A = G.P, G.A
    m0 = A.mark()
    tile = gmlp_setup(G, l)
    for ti in range(stream[2] // 128):
        r = Rec()
        tile(r, stream, ti)
        merge(P, [r])
    P.barrier()
    A.reset(m0)


def softplus_small(P, out, x, tmp, reads, extra_w=()):
    P.op("dve", lambda e: e.scalar_tensor_tensor(tmp, x, -1.0, x, ALU.mult, ALU.max), reads, [extra_w[0]])
    P.op("act", lambda e: e.activation(tmp, tmp, AF.Exp, scale=-1.0), [extra_w[0]], [extra_w[0]])
    P.op("act", lambda e: e.activation(tmp, tmp, AF.Ln, bias=1.0), [extra_w[0]], [extra_w[0]])
    P.op("dve", lambda e: e.scalar_tensor_tensor(out, x, 0.0, tmp, ALU.max, ALU.add), reads + [extra_w[0]], [extra_w[1]])


def ssd_mixer(G, l, stream, co_tile=None):
    P, A = G.P, G.A
    name, tok0, L, s = stream
    NT = L // 128
    m0 = A.mark()
    rp = G.rowp_sb
    C = G.consts
    ident = C[:, K_ID, :]
    ones = C[:, K_ONES, :]
    nega = A.alloc("snega", [128, 8])
    negones = A.alloc("snegones", [128, 128])
    yacc = A.alloc("syacc", [128, NT, 256])
    P.op("act", lambda e: e.activation(nega[:], rp[:, R_SALOG:R_SALOG + 8], AF.Exp), [rp.b], [nega.b])
    P.op("dve", lambda e: e.tensor_scalar(nega[:], nega[:], -1.0, None, ALU.mult), [nega.b], [nega.b])
    P.op("dve", lambda e: e.memset(negones[:], -1.0), [], [negones.b])
    NB = 2
    def mk(nm, shape, dt=F32):
        return [A.alloc(nm, shape, dt) for _ in range(NB)]
    zt, cx, dtt, dtm, dta = mk("szt", [128, 264]) + mk("szt", [128, 264]), mk("scx", [128, 2, 128]) + mk("scx", [128, 2, 128]), mk("sdt", [128, 8]), mk("sdtm", [128, 8]), mk("sdta", [128, 8])
    bc4 = mk("sbc4", [64, 4, 128]) + mk("sbc4", [64, 4, 128])
    xtm, btm, cbb = mk("sxtm", [128, 256]), mk("sbtm", [128, 128], BF16), mk("scbb", [64, 4, 128], BF16)
    ML, Dm, Wm = mk("sML", [128, 4, 128]), mk("sDm", [128, 4, 128]), mk("sWm", [128, 4, 128], BF16)
    scl, dtw = mk("sscl", [128, 12]), mk("sdtw", [128, 8])
    xdt, xdw = mk("sxdt", [128, 256], BF16), mk("sxdw", [128, 256], BF16)
    yt, yg, ss, yo = mk("syt", [128, 256]), mk("syg", [128, 256]), mk("sss", [128, 2]), mk("syo", [128, 2, 128], BF16)
    ps = G.psum
    def body(R, d, ti, i, li, second):
        Ltri = C[:, K_LF if d == 0 else K_LB, :]
        nmask = C[:, K_NMF if d == 0 else K_NMB, :]
        t0 = tok0 + ti * 128
        z_, c_, dt_, dm_, da_, x_, b_, cb_ = zt[li], cx[li], dtt[i], dtm[i], dta[i], xtm[i], btm[i], cbb[i]
        bc_ = bc4[li]
        ml_, D_, W_, sc_, dw_, xd_, xw_ = ML[i], Dm[i], Wm[i], scl[i], dtw[i], xdt[i], xdw[i]
        pA, pB = ps[2 * i], ps[2 * i + 1]
        pC = pD = pA
        R.dma(z_[:], G.projT[t0:t0 + 128, 0:264], reads=[G.projT.b], writes=[z_.b])
        R.dma(c_[:], G.convF[0:256, t0:t0 + 128].rearrange("(j p) t -> p j t", p=128), reads=[G.convF.b], writes=[c_.b])
        R.dma(bc_[:], G.convF[256:512, t0:t0 + 128].rearrange("(j p) t -> p j t", p=64), reads=[G.convF.b], writes=[bc_.b])
        R.op("dve", lambda e, z_=z_, dt_=dt_: e.tensor_tensor(dt_[:], z_[:, 256:264], rp[:, R_SDTB:R_SDTB + 8], ALU.add), [z_.b, rp.b], [dt_.b])
        softplus_small(R, dt_[:], dt_[:], dm_[:], [dt_.b], (dm_.b, dt_.b))
        R.op("dve", lambda e, dt_=dt_, da_=da_: e.tensor_tensor(da_[:], dt_[:], nega[:], ALU.mult), [dt_.b, nega.b], [da_.b])
        if SSD_STOP <= 1:
            return
        def f(e, c_=c_, pA=pA, bc_=bc_):
            e.transpose(pA[:, 0:128], c_[:, 0, :], ident)
            e.transpose(pA[:, 128:256], c_[:, 1, :], ident)
            e.transpose(pA[:, 256:320], bc_[:, 0, :], ident[0:64, 0:64])
            return e.transpose(pA[:, 320:384], bc_[:, 1, :], ident[0:64, 0:64])
        R.op("pe", f, [c_.b, bc_.b, C.b], [pA.b])
        R.op("act", lambda e, x_=x_, pA=pA: e.activation(x_[:], pA[:, 0:256], AF.Copy), [pA.b], [x_.b])
        R.op("act", lambda e, b_=b_, pA=pA: e.activation(b_[:], pA[:, 256:384], AF.Copy), [pA.b], [b_.b])
        R.op("pool", lambda e, cb_=cb_, bc_=bc_: e.tensor_copy(cb_[:], bc_[:]), [bc_.b], [cb_.b])
        if SSD_STOP <= 2:
            return
        def f(e, cb_=cb_, pB=pB):
            e.matmul(pB[:, 0:128], cb_[:, 0, :], cb_[:, 2, :], start=True, stop=True)
            return e.matmul(pB[:, 128:256], cb_[:, 1, :], cb_[:, 3, :], start=True, stop=True)
        R.op("pe", f, [cb_.b], [pB.b])
        if SSD_STOP <= 3:
            return
        R.op("dve", lambda e, ml_=ml_, da_=da_, Ltri=Ltri: e.tensor_tensor(
            ml_[:], bc_ap(Ltri, [[0, 4], [1, 128]]), bc_ap(da_[:, d * 4:d * 4 + 4], [[1, 4], [0, 128]]), ALU.mult), [da_.b, C.b], [ml_.b])
        def f(e, ml_=ml_, pC=pC, nmask=nmask):
            for h in range(4):
                e.matmul(pC[:, h * 128:(h + 1) * 128], ones, ml_[:, h, :], start=True, stop=False)
                e.matmul(pC[:, h * 128:(h + 1) * 128], ml_[:, h, :], negones[:], start=False, stop=False)
                ins = e.matmul(pC[:, h * 128:(h + 1) * 128], ident, nmask, start=False, stop=True)
            return ins
        R.op("pe", f, [ml_.b, C.b, negones.b], [pC.b])
        R.op("act", lambda e, D_=D_, pC=pC: e.activation(D_[:].rearrange("p h i -> p (h i)"), pC[:, :], AF.Exp), [pC.b], [D_.b])
        if SSD_STOP <= 4:
            return
        def f(e, da_=da_, pD=pD, Ltri=Ltri):
            e.matmul(pD[:, 0:4], Ltri, da_[:, d * 4:d * 4 + 4], start=True, stop=True)
            return e.matmul(pD[:, 4:8], ones, da_[:, d * 4:d * 4 + 4], start=True, stop=True)
        R.op("pe", f, [da_.b, C.b], [pD.b])
        R.op("dve", lambda e, sc_=sc_, pD=pD: e.tensor_copy(sc_[:, 0:8], pD[:, 0:8]), [pD.b], [sc_.b])
        R.op("dve", lambda e, sc_=sc_: e.tensor_copy(sc_[:, 8:12], sc_[:, 4:8]), [sc_.b], [sc_.b])
        R.op("dve", lambda e, sc_=sc_: e.tensor_tensor(sc_[:, 4:8], sc_[:, 4:8], sc_[:, 0:4], ALU.subtract), [sc_.b], [sc_.b])
        R.op("act", lambda e, sc_=sc_: e.activation(sc_[:], sc_[:], AF.Exp), [sc_.b], [sc_.b])
        if SSD_STOP <= 5:
            return
        R.op("dve", lambda e, W_=W_, D_=D_, pB=pB: e.tensor_tensor(
            W_[:].rearrange("p (g r) i -> p g r i", g=2), bc_ap(pB[:, 0:256], [[128, 2], [0, 2], [1, 128]]),
            D_[:].rearrange("p (g r) i -> p g r i", g=2), ALU.mult), [pB.b, D_.b], [W_.b])
        if SSD_STOP <= 6:
            return
        R.op("dve", lambda e, dw_=dw_, dt_=dt_, sc_=sc_: e.tensor_tensor(dw_[:, 0:4], dt_[:, d * 4:d * 4 + 4], sc_[:, 4:8], ALU.mult),
             [dt_.b, sc_.b], [dw_.b])
        R.op("dve", lambda e, xd_=xd_, x_=x_, dt_=dt_: e.tensor_tensor(
            xd_[:].rearrange("p (h q) -> p h q", h=4), x_[:].rearrange("p (h q) -> p h q", h=4),
            bc_ap(dt_[:, d * 4:d * 4 + 4], [[1, 4], [0, 64]]), ALU.mult), [x_.b, dt_.b], [xd_.b])
        R.op("dve", lambda e, xw_=xw_, x_=x_, dw_=dw_: e.tensor_tensor(
            xw_[:].rearrange("p (h q) -> p h q", h=4), x_[:].rearrange("p (h q) -> p h q", h=4),
            bc_ap(dw_[:, 0:4], [[1, 4], [0, 64]]), ALU.mult), [x_.b, dw_.b], [xw_.b])
        if SSD_STOP <= 7:
            return
        def f(e, W_=W_, xd_=xd_, pA=pA):
            for h in range(4):
                ins = e.matmul(pA[:, h * 64:(h + 1) * 64], W_[:, h, :], xd_[:, h * 64:(h + 1) * 64], start=True, stop=True)
            return ins
        R.op("pe", f, [W_.b, xd_.b], [pA.b])
        def f(e, bc_=bc_, pB=pB):
            for h in range(4):
                g = h // 2
                ins = e.matmul(pB[:, 256 + h * 64:256 + (h + 1) * 64], bc_[:, 2 + g, :],
                               G.sst[d][:, h, :], start=True, stop=True)
            return ins
        R.op("pe", f, [bc_.b, G.sst[d].b], [pB.b])
        def f(e, b_=b_, xw_=xw_, pD=pD):
            for h in range(4):
                g = h // 2
                ins = e.matmul(pD[0:64, 256 + h * 64:256 + (h + 1) * 64], b_[:, g * 64:(g + 1) * 64], xw_[:, h * 64:(h + 1) * 64], start=True, stop=True)
            return ins
        R.op("pe", f, [b_.b, xw_.b], [pD.b])
        if SSD_STOP <= 8:
            return
        for h in range(4):
            lo, hi = 0, 64
            R.op("dve", lambda e, lo=lo, hi=hi, h=h, sc_=sc_, pD=pD: e.scalar_tensor_tensor(
                G.sst[d][lo:hi, h, :], G.sst[d][lo:hi, h, :], sc_[lo:hi, 8 + h:9 + h], pD[lo:hi, 256 + h * 64:256 + (h + 1) * 64],
                ALU.mult, ALU.add), [G.sst[d].b, sc_.b, pD.b], [G.sst[d].b])
        if SSD_STOP <= 9:
            return
        y_ = yt[i]
        R.op("dve", lambda e, y_=y_, pB=pB, sc_=sc_: e.tensor_tensor(
            y_[:].rearrange("p (h q) -> p h q", h=4), pB[:, 256:512].rearrange("p (h q) -> p h q", h=4),
            bc_ap(sc_[:, 0:4], [[1, 4], [0, 64]]), ALU.mult), [pB.b, sc_.b], [y_.b])
        R.op("dve", lambda e, y_=y_, pA=pA: e.tensor_tensor(y_[:], y_[:], pA[:, 0:256], ALU.add), [y_.b, pA.b], [y_.b])
        if name == "ctx" and ti == 0:
            dbgdump(G, f"dt{d}", dt_, dt_[:], [128, 8], P=R)
            dbgdump(G, f"da{d}", da_, da_[:], [128, 8], P=R)
            dbgdump(G, f"x{d}", x_, x_[:], [128, 256], P=R)
            dbgdump(G, f"D{d}", D_, D_[:].rearrange("p h i -> p (h i)"), [128, 512], P=R)
            dbgdump(G, f"W{d}", W_, W_[:].rearrange("p h i -> p (h i)"), [128, 512], BF16, P=R)
            dbgdump(G, f"sc{d}", sc_, sc_[:], [128, 12], P=R)
            dbgdump(G, f"xd{d}", xd_, xd_[:], [128, 256], BF16, P=R)
            dbgdump(G, f"y{d}", y_, y_[:], [128, 256], P=R)
        if not second:
            R.op("dve", lambda e, x_=x_, ti=ti: e.tensor_tensor(yacc[:, ti, :], x_[:], rp[:, R_SSDD:R_SSDD + 256], ALU.mult),
                 [x_.b, rp.b], [yacc.b])
            R.op("dve", lambda e, y_=y_, ti=ti: e.tensor_tensor(yacc[:, ti, :], yacc[:, ti, :], y_[:], ALU.add), [yacc.b, y_.b], [yacc.b])
        else:
            g_, s_, o_ = yg[i], ss[i], yo[i]
            R.op("dve", lambda e, y_=y_, ti=ti: e.tensor_tensor(y_[:], y_[:], yacc[:, ti, :], ALU.add), [yacc.b, y_.b], [y_.b])
            R.op("act", lambda e, g_=g_, z_=z_: e.activation(g_[:], z_[:, 0:256], AF.Silu), [z_.b], [g_.b])
            R.op("dve", lambda e, g_=g_, y_=y_: e.tensor_tensor(g_[:], g_[:], y_[:], ALU.mult), [g_.b, y_.b], [g_.b])
            R.op("act", lambda e, g_=g_, y_=y_, s_=s_: e.activation(y_[:], g_[:], AF.Square, accum_out=s_[:, 0:1]), [g_.b], [y_.b, s_.b])
            R.op("act", lambda e, s_=s_: e.activation(s_[:, 1:2], s_[:, 0:1], AF.Sqrt, bias=G.eps[:, :], scale=1.0 / 256.0), [s_.b, G.eps.b], [s_.b])
            R.op("dve", lambda e, s_=s_: e.reciprocal(s_[:, 1:2], s_[:, 1:2]), [s_.b], [s_.b])
            R.op("dve", lambda e, g_=g_, s_=s_: e.scalar_tensor_tensor(
                g_[:], g_[:], s_[:, 1:2], rp[:, R_SSDNW:R_SSDNW + 256], ALU.mult, ALU.mult), [g_.b, s_.b, rp.b], [g_.b])
            def f(e, g_=g_, pC=pC):
                e.transpose(pC[:, 0:128], g_[:, 0:128], ident)
                return e.transpose(pC[:, 128:256], g_[:, 128:256], ident)
            R.op("pe", f, [g_.b, C.b], [pC.b])
            R.op("act", lambda e, o_=o_, pC=pC: e.activation(o_[:].rearrange("p j t -> p (j t)"), pC[:, 0:256], AF.Copy), [pC.b], [o_.b])
            R.dma(G.yT[0:256, t0:t0 + 128].rearrange("(j p) t -> p j t", p=128), o_[:], reads=[o_.b], writes=[G.yT.b], q="pool")

    cnt = 0
    for st_ in range(NT):
        recs = []
        for d in range(2):
            ti = st_ if d == 0 else NT - 1 - st_
            second = (ti >= NT // 2) if d == 0 else (ti < NT // 2)
            recs.append(Rec())
            body(recs[-1], d, ti, d, 2 * d + st_ % 2, second)
        if co_tile is not None:
            recs.append(Rec())
            co_tile(recs[-1], stream, st_)
        merge(P, recs)
    P.barrier()
    A.reset(m0)


def gdn_mixer(G, l, stream):
    P, A = G.P, G.A
    name, tok0, L, s = stream
    NT = L // 128
    m0 = A.mark()
    rp, C = G.rowp_sb, G.consts
    ident, ones = C[:, K_ID, :], C[:, K_ONES, :]
    id64 = C[0:64, K_ID, 0:64]
    negag = A.alloc("gnegag", [128, 8])
    negones = A.alloc("gnegones", [128, 128])
    oacc = A.alloc("goacc", [128, NT, 256])
    P.op("act", lambda e: e.activation(negag[:], rp[:, R_GALOG:R_GALOG + 8], AF.Exp), [rp.b], [negag.b])
    P.op("dve", lambda e: e.tensor_scalar(negag[:], negag[:], -1.0, None, ALU.mult), [negag.b], [negag.b])
    P.op("dve", lambda e: e.memset(negones[:], -1.0), [], [negones.b])
    NB = 2

    def mk(nm, shape, dt=F32):
        return [A.alloc(nm, shape, dt) for _ in range(NB)]
    pt, fm, gx, kv = mk("gpt", [128, 272]) + mk("gpt", [128, 272]), mk("gfm", [64, 12, 128]), mk("ggx", [128, 24]), mk("gkv", [128, 512])
    sc, esc, ML = mk("gsc", [128, 16]), mk("gesc", [128, 16]), mk("gML", [128, 4, 128])
    decT, decS, A0, Q0, aT = mk("gdecT", [128, 4, 128]), mk("gdecS", [128, 4, 128]), mk("gA0", [128, 4, 128]), mk("gQ0", [128, 4, 128]), mk("gaT", [128, 4, 128])
    Pb, Qb, MTb = [mk("gPb", [128, 4, 128]) for _ in range(2)], [mk("gQb", [128, 4, 128]) for _ in range(2)], [mk("gMT", [128, 4, 128]) for _ in range(3)]
    Rv, Rk, bek, usb, wsb = mk("gRv", [128, 4, 64]), mk("gRk", [128, 4, 64]), mk("gbek", [128, 4]), mk("gusb", [128, 256]), mk("gwsb", [64, 512])
    ektc, kt = [mk("gektc", [128, 4]) for _ in range(2)], [mk("gkt", [128, 4, 64]) for _ in range(2)]
    vnew, oint, ot, sq, ss, yo = mk("gvnew", [128, 256]), mk("goint", [128, 256]), mk("got", [128, 256]), mk("gsq", [128, 256]), mk("gss", [128, 8]), mk("gyo", [128, 2, 128], BF16)
    for i in range(NB):
        P.op("dve", lambda e, i=i: e.memset(vnew[i][:], 0.0), [], [vnew[i].b])
    v4 = lambda t: t[:].rearrange("p (h q) -> p h q", h=4)

    def body(R, d, ti, i, li, second):
        g0_, g1_, g2_, g3_ = G.psum[4 * d:4 * d + 4]
        b = [g0_, g2_, g3_, g2_, g3_, g2_, g3_, g1_]
        bM = g0_
        t0 = tok0 + ti * 128
        Ltri2 = C[:, K_LF2 if d == 0 else K_LB2, :]
        niT = C[:, K_NI2F if d == 0 else K_NI2B, :]
        nsS = C[:, K_NS2F if d == 0 else K_NS2B, :]
        selc = [C[:, K_SEL0, 0:1], C[:, K_SEL1, 0:1]]
        pt_, f_, gx_, kv_, sc_, es_, ml_ = pt[li], fm[i], gx[i], kv[i], sc[i], esc[i], ML[i]
        dT_, dS_, a0_, q0_, at_ = decT[i], decS[i], A0[i], Q0[i], aT[i]
        R.dma(pt_[:], G.projT[t0:t0 + 128, 264:536], reads=[G.projT.b], writes=[pt_.b])
        R.dma(f_[:], G.convF[512:1280, t0:t0 + 128].rearrange("(j p) t -> p j t", p=64), reads=[G.convF.b], writes=[f_.b])
        R.op("dve", lambda e: e.tensor_tensor(gx_[:, 0:8], pt_[:, 256:264], rp[:, R_GDTB:R_GDTB + 8], ALU.add), [pt_.b, rp.b], [gx_.b])
        softplus_small(R, gx_[:, 0:8], gx_[:, 0:8], gx_[:, 16:24], [gx_.b], (gx_.b, gx_.b))
        R.op("dve", lambda e: e.tensor_tensor(gx_[:, 0:8], gx_[:, 0:8], negag[:], ALU.mult), [gx_.b, negag.b], [gx_.b])
        R.op("act", lambda e: e.activation(gx_[:, 8:16], pt_[:, 264:272], AF.Sigmoid), [pt_.b], [gx_.b])
        gd = gx_[:, d * 4:d * 4 + 4]
        bd = gx_[:, 8 + d * 4:8 + d * 4 + 4]
        def f(e):
            for h in range(4):
                e.transpose(b[0][:, h * 64:(h + 1) * 64], f_[:, 4 + h, :], id64)
            for h in range(4):
                ins = e.transpose(b[0][:, 256 + h * 64:256 + (h + 1) * 64], f_[:, 8 + h, :], id64)
            return ins
        R.op("pe", f, [f_.b, C.b], [b[0].b])
        R.op("act", lambda e: e.activation(kv_[:], b[0][:, :], AF.Copy), [b[0].b], [kv_.b])
        def f(e):
            e.matmul(b[7][:, 0:4], Ltri2, gd, start=True, stop=True)
            e.matmul(b[7][:, 4:8], C[:, K_BO2, :], gd, start=True, stop=True)
            e.matmul(b[7][:, 8:12], C[:, K_SEL0, :], gd, start=True, stop=True)
            return e.matmul(b[7][:, 12:16], C[:, K_SEL1, :], gd, start=True, stop=True)
        R.op("pe", f, [gx_.b, C.b], [b[7].b])
        R.op("dve", lambda e: e.tensor_copy(sc_[:], b[7][:, 0:16]), [b[7].b], [sc_.b])
        R.op("dve", lambda e: e.tensor_tensor(sc_[:, 4:8], sc_[:, 4:8], sc_[:, 0:4], ALU.subtract), [sc_.b], [sc_.b])
        R.op("act", lambda e: e.activation(es_[:], sc_[:], AF.Exp), [sc_.b], [es_.b])
        R.op("dve", lambda e: e.tensor_tensor(ml_[:], bc_ap(Ltri2, [[0, 4], [1, 128]]), bc_ap(gd, [[1, 4], [0, 128]]), ALU.mult),
             [gx_.b, C.b], [ml_.b])
        def f(e):
            for h in range(4):
                o = b[1][:, h * 128:(h + 1) * 128]
                e.matmul(o, ones, ml_[:, h, :], start=True, stop=False)
                e.matmul(o, ml_[:, h, :], negones[:], start=False, stop=False)
                ins = e.matmul(o, ident, niT, start=False, stop=True)
            return ins
        R.op("pe", f, [ml_.b, C.b, negones.b], [b[1].b])
        def f(e):
            for h in range(4):
                o = b[2][:, h * 128:(h + 1) * 128]
                e.matmul(o, ml_[:, h, :], ones, start=True, stop=False)
                e.matmul(o, negones[:], ml_[:, h, :], start=False, stop=False)
                ins = e.matmul(o, ident, nsS, start=False, stop=True)
            return ins
        R.op("pe", f, [ml_.b, C.b, negones.b], [b[2].b])
        fl = lambda t: t[:].rearrange("p h i -> p (h i)")
        R.op("act", lambda e: e.activation(fl(dT_), b[1][:, :], AF.Exp), [b[1].b], [dT_.b])
        R.op("act", lambda e: e.activation(fl(dS_), b[2][:, :], AF.Exp), [b[2].b], [dS_.b])
        def f(e):
            for h in range(4):
                ins = e.matmul(b[3][:, h * 128:(h + 1) * 128], f_[:, 4 + h, :], f_[:, 4 + h, :], start=True, stop=True)
            return ins
        R.op("pe", f, [f_.b], [b[3].b])
        def f(e):
            for h in range(4):
                ins = e.matmul(b[4][:, h * 128:(h + 1) * 128], f_[:, 4 + h, :], f_[:, h, :], start=True, stop=True)
            return ins
        R.op("pe", f, [f_.b], [b[4].b])
        R.op("dve", lambda e: e.tensor_tensor(fl(a0_), b[3][:, :], fl(dS_), ALU.mult), [b[3].b, dS_.b], [a0_.b])
        R.op("dve", lambda e: e.tensor_tensor(a0_[:], a0_[:], bc_ap(bd, [[1, 4], [0, 128]]), ALU.mult), [a0_.b, gx_.b], [a0_.b])
        R.op("dve", lambda e: e.tensor_tensor(fl(at_), b[4][:, :], fl(dT_), ALU.mult), [b[4].b, dT_.b], [at_.b])
        def f(e):
            for h in range(4):
                ins = e.transpose(b[5][:, h * 128:(h + 1) * 128], a0_[:, h, :], ident)
            return ins
        R.op("pe", f, [a0_.b, C.b], [b[5].b])
        R.op("act", lambda e: e.activation(fl(q0_), b[5][:, :], AF.Copy), [b[5].b], [q0_.b])
        mt = MTb[0][i]
        R.op("dve", lambda e, mt=mt: e.tensor_tensor(mt[:], bc_ap(ident, [[0, 4], [1, 128]]), q0_[:], ALU.subtract), [q0_.b, C.b], [mt.b])
        if name == "ctx" and ti == (0 if d == 0 else 1):
            dbgdump(G, f"gA{d}", a0_, fl(a0_), [128, 512], P=R)
            dbgdump(G, f"gQ{d}", q0_, fl(q0_), [128, 512], P=R)
            dbgdump(G, f"gM0{d}", mt, fl(mt), [128, 512], P=R)
            dbgdump(G, f"gdS{d}", dS_, fl(dS_), [128, 512], P=R)
            dbgdump(G, f"ggx{d}", gx_, gx_[:], [128, 24], P=R)
            dbgdump(G, f"gkv{d}", kv_, kv_[:], [128, 512], P=R)
        Pc, Qc = a0_, q0_
        for m in range(GDN_ROUNDS):
            Pn, Qn, mtn = Pb[m % 2][i], Qb[m % 2][i], MTb[(m + 1) % 3][i]
            def f(e, Pc=Pc, Qc=Qc):
                for h in range(4):
                    ins = e.matmul(b[3][:, h * 128:(h + 1) * 128], Qc[:, h, :], Pc[:, h, :], start=True, stop=True)
                return ins
            R.op("pe", f, [Pc.b, Qc.b], [b[3].b])
            def f(e, Pc=Pc, Qc=Qc):
                for h in range(4):
                    ins = e.matmul(b[4][:, h * 128:(h + 1) * 128], Pc[:, h, :], Qc[:, h, :], start=True, stop=True)
                return ins
            R.op("pe", f, [Pc.b, Qc.b], [b[4].b])
            R.op("act", lambda e, Pn=Pn: e.activation(fl(Pn), b[3][:, :], AF.Copy), [b[3].b], [Pn.b])
            R.op("dve", lambda e, Qn=Qn: e.tensor_copy(fl(Qn), b[4][:, :]), [b[4].b], [Qn.b])
            def f(e, Pn=Pn, mt=mt):
                for h in range(4):
                    ins = e.matmul(bM[:, h * 128:(h + 1) * 128], Pn[:, h, :], mt[:, h, :], start=True, stop=True)
                return ins
            R.op("pe", f, [Pn.b, mt.b], [bM.b])
            R.op("dve", lambda e, mt=mt, mtn=mtn: e.tensor_tensor(fl(mtn), fl(mt), bM[:, :], ALU.add), [mt.b, bM.b], [mtn.b])
            Pc, Qc, mt = Pn, Qn, mtn
        rv_, rk_, bk_, u_, w_ = Rv[i], Rk[i], bek[i], usb[i], wsb[i]
        R.op("dve", lambda e: e.tensor_tensor(rv_[:], v4(kv_)[:, 4:8, :] if False else kv_[:, 256:512].rearrange("p (h q) -> p h q", h=4),
                                              bc_ap(bd, [[1, 4], [0, 64]]), ALU.mult), [kv_.b, gx_.b], [rv_.b])
        R.op("dve", lambda e: e.tensor_tensor(bk_[:], bd, es_[:, 0:4], ALU.mult), [gx_.b, es_.b], [bk_.b])
        R.op("dve", lambda e: e.tensor_tensor(rk_[:], kv_[:, 0:256].rearrange("p (h q) -> p h q", h=4),
                                              bc_ap(bk_[:, 0:4], [[1, 4], [0, 64]]), ALU.mult), [kv_.b, bk_.b], [rk_.b])
        def f(e):
            for h in range(4):
                ins = e.matmul(b[7][:, 64 + h * 64:64 + (h + 1) * 64], mt[:, h, :], rv_[:, h, :], start=True, stop=True)
            return ins
        R.op("pe", f, [mt.b, rv_.b], [b[7].b])
        def f(e):
            for h in range(4):
                ins = e.matmul(b[0][0:64, h * 128:(h + 1) * 128], rk_[:, h, :], mt[:, h, :], start=True, stop=True)
            return ins
        R.op("pe", f, [mt.b, rk_.b], [b[0].b])
        R.op("act", lambda e: e.activation(u_[:], b[7][:, 64:320], AF.Copy), [b[7].b], [u_.b])
        R.op("dve", lambda e: e.tensor_copy(w_[:], b[0][0:64, :]), [b[0].b], [w_.b])
        for c in range(2):
            ek, k_ = ektc[c][i], kt[c][i]
            R.op("dve", lambda e, ek=ek, c=c: e.tensor_scalar(ek[:], es_[:, 4:8], selc[c], None, ALU.mult), [es_.b, C.b], [ek.b])
            R.op("dve", lambda e, ek=ek, k_=k_: e.tensor_tensor(k_[:], kv_[:, 0:256].rearrange("p (h q) -> p h q", h=4),
                                                               bc_ap(ek[:, 0:4], [[1, 4], [0, 64]]), ALU.mult), [kv_.b, ek.b], [k_.b])
        vn_, oi_ = vnew[i], oint[i]
        for c in ((0, 1) if d == 0 else (1, 0)):
            lo, hi = c * 64, (c + 1) * 64
            k_ = kt[c][i]
            def f(e):
                for h in range(4):
                    e.matmul(b[5][:, h * 64:(h + 1) * 64], w_[:, h * 128:(h + 1) * 128], G.gst[d][:, h, :], start=True, stop=True)
                for h in range(4):
                    ins = e.matmul(b[5][:, 256 + h * 64:256 + (h + 1) * 64], f_[:, h, :], G.gst[d][:, h, :], start=True, stop=True)
                return ins
            R.op("pe", f, [w_.b, f_.b, G.gst[d].b], [b[5].b])
            R.op("dve", lambda e, lo=lo, hi=hi: e.tensor_tensor(vn_[lo:hi, :], u_[lo:hi, :], b[5][lo:hi, 0:256], ALU.subtract), [u_.b, b[5].b], [vn_.b])
            R.op("dve", lambda e, lo=lo, hi=hi: e.tensor_tensor(oi_[lo:hi, :].rearrange("p (h q) -> p h q", h=4),
                                                                b[5][lo:hi, 256:512].rearrange("p (h q) -> p h q", h=4),
                                                                bc_ap(es_[lo:hi, 0:4], [[1, 4], [0, 64]]), ALU.mult), [b[5].b, es_.b], [oi_.b])
            def f(e, k_=k_):
                for h in range(4):
                    ins = e.matmul(b[6][0:64, h * 64:(h + 1) * 64], k_[:, h, :], vn_[:, h * 64:(h + 1) * 64], start=True, stop=True)
                return ins
            R.op("pe", f, [k_.b, vn_.b], [b[6].b])
            for h in range(4):
                R.op("dve", lambda e, h=h, c=c: e.scalar_tensor_tensor(
                    G.gst[d][:, h, :], G.gst[d][:, h, :], es_[0:64, 8 + 4 * c + h:9 + 4 * c + h], b[6][0:64, h * 64:(h + 1) * 64],
                    ALU.mult, ALU.add), [G.gst[d].b, es_.b, b[6].b], [G.gst[d].b])
        def f(e):
            for h in range(4):
                ins = e.matmul(b[7][:, 64 + h * 64:64 + (h + 1) * 64], at_[:, h, :], vn_[:, h * 64:(h + 1) * 64], start=True, stop=True)
            return ins
        R.op("pe", f, [at_.b, vn_.b], [b[7].b])
        if name == "ctx" and ti == (0 if d == 0 else 1):
            dbgdump(G, f"gMT{d}", mt, fl(mt), [128, 512], P=R)
            dbgdump(G, f"gu{d}", u_, u_[:], [128, 256], P=R)
            dbgdump(G, f"gw{d}", w_, w_[:], [64, 512], P=R)
            dbgdump(G, f"gvn{d}", vn_, vn_[:], [128, 256], P=R)
            dbgdump(G, f"gaT{d}", at_, fl(at_), [128, 512], P=R)
        if not second:
            R.op("dve", lambda e: e.tensor_tensor(oacc[:, ti, :], b[7][:, 64:320], oi_[:], ALU.add), [b[7].b, oi_.b], [oacc.b])
            return
        o_, q_, s_, y_ = ot[i], sq[i], ss[i], yo[i]
        R.op("dve", lambda e: e.tensor_tensor(o_[:], b[7][:, 64:320], oi_[:], ALU.add), [b[7].b, oi_.b], [o_.b])
        R.op("dve", lambda e: e.tensor_tensor(o_[:], o_[:], oacc[:, ti, :], ALU.add), [o_.b, oacc.b], [o_.b])
        R.op("dve", lambda e: e.tensor_tensor(q_[:], o_[:], o_[:], ALU.mult), [o_.b], [q_.b])
        R.op("dve", lambda e: e.reduce_sum(s_[:, 0:4], q_[:].rearrange("p (h q) -> p h q", h=4), AX.X), [q_.b], [s_.b])
        R.op("act", lambda e: e.activation(s_[:, 4:8], s_[:, 0:4], AF.Sqrt, bias=G.eps[:, :], scale=1.0 / 64.0), [s_.b, G.eps.b], [s_.b])
        R.op("dve", lambda e: e.reciprocal(s_[:, 4:8], s_[:, 4:8]), [s_.b], [s_.b])
        R.op("dve", lambda e: e.tensor_tensor(v4(o_), v4(o_), bc_ap(s_[:, 4:8], [[1, 4], [0, 64]]), ALU.mult), [o_.b, s_.b], [o_.b])
        R.op("dve", lambda e: e.tensor_tensor(v4(o_), v4(o_), bc_ap(rp[:, R_GDNNW:R_GDNNW + 64], [[0, 4], [1, 64]]), ALU.mult), [o_.b, rp.b], [o_.b])
        R.op("act", lambda e: e.activation(q_[:], pt_[:, 0:256], AF.Silu), [pt_.b], [q_.b])
        R.op("dve", lambda e: e.tensor_tensor(o_[:], o_[:], q_[:], ALU.mult), [o_.b, q_.b], [o_.b])
        def f(e):
            e.transpose(b[2][:, 0:128], o_[:, 0:128], ident)
            return e.transpose(b[2][:, 128:256], o_[:, 128:256], ident)
        R.op("pe", f, [o_.b, C.b], [b[2].b])
        R.op("act", lambda e: e.activation(y_[:].rearrange("p j t -> p (j t)"), b[2][:, 0:256], AF.Copy), [b[2].b], [y_.b])
        R.dma(G.yT[512:768, t0:t0 + 128].rearrange("(j p) t -> p j t", p=128), y_[:], reads=[y_.b], writes=[G.yT.b], q="pool")

    cnt = 0
    for st_ in range(NT):
        recs = []
        for d in range(2):
            ti = st_ if d == 0 else NT - 1 - st_
            second = (ti >= NT // 2) if d == 0 else (ti < NT // 2)
            recs.append(Rec())
            body(recs[-1], d, ti, d, 2 * d + st_ % 2, second)
        merge(P, recs)
    P.barrier()
    A.reset(m0)


def x_rows(G, l, name, tok0, r0, n):
    if l == 0:
        return (G.ctx_in if name == "ctx" else G.x_in)[r0:r0 + n, :], None
    t = G.xs[l % 2]
    return t[tok0 + r0:tok0 + r0 + n, :], t.b


def phase_c(G, l, streams, hT2):
    P, A = G.P, G.A
    m0 = A.mark()
    rp = G.rowp_sb
    wo = A.alloc("wo", [128, 8, D], BF16)
    xt = [A.alloc("cxt", [128, D]) for _ in range(2)]
    load_weight_bf16(G, wo, lambda k: G.w_out[l, k * 128:(k + 1) * 128, :], D, xt, 8)
    yb = [A.alloc("cyb", [128, 8, 512], BF16) for _ in range(2)]
    t1 = [A.alloc("ct1", [128, D]) for _ in range(2)]
    xn = [A.alloc("cxn", [128, D]) for _ in range(4)]
    st = [A.alloc("cst", [128, 2, 6]) for _ in range(2)]
    mv = [A.alloc("cmv", [128, 2]) for _ in range(2)]
    rstd = [A.alloc("crstd", [128, 1]) for _ in range(2)]
    ident = G.consts[:, K_ID, :]
    po = [[G.psum[0], G.psum[1]], [G.psum[2], G.psum[3]]]
    pT = [G.psum[4], G.psum[5]]
    cb = 0
    ct = 0
    for (name, tok0, L, s) in streams:
        bs = min(512, L)
        for b0 in range(0, L, bs):
            nt = bs // 128
            y_ = yb[cb % 2]
            cb += 1
            P.dma(y_[:, :, 0:bs], G.yT[:, tok0 + b0:tok0 + b0 + bs].rearrange("(k p) t -> p k t", p=128), reads=[G.yT.b], writes=[y_.b])
            for m in range(nt):
                q = ct % 2
                ct += 1
                x_, t_, pp = xt[q], t1[q], po[q]
                r0 = b0 + m * 128
                src, sb_ = x_rows(G, l, name, tok0, r0, 128)
                P.dma(x_[:], src, reads=[sb_] if sb_ is not None else [], writes=[x_.b])
                for half in range(2):
                    def f(e, half=half, y_=y_, m=m, pp=pp):
                        for k in range(8):
                            ins = e.matmul(pp[half][:, :], y_[:, k, m * 128:(m + 1) * 128], wo[:, k, half * 512:(half + 1) * 512],
                                           start=(k == 0), stop=(k == 7))
                        return ins
                    P.op("pe", f, [y_.b, wo.b], [pp[half].b])
                    P.op("dve", lambda e, half=half, t_=t_, pp=pp, s=s: e.tensor_tensor(
                        t_[:, half * 512:(half + 1) * 512], pp[half][:, :], G.gb[:, s, 0, half * 512:(half + 1) * 512], ALU.mult),
                        [pp[half].b, G.gb.b], [t_.b])
                P.op("dve", lambda e, t_=t_, x_=x_: e.scalar_tensor_tensor(t_[:], x_[:], DN_ALPHA, t_[:], ALU.mult, ALU.add), [x_.b, t_.b], [t_.b])
                ln_stats(G, t_, 128, st[q], mv[q], rstd[q])
                P.op("dve", lambda e, t_=t_, q=q: e.tensor_scalar(t_[:], t_[:], mv[q][:, 0:1], rstd[q][:, 0:1], ALU.subtract, ALU.mult),
                     [t_.b, mv[q].b, rstd[q].b], [t_.b])
                P.op("dve", lambda e, t_=t_: e.tensor_tensor(t_[:], t_[:], rp[:, R_LN1W:R_LN1W + D], ALU.mult), [t_.b, rp.b], [t_.b])
                P.op("dve", lambda e, t_=t_: e.tensor_tensor(t_[:], t_[:], rp[:, R_LN1B:R_LN1B + D], ALU.add), [t_.b, rp.b], [t_.b])
                P.dma(G.x1[tok0 + r0:tok0 + r0 + 128, :], t_[:], reads=[t_.b], writes=[G.x1.b], q="pool")
                ln_stats(G, t_, 128, st[q], mv[q], rstd[q])
                P.op("dve", lambda e, t_=t_, q=q, m=m: e.tensor_scalar(xn[m][:], t_[:], mv[q][:, 0:1], rstd[q][:, 0:1], ALU.subtract, ALU.mult),
                     [t_.b, mv[q].b, rstd[q].b], [xn[m].b])
            for k in range(8):
                p = pT[k % 2]
                def f(e, p=p, k=k, nt=nt):
                    for m in range(nt):
                        ins = e.transpose(p[:, m * 128:(m + 1) * 128], xn[m][:, k * 128:(k + 1) * 128], ident)
                    return ins
                P.op("pe", f, [xn[m].b for m in range(nt)] + [G.consts.b], [p.b])
                c0 = tok0 + b0
                P.op("act", lambda e, p=p, k=k, bs=bs, s=s, c0=c0: e.activation(
                    hT2[:, k, c0:c0 + bs], p[:, 0:bs], AF.Identity, bias=G.modc[:, 2, k, s:s + 1], scale=G.modc[:, 3, k, s:s + 1]),
                    [p.b, G.modc.b], [hT2.b])
    P.barrier()
    A.reset(m0)


def phase_d1(G, l, streams, hT2):
    P, A = G.P, G.A
    m0 = A.mark()
    cp = G.colp_sb
    wst = [A.alloc("dwst", [128, 8, 256]) for _ in range(2)]
    wab = [A.alloc("dwab", [128, 8, 256], BF16) for _ in range(2)]
    asb = [A.alloc("dasb", [128, 514]) for _ in range(2)]
    acc = [A.alloc("dacc", [128, 512]) for _ in range(2)]
    hc = [A.alloc("dhc", [128, 512], BF16) for _ in range(3)]
    up = G.ffn_up[l, :, :].rearrange("(k p) c -> p k c", p=128)
    pa = [[G.psum[0], G.psum[1], G.psum[2]], [G.psum[3], G.psum[4], G.psum[5]]]
    cnt = 0
    for c in range(DFF // 128):
        w_, wb_ = wst[c % 2], wab[c % 2]
        P.dma(w_[:, :, 0:128], up[:, :, c * 128:(c + 1) * 128], writes=[w_.b])
        P.dma(w_[:, :, 128:256], up[:, :, DFF + c * 128:DFF + (c + 1) * 128], writes=[w_.b])
        P.op("pool", lambda e, w_=w_, wb_=wb_: e.tensor_copy(wb_[:], w_[:]), [w_.b], [wb_.b])
        for (name, tok0, L, s) in streams:
            bs = min(512, L)
            for b0 in range(0, L, bs):
                i = cnt % 2
                cnt += 1
                a_, ac_, h_ = asb[i], acc[i], hc[cnt % 3]
                p0, p1, pb = pa[i]
                t0 = tok0 + b0
                lo = max(t0 - 1, tok0)
                hi = min(t0 + bs + 1, tok0 + L)
                jlo, jhi = lo - (t0 - 1), hi - (t0 - 1)
                half = (bs + 2) // 2
                segs = [(jlo, half + 1), (half - 1, jhi)] if bs == 512 else [(jlo, jhi)]
                for si, (j0, j1) in enumerate(segs):
                    pp = p0 if si == 0 else p1
                    def f(e, pp=pp, j0=j0, j1=j1, wb_=wb_, t0=t0):
                        for k in range(8):
                            ins = e.matmul(pp[:, 0:j1 - j0], wb_[:, k, 0:128], hT2[:, k, t0 - 1 + j0:t0 - 1 + j1], start=(k == 0), stop=(k == 7))
                        return ins
                    P.op("pe", f, [wb_.b, hT2.b], [pp.b])
                def f(e, pb=pb, wb_=wb_, t0=t0, bs=bs):
                    for k in range(8):
                        ins = e.matmul(pb[:, 0:bs], wb_[:, k, 128:256], hT2[:, k, t0:t0 + bs], start=(k == 0), stop=(k == 7))
                    return ins
                P.op("pe", f, [wb_.b, hT2.b], [pb.b])
                if jlo > 0:
                    P.op("pool", lambda e, a_=a_: e.memset(a_[:, 0:1], 0.0), [], [a_.b])
                if jhi < bs + 2:
                    P.op("pool", lambda e, a_=a_, bs=bs: e.memset(a_[:, bs + 1:bs + 2], 0.0), [], [a_.b])
                if len(segs) == 2:
                    (a0, a1), (b0_, b1_) = segs
                    P.op("act", lambda e, a_=a_, p0=p0, a0=a0, a1=a1: e.activation(a_[:, a0:a1], p0[:, 0:a1 - a0], AF.Copy), [p0.b], [a_.b])
                    P.op("act", lambda e, a_=a_, p1=p1, a1=a1, b0_=b0_, b1_=b1_: e.activation(
                        a_[:, a1:b1_], p1[:, a1 - b0_:b1_ - b0_], AF.Copy), [p1.b], [a_.b])
                else:
                    (a0, a1), = segs
                    P.op("act", lambda e, a_=a_, p0=p0, a0=a0, a1=a1: e.activation(a_[:, a0:a1], p0[:, 0:a1 - a0], AF.Copy), [p0.b], [a_.b])
                wo_ = C_FFNCW + c * 3
                P.op("dve", lambda e, a_=a_, ac_=ac_, bs=bs, wo_=wo_: e.tensor_scalar(ac_[:, 0:bs], a_[:, 0:bs], cp[:, wo_:wo_ + 1], None, ALU.mult),
                     [a_.b, cp.b], [ac_.b])
                for tap in (1, 2):
                    P.op("dve", lambda e, a_=a_, ac_=ac_, bs=bs, wo_=wo_, tap=tap: e.scalar_tensor_tensor(
                        ac_[:, 0:bs], a_[:, tap:tap + bs], cp[:, wo_ + tap:wo_ + tap + 1], ac_[:, 0:bs], ALU.mult, ALU.add), [a_.b, cp.b, ac_.b], [ac_.b])
                P.op("act", lambda e, ac_=ac_, bs=bs, c=c: e.activation(ac_[:, 0:bs], ac_[:, 0:bs], AF.Silu, bias=cp[:, C_FFNCB + c:C_FFNCB + c + 1]),
                     [ac_.b, cp.b], [ac_.b])
                P.op("dve", lambda e, ac_=ac_, h_=h_, pb=pb, bs=bs: e.tensor_tensor(h_[:, 0:bs], ac_[:, 0:bs], pb[:, 0:bs], ALU.mult), [ac_.b, pb.b], [h_.b])
                P.dma(G.hid[c * 128:(c + 1) * 128, t0:t0 + bs], h_[:, 0:bs], reads=[h_.b], writes=[G.hid.b], q="pool")
    P.barrier()
    A.reset(m0)


def phase_d2(G, l, streams, last):
    P, A = G.P, G.A
    m0 = A.mark()
    rp = G.rowp_sb
    NC_ = DFF // 128
    wd = A.alloc("wd", [128, NC_, D], BF16)
    stage = [A.alloc("wdstage", [128, D]) for _ in range(2)]
    load_weight_bf16(G, wd, lambda k: G.ffn_down[l, k * 128:(k + 1) * 128, :], D, stage, NC_)
    hb = [A.alloc("ehb", [128, NC_, 512], BF16) for _ in range(2)]
    xt = [A.alloc("ext", [128, D]) for _ in range(2)]
    t1 = [A.alloc("et1", [128, D]) for _ in range(2)]
    st = [A.alloc("est", [128, 2, 6]) for _ in range(2)]
    mv = [A.alloc("emv", [128, 2]) for _ in range(2)]
    rstd = [A.alloc("erstd", [128, 1]) for _ in range(2)]
    po = [[G.psum[0], G.psum[1]], [G.psum[2], G.psum[3]]]
    xnext = G.xs[(l + 1) % 2]
    cb = 0
    ct = 0
    for (name, tok0, L, s) in streams:
        bs = min(512, L)
        for b0 in range(0, L, bs):
            nt = bs // 128
            h_ = hb[cb % 2]
            cb += 1
            P.dma(h_[:, :, 0:bs], G.hid[:, tok0 + b0:tok0 + b0 + bs].rearrange("(c p) t -> p c t", p=128), reads=[G.hid.b], writes=[h_.b])
            for m in range(nt):
                q = ct % 2
                ct += 1
                x_, t_, pp = xt[q], t1[q], po[q]
                r0 = tok0 + b0 + m * 128
                P.dma(x_[:], G.x1[r0:r0 + 128, :], reads=[G.x1.b], writes=[x_.b])
                for half in range(2):
                    def f(e, half=half, h_=h_, m=m, pp=pp):
                        for c in range(NC_):
                            ins = e.matmul(pp[half][:, :], h_[:, c, m * 128:(m + 1) * 128], wd[:, c, half * 512:(half + 1) * 512],
                                           start=(c == 0), stop=(c == NC_ - 1))
                        return ins
                    P.op("pe", f, [h_.b, wd.b], [pp[half].b])
                    P.op("dve", lambda e, half=half, t_=t_, pp=pp, s=s: e.tensor_tensor(
                        t_[:, half * 512:(half + 1) * 512], pp[half][:, :], G.gb[:, s, 1, half * 512:(half + 1) * 512], ALU.mult),
                        [pp[half].b, G.gb.b], [t_.b])
                P.op("dve", lambda e, t_=t_, x_=x_: e.scalar_tensor_tensor(t_[:], x_[:], DN_ALPHA, t_[:], ALU.mult, ALU.add), [x_.b, t_.b], [t_.b])
                ln_stats(G, t_, 128, st[q], mv[q], rstd[q])
                P.op("dve", lambda e, t_=t_, q=q: e.tensor_scalar(t_[:], t_[:], mv[q][:, 0:1], rstd[q][:, 0:1], ALU.subtract, ALU.mult),
                     [t_.b, mv[q].b, rstd[q].b], [t_.b])
                P.op("dve", lambda e, t_=t_: e.tensor_tensor(t_[:], t_[:], rp[:, R_LN2W:R_LN2W + D], ALU.mult), [t_.b, rp.b], [t_.b])
                P.op("dve", lambda e, t_=t_: e.tensor_tensor(t_[:], t_[:], rp[:, R_LN2B:R_LN2B + D], ALU.add), [t_.b, rp.b], [t_.b])
                if last:
                    rr = b0 + m * 128
                    P.dma(G.out[rr:rr + 128, :], t_[:], reads=[t_.b], writes=[G.out.b], q="pool")
                else:
                    P.dma(xnext[r0:r0 + 128, :], t_[:], reads=[t_.b], writes=[xnext.b], q="pool")
    P.barrier()
    A.reset(m0)


def _col(v, nchunk):
    return np.ascontiguousarray(v.reshape(nchunk, 128).T)


def prep_inputs(inputs):
    f = lambda a: np.ascontiguousarray(np.asarray(a, dtype=np.float32))
    I = {k: f(v) for k, v in inputs.items()}
    colp = np.zeros((DEPTH, 128, NCOL), np.float32)
    rowp = np.zeros((DEPTH, 1, NROW), np.float32)
    poolw = np.zeros((DEPTH, 128, 2, 128), np.float32)
    gws = np.zeros((DEPTH, 128, 4, 128), np.float32)
    for l in range(DEPTH):
        cw = I["ssd_conv_w"][l]
        colp[l, :, C_SSDCW:C_SSDCW + 28] = cw.T.reshape(4, 128, 7).transpose(1, 0, 2).reshape(128, 28)
        colp[l, :, C_SSDCB:C_SSDCB + 4] = _col(I["ssd_conv_b"][l], 4)
        gw = I["gdn_conv_w"][l]
        colp[l, :, C_GDNCW:C_GDNCW + 42] = gw.T.reshape(6, 128, 7).transpose(1, 0, 2).reshape(128, 42)
        fw = I["ffn_conv_w"][l]
        colp[l, :, C_FFNCW:C_FFNCW + 66] = fw.T.reshape(22, 128, 3).transpose(1, 0, 2).reshape(128, 66)
        colp[l, :, C_FFNCB:C_FFNCB + 22] = _col(I["ffn_conv_b"][l], 22)
        colp[l, :, C_PSCALE:C_PSCALE + 2] = _col(I["pool_scale"][l], 2)
        colp[l, :, C_BMOD:C_BMOD + 48] = I["b_mod"][l].reshape(6, 8, 128).transpose(2, 0, 1).reshape(128, 48)
        r = rowp[l, 0]
        r[R_SSDNW:R_SSDNW + 256] = I["ssd_norm_w"][l]
        r[R_GDNNW:R_GDNNW + 64] = I["gdn_norm_w"][l]
        r[R_GLNW:R_GLNW + 256] = I["gmlp_ln_w"][l]
        r[R_GLNB:R_GLNB + 256] = I["gmlp_ln_b"][l]
        r[R_LN1W:R_LN1W + 1024] = I["ln1_w"][l]
        r[R_LN1B:R_LN1B + 1024] = I["ln1_b"][l]
        r[R_LN2W:R_LN2W + 1024] = I["ln2_w"][l]
        r[R_LN2B:R_LN2B + 1024] = I["ln2_b"][l]
        r[R_SSDD:R_SSDD + 256] = np.repeat(I["ssd_d"][l], 64)
        r[R_GBS:R_GBS + 512] = I["gmlp_bs"][l].reshape(-1)
        r[R_SDTB:R_SDTB + 8] = I["ssd_dt_bias"][l].reshape(-1)
        r[R_SALOG:R_SALOG + 8] = I["ssd_a_log"][l].reshape(-1)
        r[R_GDTB:R_GDTB + 8] = I["gdn_dt_bias"][l].reshape(-1)
        r[R_GALOG:R_GALOG + 8] = I["gdn_a_log"][l].reshape(-1)
        r[R_BG1:R_BG1 + 1024] = I["b_mod"][l][2048:3072]
        r[R_BG2:R_BG2 + 1024] = I["b_mod"][l][5120:6144]
        pw = I["pool_w"][l]
        for g in range(4):
            j, h = g // 2, g % 2
            poolw[l, h * 64:(h + 1) * 64, j, h * 64:(h + 1) * 64] = pw[g]
        gws[l] = I["gmlp_ws"][l].transpose(2, 0, 1)
    consts = make_consts()

    def pinv(RW):
        o = np.zeros((128, 2, RW), np.float32)
        pos = np.arange(RW)
        for jc in range(2):
            for hh in range(2):
                w = (2, 4, 8, 16)[2 * jc + hh]
                lo = np.clip(pos - w // 2, 0, RW)
                hi = np.clip(pos + w - w // 2, 0, RW)
                o[hh * 64:(hh + 1) * 64, jc, :] = 1.0 / (hi - lo).astype(np.float32)
        return o
    shared = dict(consts=consts, w_mod=I["w_mod"], w_in=I["w_in"], w_out=I["w_out"], ffn_up=I["ffn_up"],
                  ffn_down=I["ffn_down"], colp=colp, rowp=rowp, poolw=poolw, gws=gws,
                  pinv_g=pinv(64), pinv_c=pinv(256))
    maps = []
    for core in range(8):
        b = core % 4
        crep = np.zeros((128, 2, 8, 128), np.float32)
        crep[:, 0] = np.repeat(_col(I["c"][b], 8)[:, :, None], 128, axis=2)
        crep[:, 1] = np.repeat(_col(I["c_ctx"], 8)[:, :, None], 128, axis=2)
        m = dict(shared)
        m.update(x_in=I["x"][b], ctx_in=I["ctx"][b], crep=crep)
        maps.append(m)
    return maps


_NC_CACHE = {}


def kernel(**inputs):
    maps = prep_inputs(inputs)
    if "nc" not in _NC_CACHE:
        _NC_CACHE["nc"] = build()
    res = run_bass_kernel_spmd(_NC_CACHE["nc"], maps, core_ids=list(range(8)))
    out = np.stack([np.asarray(res.results[b]["out"], dtype=np.float32) for b in range(4)], axis=0)
    return out
```

```python
import numpy as np
import concourse.bass as bass
import concourse.mybir as mybir
from concourse.bass_utils import run_bass_kernel_spmd

F32 = mybir.dt.float32
BF16 = mybir.dt.bfloat16
ALU = mybir.AluOpType
AF = mybir.ActivationFunctionType
AX = mybir.AxisListType

import os
SSD_STOP = int(os.environ.get('SSD_STOP', '99'))
GDN_ROUNDS = int(os.environ.get('GDN_ROUNDS', '5'))
SEG = 16000
DSEG = 1000
DMAK = 8


class Buf:
    __slots__ = ("w", "r", "name")

    def __init__(self, name=""):
        self.w = None
        self.r = {}
        self.name = name


class Prog:
    ENGS = ["pe", "act", "dve", "pool", "sp"]

    def __init__(self, nc):
        self.nc = nc
        self.ops = {e: [] for e in self.ENGS}
        self.count = {e: 0 for e in self.ENGS}
        self.sems = {}
        self.waited = {e: {} for e in self.ENGS}
        self.dma_n = {e: 0 for e in self.ENGS}
        self.last = {}

    def _sem(self, key):
        if key not in self.sems:
            self.sems[key] = self.nc.alloc_semaphore("s_" + "_".join(map(str, key)))
        return self.sems[key]

    def _need(self, eng, waits, tok):
        if tok is None:
            return
        key, val = tok
        if key[0] == "c" and key[1] == "pe" and eng == "pe":
            return
        if self.waited[eng].get(key, 0) >= val:
            return
        if waits.get(key, 0) < val:
            waits[key] = val

    def op(self, eng, fn, reads=(), writes=(), dma=False, extra=()):
        waits = {}
        for b in reads:
            self._need(eng, waits, b.w)
        for b in writes:
            self._need(eng, waits, b.w)
            for k, v in b.r.items():
                self._need(eng, waits, (k, v))
        for t in extra:
            self._need(eng, waits, t)
        if dma:
            n = self.dma_n[eng]
            self.dma_n[eng] += 1
            s, r = n % DMAK, n // DMAK
            if r >= 1:
                pk = ("d", eng, s, (r - 1) // DSEG)
                self._need(eng, waits, (pk, 16 * (((r - 1) % DSEG) + 1)))
            tok = (("d", eng, s, r // DSEG), 16 * ((r % DSEG) + 1))
            inc = 16
        elif fn is None:
            tok = None
            inc = 0
        else:
            n = self.count[eng]
            self.count[eng] += 1
            tok = (("c", eng, n // SEG), (n % SEG) + 1)
            inc = 1
        for k, v in waits.items():
            self.waited[eng][k] = v
            self._sem(k)
        if tok is not None:
            self._sem(tok[0])
            self.last[tok[0]] = tok[1]
        self.ops[eng].append((list(waits.items()), fn, tok, inc))
        if tok is not None:
            for b in reads:
                b.r[tok[0]] = tok[1]
            for b in writes:
                b.w = tok
                b.r = {}
        return tok

    def dma(self, out, in_, reads=(), writes=(), q="sp", **kw):
        return self.op(q, lambda e: e.dma_start(out=out, in_=in_, **kw), reads, writes, dma=True)

    def barrier(self):
        toks = list(self.last.items())
        for e in self.ENGS:
            self.op(e, None, extra=toks)

    def emit(self):
        nc = self.nc
        with nc.Block() as block:
            def run(name):
                def body(e):
                    for waits, fn, tok, inc in self.ops[name]:
                        for k, v in waits:
                            e.wait_ge(self.sems[k], v)
                        if fn is not None:
                            ins = fn(e)
                            ins.then_inc(self.sems[tok[0]], inc)
                return body
            block.tensor(run("pe"))
            block.scalar(run("act"))
            block.vector(run("dve"))
            block.gpsimd(run("pool"))
            block.sync(run("sp"))


class Rec:
    def __init__(self):
        self.calls = []

    def op(self, eng, fn, reads=(), writes=(), dma=False, extra=()):
        self.calls.append((eng, fn, tuple(reads), tuple(writes), dma, tuple(extra)))

    def dma(self, out, in_, reads=(), writes=(), q="sp", **kw):
        self.op(q, lambda e: e.dma_start(out=out, in_=in_, **kw), reads, writes, dma=True)


def merge(P, recs):
    idx = [0] * len(recs)
    live = True
    while live:
        live = False
        for k, r in enumerate(recs):
            if idx[k] < len(r.calls):
                P.op(*r.calls[idx[k]])
                idx[k] += 1
                live = True


class T:
    def __init__(self, h, name=""):
        self.h = h
        self.b = Buf(name)

    def __getitem__(self, k):
        return self.h[k]


def _dtsize(dt):
    return 2 if dt == BF16 else 4


class Arena:
    def __init__(self, nc, base=16512, limit=229344):
        self.nc, self.base, self.limit, self.top, self.n = nc, base, limit, base, 0

    def alloc(self, name, shape, dt=F32):
        el = 1
        for s in shape[1:]:
            el *= s
        size = (el * _dtsize(dt) + 63) // 64 * 64
        off = self.top
        self.top += size
        assert self.top <= self.limit, (name, self.top)
        self.n += 1
        return T(self.nc.alloc_sbuf_tensor_at(f"{name}{self.n}", list(shape), dt, offset=off), name)

    def mark(self):
        return self.top

    def reset(self, m):
        self.top = m


D = 1024
LC = 256
LL = 4096
TALL = LC + LL
DEPTH = 4
NIN = 2584
DFF = 2816
DN_ALPHA = (2 * DEPTH) ** 0.25
LN_EPS = 1e-6
O_Z, O_XBC, O_DT, O_POOL, O_QKV, O_GATE, O_A, O_B, O_UV = 0, 256, 768, 776, 1032, 1800, 2056, 2064, 2072
FM_GROUPS = [(O_XBC, 512), (O_POOL, 256), (O_QKV, 768), (O_UV, 512)]
NFM = 2048
NTM = 536
C_SSDCW, C_SSDCB, C_GDNCW, C_FFNCW, C_FFNCB, C_PSCALE, C_BMOD = 0, 28, 32, 74, 140, 162, 164
NCOL = 164 + 48
R_SSDNW, R_GDNNW, R_GLNW, R_GLNB, R_LN1W, R_LN1B, R_LN2W, R_LN2B = 0, 256, 320, 576, 832, 1856, 2880, 3904
R_SSDD, R_GBS, R_SDTB, R_SALOG, R_GDTB, R_GALOG, R_BG1, R_BG2 = 4928, 5184, 5696, 5704, 5712, 5720, 5728, 6752
NROW = 7776
K_ID, K_ONES, K_LF, K_LB, K_NMF, K_NMB, K_LF2, K_LB2, K_NI2F, K_NI2B, K_NS2F, K_NS2B, K_BO2, K_SEL0, K_SEL1 = range(15)
NCONST = 15
NEG = -30000.0


def make_consts():
    k = np.arange(128)[:, None]
    m = np.arange(128)[None, :]
    same = (k // 64) == (m // 64)
    c = np.zeros((128, NCONST, 128), np.float32)
    c[:, K_ID] = (k == m)
    c[:, K_ONES] = 1.0
    c[:, K_LF] = (k <= m)
    c[:, K_LB] = (k >= m)
    c[:, K_NMF] = np.where(m >= k, 0.0, NEG)
    c[:, K_NMB] = np.where(m <= k, 0.0, NEG)
    c[:, K_LF2] = (k <= m) & same
    c[:, K_LB2] = (k >= m) & same
    c[:, K_NI2F] = np.where((m >= k) & same, 0.0, NEG)
    c[:, K_NI2B] = np.where((m <= k) & same, 0.0, NEG)
    c[:, K_NS2F] = np.where((k > m) & same, 0.0, NEG)
    c[:, K_NS2B] = np.where((k < m) & same, 0.0, NEG)
    c[:, K_BO2] = same
    c[:, K_SEL0] = (k < 64) * np.ones_like(m)
    c[:, K_SEL1] = (k >= 64) * np.ones_like(m)
    return c


class Ctx:
    pass


def build(nlayers=DEPTH, stop_after=None, dbg=False, mixers=None, only=None):
    nc = bass.Bass("TRN2", target_bir_lowering=False)
    G = Ctx()
    G.nc = nc
    P = Prog(nc)
    G.P = P

    def din(name, shape):
        return nc.dram_tensor(name, list(shape), F32, kind="ExternalInput")

    G.x_in = din("x_in", [LL, D])
    G.ctx_in = din("ctx_in", [LC, D])
    G.crep = din("crep", [128, 2, 8, 128])
    G.consts_d = din("consts", [128, NCONST, 128])
    G.w_mod = din("w_mod", [DEPTH, D, 6 * D])
    G.w_in = din("w_in", [DEPTH, D, NIN])
    G.w_out = din("w_out", [DEPTH, D, D])
    G.ffn_up = din("ffn_up", [DEPTH, D, 2 * DFF])
    G.ffn_down = din("ffn_down", [DEPTH, DFF, D])
    G.colp = din("colp", [DEPTH, 128, NCOL])
    G.rowp = din("rowp", [DEPTH, 1, NROW])
    G.poolw = din("poolw", [DEPTH, 128, 2, 128])
    G.gws = din("gws", [DEPTH, 128, 4, 128])
    G.pinv_g = din("pinv_g", [128, 2, 64])
    G.pinv_c = din("pinv_c", [128, 2, 256])
    G.out = T(nc.dram_tensor("out", [LL, D], F32, kind="ExternalOutput"), "out")
    G.xs = [T(nc.dram_tensor(f"xs{i}", [TALL, D], F32), f"xs{i}") for i in range(2)]
    G.x1 = T(nc.dram_tensor("x1s", [TALL, D], F32), "x1s")
    G.projF = T(nc.dram_tensor("projF", [NFM, TALL], F32), "projF")
    G.projT = T(nc.dram_tensor("projT", [TALL, NTM], F32), "projT")
    G.convF = T(nc.dram_tensor("convF", [1280, TALL], F32), "convF")
    G.yT = T(nc.dram_tensor("yT", [D, TALL], BF16), "yT")
    G.hid = T(nc.dram_tensor("hid", [DFF, TALL], BF16), "hid")
    G.dbg = {}
    if dbg:
        G.dbg["projF"] = T(nc.dram_tensor("d_projF", [NFM, TALL], F32, kind="ExternalOutput"))
        G.dbg["projT"] = T(nc.dram_tensor("d_projT", [TALL, NTM], F32, kind="ExternalOutput"))
        G.dbg["mod"] = T(nc.dram_tensor("d_mod", [128, 64], F32, kind="ExternalOutput"))
        G.dbg["gb"] = T(nc.dram_tensor("d_gb", [128, 4096], F32, kind="ExternalOutput"))
        G.dbg["yT"] = T(nc.dram_tensor("d_yT", [D, TALL], BF16, kind="ExternalOutput"))
        G.dbg["convF"] = T(nc.dram_tensor("d_convF", [1280, TALL], F32, kind="ExternalOutput"))
        G.dbg["x1"] = T(nc.dram_tensor("d_x1", [TALL, D], F32, kind="ExternalOutput"))
        G.dbg["x2"] = T(nc.dram_tensor("d_x2", [TALL, D], F32, kind="ExternalOutput"))
        G.dbg["st"] = T(nc.dram_tensor("d_st", [64, 2 * 4 * 64], F32, kind="ExternalOutput"))

    A = Arena(nc)
    G.A = A
    G.psum = [T(nc.alloc_psum_tensor(f"ps{i}", [128, 512], F32), f"ps{i}") for i in range(8)]
    G.consts = A.alloc("consts", [128, NCONST, 128])
    G.csil = A.alloc("csil", [128, 2, 8, 128])
    G.colp_sb = A.alloc("colp", [128, NCOL])
    G.rowp_sb = A.alloc("rowp", [128, NROW])
    G.modc = A.alloc("modc", [128, 4, 8, 2])
    G.gb = A.alloc("gb", [128, 2, 2, 1024])
    G.eps = A.alloc("eps", [128, 1])
    G.sst = [A.alloc("sst", [64, 4, 64]) for _ in range(2)]
    G.gst = [A.alloc("gst", [64, 4, 64]) for _ in range(2)]
    if mixers is not None:
        G.mixers = mixers
    G.dumps_on = dbg
    G.dumps = {}

    P.dma(G.consts[:], G.consts_d[:, :, :], writes=[G.consts.b])
    P.dma(G.csil[:], G.crep[:, :, :, :], writes=[G.csil.b])
    P.op("act", lambda e: e.activation(G.csil[:], G.csil[:], AF.Silu), [G.csil.b], [G.csil.b])
    P.op("dve", lambda e: e.memset(G.eps[:], LN_EPS), [], [G.eps.b])

    streams = [("ctx", 0, LC, 1), ("lat", LC, LL, 0)]
    if only is not None:
        streams = [st_ for st_ in streams if st_[0] in only]
    for l in range(nlayers):
        G.l = l
        xin = G.xs[l % 2]
        mod_phase(G, l)
        if stop_after == "mod":
            break
        phase_a(G, l, streams)
        if stop_after == "A":
            break
        prep_pass(G, l, streams)
        mix = G.mixers if hasattr(G, "mixers") else ("pool", "gmlp", "ssd", "gdn")
        for d_ in range(2):
            P.op("dve", lambda e, d_=d_: e.memset(G.sst[d_][:], 0.0), [], [G.sst[d_].b])
            P.op("dve", lambda e, d_=d_: e.memset(G.gst[d_][:], 0.0), [], [G.gst[d_].b])
        for st_ in streams:
            if "pool" in mix:
                pool_mixer(G, l, st_)
            if "gmlp" in mix and "ssd" in mix:
                mg = A.mark()
                gt_ = gmlp_setup(G, l)
                ssd_mixer(G, l, st_, co_tile=gt_)
                A.reset(mg)
            else:
                if "gmlp" in mix:
                    gmlp_mixer(G, l, st_)
                if "ssd" in mix:
                    ssd_mixer(G, l, st_)
            if "gdn" in mix:
                gdn_mixer(G, l, st_)
        if stop_after == "mix":
            break
        last = (l == DEPTH - 1)
        cd_streams = [st_ for st_ in streams if not (last and st_[0] == "ctx")]
        mC = A.mark()
        hT2 = A.alloc("hT2", [128, 8, TALL], BF16)
        phase_c(G, l, cd_streams, hT2)
        if stop_after == "C":
            break
        phase_d1(G, l, cd_streams, hT2)
        A.reset(mC)
        if stop_after == "D1":
            break
        phase_d2(G, l, cd_streams, last)

    if dbg:
        P.barrier()
        m0 = A.mark()
        tlo = min(st_[1] for st_ in streams)
        thi = max(st_[1] + st_[2] for st_ in streams)
        TW = thi - tlo
        tmp = A.alloc("dbgtmp", [128, TALL])
        tb = A.alloc("dbgtb", [128, TALL], BF16)
        P.dma(G.dbg["mod"][:, :], G.modc[:].rearrange("p a k s -> p (a k s)"), reads=[G.modc.b], writes=[G.dbg["mod"].b], q="pool")
        P.dma(G.dbg["gb"][:, :], G.gb[:].rearrange("p s g d -> p (s g d)"), reads=[G.gb.b], writes=[G.dbg["gb"].b], q="pool")
        if stop_after == "A":
            for r in range(NFM // 128):
                P.dma(tmp[:, 0:TW], G.projF[r * 128:(r + 1) * 128, tlo:thi], reads=[G.projF.b], writes=[tmp.b])
                P.dma(G.dbg["projF"][r * 128:(r + 1) * 128, tlo:thi], tmp[:, 0:TW], reads=[tmp.b], writes=[G.dbg["projF"].b], q="pool")
            for r in range(tlo // 128, thi // 128):
                P.dma(tmp[:, 0:NTM], G.projT[r * 128:(r + 1) * 128, :], reads=[G.projT.b], writes=[tmp.b])
                P.dma(G.dbg["projT"][r * 128:(r + 1) * 128, :], tmp[:, 0:NTM], reads=[tmp.b], writes=[G.dbg["projT"].b], q="pool")
        if stop_after is None:
            for r in range(tlo // 128, thi // 128):
                P.dma(tmp[:, 0:D], G.x1[r * 128:(r + 1) * 128, :], reads=[G.x1.b], writes=[tmp.b])
                P.dma(G.dbg["x1"][r * 128:(r + 1) * 128, :], tmp[:, 0:D], reads=[tmp.b], writes=[G.dbg["x1"].b], q="pool")
                P.dma(tmp[:, 0:D], G.xs[nlayers % 2][r * 128:(r + 1) * 128, :], reads=[G.xs[nlayers % 2].b], writes=[tmp.b])
                P.dma(G.dbg["x2"][r * 128:(r + 1) * 128, :], tmp[:, 0:D], reads=[tmp.b], writes=[G.dbg["x2"].b], q="pool")
        if stop_after == "mix":
            for d_ in range(2):
                P.dma(G.dbg["st"][:, d_ * 256:(d_ + 1) * 256], G.sst[d_][:].rearrange("p h d -> p (h d)"), reads=[G.sst[d_].b], writes=[G.dbg["st"].b], q="pool")
            rows = dict(ssd=(0, 2), pool=(2, 4), gdn=(4, 6), gmlp=(6, 8))
            for mname in (G.mixers if hasattr(G, "mixers") else rows.keys()):
                for r in range(*rows[mname]):
                    P.dma(tb[:, 0:TW], G.yT[r * 128:(r + 1) * 128, tlo:thi], reads=[G.yT.b], writes=[tb.b])
                    P.dma(G.dbg["yT"][r * 128:(r + 1) * 128, tlo:thi], tb[:, 0:TW], reads=[tb.b], writes=[G.dbg["yT"].b], q="pool")
            for r in range(1280 // 128):
                P.dma(tmp[:, 0:TW], G.convF[r * 128:(r + 1) * 128, tlo:thi], reads=[G.convF.b], writes=[tmp.b])
                P.dma(G.dbg["convF"][r * 128:(r + 1) * 128, tlo:thi], tmp[:, 0:TW], reads=[tmp.b], writes=[G.dbg["convF"].b], q="pool")
        P.op("pool", None, reads=[v.b for v in G.dbg.values()])
        A.reset(m0)
    P.op("pool", None, reads=[G.out.b])
    P.emit()
    return nc


def mod_phase(G, l):
    nc, P, A = G.nc, G.P, G.A
    m0 = A.mark()
    P.dma(G.colp_sb[:], G.colp[l, :, :], writes=[G.colp_sb.b])
    P.dma(G.rowp_sb[:], G.rowp[l, 0:1, :].partition_broadcast(128), writes=[G.rowp_sb.b])
    wm = [A.alloc("wm", [128, 8, 512]) for _ in range(2)]
    pc = G.psum[0]
    pg = [G.psum[1], G.psum[2]]
    wsrc = G.w_mod[l, :, :].rearrange("(k p) c -> p k c", p=128)
    colvec = {0: 0, 1: 1, 3: 2, 4: 3}
    for n in range(12):
        w = wm[n % 2]
        P.dma(w[:], wsrc[:, :, n * 512:(n + 1) * 512], writes=[w.b])
        vec, half = n // 2, n % 2
        if vec in colvec:
            a = colvec[vec]
            for j in range(4):
                kc = half * 4 + j

                def f(e, w=w, j=j, a=a, kc=kc):
                    for k in range(8):
                        ins = e.matmul(pc[:, (a * 8 + kc) * 2:(a * 8 + kc) * 2 + 2], w[:, k, j * 128:(j + 1) * 128],
                                       G.csil[:, :, k, 0], start=(k == 0), stop=(k == 7))
                    return ins
                P.op("pe", f, [w.b, G.csil.b], [pc.b])
        else:
            gi = 0 if vec == 2 else 1
            boff = R_BG1 if gi == 0 else R_BG2
            for s in range(2):
                def f(e, w=w, s=s):
                    for k in range(8):
                        ins = e.matmul(pg[s][:, :], G.csil[:, s, k, :], w[:, k, :], start=(k == 0), stop=(k == 7))
                    return ins
                P.op("pe", f, [w.b, G.csil.b], [pg[s].b])
                P.op("dve", lambda e, s=s, gi=gi, half=half, boff=boff: e.tensor_tensor(
                    G.gb[:, s, gi, half * 512:(half + 1) * 512], pg[s][:, :],
                    G.rowp_sb[:, boff + half * 512: boff + (half + 1) * 512], ALU.add),
                    [pg[s].b, G.rowp_sb.b], [G.gb.b])
    bm = G.colp_sb[:, C_BMOD:C_BMOD + 48].rearrange("p (v k) -> p v k", v=6)
    for vec, a in colvec.items():
        for s in range(2):
            P.op("dve", lambda e, vec=vec, a=a, s=s: e.tensor_tensor(
                G.modc[:, a, :, s], pc[:, a * 16:(a + 1) * 16].rearrange("p (k s) -> p k s", s=2)[:, :, s],
                bm[:, vec, :], ALU.add), [pc.b, G.colp_sb.b], [G.modc.b])
    for a in (1, 3):
        P.op("dve", lambda e, a=a: e.tensor_scalar_add(G.modc[:, a, :, :], G.modc[:, a, :, :], 1.0), [G.modc.b], [G.modc.b])
    P.barrier()
    A.reset(m0)


def ln_stats(G, xt, np_, st, mv, rstd, P=None):
    P = P or G.P
    def f(e):
        e.bn_stats(st[0:np_, 0, :], xt[0:np_, 0:512])
        return e.bn_stats(st[0:np_, 1, :], xt[0:np_, 512:1024])
    P.op("dve", f, [xt.b], [st.b])
    P.op("dve", lambda e: e.bn_aggr(mv[0:np_, :], st[0:np_, :, :].rearrange("p a b -> p (a b)")), [st.b], [mv.b])
    P.op("act", lambda e: e.activation(rstd[0:np_, :], mv[0:np_, 1:2], AF.Sqrt, bias=G.eps[0:np_, :]), [mv.b, G.eps.b], [rstd.b])
    P.op("dve", lambda e: e.reciprocal(rstd[0:np_, :], rstd[0:np_, :]), [rstd.b], [rstd.b])


def load_weight_bf16(G, dst, src_rows, ncols, stage, nk):
    P = G.P
    for k in range(nk):
        s = stage[k % len(stage)]
        P.dma(s[:, 0:ncols], src_rows(k), writes=[s.b])
        P.op("pool", lambda e, s=s, k=k: e.tensor_copy(dst[:, k, 0:ncols], s[:, 0:ncols]), [s.b], [dst.b])


def phase_a(G, l, streams):
    nc, P, A = G.nc, G.P, G.A
    m0 = A.mark()
    wi = A.alloc("wi", [128, 8, NIN], BF16)
    stage = [A.alloc("wstage", [128, NIN]) for _ in range(2)]
    load_weight_bf16(G, wi, lambda k: G.w_in[l, k * 128:(k + 1) * 128, :], NIN, stage, 8)
    xt = [A.alloc("xt", [128, D]) for _ in range(2)]
    xn = [A.alloc("xn", [128, D]) for _ in range(4)]
    st = [A.alloc("st", [128, 2, 6]) for _ in range(2)]
    mv = [A.alloc("mv", [128, 2]) for _ in range(2)]
    rstd = [A.alloc("rstd", [128, 1]) for _ in range(2)]
    hT = [A.alloc("hT", [128, 8, 512], BF16) for _ in range(2)]
    oF = [A.alloc("oF", [128, 512]) for _ in range(3)]
    oT = [A.alloc("oT", [128, NTM]) for _ in range(2)]
    pT = [G.psum[0], G.psum[1]]
    pF = [G.psum[2], G.psum[3]]
    pTa = [G.psum[4], G.psum[5]]
    pTb = [G.psum[6], G.psum[7]]
    ident = G.consts[:, K_ID, :]
    cnt = dict(t=0, b=0, f=0, o=0)
    xsrc = G.xs[l % 2]
    for (name, tok0, L, s) in streams:
        bs = 512 if L >= 512 else L
        for b0 in range(0, L, bs):
            nt = bs // 128
            W = bs
            h = hT[cnt["b"] % 2]
            cnt["b"] += 1
            for m in range(nt):
                t = xt[cnt["t"] % 2]
                q = cnt["t"] % 2
                cnt["t"] += 1
                r0 = b0 + m * 128
                if l == 0:
                    src = (G.ctx_in if name == "ctx" else G.x_in)[r0:r0 + 128, :]
                    P.dma(t[:], src, writes=[t.b])
                else:
                    P.dma(t[:], xsrc[tok0 + r0: tok0 + r0 + 128, :], reads=[xsrc.b], writes=[t.b])
                ln_stats(G, t, 128, st[q], mv[q], rstd[q])
                P.op("dve", lambda e, t=t, q=q, m=m: e.tensor_scalar(xn[m][:], t[:], mv[q][:, 0:1], rstd[q][:, 0:1],
                                                                    ALU.subtract, ALU.mult), [t.b, mv[q].b, rstd[q].b], [xn[m].b])
            for k in range(8):
                p = pT[k % 2]

                def f(e, p=p, k=k, nt=nt):
                    for m in range(nt):
                        ins = e.transpose(p[:, m * 128:(m + 1) * 128], xn[m][:, k * 128:(k + 1) * 128], ident)
                    return ins
                P.op("pe", f, [xn[m].b for m in range(nt)] + [G.consts.b], [p.b])
                P.op("act", lambda e, p=p, k=k, h=h, W=W, s=s: e.activation(
                    h[:, k, 0:W], p[:, 0:W], AF.Identity, bias=G.modc[:, 0, k, s:s + 1], scale=G.modc[:, 1, k, s:s + 1]),
                    [p.b, G.modc.b], [h.b])
            row = 0
            for (c0, n) in FM_GROUPS:
                for j in range(n // 128):
                    p = pF[cnt["f"] % 2]
                    o = oF[cnt["f"] % 3]
                    ev = "act" if cnt["f"] % 2 == 0 else "dve"
                    cnt["f"] += 1
                    cc = c0 + j * 128

                    def f(e, p=p, cc=cc, h=h, W=W):
                        for k in range(8):
                            ins = e.matmul(p[:, 0:W], wi[:, k, cc:cc + 128], h[:, k, 0:W], start=(k == 0), stop=(k == 7))
                        return ins
                    P.op("pe", f, [wi.b, h.b], [p.b])
                    if ev == "act":
                        P.op("act", lambda e, p=p, o=o, W=W: e.activation(o[:, 0:W], p[:, 0:W], AF.Copy), [p.b], [o.b])
                    else:
                        P.op("dve", lambda e, p=p, o=o, W=W: e.tensor_copy(o[:, 0:W], p[:, 0:W]), [p.b], [o.b])
                    P.dma(G.projF[row:row + 128, tok0 + b0: tok0 + b0 + W], o[:, 0:W], reads=[o.b], writes=[G.projF.b],
                          q=("act" if ev == "act" else "pool"))
                    row += 128
            for m in range(nt):
                pa = pTa[cnt["o"] % 2]
                pb = pTb[cnt["o"] % 2]
                o = oT[cnt["o"] % 2]
                cnt["o"] += 1

                def f(e, pa=pa, pb=pb, h=h, m=m):
                    for k in range(8):
                        lw = h[:, k, m * 128:(m + 1) * 128]
                        e.matmul(pa[:, 0:256], lw, wi[:, k, O_Z:O_Z + 256], start=(k == 0), stop=(k == 7))
                        e.matmul(pb[:, 0:272], lw, wi[:, k, O_GATE:O_GATE + 272], start=(k == 0), stop=(k == 7))
                    for k in range(8):
                        lw = h[:, k, m * 128:(m + 1) * 128]
                        ins = e.matmul(pa[:, 256:264], lw, wi[:, k, O_DT:O_DT + 8], start=(k == 0), stop=(k == 7))
                    return ins
                P.op("pe", f, [wi.b, h.b], [pa.b, pb.b])
                P.op("act", lambda e, pa=pa, o=o: e.activation(o[:, 0:264], pa[:, 0:264], AF.Copy), [pa.b], [o.b])
                P.op("dve", lambda e, pb=pb, o=o: e.tensor_copy(o[:, 264:536], pb[:, 0:272]), [pb.b], [o.b])
                r0 = tok0 + b0 + m * 128
                P.dma(G.projT[r0:r0 + 128, :], o[:], reads=[o.b], writes=[G.projT.b], q="pool")
    P.barrier()
    A.reset(m0)


def dbgdump(G, name, t, ap, shape, dt=F32, P=None):
    if not getattr(G, "dumps_on", False) or name in G.dumps:
        return
    o = T(G.nc.dram_tensor("dd_" + name, list(shape), dt, kind="ExternalOutput"))
    G.dumps[name] = o
    nd = len(shape)
    P = P or G.P
    P.dma(o[tuple(slice(None) for _ in range(nd))], ap, reads=[t.b], writes=[o.b], q="pool")
    P.op("pool", None, reads=[o.b])


def bc_ap(ap, dims):
    return bass.AP(ap.tensor, ap.offset, [list(ap.ap[0])] + [list(d) for d in dims])


def prep_pass(G, l, streams):
    P, A = G.P, G.A
    m0 = A.mark()
    xin = [A.alloc("cin", [128, LL + 6]) for _ in range(2)]
    acc = [A.alloc("cacc", [128, LL]) for _ in range(2)]
    sq = [A.alloc("csq", [128, 512]) for _ in range(2)]
    rn = [A.alloc("crn", [128, 512]) for _ in range(2)]
    ps = [G.psum[0], G.psum[1]]
    bo2 = G.consts[:, K_BO2, :]
    chunks = []
    for j in range(4):
        chunks.append((j * 128, j * 128, C_SSDCW + j * 7, C_SSDCB + j, "p"))
    for j in range(6):
        chunks.append((768 + j * 128, 512 + j * 128, C_GDNCW + j * 7, None, "q" if j < 2 else ("k" if j < 4 else "p")))
    cnt = 0
    sc = 0
    for (name, tok0, L, s) in streams:
        for t in xin:
            P.op("dve", lambda e, t=t: e.memset(t[:, 0:3], 0.0), [], [t.b])
            P.op("dve", lambda e, t=t, L=L: e.memset(t[:, L + 3:L + 6], 0.0), [], [t.b])
        for (src, dst, wo, bo, kind) in chunks:
            t = xin[cnt % 2]
            a = acc[cnt % 2]
            cnt += 1
            w = G.colp_sb
            P.dma(t[:, 3:L + 3], G.projF[src:src + 128, tok0:tok0 + L], reads=[G.projF.b], writes=[t.b])
            P.op("dve", lambda e, t=t, a=a, L=L, wo=wo: e.tensor_scalar(a[:, 0:L], t[:, 0:L], w[:, wo:wo + 1], None, ALU.mult),
                 [t.b, w.b], [a.b])
            for tap in range(1, 7):
                P.op("dve", lambda e, t=t, a=a, L=L, wo=wo, tap=tap: e.scalar_tensor_tensor(
                    a[:, 0:L], t[:, tap:tap + L], w[:, wo + tap:wo + tap + 1], a[:, 0:L], ALU.mult, ALU.add),
                    [t.b, w.b, a.b], [a.b])
            if bo is not None:
                P.op("act", lambda e, a=a, L=L, bo=bo: e.activation(a[:, 0:L], a[:, 0:L], AF.Silu, bias=w[:, bo:bo + 1]),
                     [a.b, w.b], [a.b])
            else:
                P.op("act", lambda e, a=a, L=L: e.activation(a[:, 0:L], a[:, 0:L], AF.Silu), [a.b], [a.b])
            if kind in ("q", "k"):
                for sl in range(0, L, 512):
                    Wd = min(512, L - sl)
                    q_, r_, p_ = sq[sc % 2], rn[sc % 2], ps[sc % 2]
                    sc += 1
                    P.op("act", lambda e, a=a, q_=q_, sl=sl, Wd=Wd: e.activation(q_[:, 0:Wd], a[:, sl:sl + Wd], AF.Square), [a.b], [q_.b])
                    P.op("pe", lambda e, q_=q_, p_=p_, Wd=Wd: e.matmul(p_[:, 0:Wd], bo2, q_[:, 0:Wd], start=True, stop=True),
                         [q_.b, G.consts.b], [p_.b])
                    P.op("act", lambda e, r_=r_, p_=p_, Wd=Wd: e.activation(r_[:, 0:Wd], p_[:, 0:Wd], AF.Sqrt, bias=G.eps[:, :]),
                         [p_.b, G.eps.b], [r_.b])
                    P.op("dve", lambda e, r_=r_, Wd=Wd: e.reciprocal(r_[:, 0:Wd], r_[:, 0:Wd]), [r_.b], [r_.b])
                    if kind == "q":
                        P.op("dve", lambda e, a=a, r_=r_, sl=sl, Wd=Wd: e.scalar_tensor_tensor(
                            a[:, sl:sl + Wd], a[:, sl:sl + Wd], 0.125, r_[:, 0:Wd], ALU.mult, ALU.mult), [a.b, r_.b], [a.b])
                    else:
                        P.op("dve", lambda e, a=a, r_=r_, sl=sl, Wd=Wd: e.tensor_tensor(
                            a[:, sl:sl + Wd], a[:, sl:sl + Wd], r_[:, 0:Wd], ALU.mult), [a.b, r_.b], [a.b])
            P.dma(G.convF[dst:dst + 128, tok0:tok0 + L], a[:, 0:L], reads=[a.b], writes=[G.convF.b], q="pool")
    P.barrier()
    A.reset(m0)


def pool_mixer(G, l, stream):
    P, A = G.P, G.A
    name, tok0, L, s = stream
    m0 = A.mark()
    RW = 64 if name == "lat" else L
    NR = L // RW
    PW = RW + 16
    F = NR * PW
    xp = A.alloc("pxp", [128, NR, PW])
    ca = A.alloc("pca", [128, NR, PW])
    cb = A.alloc("pcb", [128, NR, PW])
    tmp = A.alloc("ptmp", [128, NR, RW])
    pooled = A.alloc("ppool", [128, NR, RW], BF16)
    pinv = A.alloc("pinv", [128, 2, RW])
    pwf = A.alloc("pwf", [128, 2, 128])
    pwb = A.alloc("pwb", [128, 2, 128], BF16)
    yo = [A.alloc("pyo", [128, 512], BF16) for _ in range(2)]
    ps = [G.psum[2], G.psum[3]]
    P.dma(pinv[:], (G.pinv_g if name == "lat" else G.pinv_c)[:, :, :], writes=[pinv.b])
    P.dma(pwf[:], G.poolw[l, :, :, :], writes=[pwf.b])
    P.op("pool", lambda e: e.tensor_copy(pwb[:], pwf[:]), [pwf.b], [pwb.b])
    fl = lambda t: t[:].rearrange("p r w -> p (r w)")
    cnt = 0
    for jc in range(2):
        P.op("dve", lambda e: e.memset(xp[:], 0.0), [], [xp.b])
        P.dma(xp[:, :, 8:8 + RW], G.projF[512 + jc * 128:512 + (jc + 1) * 128, tok0:tok0 + L].rearrange("p (r w) -> p r w", w=RW),
              reads=[G.projF.b], writes=[xp.b])
        xf, af, bf = fl(xp), fl(ca), fl(cb)
        P.op("dve", lambda e: e.tensor_tensor(af[:, 0:F - 1], xf[:, 0:F - 1], xf[:, 1:F], ALU.add), [xp.b], [ca.b])
        if jc == 0:
            P.op("dve", lambda e: e.tensor_tensor(bf[64:128, 0:F - 3], af[64:128, 0:F - 3], af[64:128, 2:F - 1], ALU.add), [ca.b], [cb.b])
            srcs = [(0, 64, 2, ca), (64, 128, 4, cb)]
        else:
            P.op("dve", lambda e: e.tensor_tensor(bf[:, 0:F - 3], af[:, 0:F - 3], af[:, 2:F - 1], ALU.add), [ca.b], [cb.b])
            P.op("dve", lambda e: e.tensor_tensor(af[:, 0:F - 7], bf[:, 0:F - 7], bf[:, 4:F - 3], ALU.add), [cb.b, ca.b], [ca.b])
            P.op("dve", lambda e: e.tensor_tensor(bf[64:128, 0:F - 15], af[64:128, 0:F - 15], af[64:128, 8:F - 7], ALU.add), [ca.b, cb.b], [cb.b])
            srcs = [(0, 64, 8, ca), (64, 128, 16, cb)]
        for (lo, hi, w, cw) in srcs:
            o = 8 - w // 2
            P.op("dve", lambda e, lo=lo, hi=hi, cw=cw, o=o, jc=jc: e.tensor_tensor(
                tmp[lo:hi, :, :], cw[lo:hi, :, o:o + RW], bc_ap(pinv[lo:hi, jc, :], [[0, NR], [1, RW]]), ALU.mult),
                [cw.b, pinv.b], [tmp.b])
            P.op("dve", lambda e, lo=lo, hi=hi: e.tensor_tensor(pooled[lo:hi, :, :], tmp[lo:hi, :, :], xp[lo:hi, :, 8:8 + RW], ALU.subtract),
                 [tmp.b, xp.b], [pooled.b])
        pf = pooled[:].rearrange("p r w -> p (r w)")
        for sl in range(0, L, 512):
            Wd = min(512, L - sl)
            p_, y_ = ps[cnt % 2], yo[cnt % 2]
            cnt += 1
            P.op("pe", lambda e, p_=p_, sl=sl, Wd=Wd, jc=jc: e.matmul(p_[:, 0:Wd], pwb[:, jc, :], pf[:, sl:sl + Wd], start=True, stop=True),
                 [pwb.b, pooled.b], [p_.b])
            P.op("act", lambda e, p_=p_, y_=y_, Wd=Wd, jc=jc: e.activation(
                y_[:, 0:Wd], p_[:, 0:Wd], AF.Copy, scale=G.colp_sb[:, C_PSCALE + jc:C_PSCALE + jc + 1]), [p_.b, G.colp_sb.b], [y_.b])
            P.dma(G.yT[256 + jc * 128:256 + (jc + 1) * 128, tok0 + sl:tok0 + sl + Wd], y_[:, 0:Wd], reads=[y_.b], writes=[G.yT.b], q="act")
    P.barrier()
    A.reset(m0)


def gmlp_setup(G, l):
    P, A = G.P, G.A
    NB = 2
    uv = [A.alloc("guv", [128, 4, 128]) for _ in range(NB)]
    gt = [A.alloc("ggt", [128, 4, 128]) for _ in range(NB)]
    vt = [A.alloc("gvt", [128, 256]) for _ in range(NB)]
    vb = [A.alloc("gvb", [128, 256], BF16) for _ in range(NB)]
    st = [A.alloc("gst_", [128, 6]) for _ in range(NB)]
    mv = [A.alloc("gmv", [128, 2]) for _ in range(NB)]
    rs = [A.alloc("grs", [128, 1]) for _ in range(NB)]
    tm = [A.alloc("gtm", [128, 2, 128]) for _ in range(NB)]
    yo = [A.alloc("gyo", [128, 2, 128], BF16) for _ in range(NB)]
    wsf = A.alloc("gwsf", [128, 4, 128])
    wsb = A.alloc("gwsb", [128, 4, 128], BF16)
    P.dma(wsf[:], G.gws[l, :, :, :], writes=[wsf.b])
    P.op("pool", lambda e: e.tensor_copy(wsb[:], wsf[:]), [wsf.b], [wsb.b])
    pT = [G.psum[4], G.psum[5]]
    pq = [G.psum[6], G.psum[7]]
    ident = G.consts[:, K_ID, :]
    rp = G.rowp_sb
    def tile(R, stream, ti):
        name, tok0, L, s = stream
        i = ti % NB
        u, g, v, vbb, p1, p2, tt, y = uv[i], gt[i], vt[i], vb[i], pT[i], pq[i], tm[i], yo[i]
        t0 = tok0 + ti * 128
        R.dma(u[:], G.projF[1536:2048, t0:t0 + 128].rearrange("(j p) t -> p j t", p=128), reads=[G.projF.b], writes=[u.b])
        uf = u[:].rearrange("p j t -> p (j t)")
        gf = g[:].rearrange("p j t -> p (j t)")
        R.op("dve", lambda e, uf=uf, gf=gf: e.tensor_tensor(gf, uf, uf, ALU.mult), [u.b], [g.b])
        R.op("dve", lambda e, gf=gf: e.tensor_scalar(gf, gf, 0.044715, 1.0, ALU.mult, ALU.add), [g.b], [g.b])
        R.op("dve", lambda e, uf=uf, gf=gf: e.tensor_tensor(gf, gf, uf, ALU.mult), [g.b, u.b], [g.b])
        R.op("act", lambda e, gf=gf: e.activation(gf, gf, AF.Sigmoid, scale=1.5957691216057308), [g.b], [g.b])
        R.op("dve", lambda e, uf=uf, gf=gf: e.tensor_tensor(gf, gf, uf, ALU.mult), [g.b, u.b], [g.b])

        def f(e, g=g, p1=p1):
            e.transpose(p1[:, 0:128], g[:, 2, :], ident)
            return e.transpose(p1[:, 128:256], g[:, 3, :], ident)
        R.op("pe", f, [g.b, G.consts.b], [p1.b])
        R.op("act", lambda e, v=v, p1=p1: e.activation(v[:], p1[:, 0:256], AF.Copy), [p1.b], [v.b])
        R.op("dve", lambda e, v=v, i=i: e.bn_stats(st[i][:], v[:]), [v.b], [st[i].b])
        R.op("dve", lambda e, i=i: e.bn_aggr(mv[i][:], st[i][:]), [st[i].b], [mv[i].b])
        R.op("act", lambda e, i=i: e.activation(rs[i][:], mv[i][:, 1:2], AF.Sqrt, bias=G.eps[:, :]), [mv[i].b, G.eps.b], [rs[i].b])
        R.op("dve", lambda e, i=i: e.reciprocal(rs[i][:], rs[i][:]), [rs[i].b], [rs[i].b])
        R.op("dve", lambda e, v=v, i=i: e.tensor_scalar(v[:], v[:], mv[i][:, 0:1], rs[i][:, 0:1], ALU.subtract, ALU.mult),
             [v.b, mv[i].b, rs[i].b], [v.b])
        R.op("dve", lambda e, v=v: e.tensor_tensor(v[:], v[:], rp[:, R_GLNW:R_GLNW + 256], ALU.mult), [v.b, rp.b], [v.b])
        R.op("dve", lambda e, v=v, vbb=vbb: e.tensor_tensor(vbb[:], v[:], rp[:, R_GLNB:R_GLNB + 256], ALU.add), [v.b, rp.b], [vbb.b])

        def f2(e, vbb=vbb, p2=p2):
            e.matmul(p2[:, 0:256], vbb[:, 0:128], wsb[:, 0:2, :].rearrange("p g i -> p (g i)"), start=True, stop=True)
            return e.matmul(p2[:, 256:512], vbb[:, 128:256], wsb[:, 2:4, :].rearrange("p g i -> p (g i)"), start=True, stop=True)
        R.op("pe", f2, [vbb.b, wsb.b], [p2.b])
        for q in range(2):
            for hh in range(2):
                gg = 2 * q + hh
                lo, hi = hh * 64, (hh + 1) * 64
                c0 = q * 256 + hh * 128
                R.op("dve", lambda e, lo=lo, hi=hi, c0=c0, q=q, gg=gg, tt=tt, p2=p2: e.tensor_tensor(
                    tt[lo:hi, q, :], p2[lo:hi, c0:c0 + 128], rp[lo:hi, R_GBS + gg * 128:R_GBS + (gg + 1) * 128], ALU.add),
                    [p2.b, rp.b], [tt.b])
                R.op("dve", lambda e, lo=lo, hi=hi, q=q, tt=tt, y=y, g=g: e.tensor_tensor(
                    y[lo:hi, q, :], tt[lo:hi, q, :], g[lo:hi, q, :], ALU.mult), [tt.b, g.b], [y.b])
        R.dma(G.yT[768:1024, t0:t0 + 128].rearrange("(j p) t -> p j t", p=128), y[:], reads=[y.b], writes=[G.yT.b], q="pool")
    return tile


def gmlp_mixer(G, l, stream):
    P, A = G.P, G.A
    m0 = A.mark()
    tile = gmlp_setup(G, l)
    for ti in range(stream[2] // 128):
        r = Rec()
        tile(r, stream, ti)
        merge(P, [r])
    P.barrier()
    A.reset(m0)


def softplus_small(P, out, x, tmp, reads, extra_w=()):
    P.op("dve", lambda e: e.scalar_tensor_tensor(tmp, x, -1.0, x, ALU.mult, ALU.max), reads, [extra_w[0]])
    P.op("act", lambda e: e.activation(tmp, tmp, AF.Exp, scale=-1.0), [extra_w[0]], [extra_w[0]])
    P.op("act", lambda e: e.activation(tmp, tmp, AF.Ln, bias=1.0), [extra_w[0]], [extra_w[0]])
    P.op("dve", lambda e: e.scalar_tensor_tensor(out, x, 0.0, tmp, ALU.max, ALU.add), reads + [extra_w[0]], [extra_w[1]])


def ssd_mixer(G, l, stream, co_tile=None):
    P, A = G.P, G.A
    name, tok0, L, s = stream
    NT = L // 128
    m0 = A.mark()
    rp = G.rowp_sb
    C = G.consts
    ident = C[:, K_ID, :]
    ones = C[:, K_ONES, :]
    nega = A.alloc("snega", [128, 8])
    negones = A.alloc("snegones", [128, 128])
    yacc = A.alloc("syacc", [128, NT, 256])
    P.op("act", lambda e: e.activation(nega[:], rp[:, R_SALOG:R_SALOG + 8], AF.Exp), [rp.b], [nega.b])
    P.op("dve", lambda e: e.tensor_scalar(nega[:], nega[:], -1.0, None, ALU.mult), [nega.b], [nega.b])
    P.op("dve", lambda e: e.memset(negones[:], -1.0), [], [negones.b])
    NB = 2
    def mk(nm, shape, dt=F32):
        return [A.alloc(nm, shape, dt) for _ in range(NB)]
    zt, cx, dtt, dtm, dta = mk("szt", [128, 264]) + mk("szt", [128, 264]), mk("scx", [128, 2, 128]) + mk("scx", [128, 2, 128]), mk("sdt", [128, 8]), mk("sdtm", [128, 8]), mk("sdta", [128, 8])
    bc4 = mk("sbc4", [64, 4, 128]) + mk("sbc4", [64, 4, 128])
    xtm, btm, cbb = mk("sxtm", [128, 256]), mk("sbtm", [128, 128], BF16), mk("scbb", [64, 4, 128], BF16)
    ML, Dm, Wm = mk("sML", [128, 4, 128]), mk("sDm", [128, 4, 128]), mk("sWm", [128, 4, 128], BF16)
    scl, dtw = mk("sscl", [128, 12]), mk("sdtw", [128, 8])
    xdt, xdw = mk("sxdt", [128, 256], BF16), mk("sxdw", [128, 256], BF16)
    yt, yg, ss, yo = mk("syt", [128, 256]), mk("syg", [128, 256]), mk("sss", [128, 2]), mk("syo", [128, 2, 128], BF16)
    ps = G.psum
    def body(R, d, ti, i, li, second):
        Ltri = C[:, K_LF if d == 0 else K_LB, :]
        nmask = C[:, K_NMF if d == 0 else K_NMB, :]
        t0 = tok0 + ti * 128
        z_, c_, dt_, dm_, da_, x_, b_, cb_ = zt[li], cx[li], dtt[i], dtm[i], dta[i], xtm[i], btm[i], cbb[i]
        bc_ = bc4[li]
        ml_, D_, W_, sc_, dw_, xd_, xw_ = ML[i], Dm[i], Wm[i], scl[i], dtw[i], xdt[i], xdw[i]
        pA, pB = ps[2 * i], ps[2 * i + 1]
        pC = pD = pA
        R.dma(z_[:], G.projT[t0:t0 + 128, 0:264], reads=[G.projT.b], writes=[z_.b])
        R.dma(c_[:], G.convF[0:256, t0:t0 + 128].rearrange("(j p) t -> p j t", p=128), reads=[G.convF.b], writes=[c_.b])
        R.dma(bc_[:], G.convF[256:512, t0:t0 + 128].rearrange("(j p) t -> p j t", p=64), reads=[G.convF.b], writes=[bc_.b])
        R.op("dve", lambda e, z_=z_, dt_=dt_: e.tensor_tensor(dt_[:], z_[:, 256:264], rp[:, R_SDTB:R_SDTB + 8], ALU.add), [z_.b, rp.b], [dt_.b])
        softplus_small(R, dt_[:], dt_[:], dm_[:], [dt_.b], (dm_.b, dt_.b))
        R.op("dve", lambda e, dt_=dt_, da_=da_: e.tensor_tensor(da_[:], dt_[:], nega[:], ALU.mult), [dt_.b, nega.b], [da_.b])
        if SSD_STOP <= 1:
            return
        def f(e, c_=c_, pA=pA, bc_=bc_):
            e.transpose(pA[:, 0:128], c_[:, 0, :], ident)
            e.transpose(pA[:, 128:256], c_[:, 1, :], ident)
            e.transpose(pA[:, 256:320], bc_[:, 0, :], ident[0:64, 0:64])
            return e.transpose(pA[:, 320:384], bc_[:, 1, :], ident[0:64, 0:64])
        R.op("pe", f, [c_.b, bc_.b, C.b], [pA.b])
        R.op("act", lambda e, x_=x_, pA=pA: e.activation(x_[:], pA[:, 0:256], AF.Copy), [pA.b], [x_.b])
        R.op("act", lambda e, b_=b_, pA=pA: e.activation(b_[:], pA[:, 256:384], AF.Copy), [pA.b], [b_.b])
        R.op("pool", lambda e, cb_=cb_, bc_=bc_: e.tensor_copy(cb_[:], bc_[:]), [bc_.b], [cb_.b])
        if SSD_STOP <= 2:
            return
        def f(e, cb_=cb_, pB=pB):
            e.matmul(pB[:, 0:128], cb_[:, 0, :], cb_[:, 2, :], start=True, stop=True)
            return e.matmul(pB[:, 128:256], cb_[:, 1, :], cb_[:, 3, :], start=True, stop=True)
        R.op("pe", f, [cb_.b], [pB.b])
        if SSD_STOP <= 3:
            return
        R.op("dve", lambda e, ml_=ml_, da_=da_, Ltri=Ltri: e.tensor_tensor(
            ml_[:], bc_ap(Ltri, [[0, 4], [1, 128]]), bc_ap(da_[:, d * 4:d * 4 + 4], [[1, 4], [0, 128]]), ALU.mult), [da_.b, C.b], [ml_.b])
        def f(e, ml_=ml_, pC=pC, nmask=nmask):
            for h in range(4):
                e.matmul(pC[:, h * 128:(h + 1) * 128], ones, ml_[:, h, :], start=True, stop=False)
                e.matmul(pC[:, h * 128:(h + 1) * 128], ml_[:, h, :], negones[:], start=False, stop=False)
                ins = e.matmul(pC[:, h * 128:(h + 1) * 128], ident, nmask, start=False, stop=True)
            return ins
        R.op("pe", f, [ml_.b, C.b, negones.b], [pC.b])
        R.op("act", lambda e, D_=D_, pC=pC: e.activation(D_[:].rearrange("p h i -> p (h i)"), pC[:, :], AF.Exp), [pC.b], [D_.b])
        if SSD_STOP <= 4:
            return
        def f(e, da_=da_, pD=pD, Ltri=Ltri):
            e.matmul(pD[:, 0:4], Ltri, da_[:, d * 4:d * 4 + 4], start=True, stop=True)
            return e.matmul(pD[:, 4:8], ones, da_[:, d * 4:d * 4 + 4], start=True, stop=True)
        R.op("pe", f, [da_.b, C.b], [pD.b])
        R.op("dve", lambda e, sc_=sc_, pD=pD: e.tensor_copy(sc_[:, 0:8], pD[:, 0:8]), [pD.b], [sc_.b])
        R.op("dve", lambda e, sc_=sc_: e.tensor_copy(sc_[:, 8:12], sc_[:, 4:8]), [sc_.b], [sc_.b])
        R.op("dve", lambda e, sc_=sc_: e.tensor_tensor(sc_[:, 4:8], sc_[:, 4:8], sc_[:, 0:4], ALU.subtract), [sc_.b], [sc_.b])
        R.op("act", lambda e, sc_=sc_: e.activation(sc_[:], sc_[:], AF.Exp), [sc_.b], [sc_.b])
        if SSD_STOP <= 5:
            return
        R.op("dve", lambda e, W_=W_, D_=D_, pB=pB: e.tensor_tensor(
            W_[:].rearrange("p (g r) i -> p g r i", g=2), bc_ap(pB[:, 0:256], [[128, 2], [0, 2], [1, 128]]),
            D_[:].rearrange("p (g r) i -> p g r i", g=2), ALU.mult), [pB.b, D_.b], [W_.b])
        if SSD_STOP <= 6:
            return
        R.op("dve", lambda e, dw_=dw_, dt_=dt_, sc_=sc_: e.tensor_tensor(dw_[:, 0:4], dt_[:, d * 4:d * 4 + 4], sc_[:, 4:8], ALU.mult),
             [dt_.b, sc_.b], [dw_.b])
        R.op("dve", lambda e, xd_=xd_, x_=x_, dt_=dt_: e.tensor_tensor(
            xd_[:].rearrange("p (h q) -> p h q", h=4), x_[:].rearrange("p (h q) -> p h q", h=4),
            bc_ap(dt_[:, d * 4:d * 4 + 4], [[1, 4], [0, 64]]), ALU.mult), [x_.b, dt_.b], [xd_.b])
        R.op("dve", lambda e, xw_=xw_, x_=x_, dw_=dw_: e.tensor_tensor(
            xw_[:].rearrange("p (h q) -> p h q", h=4), x_[:].rearrange("p (h q) -> p h q", h=4),
            bc_ap(dw_[:, 0:4], [[1, 4], [0, 64]]), ALU.mult), [x_.b, dw_.b], [xw_.b])
        if SSD_STOP <= 7:
            return
        def f(e, W_=W_, xd_=xd_, pA=pA):
            for h in range(4):
                ins = e.matmul(pA[:, h * 64:(h + 1) * 64], W_[:, h, :], xd_[:, h * 64:(h + 1) * 64], start=True, stop=True)
            return ins
        R.op("pe", f, [W_.b, xd_.b], [pA.b])
        def f(e, bc_=bc_, pB=pB):
            for h in range(4):
                g = h // 2
                ins = e.matmul(pB[:, 256 + h * 64:256 + (h + 1) * 64], bc_[:, 2 + g, :],
                               G.sst[d][:, h, :], start=True, stop=True)
            return ins
        R.op("pe", f, [bc_.b, G.sst[d].b], [pB.b])
        def f(e, b_=b_, xw_=xw_, pD=pD):
            for h in range(4):
                g = h // 2
                ins = e.matmul(pD[0:64, 256 + h * 64:256 + (h + 1) * 64], b_[:, g * 64:(g + 1) * 64], xw_[:, h * 64:(h + 1) * 64], start=True, stop=True)
            return ins
        R.op("pe", f, [b_.b, xw_.b], [pD.b])
        if SSD_STOP <= 8:
            return
        for h in range(4):
            lo, hi = 0, 64
            R.op("dve", lambda e, lo=lo, hi=hi, h=h, sc_=sc_, pD=pD: e.scalar_tensor_tensor(
                G.sst[d][lo:hi, h, :], G.sst[d][lo:hi, h, :], sc_[lo:hi, 8 + h:9 + h], pD[lo:hi, 256 + h * 64:256 + (h + 1) * 64],
                ALU.mult, ALU.add), [G.sst[d].b, sc_.b, pD.b], [G.sst[d].b])
        if SSD_STOP <= 9:
            return
        y_ = yt[i]
        R.op("dve", lambda e, y_=y_, pB=pB, sc_=sc_: e.tensor_tensor(
            y_[:].rearrange("p (h q) -> p h q", h=4), pB[:, 256:512].rearrange("p (h q) -> p h q", h=4),
            bc_ap(sc_[:, 0:4], [[1, 4], [0, 64]]), ALU.mult), [pB.b, sc_.b], [y_.b])
        R.op("dve", lambda e, y_=y_, pA=pA: e.tensor_tensor(y_[:], y_[:], pA[:, 0:256], ALU.add), [y_.b, pA.b], [y_.b])
        if name == "ctx" and ti == 0:
            dbgdump(G, f"dt{d}", dt_, dt_[:], [128, 8], P=R)
            dbgdump(G, f"da{d}", da_, da_[:], [128, 8], P=R)
            dbgdump(G, f"x{d}", x_, x_[:], [128, 256], P=R)
            dbgdump(G, f"D{d}", D_, D_[:].rearrange("p h i -> p (h i)"), [128, 512], P=R)
            dbgdump(G, f"W{d}", W_, W_[:].rearrange("p h i -> p (h i)"), [128, 512], BF16, P=R)
            dbgdump(G, f"sc{d}", sc_, sc_[:], [128, 12], P=R)
            dbgdump(G, f"xd{d}", xd_, xd_[:], [128, 256], BF16, P=R)
            dbgdump(G, f"y{d}", y_, y_[:], [128, 256], P=R)
        if not second:
            R.op("dve", lambda e, x_=x_, ti=ti: e.tensor_tensor(yacc[:, ti, :], x_[:], rp[:, R_SSDD:R_SSDD + 256], ALU.mult),
                 [x_.b, rp.b], [yacc.b])
            R.op("dve", lambda e, y_=y_, ti=ti: e.tensor_tensor(yacc[:, ti, :], yacc[:, ti, :], y_[:], ALU.add), [yacc.b, y_.b], [yacc.b])
        else:
            g_, s_, o_ = yg[i], ss[i], yo[i]
            R.op("dve", lambda e, y_=y_, ti=ti: e.tensor_tensor(y_[:], y_[:], yacc[:, ti, :], ALU.add), [yacc.b, y_.b], [y_.b])
            R.op("act", lambda e, g_=g_, z_=z_: e.activation(g_[:], z_[:, 0:256], AF.Silu), [z_.b], [g_.b])
            R.op("dve", lambda e, g_=g_, y_=y_: e.tensor_tensor(g_[:], g_[:], y_[:], ALU.mult), [g_.b, y_.b], [g_.b])
            R.op("act", lambda e, g_=g_, y_=y_, s_=s_: e.activation(y_[:], g_[:], AF.Square, accum_out=s_[:, 0:1]), [g_.b], [y_.b, s_.b])
            R.op("act", lambda e, s_=s_: e.activation(s_[:, 1:2], s_[:, 0:1], AF.Sqrt, bias=G.eps[:, :], scale=1.0 / 256.0), [s_.b, G.eps.b], [s_.b])
            R.op("dve", lambda e, s_=s_: e.reciprocal(s_[:, 1:2], s_[:, 1:2]), [s_.b], [s_.b])
            R.op("dve", lambda e, g_=g_, s_=s_: e.scalar_tensor_tensor(
                g_[:], g_[:], s_[:, 1:2], rp[:, R_SSDNW:R_SSDNW + 256], ALU.mult, ALU.mult), [g_.b, s_.b, rp.b], [g_.b])
            def f(e, g_=g_, pC=pC):
                e.transpose(pC[:, 0:128], g_[:, 0:128], ident)
                return e.transpose(pC[:, 128:256], g_[:, 128:256], ident)
            R.op("pe", f, [g_.b, C.b], [pC.b])
            R.op("act", lambda e, o_=o_, pC=pC: e.activation(o_[:].rearrange("p j t -> p (j t)"), pC[:, 0:256], AF.Copy), [pC.b], [o_.b])
            R.dma(G.yT[0:256, t0:t0 + 128].rearrange("(j p) t -> p j t", p=128), o_[:], reads=[o_.b], writes=[G.yT.b], q="pool")

    cnt = 0
    for st_ in range(NT):
        recs = []
        for d in range(2):
            ti = st_ if d == 0 else NT - 1 - st_
            second = (ti >= NT // 2) if d == 0 else (ti < NT // 2)
            recs.append(Rec())
            body(recs[-1], d, ti, d, 2 * d + st_ % 2, second)
        if co_tile is not None:
            recs.append(Rec())
            co_tile(recs[-1], stream, st_)
        merge(P, recs)
    P.barrier()
    A.reset(m0)


def gdn_mixer(G, l, stream):
    P, A = G.P, G.A
    name, tok0, L, s = stream
    NT = L // 128
    m0 = A.mark()
    rp, C = G.rowp_sb, G.consts
    ident, ones = C[:, K_ID, :], C[:, K_ONES, :]
    id64 = C[0:64, K_ID, 0:64]
    negag = A.alloc("gnegag", [128, 8])
    negones = A.alloc("gnegones", [128, 128])
    oacc = A.alloc("goacc", [128, NT, 256])
    P.op("act", lambda e: e.activation(negag[:], rp[:, R_GALOG:R_GALOG + 8], AF.Exp), [rp.b], [negag.b])
    P.op("dve", lambda e: e.tensor_scalar(negag[:], negag[:], -1.0, None, ALU.mult), [negag.b], [negag.b])
    P.op("dve", lambda e: e.memset(negones[:], -1.0), [], [negones.b])
    NB = 2

    def mk(nm, shape, dt=F32):
        return [A.alloc(nm, shape, dt) for _ in range(NB)]
    pt, fm, gx, kv = mk("gpt", [128, 272]) + mk("gpt", [128, 272]), mk("gfm", [64, 12, 128]), mk("ggx", [128, 24]), mk("gkv", [128, 512])
    sc, esc, ML = mk("gsc", [128, 16]), mk("gesc", [128, 16]), mk("gML", [128, 4, 128])
    decT, decS, A0, Q0, aT = mk("gdecT", [128, 4, 128]), mk("gdecS", [128, 4, 128]), mk("gA0", [128, 4, 128]), mk("gQ0", [128, 4, 128]), mk("gaT", [128, 4, 128])
    Pb, Qb, MTb = [mk("gPb", [128, 4, 128]) for _ in range(2)], [mk("gQb", [128, 4, 128]) for _ in range(2)], [mk("gMT", [128, 4, 128]) for _ in range(3)]
    Rv, Rk, bek, usb, wsb = mk("gRv", [128, 4, 64]), mk("gRk", [128, 4, 64]), mk("gbek", [128, 4]), mk("gusb", [128, 256]), mk("gwsb", [64, 512])
    ektc, kt = [mk("gektc", [128, 4]) for _ in range(2)], [mk("gkt", [128, 4, 64]) for _ in range(2)]
    vnew, oint, ot, sq, ss, yo = mk("gvnew", [128, 256]), mk("goint", [128, 256]), mk("got", [128, 256]), mk("gsq", [128, 256]), mk("gss", [128, 8]), mk("gyo", [128, 2, 128], BF16)
    for i in range(NB):
        P.op("dve", lambda e, i=i: e.memset(vnew[i][:], 0.0), [], [vnew[i].b])
    v4 = lambda t: t[:].rearrange("p (h q) -> p h q", h=4)

    def body(R, d, ti, i, li, second):
        g0_, g1_, g2_, g3_ = G.psum[4 * d:4 * d + 4]
        b = [g0_, g2_, g3_, g2_, g3_, g2_, g3_, g1_]
        bM = g0_
        t0 = tok0 + ti * 128
        Ltri2 = C[:, K_LF2 if d == 0 else K_LB2, :]
        niT = C[:, K_NI2F if d == 0 else K_NI2B, :]
        nsS = C[:, K_NS2F if d == 0 else K_NS2B, :]
        selc = [C[:, K_SEL0, 0:1], C[:, K_SEL1, 0:1]]
        pt_, f_, gx_, kv_, sc_, es_, ml_ = pt[li], fm[i], gx[i], kv[i], sc[i], esc[i], ML[i]
        dT_, dS_, a0_, q0_, at_ = decT[i], decS[i], A0[i], Q0[i], aT[i]
        R.dma(pt_[:], G.projT[t0:t0 + 128, 264:536], reads=[G.projT.b], writes=[pt_.b])
        R.dma(f_[:], G.convF[512:1280, t0:t0 + 128].rearrange("(j p) t -> p j t", p=64), reads=[G.convF.b], writes=[f_.b])
        R.op("dve", lambda e: e.tensor_tensor(gx_[:, 0:8], pt_[:, 256:264], rp[:, R_GDTB:R_GDTB + 8], ALU.add), [pt_.b, rp.b], [gx_.b])
        softplus_small(R, gx_[:, 0:8], gx_[:, 0:8], gx_[:, 16:24], [gx_.b], (gx_.b, gx_.b))
        R.op("dve", lambda e: e.tensor_tensor(gx_[:, 0:8], gx_[:, 0:8], negag[:], ALU.mult), [gx_.b, negag.b], [gx_.b])
        R.op("act", lambda e: e.activation(gx_[:, 8:16], pt_[:, 264:272], AF.Sigmoid), [pt_.b], [gx_.b])
        gd = gx_[:, d * 4:d * 4 + 4]
        bd = gx_[:, 8 + d * 4:8 + d * 4 + 4]
        def f(e):
            for h in range(4):
                e.transpose(b[0][:, h * 64:(h + 1) * 64], f_[:, 4 + h, :], id64)
            for h in range(4):
                ins = e.transpose(b[0][:, 256 + h * 64:256 + (h + 1) * 64], f_[:, 8 + h, :], id64)
            return ins
        R.op("pe", f, [f_.b, C.b], [b[0].b])
        R.op("act", lambda e: e.activation(kv_[:], b[0][:, :], AF.Copy), [b[0].b], [kv_.b])
        def f(e):
            e.matmul(b[7][:, 0:4], Ltri2, gd, start=True, stop=True)
            e.matmul(b[7][:, 4:8], C[:, K_BO2, :], gd, start=True, stop=True)
            e.matmul(b[7][:, 8:12], C[:, K_SEL0, :], gd, start=True, stop=True)
            return e.matmul(b[7][:, 12:16], C[:, K_SEL1, :], gd, start=True, stop=True)
        R.op("pe", f, [gx_.b, C.b], [b[7].b])
        R.op("dve", lambda e: e.tensor_copy(sc_[:], b[7][:, 0:16]), [b[7].b], [sc_.b])
        R.op("dve", lambda e: e.tensor_tensor(sc_[:, 4:8], sc_[:, 4:8], sc_[:, 0:4], ALU.subtract), [sc_.b], [sc_.b])
        R.op("act", lambda e: e.activation(es_[:], sc_[:], AF.Exp), [sc_.b], [es_.b])
        R.op("dve", lambda e: e.tensor_tensor(ml_[:], bc_ap(Ltri2, [[0, 4], [1, 128]]), bc_ap(gd, [[1, 4], [0, 128]]), ALU.mult),
             [gx_.b, C.b], [ml_.b])
        def f(e):
            for h in range(4):
                o = b[1][:, h * 128:(h + 1) * 128]
                e.matmul(o, ones, ml_[:, h, :], start=True, stop=False)
                e.matmul(o, ml_[:, h, :], negones[:], start=False, stop=False)
                ins = e.matmul(o, ident, niT, start=False, stop=True)
            return ins
        R.op("pe", f, [ml_.b, C.b, negones.b], [b[1].b])
        def f(e):
            for h in range(4):
                o = b[2][:, h * 128:(h + 1) * 128]
                e.matmul(o, ml_[:, h, :], ones, start=True, stop=False)
                e.matmul(o, negones[:], ml_[:, h, :], start=False, stop=False)
                ins = e.matmul(o, ident, nsS, start=False, stop=True)
            return ins
        R.op("pe", f, [ml_.b, C.b, negones.b], [b[2].b])
        fl = lambda t: t[:].rearrange("p h i -> p (h i)")
        R.op("act", lambda e: e.activation(fl(dT_), b[1][:, :], AF.Exp), [b[1].b], [dT_.b])
        R.op("act", lambda e: e.activation(fl(dS_), b[2][:, :], AF.Exp), [b[2].b], [dS_.b])
        def f(e):
            for h in range(4):
                ins = e.matmul(b[3][:, h * 128:(h + 1) * 128], f_[:, 4 + h, :], f_[:, 4 + h, :], start=True, stop=True)
            return ins
        R.op("pe", f, [f_.b], [b[3].b])
        def f(e):
            for h in range(4):
                ins = e.matmul(b[4][:, h * 128:(h + 1) * 128], f_[:, 4 + h, :], f_[:, h, :], start=True, stop=True)
            return ins
        R.op("pe", f, [f_.b], [b[4].b])
        R.op("dve", lambda e: e.tensor_tensor(fl(a0_), b[3][:, :], fl(dS_), ALU.mult), [b[3].b, dS_.b], [a0_.b])
        R.op("dve", lambda e: e.tensor_tensor(a0_[:], a0_[:], bc_ap(bd, [[1, 4], [0, 128]]), ALU.mult), [a0_.b, gx_.b], [a0_.b])
        R.op("dve", lambda e: e.tensor_tensor(fl(at_), b[4][:, :], fl(dT_), ALU.mult), [b[4].b, dT_.b], [at_.b])
        def f(e):
            for h in range(4):
                ins = e.transpose(b[5][:, h * 128:(h + 1) * 128], a0_[:, h, :], ident)
            return ins
        R.op("pe", f, [a0_.b, C.b], [b[5].b])
        R.op("act", lambda e: e.activation(fl(q0_), b[5][:, :], AF.Copy), [b[5].b], [q0_.b])
        mt = MTb[0][i]
        R.op("dve", lambda e, mt=mt: e.tensor_tensor(mt[:], bc_ap(ident, [[0, 4], [1, 128]]), q0_[:], ALU.subtract), [q0_.b, C.b], [mt.b])
        if name == "ctx" and ti == (0 if d == 0 else 1):
            dbgdump(G, f"gA{d}", a0_, fl(a0_), [128, 512], P=R)
            dbgdump(G, f"gQ{d}", q0_, fl(q0_), [128, 512], P=R)
            dbgdump(G, f"gM0{d}", mt, fl(mt), [128, 512], P=R)
            dbgdump(G, f"gdS{d}", dS_, fl(dS_), [128, 512], P=R)
            dbgdump(G, f"ggx{d}", gx_, gx_[:], [128, 24], P=R)
            dbgdump(G, f"gkv{d}", kv_, kv_[:], [128, 512], P=R)
        Pc, Qc = a0_, q0_
        for m in range(GDN_ROUNDS):
            Pn, Qn, mtn = Pb[m % 2][i], Qb[m % 2][i], MTb[(m + 1) % 3][i]
            def f(e, Pc=Pc, Qc=Qc):
                for h in range(4):
                    ins = e.matmul(b[3][:, h * 128:(h + 1) * 128], Qc[:, h, :], Pc[:, h, :], start=True, stop=True)
                return ins
            R.op("pe", f, [Pc.b, Qc.b], [b[3].b])
            def f(e, Pc=Pc, Qc=Qc):
                for h in range(4):
                    ins = e.matmul(b[4][:, h * 128:(h + 1) * 128], Pc[:, h, :], Qc[:, h, :], start=True, stop=True)
                return ins
            R.op("pe", f, [Pc.b, Qc.b], [b[4].b])
            R.op("act", lambda e, Pn=Pn: e.activation(fl(Pn), b[3][:, :], AF.Copy), [b[3].b], [Pn.b])
            R.op("dve", lambda e, Qn=Qn: e.tensor_copy(fl(Qn), b[4][:, :]), [b[4].b], [Qn.b])
            def f(e, Pn=Pn, mt=mt):
                for h in range(4):
                    ins = e.matmul(bM[:, h * 128:(h + 1) * 128], Pn[:, h, :], mt[:, h, :], start=True, stop=True)
                return ins
            R.op("pe", f, [Pn.b, mt.b], [bM.b])
            R.op("dve", lambda e, mt=mt, mtn=mtn: e.tensor_tensor(fl(mtn), fl(mt), bM[:, :], ALU.add), [mt.b, bM.b], [mtn.b])
            Pc, Qc, mt = Pn, Qn, mtn
        rv_, rk_, bk_, u_, w_ = Rv[i], Rk[i], bek[i], usb[i], wsb[i]
        R.op("dve", lambda e: e.tensor_tensor(rv_[:], v4(kv_)[:, 4:8, :] if False else kv_[:, 256:512].rearrange("p (h q) -> p h q", h=4),
                                              bc_ap(bd, [[1, 4], [0, 64]]), ALU.mult), [kv_.b, gx_.b], [rv_.b])
        R.op("dve", lambda e: e.tensor_tensor(bk_[:], bd, es_[:, 0:4], ALU.mult), [gx_.b, es_.b], [bk_.b])
        R.op("dve", lambda e: e.tensor_tensor(rk_[:], kv_[:, 0:256].rearrange("p (h q) -> p h q", h=4),
                                              bc_ap(bk_[:, 0:4], [[1, 4], [0, 64]]), ALU.mult), [kv_.b, bk_.b], [rk_.b])
        def f(e):
            for h in range(4):
                ins = e.matmul(b[7][:, 64 + h * 64:64 + (h + 1) * 64], mt[:, h, :], rv_[:, h, :], start=True, stop=True)
            return ins
        R.op("pe", f, [mt.b, rv_.b], [b[7].b])
        def f(e):
            for h in range(4):
                ins = e.matmul(b[0][0:64, h * 128:(h + 1) * 128], rk_[:, h, :], mt[:, h, :], start=True, stop=True)
            return ins
        R.op("pe", f, [mt.b, rk_.b], [b[0].b])
        R.op("act", lambda e: e.activation(u_[:], b[7][:, 64:320], AF.Copy), [b[7].b], [u_.b])
        R.op("dve", lambda e: e.tensor_copy(w_[:], b[0][0:64, :]), [b[0].b], [w_.b])
        for c in range(2):
            ek, k_ = ektc[c][i], kt[c][i]
            R.op("dve", lambda e, ek=ek, c=c: e.tensor_scalar(ek[:], es_[:, 4:8], selc[c], None, ALU.mult), [es_.b, C.b], [ek.b])
            R.op("dve", lambda e, ek=ek, k_=k_: e.tensor_tensor(k_[:], kv_[:, 0:256].rearrange("p (h q) -> p h q", h=4),
                                                               bc_ap(ek[:, 0:4], [[1, 4], [0, 64]]), ALU.mult), [kv_.b, ek.b], [k_.b])
        vn_, oi_ = vnew[i], oint[i]
        for c in ((0, 1) if d == 0 else (1, 0)):
            lo, hi = c * 64, (c + 1) * 64
            k_ = kt[c][i]
            def f(e):
                for h in range(4):
                    e.matmul(b[5][:, h * 64:(h + 1) * 64], w_[:, h * 128:(h + 1) * 128], G.gst[d][:, h, :], start=True, stop=True)
                for h in range(4):
                    ins = e.matmul(b[5][:, 256 + h * 64:256 + (h + 1) * 64], f_[:, h, :], G.gst[d][:, h, :], start=True, stop=True)
                return ins
            R.op("pe", f, [w_.b, f_.b, G.gst[d].b], [b[5].b])
            R.op("dve", lambda e, lo=lo, hi=hi: e.tensor_tensor(vn_[lo:hi, :], u_[lo:hi, :], b[5][lo:hi, 0:256], ALU.subtract), [u_.b, b[5].b], [vn_.b])
            R.op("dve", lambda e, lo=lo, hi=hi: e.tensor_tensor(oi_[lo:hi, :].rearrange("p (h q) -> p h q", h=4),
                                                                b[5][lo:hi, 256:512].rearrange("p (h q) -> p h q", h=4),
                                                                bc_ap(es_[lo:hi, 0:4], [[1, 4], [0, 64]]), ALU.mult), [b[5].b, es_.b], [oi_.b])
            def f(e, k_=k_):
                for h in range(4):
                    ins = e.matmul(b[6][0:64, h * 64:(h + 1) * 64], k_[:, h, :], vn_[:, h * 64:(h + 1) * 64], start=True, stop=True)
                return ins
            R.op("pe", f, [k_.b, vn_.b], [b[6].b])
            for h in range(4):
                R.op("dve", lambda e, h=h, c=c: e.scalar_tensor_tensor(
                    G.gst[d][:, h, :], G.gst[d][:, h, :], es_[0:64, 8 + 4 * c + h:9 + 4 * c + h], b[6][0:64, h * 64:(h + 1) * 64],
                    ALU.mult, ALU.add), [G.gst[d].b, es_.b, b[6].b], [G.gst[d].b])
        def f(e):
            for h in range(4):
                ins = e.matmul(b[7][:, 64 + h * 64:64 + (h + 1) * 64], at_[:, h, :], vn_[:, h * 64:(h + 1) * 64], start=True, stop=True)
            return ins
        R.op("pe", f, [at_.b, vn_.b], [b[7].b])
        if name == "ctx" and ti == (0 if d == 0 else 1):
            dbgdump(G, f"gMT{d}", mt, fl(mt), [128, 512], P=R)
            dbgdump(G, f"gu{d}", u_, u_[:], [128, 256], P=R)
            dbgdump(G, f"gw{d}", w_, w_[:], [64, 512], P=R)
            dbgdump(G, f"gvn{d}", vn_, vn_[:], [128, 256], P=R)
            dbgdump(G, f"gaT{d}", at_, fl(at_), [128, 512], P=R)
        if not second:
            R.op("dve", lambda e: e.tensor_tensor(oacc[:, ti, :], b[7][:, 64:320], oi_[:], ALU.add), [b[7].b, oi_.b], [oacc.b])
            return
        o_, q_, s_, y_ = ot[i], sq[i], ss[i], yo[i]
        R.op("dve", lambda e: e.tensor_tensor(o_[:], b[7][:, 64:320], oi_[:], ALU.add), [b[7].b, oi_.b], [o_.b])
        R.op("dve", lambda e: e.tensor_tensor(o_[:], o_[:], oacc[:, ti, :], ALU.add), [o_.b, oacc.b], [o_.b])
        R.op("dve", lambda e: e.tensor_tensor(q_[:], o_[:], o_[:], ALU.mult), [o_.b], [q_.b])
        R.op("dve", lambda e: e.reduce_sum(s_[:, 0:4], q_[:].rearrange("p (h q) -> p h q", h=4), AX.X), [q_.b], [s_.b])
        R.op("act", lambda e: e.activation(s_[:, 4:8], s_[:, 0:4], AF.Sqrt, bias=G.eps[:, :], scale=1.0 / 64.0), [s_.b, G.eps.b], [s_.b])
        R.op("dve", lambda e: e.reciprocal(s_[:, 4:8], s_[:, 4:8]), [s_.b], [s_.b])
        R.op("dve", lambda e: e.tensor_tensor(v4(o_), v4(o_), bc_ap(s_[:, 4:8], [[1, 4], [0, 64]]), ALU.mult), [o_.b, s_.b], [o_.b])
        R.op("dve", lambda e: e.tensor_tensor(v4(o_), v4(o_), bc_ap(rp[:, R_GDNNW:R_GDNNW + 64], [[0, 4], [1, 64]]), ALU.mult), [o_.b, rp.b], [o_.b])
        R.op("act", lambda e: e.activation(q_[:], pt_[:, 0:256], AF.Silu), [pt_.b], [q_.b])
        R.op("dve", lambda e: e.tensor_tensor(o_[:], o_[:], q_[:], ALU.mult), [o_.b, q_.b], [o_.b])
        def f(e):
            e.transpose(b[2][:, 0:128], o_[:, 0:128], ident)
            return e.transpose(b[2][:, 128:256], o_[:, 128:256], ident)
        R.op("pe", f, [o_.b, C.b], [b[2].b])
        R.op("act", lambda e: e.activation(y_[:].rearrange("p j t -> p (j t)"), b[2][:, 0:256], AF.Copy), [b[2].b], [y_.b])
        R.dma(G.yT[512:768, t0:t0 + 128].rearrange("(j p) t -> p j t", p=128), y_[:], reads=[y_.b], writes=[G.yT.b], q="pool")

    cnt = 0
    for st_ in range(NT):
        recs = []
        for d in range(2):
            ti = st_ if d == 0 else NT - 1 - st_
            second = (ti >= NT // 2) if d == 0 else (ti < NT // 2)
            recs.append(Rec())
            body(recs[-1], d, ti, d, 2 * d + st_ % 2, second)
        merge(P, recs)
    P.barrier()
    A.reset(m0)


def x_rows(G, l, name, tok0, r0, n):
    if l == 0:
        return (G.ctx_in if name == "ctx" else G.x_in)[r0:r0 + n, :], None
    t = G.xs[l % 2]
    return t[tok0 + r0:tok0 + r0 + n, :], t.b


def phase_c(G, l, streams, hT2):
    P, A = G.P, G.A
    m0 = A.mark()
    rp = G.rowp_sb
    wo = A.alloc("wo", [128, 8, D], BF16)
    xt = [A.alloc("cxt", [128, D]) for _ in range(2)]
    load_weight_bf16(G, wo, lambda k: G.w_out[l, k * 128:(k + 1) * 128, :], D, xt, 8)
    yb = [A.alloc("cyb", [128, 8, 512], BF16) for _ in range(2)]
    t1 = [A.alloc("ct1", [128, D]) for _ in range(2)]
    xn = [A.alloc("cxn", [128, D]) for _ in range(4)]
    st = [A.alloc("cst", [128, 2, 6]) for _ in range(2)]
    mv = [A.alloc("cmv", [128, 2]) for _ in range(2)]
    rstd = [A.alloc("crstd", [128, 1]) for _ in range(2)]
    ident = G.consts[:, K_ID, :]
    po = [[G.psum[0], G.psum[1]], [G.psum[2], G.psum[3]]]
    pT = [G.psum[4], G.psum[5]]
    cb = 0
    ct = 0
    for (name, tok0, L, s) in streams:
        bs = min(512, L)
        for b0 in range(0, L, bs):
            nt = bs // 128
            y_ = yb[cb % 2]
            cb += 1
            P.dma(y_[:, :, 0:bs], G.yT[:, tok0 + b0:tok0 + b0 + bs].rearrange("(k p) t -> p k t", p=128), reads=[G.yT.b], writes=[y_.b])
            recs = []
            for m in range(nt):
                R = Rec()
                recs.append(R)
                q = ct % 2
                ct += 1
                x_, t_, pp = xt[q], t1[q], po[q]
                r0 = b0 + m * 128
                src, sb_ = x_rows(G, l, name, tok0, r0, 128)
                R.dma(x_[:], src, reads=[sb_] if sb_ is not None else [], writes=[x_.b])
                for half in range(2):
                    def f(e, half=half, y_=y_, m=m, pp=pp):
                        for k in range(8):
                            ins = e.matmul(pp[half][:, :], y_[:, k, m * 128:(m + 1) * 128], wo[:, k, half * 512:(half + 1) * 512],
                                           start=(k == 0), stop=(k == 7))
                        return ins
                    R.op("pe", f, [y_.b, wo.b], [pp[half].b])
                    R.op("dve", lambda e, half=half, t_=t_, pp=pp, s=s: e.tensor_tensor(
                        t_[:, half * 512:(half + 1) * 512], pp[half][:, :], G.gb[:, s, 0, half * 512:(half + 1) * 512], ALU.mult),
                        [pp[half].b, G.gb.b], [t_.b])
                R.op("dve", lambda e, t_=t_, x_=x_: e.scalar_tensor_tensor(t_[:], x_[:], DN_ALPHA, t_[:], ALU.mult, ALU.add), [x_.b, t_.b], [t_.b])
                ln_stats(G, t_, 128, st[q], mv[q], rstd[q], P=R)
                R.op("dve", lambda e, t_=t_, q=q: e.tensor_scalar(t_[:], t_[:], mv[q][:, 0:1], rstd[q][:, 0:1], ALU.subtract, ALU.mult),
                     [t_.b, mv[q].b, rstd[q].b], [t_.b])
                R.op("pool", lambda e, t_=t_: e.tensor_tensor(t_[:], t_[:], rp[:, R_LN1W:R_LN1W + D], ALU.mult), [t_.b, rp.b], [t_.b])
                R.op("pool", lambda e, t_=t_: e.tensor_tensor(t_[:], t_[:], rp[:, R_LN1B:R_LN1B + D], ALU.add), [t_.b, rp.b], [t_.b])
                R.dma(G.x1[tok0 + r0:tok0 + r0 + 128, :], t_[:], reads=[t_.b], writes=[G.x1.b], q="pool")
                ln_stats(G, t_, 128, st[q], mv[q], rstd[q], P=R)
                R.op("dve", lambda e, t_=t_, q=q, m=m: e.tensor_scalar(xn[m][:], t_[:], mv[q][:, 0:1], rstd[q][:, 0:1], ALU.subtract, ALU.mult),
                     [t_.b, mv[q].b, rstd[q].b], [xn[m].b])
            for m in range(0, nt, 2):
                merge(P, recs[m:m + 2])
            for k in range(8):
                p = pT[k % 2]
                def f(e, p=p, k=k, nt=nt):
                    for m in range(nt):
                        ins = e.transpose(p[:, m * 128:(m + 1) * 128], xn[m][:, k * 128:(k + 1) * 128], ident)
                    return ins
                P.op("pe", f, [xn[m].b for m in range(nt)] + [G.consts.b], [p.b])
                c0 = tok0 + b0
                P.op("act", lambda e, p=p, k=k, bs=bs, s=s, c0=c0: e.activation(
                    hT2[:, k, c0:c0 + bs], p[:, 0:bs], AF.Identity, bias=G.modc[:, 2, k, s:s + 1], scale=G.modc[:, 3, k, s:s + 1]),
                    [p.b, G.modc.b], [hT2.b])
    P.barrier()
    A.reset(m0)


def phase_d1(G, l, streams, hT2):
    P, A = G.P, G.A
    m0 = A.mark()
    cp = G.colp_sb
    wst = [A.alloc("dwst", [128, 8, 256]) for _ in range(2)]
    wab = [A.alloc("dwab", [128, 8, 256], BF16) for _ in range(2)]
    asb = [A.alloc("dasb", [128, 514]) for _ in range(4)]
    acc = [A.alloc("dacc", [128, 512]) for _ in range(4)]
    hc = [A.alloc("dhc", [128, 512], BF16) for _ in range(4)]
    up = G.ffn_up[l, :, :].rearrange("(k p) c -> p k c", p=128)
    pa = [[G.psum[0], G.psum[1]], [G.psum[2], G.psum[3]]]
    pbs = [G.psum[4], G.psum[5], G.psum[6], G.psum[7]]
    cnt = 0
    for c in range(DFF // 128):
        w_, wb_ = wst[c % 2], wab[c % 2]
        P.dma(w_[:, :, 0:128], up[:, :, c * 128:(c + 1) * 128], writes=[w_.b])
        P.dma(w_[:, :, 128:256], up[:, :, DFF + c * 128:DFF + (c + 1) * 128], writes=[w_.b])
        P.op("pool", lambda e, w_=w_, wb_=wb_: e.tensor_copy(wb_[:], w_[:]), [w_.b], [wb_.b])
        for (name, tok0, L, s) in streams:
            bs = min(512, L)
            for b0 in range(0, L, bs):
                i = cnt % 4
                p0, p1 = pa[cnt % 2]
                pb = pbs[i]
                cnt += 1
                a_, ac_, h_ = asb[i], acc[i], hc[i]
                t0 = tok0 + b0
                lo = max(t0 - 1, tok0)
                hi = min(t0 + bs + 1, tok0 + L)
                jlo, jhi = lo - (t0 - 1), hi - (t0 - 1)
                half = (bs + 2) // 2
                segs = [(jlo, half + 1), (half - 1, jhi)] if bs == 512 else [(jlo, jhi)]
                for si, (j0, j1) in enumerate(segs):
                    pp = p0 if si == 0 else p1
                    def f(e, pp=pp, j0=j0, j1=j1, wb_=wb_, t0=t0):
                        for k in range(8):
                            ins = e.matmul(pp[:, 0:j1 - j0], wb_[:, k, 0:128], hT2[:, k, t0 - 1 + j0:t0 - 1 + j1], start=(k == 0), stop=(k == 7))
                        return ins
                    P.op("pe", f, [wb_.b, hT2.b], [pp.b])
                def f(e, pb=pb, wb_=wb_, t0=t0, bs=bs):
                    for k in range(8):
                        ins = e.matmul(pb[:, 0:bs], wb_[:, k, 128:256], hT2[:, k, t0:t0 + bs], start=(k == 0), stop=(k == 7))
                    return ins
                P.op("pe", f, [wb_.b, hT2.b], [pb.b])
                if jlo > 0:
                    P.op("pool", lambda e, a_=a_: e.memset(a_[:, 0:1], 0.0), [], [a_.b])
                if jhi < bs + 2:
                    P.op("pool", lambda e, a_=a_, bs=bs: e.memset(a_[:, bs + 1:bs + 2], 0.0), [], [a_.b])
                if len(segs) == 2:
                    (a0, a1), (b0_, b1_) = segs
                    P.op("act", lambda e, a_=a_, p0=p0, a0=a0, a1=a1: e.activation(a_[:, a0:a1], p0[:, 0:a1 - a0], AF.Copy), [p0.b], [a_.b])
                    P.op("act", lambda e, a_=a_, p1=p1, a1=a1, b0_=b0_, b1_=b1_: e.activation(
                        a_[:, a1:b1_], p1[:, a1 - b0_:b1_ - b0_], AF.Copy), [p1.b], [a_.b])
                else:
                    (a0, a1), = segs
                    P.op("act", lambda e, a_=a_, p0=p0, a0=a0, a1=a1: e.activation(a_[:, a0:a1], p0[:, 0:a1 - a0], AF.Copy), [p0.b], [a_.b])
                wo_ = C_FFNCW + c * 3
                P.op("dve", lambda e, a_=a_, ac_=ac_, bs=bs, wo_=wo_: e.tensor_scalar(ac_[:, 0:bs], a_[:, 0:bs], cp[:, wo_:wo_ + 1], None, ALU.mult),
                     [a_.b, cp.b], [ac_.b])
                for tap in (1, 2):
                    P.op("dve", lambda e, a_=a_, ac_=ac_, bs=bs, wo_=wo_, tap=tap: e.scalar_tensor_tensor(
                        ac_[:, 0:bs], a_[:, tap:tap + bs], cp[:, wo_ + tap:wo_ + tap + 1], ac_[:, 0:bs], ALU.mult, ALU.add), [a_.b, cp.b, ac_.b], [ac_.b])
                P.op("act", lambda e, ac_=ac_, bs=bs, c=c: e.activation(ac_[:, 0:bs], ac_[:, 0:bs], AF.Silu, bias=cp[:, C_FFNCB + c:C_FFNCB + c + 1]),
                     [ac_.b, cp.b], [ac_.b])
                P.op("dve", lambda e, ac_=ac_, h_=h_, pb=pb, bs=bs: e.tensor_tensor(h_[:, 0:bs], ac_[:, 0:bs], pb[:, 0:bs], ALU.mult), [ac_.b, pb.b], [h_.b])
                P.dma(G.hid[c * 128:(c + 1) * 128, t0:t0 + bs], h_[:, 0:bs], reads=[h_.b], writes=[G.hid.b], q="pool")
    P.barrier()
    A.reset(m0)


def phase_d2(G, l, streams, last):
    P, A = G.P, G.A
    m0 = A.mark()
    rp = G.rowp_sb
    NC_ = DFF // 128
    wd = A.alloc("wd", [128, NC_, D], BF16)
    stage = [A.alloc("wdstage", [128, D]) for _ in range(2)]
    load_weight_bf16(G, wd, lambda k: G.ffn_down[l, k * 128:(k + 1) * 128, :], D, stage, NC_)
    hb = [A.alloc("ehb", [128, NC_, 512], BF16) for _ in range(2)]
    xt = [A.alloc("ext", [128, D]) for _ in range(2)]
    t1 = [A.alloc("et1", [128, D]) for _ in range(2)]
    st = [A.alloc("est", [128, 2, 6]) for _ in range(2)]
    mv = [A.alloc("emv", [128, 2]) for _ in range(2)]
    rstd = [A.alloc("erstd", [128, 1]) for _ in range(2)]
    po = [[G.psum[0], G.psum[1]], [G.psum[2], G.psum[3]]]
    xnext = G.xs[(l + 1) % 2]
    cb = 0
    ct = 0
    for (name, tok0, L, s) in streams:
        bs = min(512, L)
        for b0 in range(0, L, bs):
            nt = bs // 128
            h_ = hb[cb % 2]
            cb += 1
            P.dma(h_[:, :, 0:bs], G.hid[:, tok0 + b0:tok0 + b0 + bs].rearrange("(c p) t -> p c t", p=128), reads=[G.hid.b], writes=[h_.b])
            recs = []
            for m in range(nt):
                R = Rec()
                recs.append(R)
                q = ct % 2
                ct += 1
                x_, t_, pp = xt[q], t1[q], po[q]
                r0 = tok0 + b0 + m * 128
                R.dma(x_[:], G.x1[r0:r0 + 128, :], reads=[G.x1.b], writes=[x_.b])
                for half in range(2):
                    def f(e, half=half, h_=h_, m=m, pp=pp):
                        for c in range(NC_):
                            ins = e.matmul(pp[half][:, :], h_[:, c, m * 128:(m + 1) * 128], wd[:, c, half * 512:(half + 1) * 512],
                                           start=(c == 0), stop=(c == NC_ - 1))
                        return ins
                    R.op("pe", f, [h_.b, wd.b], [pp[half].b])
                    R.op("dve", lambda e, half=half, t_=t_, pp=pp, s=s: e.tensor_tensor(
                        t_[:, half * 512:(half + 1) * 512], pp[half][:, :], G.gb[:, s, 1, half * 512:(half + 1) * 512], ALU.mult),
                        [pp[half].b, G.gb.b], [t_.b])
                R.op("dve", lambda e, t_=t_, x_=x_: e.scalar_tensor_tensor(t_[:], x_[:], DN_ALPHA, t_[:], ALU.mult, ALU.add), [x_.b, t_.b], [t_.b])
                ln_stats(G, t_, 128, st[q], mv[q], rstd[q], P=R)
                R.op("dve", lambda e, t_=t_, q=q: e.tensor_scalar(t_[:], t_[:], mv[q][:, 0:1], rstd[q][:, 0:1], ALU.subtract, ALU.mult),
                     [t_.b, mv[q].b, rstd[q].b], [t_.b])
                R.op("pool", lambda e, t_=t_: e.tensor_tensor(t_[:], t_[:], rp[:, R_LN2W:R_LN2W + D], ALU.mult), [t_.b, rp.b], [t_.b])
                R.op("pool", lambda e, t_=t_: e.tensor_tensor(t_[:], t_[:], rp[:, R_LN2B:R_LN2B + D], ALU.add), [t_.b, rp.b], [t_.b])
                if last:
                    rr = b0 + m * 128
                    R.dma(G.out[rr:rr + 128, :], t_[:], reads=[t_.b], writes=[G.out.b], q="pool")
                else:
                    R.dma(xnext[r0:r0 + 128, :], t_[:], reads=[t_.b], writes=[xnext.b], q="pool")
            for m in range(nt):
                merge(P, recs[m:m + 1])
    P.barrier()
    A.reset(m0)


def _col(v, nchunk):
    return np.ascontiguousarray(v.reshape(nchunk, 128).T)


def prep_inputs(inputs):
    f = lambda a: np.ascontiguousarray(np.asarray(a, dtype=np.float32))
    I = {k: f(v) for k, v in inputs.items()}
    colp = np.zeros((DEPTH, 128, NCOL), np.float32)
    rowp = np.zeros((DEPTH, 1, NROW), np.float32)
    poolw = np.zeros((DEPTH, 128, 2, 128), np.float32)
    gws = np.zeros((DEPTH, 128, 4, 128), np.float32)
    for l in range(DEPTH):
        cw = I["ssd_conv_w"][l]
        colp[l, :, C_SSDCW:C_SSDCW + 28] = cw.T.reshape(4, 128, 7).transpose(1, 0, 2).reshape(128, 28)
        colp[l, :, C_SSDCB:C_SSDCB + 4] = _col(I["ssd_conv_b"][l], 4)
        gw = I["gdn_conv_w"][l]
        colp[l, :, C_GDNCW:C_GDNCW + 42] = gw.T.reshape(6, 128, 7).transpose(1, 0, 2).reshape(128, 42)
        fw = I["ffn_conv_w"][l]
        colp[l, :, C_FFNCW:C_FFNCW + 66] = fw.T.reshape(22, 128, 3).transpose(1, 0, 2).reshape(128, 66)
        colp[l, :, C_FFNCB:C_FFNCB + 22] = _col(I["ffn_conv_b"][l], 22)
        colp[l, :, C_PSCALE:C_PSCALE + 2] = _col(I["pool_scale"][l], 2)
        colp[l, :, C_BMOD:C_BMOD + 48] = I["b_mod"][l].reshape(6, 8, 128).transpose(2, 0, 1).reshape(128, 48)
        r = rowp[l, 0]
        r[R_SSDNW:R_SSDNW + 256] = I["ssd_norm_w"][l]
        r[R_GDNNW:R_GDNNW + 64] = I["gdn_norm_w"][l]
        r[R_GLNW:R_GLNW + 256] = I["gmlp_ln_w"][l]
        r[R_GLNB:R_GLNB + 256] = I["gmlp_ln_b"][l]
        r[R_LN1W:R_LN1W + 1024] = I["ln1_w"][l]
        r[R_LN1B:R_LN1B + 1024] = I["ln1_b"][l]
        r[R_LN2W:R_LN2W + 1024] = I["ln2_w"][l]
        r[R_LN2B:R_LN2B + 1024] = I["ln2_b"][l]
        r[R_SSDD:R_SSDD + 256] = np.repeat(I["ssd_d"][l], 64)
        r[R_GBS:R_GBS + 512] = I["gmlp_bs"][l].reshape(-1)
        r[R_SDTB:R_SDTB + 8] = I["ssd_dt_bias"][l].reshape(-1)
        r[R_SALOG:R_SALOG + 8] = I["ssd_a_log"][l].reshape(-1)
        r[R_GDTB:R_GDTB + 8] = I["gdn_dt_bias"][l].reshape(-1)
        r[R_GALOG:R_GALOG + 8] = I["gdn_a_log"][l].reshape(-1)
        r[R_BG1:R_BG1 + 1024] = I["b_mod"][l][2048:3072]
        r[R_BG2:R_BG2 + 1024] = I["b_mod"][l][5120:6144]
        pw = I["pool_w"][l]
        for g in range(4):
            j, h = g // 2, g % 2
            poolw[l, h * 64:(h + 1) * 64, j, h * 64:(h + 1) * 64] = pw[g]
        gws[l] = I["gmlp_ws"][l].transpose(2, 0, 1)
    consts = make_consts()

    def pinv(RW):
        o = np.zeros((128, 2, RW), np.float32)
        pos = np.arange(RW)
        for jc in range(2):
            for hh in range(2):
                w = (2, 4, 8, 16)[2 * jc + hh]
                lo = np.clip(pos - w // 2, 0, RW)
                hi = np.clip(pos + w - w // 2, 0, RW)
                o[hh * 64:(hh + 1) * 64, jc, :] = 1.0 / (hi - lo).astype(np.float32)
        return o
    shared = dict(consts=consts, w_mod=I["w_mod"], w_in=I["w_in"], w_out=I["w_out"], ffn_up=I["ffn_up"],
                  ffn_down=I["ffn_down"], colp=colp, rowp=rowp, poolw=poolw, gws=gws,
                  pinv_g=pinv(64), pinv_c=pinv(256))
    maps = []
    for core in range(8):
        b = core % 4
        crep = np.zeros((128, 2, 8, 128), np.float32)
        crep[:, 0] = np.repeat(_col(I["c"][b], 8)[:, :, None], 128, axis=2)
        crep[:, 1] = np.repeat(_col(I["c_ctx"], 8)[:, :, None], 128, axis=2)
        m = dict(shared)
        m.update(x_in=I["x"][b], ctx_in=I["ctx"][b], crep=crep)
        maps.append(m)
    return maps


_NC_CACHE = {}


def kernel(**inputs):
    maps = prep_inputs(inputs)
    if "nc" not in _NC_CACHE:
        _NC_CACHE["nc"] = build()
    res = run_bass_kernel_spmd(_NC_CACHE["nc"], maps, core_ids=list(range(8)))
    out = np.stack([np.asarray(res.results[b]["out"], dtype=np.float32) for b in range(4)], axis=0)
    return out
```

```python
import numpy as np
import concourse.bass as bass
import concourse.mybir as mybir
from concourse.bass_utils import run_bass_kernel_spmd

F32 = mybir.dt.float32
BF16 = mybir.dt.bfloat16
ALU = mybir.AluOpType
AF = mybir.ActivationFunctionType
AX = mybir.AxisListType

import os
SSD_STOP = int(os.environ.get('SSD_STOP', '99'))
GDN_ROUNDS = int(os.environ.get('GDN_ROUNDS', '5'))
SEG = 16000
DSEG = 1000
DMAK = 8


class Buf:
    __slots__ = ("w", "r", "name")

    def __init__(self, name=""):
        self.w = None
        self.r = {}
        self.name = name


class Prog:
    ENGS = ["pe", "act", "dve", "pool", "sp"]

    def __init__(self, nc):
        self.nc = nc
        self.ops = {e: [] for e in self.ENGS}
        self.count = {e: 0 for e in self.ENGS}
        self.sems = {}
        self.waited = {e: {} for e in self.ENGS}
        self.dma_n = {e: 0 for e in self.ENGS}
        self.last = {}

    def _sem(self, key):
        if key not in self.sems:
            self.sems[key] = self.nc.alloc_semaphore("s_" + "_".join(map(str, key)))
        return self.sems[key]

    def _need(self, eng, waits, tok):
        if tok is None:
            return
        key, val = tok
        if key[0] == "c" and key[1] == "pe" and eng == "pe":
            return
        if self.waited[eng].get(key, 0) >= val:
            return
        if waits.get(key, 0) < val:
            waits[key] = val

    def op(self, eng, fn, reads=(), writes=(), dma=False, extra=()):
        waits = {}
        for b in reads:
            self._need(eng, waits, b.w)
        for b in writes:
            self._need(eng, waits, b.w)
            for k, v in b.r.items():
                self._need(eng, waits, (k, v))
        for t in extra:
            self._need(eng, waits, t)
        if dma:
            n = self.dma_n[eng]
            self.dma_n[eng] += 1
            s, r = n % DMAK, n // DMAK
            if r >= 1:
                pk = ("d", eng, s, (r - 1) // DSEG)
                self._need(eng, waits, (pk, 16 * (((r - 1) % DSEG) + 1)))
            tok = (("d", eng, s, r // DSEG), 16 * ((r % DSEG) + 1))
            inc = 16
        elif fn is None:
            tok = None
            inc = 0
        else:
            n = self.count[eng]
            self.count[eng] += 1
            tok = (("c", eng, n // SEG), (n % SEG) + 1)
            inc = 1
        for k, v in waits.items():
            self.waited[eng][k] = v
            self._sem(k)
        if tok is not None:
            self._sem(tok[0])
            self.last[tok[0]] = tok[1]
        self.ops[eng].append((list(waits.items()), fn, tok, inc))
        if tok is not None:
            for b in reads:
                b.r[tok[0]] = tok[1]
            for b in writes:
                b.w = tok
                b.r = {}
        return tok

    def dma(self, out, in_, reads=(), writes=(), q="sp", **kw):
        return self.op(q, lambda e: e.dma_start(out=out, in_=in_, **kw), reads, writes, dma=True)

    def barrier(self):
        toks = list(self.last.items())
        for e in self.ENGS:
            self.op(e, None, extra=toks)

    def emit(self):
        nc = self.nc
        with nc.Block() as block:
            def run(name):
                def body(e):
                    for waits, fn, tok, inc in self.ops[name]:
                        for k, v in waits:
                            e.wait_ge(self.sems[k], v)
                        if fn is not None:
                            ins = fn(e)
                            ins.then_inc(self.sems[tok[0]], inc)
                return body
            block.tensor(run("pe"))
            block.scalar(run("act"))
            block.vector(run("dve"))
            block.gpsimd(run("pool"))
            block.sync(run("sp"))


class Rec:
    def __init__(self):
        self.calls = []

    def op(self, eng, fn, reads=(), writes=(), dma=False, extra=()):
        self.calls.append((eng, fn, tuple(reads), tuple(writes), dma, tuple(extra)))

    def dma(self, out, in_, reads=(), writes=(), q="sp", **kw):
        self.op(q, lambda e: e.dma_start(out=out, in_=in_, **kw), reads, writes, dma=True)


def merge(P, recs):
    idx = [0] * len(recs)
    live = True
    while live:
        live = False
        for k, r in enumerate(recs):
            if idx[k] < len(r.calls):
                P.op(*r.calls[idx[k]])
                idx[k] += 1
                live = True


class T:
    def __init__(self, h, name=""):
        self.h = h
        self.b = Buf(name)

    def __getitem__(self, k):
        return self.h[k]


def _dtsize(dt):
    return 2 if dt == BF16 else 4


class Arena:
    def __init__(self, nc, base=16512, limit=229344):
        self.nc, self.base, self.limit, self.top, self.n = nc, base, limit, base, 0

    def alloc(self, name, shape, dt=F32):
        el = 1
        for s in shape[1:]:
            el *= s
        size = (el * _dtsize(dt) + 63) // 64 * 64
        off = self.top
        self.top += size
        assert self.top <= self.limit, (name, self.top)
        self.n += 1
        return T(self.nc.alloc_sbuf_tensor_at(f"{name}{self.n}", list(shape), dt, offset=off), name)

    def mark(self):
        return self.top

    def reset(self, m):
        self.top = m


D = 1024
LC = 256
LL = 4096
TALL = LC + LL
DEPTH = 4
NIN = 2584
DFF = 2816
DN_ALPHA = (2 * DEPTH) ** 0.25
LN_EPS = 1e-6
O_Z, O_XBC, O_DT, O_POOL, O_QKV, O_GATE, O_A, O_B, O_UV = 0, 256, 768, 776, 1032, 1800, 2056, 2064, 2072
FM_GROUPS = [(O_XBC, 512), (O_POOL, 256), (O_QKV, 768), (O_UV, 512)]
NFM = 2048
NTM = 536
C_SSDCW, C_SSDCB, C_GDNCW, C_FFNCW, C_FFNCB, C_PSCALE, C_BMOD = 0, 28, 32, 74, 140, 162, 164
NCOL = 164 + 48
R_SSDNW, R_GDNNW, R_GLNW, R_GLNB, R_LN1W, R_LN1B, R_LN2W, R_LN2B = 0, 256, 320, 576, 832, 1856, 2880, 3904
R_SSDD, R_GBS, R_SDTB, R_SALOG, R_GDTB, R_GALOG, R_BG1, R_BG2 = 4928, 5184, 5696, 5704, 5712, 5720, 5728, 6752
NROW = 7776
K_ID, K_ONES, K_LF, K_LB, K_NMF, K_NMB, K_LF2, K_LB2, K_NI2F, K_NI2B, K_NS2F, K_NS2B, K_BO2, K_SEL0, K_SEL1 = range(15)
NCONST = 15
NEG = -30000.0


def make_consts():
    k = np.arange(128)[:, None]
    m = np.arange(128)[None, :]
    same = (k // 64) == (m // 64)
    c = np.zeros((128, NCONST, 128), np.float32)
    c[:, K_ID] = (k == m)
    c[:, K_ONES] = 1.0
    c[:, K_LF] = (k <= m)
    c[:, K_LB] = (k >= m)
    c[:, K_NMF] = np.where(m >= k, 0.0, NEG)
    c[:, K_NMB] = np.where(m <= k, 0.0, NEG)
    c[:, K_LF2] = (k <= m) & same
    c[:, K_LB2] = (k >= m) & same
    c[:, K_NI2F] = np.where((m >= k) & same, 0.0, NEG)
    c[:, K_NI2B] = np.where((m <= k) & same, 0.0, NEG)
    c[:, K_NS2F] = np.where((k > m) & same, 0.0, NEG)
    c[:, K_NS2B] = np.where((k < m) & same, 0.0, NEG)
    c[:, K_BO2] = same
    c[:, K_SEL0] = (k < 64) * np.ones_like(m)
    c[:, K_SEL1] = (k >= 64) * np.ones_like(m)
    return c


class Ctx:
    pass


def build(nlayers=DEPTH, stop_after=None, dbg=False, mixers=None, only=None):
    nc = bass.Bass("TRN2", target_bir_lowering=False)
    G = Ctx()
    G.nc = nc
    P = Prog(nc)
    G.P = P

    def din(name, shape):
        return nc.dram_tensor(name, list(shape), F32, kind="ExternalInput")

    G.x_in = din("x_in", [LL, D])
    G.ctx_in = din("ctx_in", [LC, D])
    G.crep = din("crep", [128, 2, 8, 128])
    G.consts_d = din("consts", [128, NCONST, 128])
    G.w_mod = din("w_mod", [DEPTH, D, 6 * D])
    G.w_in = din("w_in", [DEPTH, D, NIN])
    G.w_out = din("w_out", [DEPTH, D, D])
    G.ffn_up = din("ffn_up", [DEPTH, D, 2 * DFF])
    G.ffn_down = din("ffn_down", [DEPTH, DFF, D])
    G.colp = din("colp", [DEPTH, 128, NCOL])
    G.rowp = din("rowp", [DEPTH, 1, NROW])
    G.poolw = din("poolw", [DEPTH, 128, 2, 128])
    G.gws = din("gws", [DEPTH, 128, 4, 128])
    G.pinv_g = din("pinv_g", [128, 2, 64])
    G.pinv_c = din("pinv_c", [128, 2, 256])
    G.out = T(nc.dram_tensor("out", [LL, D], F32, kind="ExternalOutput"), "out")
    G.xs = [T(nc.dram_tensor(f"xs{i}", [TALL, D], F32), f"xs{i}") for i in range(2)]
    G.x1 = T(nc.dram_tensor("x1s", [TALL, D], F32), "x1s")
    G.projF = T(nc.dram_tensor("projF", [NFM, TALL], F32), "projF")
    G.projT = T(nc.dram_tensor("projT", [TALL, NTM], F32), "projT")
    G.convF = T(nc.dram_tensor("convF", [1280, TALL], F32), "convF")
    G.yT = T(nc.dram_tensor("yT", [D, TALL], BF16), "yT")
    G.hid = T(nc.dram_tensor("hid", [DFF, TALL], BF16), "hid")
    G.dbg = {}
    if dbg:
        G.dbg["projF"] = T(nc.dram_tensor("d_projF", [NFM, TALL], F32, kind="ExternalOutput"))
        G.dbg["projT"] = T(nc.dram_tensor("d_projT", [TALL, NTM], F32, kind="ExternalOutput"))
        G.dbg["mod"] = T(nc.dram_tensor("d_mod", [128, 64], F32, kind="ExternalOutput"))
        G.dbg["gb"] = T(nc.dram_tensor("d_gb", [128, 4096], F32, kind="ExternalOutput"))
        G.dbg["yT"] = T(nc.dram_tensor("d_yT", [D, TALL], BF16, kind="ExternalOutput"))
        G.dbg["convF"] = T(nc.dram_tensor("d_convF", [1280, TALL], F32, kind="ExternalOutput"))
        G.dbg["x1"] = T(nc.dram_tensor("d_x1", [TALL, D], F32, kind="ExternalOutput"))
        G.dbg["x2"] = T(nc.dram_tensor("d_x2", [TALL, D], F32, kind="ExternalOutput"))
        G.dbg["st"] = T(nc.dram_tensor("d_st", [64, 2 * 4 * 64], F32, kind="ExternalOutput"))

    A = Arena(nc)
    G.A = A
    G.psum = [T(nc.alloc_psum_tensor(f"ps{i}", [128, 512], F32), f"ps{i}") for i in range(8)]
    G.consts = A.alloc("consts", [128, NCONST, 128])
    G.csil = A.alloc("csil", [128, 2, 8, 128])
    G.colp_sb = A.alloc("colp", [128, NCOL])
    G.rowp_sb = A.alloc("rowp", [128, NROW])
    G.modc = A.alloc("modc", [128, 4, 8, 2])
    G.gb = A.alloc("gb", [128, 2, 2, 1024])
    G.eps = A.alloc("eps", [128, 1])
    G.sst = [A.alloc("sst", [64, 4, 64]) for _ in range(2)]
    G.gst = [A.alloc("gst", [64, 4, 64]) for _ in range(2)]
    G.gsb = [A.alloc("gsb", [64, 4, 64], BF16) for _ in range(2)]
    if mixers is not None:
        G.mixers = mixers
    G.dumps_on = dbg
    G.dumps = {}

    P.dma(G.consts[:], G.consts_d[:, :, :], writes=[G.consts.b])
    P.dma(G.csil[:], G.crep[:, :, :, :], writes=[G.csil.b])
    P.op("act", lambda e: e.activation(G.csil[:], G.csil[:], AF.Silu), [G.csil.b], [G.csil.b])
    P.op("dve", lambda e: e.memset(G.eps[:], LN_EPS), [], [G.eps.b])

    streams = [("ctx", 0, LC, 1), ("lat", LC, LL, 0)]
    if only is not None:
        streams = [st_ for st_ in streams if st_[0] in only]
    for l in range(nlayers):
        G.l = l
        xin = G.xs[l % 2]
        mod_phase(G, l)
        if stop_after == "mod":
            break
        phase_a(G, l, streams)
        if stop_after == "A":
            break
        prep_pass(G, l, streams)
        mix = G.mixers if hasattr(G, "mixers") else ("pool", "gmlp", "ssd", "gdn")
        for d_ in range(2):
            P.op("dve", lambda e, d_=d_: e.memset(G.sst[d_][:], 0.0), [], [G.sst[d_].b])
            P.op("dve", lambda e, d_=d_: e.memset(G.gst[d_][:], 0.0), [], [G.gst[d_].b])
            P.op("dve", lambda e, d_=d_: e.memset(G.gsb[d_][:], 0.0), [], [G.gsb[d_].b])
        for st_ in streams:
            if "pool" in mix:
                pool_mixer(G, l, st_)
            if "gmlp" in mix and "ssd" in mix:
                mg = A.mark()
                gt_ = gmlp_setup(G, l)
                ssd_mixer(G, l, st_, co_tile=gt_)
                A.reset(mg)
            else:
                if "gmlp" in mix:
                    gmlp_mixer(G, l, st_)
                if "ssd" in mix:
                    ssd_mixer(G, l, st_)
            if "gdn" in mix:
                gdn_mixer(G, l, st_)
        if stop_after == "mix":
            break
        last = (l == DEPTH - 1)
        cd_streams = [st_ for st_ in streams if not (last and st_[0] == "ctx")]
        mC = A.mark()
        hT2 = A.alloc("hT2", [128, 8, TALL], BF16)
        phase_c(G, l, cd_streams, hT2)
        if stop_after == "C":
            break
        phase_d1(G, l, cd_streams, hT2)
        A.reset(mC)
        if stop_after == "D1":
            break
        phase_d2(G, l, cd_streams, last)

    if dbg:
        P.barrier()
        m0 = A.mark()
        tlo = min(st_[1] for st_ in streams)
        thi = max(st_[1] + st_[2] for st_ in streams)
        TW = thi - tlo
        tmp = A.alloc("dbgtmp", [128, TALL])
        tb = A.alloc("dbgtb", [128, TALL], BF16)
        P.dma(G.dbg["mod"][:, :], G.modc[:].rearrange("p a k s -> p (a k s)"), reads=[G.modc.b], writes=[G.dbg["mod"].b], q="pool")
        P.dma(G.dbg["gb"][:, :], G.gb[:].rearrange("p s g d -> p (s g d)"), reads=[G.gb.b], writes=[G.dbg["gb"].b], q="pool")
        if stop_after == "A":
            for r in range(NFM // 128):
                P.dma(tmp[:, 0:TW], G.projF[r * 128:(r + 1) * 128, tlo:thi], reads=[G.projF.b], writes=[tmp.b])
                P.dma(G.dbg["projF"][r * 128:(r + 1) * 128, tlo:thi], tmp[:, 0:TW], reads=[tmp.b], writes=[G.dbg["projF"].b], q="pool")
            for r in range(tlo // 128, thi // 128):
                P.dma(tmp[:, 0:NTM], G.projT[r * 128:(r + 1) * 128, :], reads=[G.projT.b], writes=[tmp.b])
                P.dma(G.dbg["projT"][r * 128:(r + 1) * 128, :], tmp[:, 0:NTM], reads=[tmp.b], writes=[G.dbg["projT"].b], q="pool")
        if stop_after is None:
            for r in range(tlo // 128, thi // 128):
                P.dma(tmp[:, 0:D], G.x1[r * 128:(r + 1) * 128, :], reads=[G.x1.b], writes=[tmp.b])
                P.dma(G.dbg["x1"][r * 128:(r + 1) * 128, :], tmp[:, 0:D], reads=[tmp.b], writes=[G.dbg["x1"].b], q="pool")
                P.dma(tmp[:, 0:D], G.xs[nlayers % 2][r * 128:(r + 1) * 128, :], reads=[G.xs[nlayers % 2].b], writes=[tmp.b])
                P.dma(G.dbg["x2"][r * 128:(r + 1) * 128, :], tmp[:, 0:D], reads=[tmp.b], writes=[G.dbg["x2"].b], q="pool")
        if stop_after == "mix":
            for d_ in range(2):
                P.dma(G.dbg["st"][:, d_ * 256:(d_ + 1) * 256], G.sst[d_][:].rearrange("p h d -> p (h d)"), reads=[G.sst[d_].b], writes=[G.dbg["st"].b], q="pool")
            rows = dict(ssd=(0, 2), pool=(2, 4), gdn=(4, 6), gmlp=(6, 8))
            for mname in (G.mixers if hasattr(G, "mixers") else rows.keys()):
                for r in range(*rows[mname]):
                    P.dma(tb[:, 0:TW], G.yT[r * 128:(r + 1) * 128, tlo:thi], reads=[G.yT.b], writes=[tb.b])
                    P.dma(G.dbg["yT"][r * 128:(r + 1) * 128, tlo:thi], tb[:, 0:TW], reads=[tb.b], writes=[G.dbg["yT"].b], q="pool")
            for r in range(1280 // 128):
                P.dma(tmp[:, 0:TW], G.convF[r * 128:(r + 1) * 128, tlo:thi], reads=[G.convF.b], writes=[tmp.b])
                P.dma(G.dbg["convF"][r * 128:(r + 1) * 128, tlo:thi], tmp[:, 0:TW], reads=[tmp.b], writes=[G.dbg["convF"].b], q="pool")
        P.op("pool", None, reads=[v.b for v in G.dbg.values()])
        A.reset(m0)
    P.op("pool", None, reads=[G.out.b])
    P.emit()
    return nc


def mod_phase(G, l):
    nc, P, A = G.nc, G.P, G.A
    m0 = A.mark()
    P.dma(G.colp_sb[:], G.colp[l, :, :], writes=[G.colp_sb.b])
    P.dma(G.rowp_sb[:], G.rowp[l, 0:1, :].partition_broadcast(128), writes=[G.rowp_sb.b])
    wm = [A.alloc("wm", [128, 8, 512]) for _ in range(2)]
    pc = G.psum[0]
    pg = [G.psum[1], G.psum[2]]
    wsrc = G.w_mod[l, :, :].rearrange("(k p) c -> p k c", p=128)
    colvec = {0: 0, 1: 1, 3: 2, 4: 3}
    for n in range(12):
        w = wm[n % 2]
        P.dma(w[:], wsrc[:, :, n * 512:(n + 1) * 512], writes=[w.b])
        vec, half = n // 2, n % 2
        if vec in colvec:
            a = colvec[vec]
            for j in range(4):
                kc = half * 4 + j

                def f(e, w=w, j=j, a=a, kc=kc):
                    for k in range(8):
                        ins = e.matmul(pc[:, (a * 8 + kc) * 2:(a * 8 + kc) * 2 + 2], w[:, k, j * 128:(j + 1) * 128],
                                       G.csil[:, :, k, 0], start=(k == 0), stop=(k == 7))
                    return ins
                P.op("pe", f, [w.b, G.csil.b], [pc.b])
        else:
            gi = 0 if vec == 2 else 1
            boff = R_BG1 if gi == 0 else R_BG2
            for s in range(2):
                def f(e, w=w, s=s):
                    for k in range(8):
                        ins = e.matmul(pg[s][:, :], G.csil[:, s, k, :], w[:, k, :], start=(k == 0), stop=(k == 7))
                    return ins
                P.op("pe", f, [w.b, G.csil.b], [pg[s].b])
                P.op("dve", lambda e, s=s, gi=gi, half=half, boff=boff: e.tensor_tensor(
                    G.gb[:, s, gi, half * 512:(half + 1) * 512], pg[s][:, :],
                    G.rowp_sb[:, boff + half * 512: boff + (half + 1) * 512], ALU.add),
                    [pg[s].b, G.rowp_sb.b], [G.gb.b])
    bm = G.colp_sb[:, C_BMOD:C_BMOD + 48].rearrange("p (v k) -> p v k", v=6)
    for vec, a in colvec.items():
        for s in range(2):
            P.op("dve", lambda e, vec=vec, a=a, s=s: e.tensor_tensor(
                G.modc[:, a, :, s], pc[:, a * 16:(a + 1) * 16].rearrange("p (k s) -> p k s", s=2)[:, :, s],
                bm[:, vec, :], ALU.add), [pc.b, G.colp_sb.b], [G.modc.b])
    for a in (1, 3):
        P.op("dve", lambda e, a=a: e.tensor_scalar_add(G.modc[:, a, :, :], G.modc[:, a, :, :], 1.0), [G.modc.b], [G.modc.b])
    P.barrier()
    A.reset(m0)


def ln_stats(G, xt, np_, st, mv, rstd, P=None):
    P = P or G.P
    def f(e):
        e.bn_stats(st[0:np_, 0, :], xt[0:np_, 0:512])
        return e.bn_stats(st[0:np_, 1, :], xt[0:np_, 512:1024])
    P.op("dve", f, [xt.b], [st.b])
    P.op("dve", lambda e: e.bn_aggr(mv[0:np_, :], st[0:np_, :, :].rearrange("p a b -> p (a b)")), [st.b], [mv.b])
    P.op("act", lambda e: e.activation(rstd[0:np_, :], mv[0:np_, 1:2], AF.Sqrt, bias=G.eps[0:np_, :]), [mv.b, G.eps.b], [rstd.b])
    P.op("dve", lambda e: e.reciprocal(rstd[0:np_, :], rstd[0:np_, :]), [rstd.b], [rstd.b])


def load_weight_bf16(G, dst, src_rows, ncols, stage, nk):
    P = G.P
    for k in range(nk):
        s = stage[k % len(stage)]
        P.dma(s[:, 0:ncols], src_rows(k), writes=[s.b])
        P.op("pool", lambda e, s=s, k=k: e.tensor_copy(dst[:, k, 0:ncols], s[:, 0:ncols]), [s.b], [dst.b])


def phase_a(G, l, streams):
    nc, P, A = G.nc, G.P, G.A
    m0 = A.mark()
    wi = A.alloc("wi", [128, 8, NIN], BF16)
    stage = [A.alloc("wstage", [128, NIN]) for _ in range(2)]
    load_weight_bf16(G, wi, lambda k: G.w_in[l, k * 128:(k + 1) * 128, :], NIN, stage, 8)
    xt = [A.alloc("xt", [128, D]) for _ in range(2)]
    xn = [A.alloc("xn", [128, D]) for _ in range(4)]
    st = [A.alloc("st", [128, 2, 6]) for _ in range(2)]
    mv = [A.alloc("mv", [128, 2]) for _ in range(2)]
    rstd = [A.alloc("rstd", [128, 1]) for _ in range(2)]
    hT = [A.alloc("hT", [128, 8, 512], BF16) for _ in range(2)]
    oF = [A.alloc("oF", [128, 512]) for _ in range(3)]
    oT = [A.alloc("oT", [128, NTM]) for _ in range(2)]
    pT = [G.psum[0], G.psum[1]]
    pF = [G.psum[2], G.psum[3]]
    pTa = [G.psum[4], G.psum[5]]
    pTb = [G.psum[6], G.psum[7]]
    ident = G.consts[:, K_ID, :]
    cnt = dict(t=0, b=0, f=0, o=0)
    xsrc = G.xs[l % 2]
    for (name, tok0, L, s) in streams:
        bs = 512 if L >= 512 else L
        for b0 in range(0, L, bs):
            nt = bs // 128
            W = bs
            h = hT[cnt["b"] % 2]
            cnt["b"] += 1
            for m in range(nt):
                t = xt[cnt["t"] % 2]
                q = cnt["t"] % 2
                cnt["t"] += 1
                r0 = b0 + m * 128
                if l == 0:
                    src = (G.ctx_in if name == "ctx" else G.x_in)[r0:r0 + 128, :]
                    P.dma(t[:], src, writes=[t.b])
                else:
                    P.dma(t[:], xsrc[tok0 + r0: tok0 + r0 + 128, :], reads=[xsrc.b], writes=[t.b])
                ln_stats(G, t, 128, st[q], mv[q], rstd[q])
                P.op("dve", lambda e, t=t, q=q, m=m: e.tensor_scalar(xn[m][:], t[:], mv[q][:, 0:1], rstd[q][:, 0:1],
                                                                    ALU.subtract, ALU.mult), [t.b, mv[q].b, rstd[q].b], [xn[m].b])
            for k in range(8):
                p = pT[k % 2]

                def f(e, p=p, k=k, nt=nt):
                    for m in range(nt):
                        ins = e.transpose(p[:, m * 128:(m + 1) * 128], xn[m][:, k * 128:(k + 1) * 128], ident)
                    return ins
                P.op("pe", f, [xn[m].b for m in range(nt)] + [G.consts.b], [p.b])
                P.op("act", lambda e, p=p, k=k, h=h, W=W, s=s: e.activation(
                    h[:, k, 0:W], p[:, 0:W], AF.Identity, bias=G.modc[:, 0, k, s:s + 1], scale=G.modc[:, 1, k, s:s + 1]),
                    [p.b, G.modc.b], [h.b])
            row = 0
            for (c0, n) in FM_GROUPS:
                for j in range(n // 128):
                    p = pF[cnt["f"] % 2]
                    o = oF[cnt["f"] % 3]
                    ev = "act" if cnt["f"] % 2 == 0 else "dve"
                    cnt["f"] += 1
                    cc = c0 + j * 128

                    def f(e, p=p, cc=cc, h=h, W=W):
                        for k in range(8):
                            ins = e.matmul(p[:, 0:W], wi[:, k, cc:cc + 128], h[:, k, 0:W], start=(k == 0), stop=(k == 7))
                        return ins
                    P.op("pe", f, [wi.b, h.b], [p.b])
                    if ev == "act":
                        P.op("act", lambda e, p=p, o=o, W=W: e.activation(o[:, 0:W], p[:, 0:W], AF.Copy), [p.b], [o.b])
                    else:
                        P.op("dve", lambda e, p=p, o=o, W=W: e.tensor_copy(o[:, 0:W], p[:, 0:W]), [p.b], [o.b])
                    P.dma(G.projF[row:row + 128, tok0 + b0: tok0 + b0 + W], o[:, 0:W], reads=[o.b], writes=[G.projF.b],
                          q=("act" if ev == "act" else "pool"))
                    row += 128
            for m in range(nt):
                pa = pTa[cnt["o"] % 2]
                pb = pTb[cnt["o"] % 2]
                o = oT[cnt["o"] % 2]
                cnt["o"] += 1

                def f(e, pa=pa, pb=pb, h=h, m=m):
                    for k in range(8):
                        lw = h[:, k, m * 128:(m + 1) * 128]
                        e.matmul(pa[:, 0:256], lw, wi[:, k, O_Z:O_Z + 256], start=(k == 0), stop=(k == 7))
                        e.matmul(pb[:, 0:272], lw, wi[:, k, O_GATE:O_GATE + 272], start=(k == 0), stop=(k == 7))
                    for k in range(8):
                        lw = h[:, k, m * 128:(m + 1) * 128]
                        ins = e.matmul(pa[:, 256:264], lw, wi[:, k, O_DT:O_DT + 8], start=(k == 0), stop=(k == 7))
                    return ins
                P.op("pe", f, [wi.b, h.b], [pa.b, pb.b])
                P.op("act", lambda e, pa=pa, o=o: e.activation(o[:, 0:264], pa[:, 0:264], AF.Copy), [pa.b], [o.b])
                P.op("dve", lambda e, pb=pb, o=o: e.tensor_copy(o[:, 264:536], pb[:, 0:272]), [pb.b], [o.b])
                r0 = tok0 + b0 + m * 128
                P.dma(G.projT[r0:r0 + 128, :], o[:], reads=[o.b], writes=[G.projT.b], q="pool")
    P.barrier()
    A.reset(m0)


def dbgdump(G, name, t, ap, shape, dt=F32, P=None):
    if not getattr(G, "dumps_on", False) or name in G.dumps:
        return
    o = T(G.nc.dram_tensor("dd_" + name, list(shape), dt, kind="ExternalOutput"))
    G.dumps[name] = o
    nd = len(shape)
    P = P or G.P
    P.dma(o[tuple(slice(None) for _ in range(nd))], ap, reads=[t.b], writes=[o.b], q="pool")
    P.op("pool", None, reads=[o.b])


def bc_ap(ap, dims):
    return bass.AP(ap.tensor, ap.offset, [list(ap.ap[0])] + [list(d) for d in dims])


def prep_pass(G, l, streams):
    P, A = G.P, G.A
    m0 = A.mark()
    xin = [A.alloc("cin", [128, LL + 6]) for _ in range(2)]
    acc = [A.alloc("cacc", [128, LL]) for _ in range(2)]
    sq = [A.alloc("csq", [128, 512]) for _ in range(2)]
    rn = [A.alloc("crn", [128, 512]) for _ in range(2)]
    ps = [G.psum[0], G.psum[1]]
    bo2 = G.consts[:, K_BO2, :]
    chunks = []
    for j in range(4):
        chunks.append((j * 128, j * 128, C_SSDCW + j * 7, C_SSDCB + j, "p"))
    for j in range(6):
        chunks.append((768 + j * 128, 512 + j * 128, C_GDNCW + j * 7, None, "q" if j < 2 else ("k" if j < 4 else "p")))
    cnt = 0
    sc = 0
    for (name, tok0, L, s) in streams:
        for t in xin:
            P.op("dve", lambda e, t=t: e.memset(t[:, 0:3], 0.0), [], [t.b])
            P.op("dve", lambda e, t=t, L=L: e.memset(t[:, L + 3:L + 6], 0.0), [], [t.b])
        for (src, dst, wo, bo, kind) in chunks:
            t = xin[cnt % 2]
            a = acc[cnt % 2]
            cnt += 1
            w = G.colp_sb
            P.dma(t[:, 3:L + 3], G.projF[src:src + 128, tok0:tok0 + L], reads=[G.projF.b], writes=[t.b])
            P.op("dve", lambda e, t=t, a=a, L=L, wo=wo: e.tensor_scalar(a[:, 0:L], t[:, 0:L], w[:, wo:wo + 1], None, ALU.mult),
                 [t.b, w.b], [a.b])
            for tap in range(1, 7):
                P.op("dve", lambda e, t=t, a=a, L=L, wo=wo, tap=tap: e.scalar_tensor_tensor(
                    a[:, 0:L], t[:, tap:tap + L], w[:, wo + tap:wo + tap + 1], a[:, 0:L], ALU.mult, ALU.add),
                    [t.b, w.b, a.b], [a.b])
            if bo is not None:
                P.op("act", lambda e, a=a, L=L, bo=bo: e.activation(a[:, 0:L], a[:, 0:L], AF.Silu, bias=w[:, bo:bo + 1]),
                     [a.b, w.b], [a.b])
            else:
                P.op("act", lambda e, a=a, L=L: e.activation(a[:, 0:L], a[:, 0:L], AF.Silu), [a.b], [a.b])
            if kind in ("q", "k"):
                for sl in range(0, L, 512):
                    Wd = min(512, L - sl)
                    q_, r_, p_ = sq[sc % 2], rn[sc % 2], ps[sc % 2]
                    sc += 1
                    P.op("act", lambda e, a=a, q_=q_, sl=sl, Wd=Wd: e.activation(q_[:, 0:Wd], a[:, sl:sl + Wd], AF.Square), [a.b], [q_.b])
                    P.op("pe", lambda e, q_=q_, p_=p_, Wd=Wd: e.matmul(p_[:, 0:Wd], bo2, q_[:, 0:Wd], start=True, stop=True),
                         [q_.b, G.consts.b], [p_.b])
                    P.op("act", lambda e, r_=r_, p_=p_, Wd=Wd: e.activation(r_[:, 0:Wd], p_[:, 0:Wd], AF.Sqrt, bias=G.eps[:, :]),
                         [p_.b, G.eps.b], [r_.b])
                    P.op("dve", lambda e, r_=r_, Wd=Wd: e.reciprocal(r_[:, 0:Wd], r_[:, 0:Wd]), [r_.b], [r_.b])
                    if kind == "q":
                        P.op("dve", lambda e, a=a, r_=r_, sl=sl, Wd=Wd: e.scalar_tensor_tensor(
                            a[:, sl:sl + Wd], a[:, sl:sl + Wd], 0.125, r_[:, 0:Wd], ALU.mult, ALU.mult), [a.b, r_.b], [a.b])
                    else:
                        P.op("dve", lambda e, a=a, r_=r_, sl=sl, Wd=Wd: e.tensor_tensor(
                            a[:, sl:sl + Wd], a[:, sl:sl + Wd], r_[:, 0:Wd], ALU.mult), [a.b, r_.b], [a.b])
            P.dma(G.convF[dst:dst + 128, tok0:tok0 + L], a[:, 0:L], reads=[a.b], writes=[G.convF.b], q="pool")
    P.barrier()
    A.reset(m0)


def pool_mixer(G, l, stream):
    P, A = G.P, G.A
    name, tok0, L, s = stream
    m0 = A.mark()
    RW = 64 if name == "lat" else L
    NR = L // RW
    PW = RW + 16
    F = NR * PW
    xp = A.alloc("pxp", [128, NR, PW])
    ca = A.alloc("pca", [128, NR, PW])
    cb = A.alloc("pcb", [128, NR, PW])
    tmp = A.alloc("ptmp", [128, NR, RW])
    pooled = A.alloc("ppool", [128, NR, RW], BF16)
    pinv = A.alloc("pinv", [128, 2, RW])
    pwf = A.alloc("pwf", [128, 2, 128])
    pwb = A.alloc("pwb", [128, 2, 128], BF16)
    yo = [A.alloc("pyo", [128, 512], BF16) for _ in range(2)]
    ps = [G.psum[2], G.psum[3]]
    P.dma(pinv[:], (G.pinv_g if name == "lat" else G.pinv_c)[:, :, :], writes=[pinv.b])
    P.dma(pwf[:], G.poolw[l, :, :, :], writes=[pwf.b])
    P.op("pool", lambda e: e.tensor_copy(pwb[:], pwf[:]), [pwf.b], [pwb.b])
    fl = lambda t: t[:].rearrange("p r w -> p (r w)")
    cnt = 0
    for jc in range(2):
        P.op("dve", lambda e: e.memset(xp[:], 0.0), [], [xp.b])
        P.dma(xp[:, :, 8:8 + RW], G.projF[512 + jc * 128:512 + (jc + 1) * 128, tok0:tok0 + L].rearrange("p (r w) -> p r w", w=RW),
              reads=[G.projF.b], writes=[xp.b])
        xf, af, bf = fl(xp), fl(ca), fl(cb)
        P.op("dve", lambda e: e.tensor_tensor(af[:, 0:F - 1], xf[:, 0:F - 1], xf[:, 1:F], ALU.add), [xp.b], [ca.b])
        if jc == 0:
            P.op("dve", lambda e: e.tensor_tensor(bf[64:128, 0:F - 3], af[64:128, 0:F - 3], af[64:128, 2:F - 1], ALU.add), [ca.b], [cb.b])
            srcs = [(0, 64, 2, ca), (64, 128, 4, cb)]
        else:
            P.op("dve", lambda e: e.tensor_tensor(bf[:, 0:F - 3], af[:, 0:F - 3], af[:, 2:F - 1], ALU.add), [ca.b], [cb.b])
            P.op("dve", lambda e: e.tensor_tensor(af[:, 0:F - 7], bf[:, 0:F - 7], bf[:, 4:F - 3], ALU.add), [cb.b, ca.b], [ca.b])
            P.op("dve", lambda e: e.tensor_tensor(bf[64:128, 0:F - 15], af[64:128, 0:F - 15], af[64:128, 8:F - 7], ALU.add), [ca.b, cb.b], [cb.b])
            srcs = [(0, 64, 8, ca), (64, 128, 16, cb)]
        for (lo, hi, w, cw) in srcs:
            o = 8 - w // 2
            P.op("dve", lambda e, lo=lo, hi=hi, cw=cw, o=o, jc=jc: e.tensor_tensor(
                tmp[lo:hi, :, :], cw[lo:hi, :, o:o + RW], bc_ap(pinv[lo:hi, jc, :], [[0, NR], [1, RW]]), ALU.mult),
                [cw.b, pinv.b], [tmp.b])
            P.op("dve", lambda e, lo=lo, hi=hi: e.tensor_tensor(pooled[lo:hi, :, :], tmp[lo:hi, :, :], xp[lo:hi, :, 8:8 + RW], ALU.subtract),
                 [tmp.b, xp.b], [pooled.b])
        pf = pooled[:].rearrange("p r w -> p (r w)")
        for sl in range(0, L, 512):
            Wd = min(512, L - sl)
            p_, y_ = ps[cnt % 2], yo[cnt % 2]
            cnt += 1
            P.op("pe", lambda e, p_=p_, sl=sl, Wd=Wd, jc=jc: e.matmul(p_[:, 0:Wd], pwb[:, jc, :], pf[:, sl:sl + Wd], start=True, stop=True),
                 [pwb.b, pooled.b], [p_.b])
            P.op("act", lambda e, p_=p_, y_=y_, Wd=Wd, jc=jc: e.activation(
                y_[:, 0:Wd], p_[:, 0:Wd], AF.Copy, scale=G.colp_sb[:, C_PSCALE + jc:C_PSCALE + jc + 1]), [p_.b, G.colp_sb.b], [y_.b])
            P.dma(G.yT[256 + jc * 128:256 + (jc + 1) * 128, tok0 + sl:tok0 + sl + Wd], y_[:, 0:Wd], reads=[y_.b], writes=[G.yT.b], q="act")
    P.barrier()
    A.reset(m0)


def gmlp_setup(G, l):
    P, A = G.P, G.A
    NB = 2
    uv = [A.alloc("guv", [128, 4, 128]) for _ in range(NB)]
    gt = [A.alloc("ggt", [128, 4, 128]) for _ in range(NB)]
    vt = [A.alloc("gvt", [128, 256]) for _ in range(NB)]
    vb = [A.alloc("gvb", [128, 256], BF16) for _ in range(NB)]
    st = [A.alloc("gst_", [128, 6]) for _ in range(NB)]
    mv = [A.alloc("gmv", [128, 2]) for _ in range(NB)]
    rs = [A.alloc("grs", [128, 1]) for _ in range(NB)]
    tm = [A.alloc("gtm", [128, 2, 128]) for _ in range(NB)]
    yo = [A.alloc("gyo", [128, 2, 128], BF16) for _ in range(NB)]
    wsf = A.alloc("gwsf", [128, 4, 128])
    wsb = A.alloc("gwsb", [128, 4, 128], BF16)
    P.dma(wsf[:], G.gws[l, :, :, :], writes=[wsf.b])
    P.op("pool", lambda e: e.tensor_copy(wsb[:], wsf[:]), [wsf.b], [wsb.b])
    pT = [G.psum[4], G.psum[5]]
    pq = [G.psum[6], G.psum[7]]
    ident = G.consts[:, K_ID, :]
    rp = G.rowp_sb
    def tile(R, stream, ti):
        name, tok0, L, s = stream
        i = ti % NB
        u, g, v, vbb, p1, p2, tt, y = uv[i], gt[i], vt[i], vb[i], pT[i], pq[i], tm[i], yo[i]
        t0 = tok0 + ti * 128
        R.dma(u[:], G.projF[1536:2048, t0:t0 + 128].rearrange("(j p) t -> p j t", p=128), reads=[G.projF.b], writes=[u.b])
        uf = u[:].rearrange("p j t -> p (j t)")
        gf = g[:].rearrange("p j t -> p (j t)")
        R.op("dve", lambda e, uf=uf, gf=gf: e.tensor_tensor(gf, uf, uf, ALU.mult), [u.b], [g.b])
        R.op("dve", lambda e, gf=gf: e.tensor_scalar(gf, gf, 0.044715, 1.0, ALU.mult, ALU.add), [g.b], [g.b])
        R.op("dve", lambda e, uf=uf, gf=gf: e.tensor_tensor(gf, gf, uf, ALU.mult), [g.b, u.b], [g.b])
        R.op("act", lambda e, gf=gf: e.activation(gf, gf, AF.Sigmoid, scale=1.5957691216057308), [g.b], [g.b])
        R.op("dve", lambda e, uf=uf, gf=gf: e.tensor_tensor(gf, gf, uf, ALU.mult), [g.b, u.b], [g.b])

        def f(e, g=g, p1=p1):
            e.transpose(p1[:, 0:128], g[:, 2, :], ident)
            return e.transpose(p1[:, 128:256], g[:, 3, :], ident)
        R.op("pe", f, [g.b, G.consts.b], [p1.b])
        R.op("act", lambda e, v=v, p1=p1: e.activation(v[:], p1[:, 0:256], AF.Copy), [p1.b], [v.b])
        R.op("dve", lambda e, v=v, i=i: e.bn_stats(st[i][:], v[:]), [v.b], [st[i].b])
        R.op("dve", lambda e, i=i: e.bn_aggr(mv[i][:], st[i][:]), [st[i].b], [mv[i].b])
        R.op("act", lambda e, i=i: e.activation(rs[i][:], mv[i][:, 1:2], AF.Sqrt, bias=G.eps[:, :]), [mv[i].b, G.eps.b], [rs[i].b])
        R.op("dve", lambda e, i=i: e.reciprocal(rs[i][:], rs[i][:]), [rs[i].b], [rs[i].b])
        R.op("dve", lambda e, v=v, i=i: e.tensor_scalar(v[:], v[:], mv[i][:, 0:1], rs[i][:, 0:1], ALU.subtract, ALU.mult),
             [v.b, mv[i].b, rs[i].b], [v.b])
        R.op("dve", lambda e, v=v: e.tensor_tensor(v[:], v[:], rp[:, R_GLNW:R_GLNW + 256], ALU.mult), [v.b, rp.b], [v.b])
        R.op("dve", lambda e, v=v, vbb=vbb: e.tensor_tensor(vbb[:], v[:], rp[:, R_GLNB:R_GLNB + 256], ALU.add), [v.b, rp.b], [vbb.b])

        def f2(e, vbb=vbb, p2=p2):
            e.matmul(p2[:, 0:256], vbb[:, 0:128], wsb[:, 0:2, :].rearrange("p g i -> p (g i)"), start=True, stop=True)
            return e.matmul(p2[:, 256:512], vbb[:, 128:256], wsb[:, 2:4, :].rearrange("p g i -> p (g i)"), start=True, stop=True)
        R.op("pe", f2, [vbb.b, wsb.b], [p2.b])
        for q in range(2):
            for hh in range(2):
                gg = 2 * q + hh
                lo, hi = hh * 64, (hh + 1) * 64
                c0 = q * 256 + hh * 128
                R.op("dve", lambda e, lo=lo, hi=hi, c0=c0, q=q, gg=gg, tt=tt, p2=p2: e.tensor_tensor(
                    tt[lo:hi, q, :], p2[lo:hi, c0:c0 + 128], rp[lo:hi, R_GBS + gg * 128:R_GBS + (gg + 1) * 128], ALU.add),
                    [p2.b, rp.b], [tt.b])
                R.op("dve", lambda e, lo=lo, hi=hi, q=q, tt=tt, y=y, g=g: e.tensor_tensor(
                    y[lo:hi, q, :], tt[lo:hi, q, :], g[lo:hi, q, :], ALU.mult), [tt.b, g.b], [y.b])
        R.dma(G.yT[768:1024, t0:t0 + 128].rearrange("(j p) t -> p j t", p=128), y[:], reads=[y.b], writes=[G.yT.b], q="pool")
    return tile


def gmlp_mixer(G, l, stream):
    P, A = G.P, G.A
    m0 = A.mark()
    tile = gmlp_setup(G, l)
    for ti in range(stream[2] // 128):
        r = Rec()
        tile(r, stream, ti)
        merge(P, [r])
    P.barrier()
    A.reset(m0)


def softplus_small(P, out, x, tmp, reads, extra_w=()):
    P.op("dve", lambda e: e.scalar_tensor_tensor(tmp, x, -1.0, x, ALU.mult, ALU.max), reads, [extra_w[0]])
    P.op("act", lambda e: e.activation(tmp, tmp, AF.Exp, scale=-1.0), [extra_w[0]], [extra_w[0]])
    P.op("act", lambda e: e.activation(tmp, tmp, AF.Ln, bias=1.0), [extra_w[0]], [extra_w[0]])
    P.op("dve", lambda e: e.scalar_tensor_tensor(out, x, 0.0, tmp, ALU.max, ALU.add), reads + [extra_w[0]], [extra_w[1]])


def ssd_mixer(G, l, stream, co_tile=None):
    P, A = G.P, G.A
    name, tok0, L, s = stream
    NT = L // 128
    m0 = A.mark()
    rp = G.rowp_sb
    C = G.consts
    ident = C[:, K_ID, :]
    ones = C[:, K_ONES, :]
    nega = A.alloc("snega", [128, 8])
    negones = A.alloc("snegones", [128, 128])
    yacc = A.alloc("syacc", [128, NT, 256])
    P.op("act", lambda e: e.activation(nega[:], rp[:, R_SALOG:R_SALOG + 8], AF.Exp), [rp.b], [nega.b])
    P.op("dve", lambda e: e.tensor_scalar(nega[:], nega[:], -1.0, None, ALU.mult), [nega.b], [nega.b])
    P.op("dve", lambda e: e.memset(negones[:], -1.0), [], [negones.b])
    NB = 2
    def mk(nm, shape, dt=F32):
        return [A.alloc(nm, shape, dt) for _ in range(NB)]
    zt, cx, dtt, dtm, dta = mk("szt", [128, 264]) + mk("szt", [128, 264]), mk("scx", [128, 2, 128]) + mk("scx", [128, 2, 128]), mk("sdt", [128, 8]), mk("sdtm", [128, 8]), mk("sdta", [128, 8])
    bc4 = mk("sbc4", [64, 4, 128]) + mk("sbc4", [64, 4, 128])
    xtm, btm, cbb = mk("sxtm", [128, 256]), mk("sbtm", [128, 128], BF16), mk("scbb", [64, 4, 128], BF16)
    ML, Dm, Wm = mk("sML", [128, 4, 128]), mk("sDm", [128, 4, 128]), mk("sWm", [128, 4, 128], BF16)
    scl, dtw = mk("sscl", [128, 12]), mk("sdtw", [128, 8])
    xdt, xdw = mk("sxdt", [128, 256], BF16), mk("sxdw", [128, 256], BF16)
    yt, yg, ss, yo = mk("syt", [128, 256]), mk("syg", [128, 256]), mk("sss", [128, 2]), mk("syo", [128, 2, 128], BF16)
    ps = G.psum
    def body(R, d, ti, i, li, second):
        Ltri = C[:, K_LF if d == 0 else K_LB, :]
        nmask = C[:, K_NMF if d == 0 else K_NMB, :]
        t0 = tok0 + ti * 128
        z_, c_, dt_, dm_, da_, x_, b_, cb_ = zt[li], cx[li], dtt[i], dtm[i], dta[i], xtm[i], btm[i], cbb[i]
        bc_ = bc4[li]
        ml_, D_, W_, sc_, dw_, xd_, xw_ = ML[i], Dm[i], Wm[i], scl[i], dtw[i], xdt[i], xdw[i]
        pA, pB = ps[2 * i], ps[2 * i + 1]
        pC = pD = pA
        R.dma(z_[:], G.projT[t0:t0 + 128, 0:264], reads=[G.projT.b], writes=[z_.b])
        R.dma(c_[:], G.convF[0:256, t0:t0 + 128].rearrange("(j p) t -> p j t", p=128), reads=[G.convF.b], writes=[c_.b])
        R.dma(bc_[:], G.convF[256:512, t0:t0 + 128].rearrange("(j p) t -> p j t", p=64), reads=[G.convF.b], writes=[bc_.b])
        R.op("dve", lambda e, z_=z_, dt_=dt_: e.tensor_tensor(dt_[:], z_[:, 256:264], rp[:, R_SDTB:R_SDTB + 8], ALU.add), [z_.b, rp.b], [dt_.b])
        softplus_small(R, dt_[:], dt_[:], dm_[:], [dt_.b], (dm_.b, dt_.b))
        R.op("dve", lambda e, dt_=dt_, da_=da_: e.tensor_tensor(da_[:], dt_[:], nega[:], ALU.mult), [dt_.b, nega.b], [da_.b])
        if SSD_STOP <= 1:
            return
        def f(e, c_=c_, pA=pA, bc_=bc_):
            e.transpose(pA[:, 0:128], c_[:, 0, :], ident)
            e.transpose(pA[:, 128:256], c_[:, 1, :], ident)
            e.transpose(pA[:, 256:320], bc_[:, 0, :], ident[0:64, 0:64])
            return e.transpose(pA[:, 320:384], bc_[:, 1, :], ident[0:64, 0:64])
        R.op("pe", f, [c_.b, bc_.b, C.b], [pA.b])
        R.op("act", lambda e, x_=x_, pA=pA: e.activation(x_[:], pA[:, 0:256], AF.Copy), [pA.b], [x_.b])
        R.op("act", lambda e, b_=b_, pA=pA: e.activation(b_[:], pA[:, 256:384], AF.Copy), [pA.b], [b_.b])
        R.op("pool", lambda e, cb_=cb_, bc_=bc_: e.tensor_copy(cb_[:], bc_[:]), [bc_.b], [cb_.b])
        if SSD_STOP <= 2:
            return
        def f(e, cb_=cb_, pB=pB):
            e.matmul(pB[:, 0:128], cb_[:, 0, :], cb_[:, 2, :], start=True, stop=True)
            return e.matmul(pB[:, 128:256], cb_[:, 1, :], cb_[:, 3, :], start=True, stop=True)
        R.op("pe", f, [cb_.b], [pB.b])
        if SSD_STOP <= 3:
            return
        R.op("dve", lambda e, ml_=ml_, da_=da_, Ltri=Ltri: e.tensor_tensor(
            ml_[:], bc_ap(Ltri, [[0, 4], [1, 128]]), bc_ap(da_[:, d * 4:d * 4 + 4], [[1, 4], [0, 128]]), ALU.mult), [da_.b, C.b], [ml_.b])
        def f(e, ml_=ml_, pC=pC, nmask=nmask):
            e.matmul(pC[:, :], ones, ml_[:].rearrange("p h i -> p (h i)"), start=True, stop=False)
            for h in range(4):
                e.matmul(pC[:, h * 128:(h + 1) * 128], ml_[:, h, :], negones[:], start=False, stop=False)
            return e.matmul(pC[:, :], ident, bc_ap(nmask, [[0, 4], [1, 128]]), start=False, stop=True)
        R.op("pe", f, [ml_.b, C.b, negones.b], [pC.b])
        R.op("act", lambda e, D_=D_, pC=pC: e.activation(D_[:].rearrange("p h i -> p (h i)"), pC[:, :], AF.Exp), [pC.b], [D_.b])
        if SSD_STOP <= 4:
            return
        def f(e, da_=da_, pD=pD, Ltri=Ltri):
            e.matmul(pD[:, 0:4], Ltri, da_[:, d * 4:d * 4 + 4], start=True, stop=True)
            return e.matmul(pD[:, 4:8], ones, da_[:, d * 4:d * 4 + 4], start=True, stop=True)
        R.op("pe", f, [da_.b, C.b], [pD.b])
        R.op("dve", lambda e, sc_=sc_, pD=pD: e.tensor_copy(sc_[:, 0:8], pD[:, 0:8]), [pD.b], [sc_.b])
        R.op("dve", lambda e, sc_=sc_: e.tensor_copy(sc_[:, 8:12], sc_[:, 4:8]), [sc_.b], [sc_.b])
        R.op("dve", lambda e, sc_=sc_: e.tensor_tensor(sc_[:, 4:8], sc_[:, 4:8], sc_[:, 0:4], ALU.subtract), [sc_.b], [sc_.b])
        R.op("act", lambda e, sc_=sc_: e.activation(sc_[:], sc_[:], AF.Exp), [sc_.b], [sc_.b])
        if SSD_STOP <= 5:
            return
        R.op("dve", lambda e, W_=W_, D_=D_, pB=pB: e.tensor_tensor(
            W_[:].rearrange("p (g r) i -> p g r i", g=2), bc_ap(pB[:, 0:256], [[128, 2], [0, 2], [1, 128]]),
            D_[:].rearrange("p (g r) i -> p g r i", g=2), ALU.mult), [pB.b, D_.b], [W_.b])
        if SSD_STOP <= 6:
            return
        R.op("dve", lambda e, dw_=dw_, dt_=dt_, sc_=sc_: e.tensor_tensor(dw_[:, 0:4], dt_[:, d * 4:d * 4 + 4], sc_[:, 4:8], ALU.mult),
             [dt_.b, sc_.b], [dw_.b])
        R.op("dve", lambda e, xd_=xd_, x_=x_, dt_=dt_: e.tensor_tensor(
            xd_[:].rearrange("p (h q) -> p h q", h=4), x_[:].rearrange("p (h q) -> p h q", h=4),
            bc_ap(dt_[:, d * 4:d * 4 + 4], [[1, 4], [0, 64]]), ALU.mult), [x_.b, dt_.b], [xd_.b])
        R.op("dve", lambda e, xw_=xw_, x_=x_, dw_=dw_: e.tensor_tensor(
            xw_[:].rearrange("p (h q) -> p h q", h=4), x_[:].rearrange("p (h q) -> p h q", h=4),
            bc_ap(dw_[:, 0:4], [[1, 4], [0, 64]]), ALU.mult), [x_.b, dw_.b], [xw_.b])
        if SSD_STOP <= 7:
            return
        def f(e, W_=W_, xd_=xd_, pA=pA):
            for h in range(4):
                ins = e.matmul(pA[:, h * 64:(h + 1) * 64], W_[:, h, :], xd_[:, h * 64:(h + 1) * 64], start=True, stop=True)
            return ins
        R.op("pe", f, [W_.b, xd_.b], [pA.b])
        def f(e, bc_=bc_, pB=pB):
            for h in range(4):
                g = h // 2
                ins = e.matmul(pB[:, 256 + h * 64:256 + (h + 1) * 64], bc_[:, 2 + g, :],
                               G.sst[d][:, h, :], start=True, stop=True)
            return ins
        R.op("pe", f, [bc_.b, G.sst[d].b], [pB.b])
        def f(e, b_=b_, xw_=xw_, pD=pD):
            for h in range(4):
                g = h // 2
                ins = e.matmul(pD[0:64, 256 + h * 64:256 + (h + 1) * 64], b_[:, g * 64:(g + 1) * 64], xw_[:, h * 64:(h + 1) * 64], start=True, stop=True)
            return ins
        R.op("pe", f, [b_.b, xw_.b], [pD.b])
        if SSD_STOP <= 8:
            return
        for h in range(4):
            lo, hi = 0, 64
            R.op("dve", lambda e, lo=lo, hi=hi, h=h, sc_=sc_, pD=pD: e.scalar_tensor_tensor(
                G.sst[d][lo:hi, h, :], G.sst[d][lo:hi, h, :], sc_[lo:hi, 8 + h:9 + h], pD[lo:hi, 256 + h * 64:256 + (h + 1) * 64],
                ALU.mult, ALU.add), [G.sst[d].b, sc_.b, pD.b], [G.sst[d].b])
        if SSD_STOP <= 9:
            return
        y_ = yt[i]
        R.op("dve", lambda e, y_=y_, pB=pB, sc_=sc_: e.tensor_tensor(
            y_[:].rearrange("p (h q) -> p h q", h=4), pB[:, 256:512].rearrange("p (h q) -> p h q", h=4),
            bc_ap(sc_[:, 0:4], [[1, 4], [0, 64]]), ALU.mult), [pB.b, sc_.b], [y_.b])
        R.op("dve", lambda e, y_=y_, pA=pA: e.tensor_tensor(y_[:], y_[:], pA[:, 0:256], ALU.add), [y_.b, pA.b], [y_.b])
        if name == "ctx" and ti == 0:
            dbgdump(G, f"dt{d}", dt_, dt_[:], [128, 8], P=R)
            dbgdump(G, f"da{d}", da_, da_[:], [128, 8], P=R)
            dbgdump(G, f"x{d}", x_, x_[:], [128, 256], P=R)
            dbgdump(G, f"D{d}", D_, D_[:].rearrange("p h i -> p (h i)"), [128, 512], P=R)
            dbgdump(G, f"W{d}", W_, W_[:].rearrange("p h i -> p (h i)"), [128, 512], BF16, P=R)
            dbgdump(G, f"sc{d}", sc_, sc_[:], [128, 12], P=R)
            dbgdump(G, f"xd{d}", xd_, xd_[:], [128, 256], BF16, P=R)
            dbgdump(G, f"y{d}", y_, y_[:], [128, 256], P=R)
        if not second:
            R.op("dve", lambda e, x_=x_, ti=ti: e.tensor_tensor(yacc[:, ti, :], x_[:], rp[:, R_SSDD:R_SSDD + 256], ALU.mult),
                 [x_.b, rp.b], [yacc.b])
            R.op("dve", lambda e, y_=y_, ti=ti: e.tensor_tensor(yacc[:, ti, :], yacc[:, ti, :], y_[:], ALU.add), [yacc.b, y_.b], [yacc.b])
        else:
            g_, s_, o_ = yg[i], ss[i], yo[i]
            R.op("dve", lambda e, y_=y_, ti=ti: e.tensor_tensor(y_[:], y_[:], yacc[:, ti, :], ALU.add), [yacc.b, y_.b], [y_.b])
            R.op("act", lambda e, g_=g_, z_=z_: e.activation(g_[:], z_[:, 0:256], AF.Silu), [z_.b], [g_.b])
            R.op("dve", lambda e, g_=g_, y_=y_: e.tensor_tensor(g_[:], g_[:], y_[:], ALU.mult), [g_.b, y_.b], [g_.b])
            R.op("act", lambda e, g_=g_, y_=y_, s_=s_: e.activation(y_[:], g_[:], AF.Square, accum_out=s_[:, 0:1]), [g_.b], [y_.b, s_.b])
            R.op("act", lambda e, s_=s_: e.activation(s_[:, 1:2], s_[:, 0:1], AF.Sqrt, bias=G.eps[:, :], scale=1.0 / 256.0), [s_.b, G.eps.b], [s_.b])
            R.op("dve", lambda e, s_=s_: e.reciprocal(s_[:, 1:2], s_[:, 1:2]), [s_.b], [s_.b])
            R.op("dve", lambda e, g_=g_, s_=s_: e.scalar_tensor_tensor(
                g_[:], g_[:], s_[:, 1:2], rp[:, R_SSDNW:R_SSDNW + 256], ALU.mult, ALU.mult), [g_.b, s_.b, rp.b], [g_.b])
            def f(e, g_=g_, pC=pC):
                e.transpose(pC[:, 0:128], g_[:, 0:128], ident)
                return e.transpose(pC[:, 128:256], g_[:, 128:256], ident)
            R.op("pe", f, [g_.b, C.b], [pC.b])
            R.op("act", lambda e, o_=o_, pC=pC: e.activation(o_[:].rearrange("p j t -> p (j t)"), pC[:, 0:256], AF.Copy), [pC.b], [o_.b])
            R.dma(G.yT[0:256, t0:t0 + 128].rearrange("(j p) t -> p j t", p=128), o_[:], reads=[o_.b], writes=[G.yT.b], q="pool")

    cnt = 0
    for st_ in range(NT):
        recs = []
        for d in range(2):
            ti = st_ if d == 0 else NT - 1 - st_
            second = (ti >= NT // 2) if d == 0 else (ti < NT // 2)
            recs.append(Rec())
            body(recs[-1], d, ti, d, 2 * d + st_ % 2, second)
        if co_tile is not None:
            recs.append(Rec())
            co_tile(recs[-1], stream, st_)
        merge(P, recs)
    P.barrier()
    A.reset(m0)


def gdn_mixer(G, l, stream):
    P, A = G.P, G.A
    name, tok0, L, s = stream
    NT = L // 128
    m0 = A.mark()
    rp, C = G.rowp_sb, G.consts
    ident, ones = C[:, K_ID, :], C[:, K_ONES, :]
    id64 = C[0:64, K_ID, 0:64]
    negag = A.alloc("gnegag", [128, 8])
    negones = A.alloc("gnegones", [128, 128])
    oacc = A.alloc("goacc", [128, NT, 256])
    P.op("act", lambda e: e.activation(negag[:], rp[:, R_GALOG:R_GALOG + 8], AF.Exp), [rp.b], [negag.b])
    P.op("dve", lambda e: e.tensor_scalar(negag[:], negag[:], -1.0, None, ALU.mult), [negag.b], [negag.b])
    P.op("dve", lambda e: e.memset(negones[:], -1.0), [], [negones.b])
    NB = 2

    def mk(nm, shape, dt=F32):
        return [A.alloc(nm, shape, dt) for _ in range(NB)]
    pt, fm, gx, kv = mk("gpt", [128, 272]) + mk("gpt", [128, 272]), mk("gfm", [64, 12, 128]), mk("ggx", [128, 24]), mk("gkv", [128, 512])
    sc, esc, ML = mk("gsc", [128, 16]), mk("gesc", [128, 16]), mk("gML", [128, 4, 128])
    decT, decS, A0, Q0, aT = mk("gdecT", [128, 4, 128]), mk("gdecS", [128, 4, 128]), mk("gA0", [128, 4, 128]), mk("gQ0", [128, 4, 128]), mk("gaT", [128, 4, 128], BF16)
    Pb, Qb, MTb = [mk("gPb", [128, 4, 128]) for _ in range(2)], [mk("gQb", [128, 4, 128]) for _ in range(2)], [mk("gMT", [128, 4, 128]) for _ in range(3)]
    Rv, Rk, bek, usb, wsb = mk("gRv", [128, 4, 64]), mk("gRk", [128, 4, 64]), mk("gbek", [128, 4]), mk("gusb", [128, 256]), mk("gwsb", [64, 512], BF16)
    ektc, kt = [mk("gektc", [128, 4]) for _ in range(2)], [mk("gkt", [128, 4, 64], BF16) for _ in range(2)]
    qbf = mk("gqbf", [64, 4, 128], BF16)
    vnew, oint, ot, sq, ss, yo = mk("gvnew", [128, 256], BF16), mk("goint", [128, 256]), mk("got", [128, 256]), mk("gsq", [128, 256]), mk("gss", [128, 8]), mk("gyo", [128, 2, 128], BF16)
    for i in range(NB):
        P.op("dve", lambda e, i=i: e.memset(vnew[i][:], 0.0), [], [vnew[i].b])
    v4 = lambda t: t[:].rearrange("p (h q) -> p h q", h=4)

    def body(R, d, ti, i, li, second):
        g0_, g1_, g2_, g3_ = G.psum[4 * d:4 * d + 4]
        b = [g0_, g2_, g3_, g2_, g3_, g2_, g3_, g1_]
        bM = g0_
        t0 = tok0 + ti * 128
        Ltri2 = C[:, K_LF2 if d == 0 else K_LB2, :]
        niT = C[:, K_NI2F if d == 0 else K_NI2B, :]
        nsS = C[:, K_NS2F if d == 0 else K_NS2B, :]
        selc = [C[:, K_SEL0, 0:1], C[:, K_SEL1, 0:1]]
        pt_, f_, gx_, kv_, sc_, es_, ml_ = pt[li], fm[i], gx[i], kv[i], sc[i], esc[i], ML[i]
        dT_, dS_, a0_, q0_, at_ = decT[i], decS[i], A0[i], Q0[i], aT[i]
        R.dma(pt_[:], G.projT[t0:t0 + 128, 264:536], reads=[G.projT.b], writes=[pt_.b])
        R.dma(f_[:], G.convF[512:1280, t0:t0 + 128].rearrange("(j p) t -> p j t", p=64), reads=[G.convF.b], writes=[f_.b])
        R.op("dve", lambda e: e.tensor_tensor(gx_[:, 0:8], pt_[:, 256:264], rp[:, R_GDTB:R_GDTB + 8], ALU.add), [pt_.b, rp.b], [gx_.b])
        softplus_small(R, gx_[:, 0:8], gx_[:, 0:8], gx_[:, 16:24], [gx_.b], (gx_.b, gx_.b))
        R.op("dve", lambda e: e.tensor_tensor(gx_[:, 0:8], gx_[:, 0:8], negag[:], ALU.mult), [gx_.b, negag.b], [gx_.b])
        R.op("act", lambda e: e.activation(gx_[:, 8:16], pt_[:, 264:272], AF.Sigmoid), [pt_.b], [gx_.b])
        gd = gx_[:, d * 4:d * 4 + 4]
        bd = gx_[:, 8 + d * 4:8 + d * 4 + 4]
        qb_ = qbf[i]
        R.op("pool", lambda e: e.tensor_copy(qb_[:], f_[:, 0:4, :]), [f_.b], [qb_.b])
        def f(e):
            for h in range(4):
                e.transpose(b[0][:, h * 64:(h + 1) * 64], f_[:, 4 + h, :], id64)
            for h in range(4):
                ins = e.transpose(b[0][:, 256 + h * 64:256 + (h + 1) * 64], f_[:, 8 + h, :], id64)
            return ins
        R.op("pe", f, [f_.b, C.b], [b[0].b])
        R.op("act", lambda e: e.activation(kv_[:], b[0][:, :], AF.Copy), [b[0].b], [kv_.b])
        def f(e):
            e.matmul(b[7][:, 0:4], Ltri2, gd, start=True, stop=True)
            e.matmul(b[7][:, 4:8], C[:, K_BO2, :], gd, start=True, stop=True)
            e.matmul(b[7][:, 8:12], C[:, K_SEL0, :], gd, start=True, stop=True)
            return e.matmul(b[7][:, 12:16], C[:, K_SEL1, :], gd, start=True, stop=True)
        R.op("pe", f, [gx_.b, C.b], [b[7].b])
        R.op("dve", lambda e: e.tensor_copy(sc_[:], b[7][:, 0:16]), [b[7].b], [sc_.b])
        R.op("dve", lambda e: e.tensor_tensor(sc_[:, 4:8], sc_[:, 4:8], sc_[:, 0:4], ALU.subtract), [sc_.b], [sc_.b])
        R.op("act", lambda e: e.activation(es_[:], sc_[:], AF.Exp), [sc_.b], [es_.b])
        R.op("dve", lambda e: e.tensor_tensor(ml_[:], bc_ap(Ltri2, [[0, 4], [1, 128]]), bc_ap(gd, [[1, 4], [0, 128]]), ALU.mult),
             [gx_.b, C.b], [ml_.b])
        mlf = ml_[:].rearrange("p h i -> p (h i)")
        def f(e):
            e.matmul(b[1][:, :], ones, mlf, start=True, stop=False)
            for h in range(4):
                e.matmul(b[1][:, h * 128:(h + 1) * 128], ml_[:, h, :], negones[:], start=False, stop=False)
            return e.matmul(b[1][:, :], ident, bc_ap(niT, [[0, 4], [1, 128]]), start=False, stop=True)
        R.op("pe", f, [ml_.b, C.b, negones.b], [b[1].b])
        def f(e):
            e.matmul(b[2][:, :], negones[:], mlf, start=True, stop=False)
            for h in range(4):
                e.matmul(b[2][:, h * 128:(h + 1) * 128], ml_[:, h, :], ones, start=False, stop=False)
            return e.matmul(b[2][:, :], ident, bc_ap(nsS, [[0, 4], [1, 128]]), start=False, stop=True)
        R.op("pe", f, [ml_.b, C.b, negones.b], [b[2].b])
        fl = lambda t: t[:].rearrange("p h i -> p (h i)")
        R.op("act", lambda e: e.activation(fl(dT_), b[1][:, :], AF.Exp), [b[1].b], [dT_.b])
        R.op("act", lambda e: e.activation(fl(dS_), b[2][:, :], AF.Exp), [b[2].b], [dS_.b])
        def f(e):
            for h in range(4):
                ins = e.matmul(b[3][:, h * 128:(h + 1) * 128], f_[:, 4 + h, :], f_[:, 4 + h, :], start=True, stop=True)
            return ins
        R.op("pe", f, [f_.b], [b[3].b])
        def f(e):
            for h in range(4):
                ins = e.matmul(b[4][:, h * 128:(h + 1) * 128], f_[:, 4 + h, :], f_[:, h, :], start=True, stop=True)
            return ins
        R.op("pe", f, [f_.b], [b[4].b])
        R.op("dve", lambda e: e.tensor_tensor(fl(a0_), b[3][:, :], fl(dS_), ALU.mult), [b[3].b, dS_.b], [a0_.b])
        R.op("dve", lambda e: e.tensor_tensor(a0_[:], a0_[:], bc_ap(bd, [[1, 4], [0, 128]]), ALU.mult), [a0_.b, gx_.b], [a0_.b])
        R.op("dve", lambda e: e.tensor_tensor(fl(at_), b[4][:, :], fl(dT_), ALU.mult), [b[4].b, dT_.b], [at_.b])
        def f(e):
            for h in range(4):
                ins = e.transpose(b[5][:, h * 128:(h + 1) * 128], a0_[:, h, :], ident)
            return ins
        R.op("pe", f, [a0_.b, C.b], [b[5].b])
        R.op("act", lambda e: e.activation(fl(q0_), b[5][:, :], AF.Copy), [b[5].b], [q0_.b])
        mt = MTb[0][i]
        R.op("dve", lambda e, mt=mt: e.tensor_tensor(mt[:], bc_ap(ident, [[0, 4], [1, 128]]), q0_[:], ALU.subtract), [q0_.b, C.b], [mt.b])
        if name == "ctx" and ti == (0 if d == 0 else 1):
            dbgdump(G, f"gA{d}", a0_, fl(a0_), [128, 512], P=R)
            dbgdump(G, f"gQ{d}", q0_, fl(q0_), [128, 512], P=R)
            dbgdump(G, f"gM0{d}", mt, fl(mt), [128, 512], P=R)
            dbgdump(G, f"gdS{d}", dS_, fl(dS_), [128, 512], P=R)
            dbgdump(G, f"ggx{d}", gx_, gx_[:], [128, 24], P=R)
            dbgdump(G, f"gkv{d}", kv_, kv_[:], [128, 512], P=R)
        Pc, Qc = a0_, q0_
        for m in range(GDN_ROUNDS):
            Pn, Qn, mtn = Pb[m % 2][i], Qb[m % 2][i], MTb[(m + 1) % 3][i]
            def f(e, Pc=Pc, Qc=Qc):
                for h in range(4):
                    ins = e.matmul(b[3][:, h * 128:(h + 1) * 128], Qc[:, h, :], Pc[:, h, :], start=True, stop=True)
                return ins
            R.op("pe", f, [Pc.b, Qc.b], [b[3].b])
            def f(e, Pc=Pc, Qc=Qc):
                for h in range(4):
                    ins = e.matmul(b[4][:, h * 128:(h + 1) * 128], Pc[:, h, :], Qc[:, h, :], start=True, stop=True)
                return ins
            R.op("pe", f, [Pc.b, Qc.b], [b[4].b])
            R.op("act", lambda e, Pn=Pn: e.activation(fl(Pn), b[3][:, :], AF.Copy), [b[3].b], [Pn.b])
            R.op("dve", lambda e, Qn=Qn: e.tensor_copy(fl(Qn), b[4][:, :]), [b[4].b], [Qn.b])
            def f(e, Pn=Pn, mt=mt):
                for h in range(4):
                    ins = e.matmul(bM[:, h * 128:(h + 1) * 128], Pn[:, h, :], mt[:, h, :], start=True, stop=True)
                return ins
            R.op("pe", f, [Pn.b, mt.b], [bM.b])
            R.op("dve", lambda e, mt=mt, mtn=mtn: e.tensor_tensor(fl(mtn), fl(mt), bM[:, :], ALU.add), [mt.b, bM.b], [mtn.b])
            Pc, Qc, mt = Pn, Qn, mtn
        rv_, rk_, bk_, u_, w_ = Rv[i], Rk[i], bek[i], usb[i], wsb[i]
        R.op("dve", lambda e: e.tensor_tensor(rv_[:], v4(kv_)[:, 4:8, :] if False else kv_[:, 256:512].rearrange("p (h q) -> p h q", h=4),
                                              bc_ap(bd, [[1, 4], [0, 64]]), ALU.mult), [kv_.b, gx_.b], [rv_.b])
        R.op("dve", lambda e: e.tensor_tensor(bk_[:], bd, es_[:, 0:4], ALU.mult), [gx_.b, es_.b], [bk_.b])
        R.op("dve", lambda e: e.tensor_tensor(rk_[:], kv_[:, 0:256].rearrange("p (h q) -> p h q", h=4),
                                              bc_ap(bk_[:, 0:4], [[1, 4], [0, 64]]), ALU.mult), [kv_.b, bk_.b], [rk_.b])
        def f(e):
            for h in range(4):
                ins = e.matmul(b[7][:, 64 + h * 64:64 + (h + 1) * 64], mt[:, h, :], rv_[:, h, :], start=True, stop=True)
            return ins
        R.op("pe", f, [mt.b, rv_.b], [b[7].b])
        def f(e):
            for h in range(4):
                ins = e.matmul(b[0][0:64, h * 128:(h + 1) * 128], rk_[:, h, :], mt[:, h, :], start=True, stop=True)
            return ins
        R.op("pe", f, [mt.b, rk_.b], [b[0].b])
        R.op("act", lambda e: e.activation(u_[:], b[7][:, 64:320], AF.Copy), [b[7].b], [u_.b])
        R.op("dve", lambda e: e.tensor_copy(w_[:], b[0][0:64, :]), [b[0].b], [w_.b])
        for c in range(2):
            ek, k_ = ektc[c][i], kt[c][i]
            R.op("dve", lambda e, ek=ek, c=c: e.tensor_scalar(ek[:], es_[:, 4:8], selc[c], None, ALU.mult), [es_.b, C.b], [ek.b])
            R.op("dve", lambda e, ek=ek, k_=k_: e.tensor_tensor(k_[:], kv_[:, 0:256].rearrange("p (h q) -> p h q", h=4),
                                                               bc_ap(ek[:, 0:4], [[1, 4], [0, 64]]), ALU.mult), [kv_.b, ek.b], [k_.b])
        vn_, oi_ = vnew[i], oint[i]
        for c in ((0, 1) if d == 0 else (1, 0)):
            lo, hi = c * 64, (c + 1) * 64
            k_ = kt[c][i]
            def f(e):
                for h in range(4):
                    e.matmul(b[5][:, h * 64:(h + 1) * 64], w_[:, h * 128:(h + 1) * 128], G.gsb[d][:, h, :], start=True, stop=True)
                for h in range(4):
                    ins = e.matmul(b[5][:, 256 + h * 64:256 + (h + 1) * 64], qb_[:, h, :], G.gsb[d][:, h, :], start=True, stop=True)
                return ins
            R.op("pe", f, [w_.b, qb_.b, G.gsb[d].b], [b[5].b])
            R.op("dve", lambda e, lo=lo, hi=hi: e.tensor_tensor(vn_[lo:hi, :], u_[lo:hi, :], b[5][lo:hi, 0:256], ALU.subtract), [u_.b, b[5].b], [vn_.b])
            R.op("dve", lambda e, lo=lo, hi=hi: e.tensor_tensor(oi_[lo:hi, :].rearrange("p (h q) -> p h q", h=4),
                                                                b[5][lo:hi, 256:512].rearrange("p (h q) -> p h q", h=4),
                                                                bc_ap(es_[lo:hi, 0:4], [[1, 4], [0, 64]]), ALU.mult), [b[5].b, es_.b], [oi_.b])
            def f(e, k_=k_):
                for h in range(4):
                    ins = e.matmul(b[6][0:64, h * 64:(h + 1) * 64], k_[:, h, :], vn_[:, h * 64:(h + 1) * 64], start=True, stop=True)
                return ins
            R.op("pe", f, [k_.b, vn_.b], [b[6].b])
            for h in range(4):
                R.op("dve", lambda e, h=h, c=c: e.scalar_tensor_tensor(
                    G.gst[d][:, h, :], G.gst[d][:, h, :], es_[0:64, 8 + 4 * c + h:9 + 4 * c + h], b[6][0:64, h * 64:(h + 1) * 64],
                    ALU.mult, ALU.add), [G.gst[d].b, es_.b, b[6].b], [G.gst[d].b])
            R.op("pool", lambda e: e.tensor_copy(G.gsb[d][:], G.gst[d][:]), [G.gst[d].b], [G.gsb[d].b])
        def f(e):
            for h in range(4):
                ins = e.matmul(b[7][:, 64 + h * 64:64 + (h + 1) * 64], at_[:, h, :], vn_[:, h * 64:(h + 1) * 64], start=True, stop=True)
            return ins
        R.op("pe", f, [at_.b, vn_.b], [b[7].b])
        if name == "ctx" and ti == (0 if d == 0 else 1):
            dbgdump(G, f"gMT{d}", mt, fl(mt), [128, 512], P=R)
            dbgdump(G, f"gu{d}", u_, u_[:], [128, 256], P=R)
            dbgdump(G, f"gw{d}", w_, w_[:], [64, 512], BF16, P=R)
            dbgdump(G, f"gvn{d}", vn_, vn_[:], [128, 256], BF16, P=R)
            dbgdump(G, f"gaT{d}", at_, fl(at_), [128, 512], BF16, P=R)
        if not second:
            R.op("dve", lambda e: e.tensor_tensor(oacc[:, ti, :], b[7][:, 64:320], oi_[:], ALU.add), [b[7].b, oi_.b], [oacc.b])
            return
        o_, q_, s_, y_ = ot[i], sq[i], ss[i], yo[i]
        R.op("dve", lambda e: e.tensor_tensor(o_[:], b[7][:, 64:320], oi_[:], ALU.add), [b[7].b, oi_.b], [o_.b])
        R.op("dve", lambda e: e.tensor_tensor(o_[:], o_[:], oacc[:, ti, :], ALU.add), [o_.b, oacc.b], [o_.b])
        R.op("dve", lambda e: e.tensor_tensor(q_[:], o_[:], o_[:], ALU.mult), [o_.b], [q_.b])
        R.op("dve", lambda e: e.reduce_sum(s_[:, 0:4], q_[:].rearrange("p (h q) -> p h q", h=4), AX.X), [q_.b], [s_.b])
        R.op("act", lambda e: e.activation(s_[:, 4:8], s_[:, 0:4], AF.Sqrt, bias=G.eps[:, :], scale=1.0 / 64.0), [s_.b, G.eps.b], [s_.b])
        R.op("dve", lambda e: e.reciprocal(s_[:, 4:8], s_[:, 4:8]), [s_.b], [s_.b])
        R.op("dve", lambda e: e.tensor_tensor(v4(o_), v4(o_), bc_ap(s_[:, 4:8], [[1, 4], [0, 64]]), ALU.mult), [o_.b, s_.b], [o_.b])
        R.op("dve", lambda e: e.tensor_tensor(v4(o_), v4(o_), bc_ap(rp[:, R_GDNNW:R_GDNNW + 64], [[0, 4], [1, 64]]), ALU.mult), [o_.b, rp.b], [o_.b])
        R.op("act", lambda e: e.activation(q_[:], pt_[:, 0:256], AF.Silu), [pt_.b], [q_.b])
        R.op("dve", lambda e: e.tensor_tensor(o_[:], o_[:], q_[:], ALU.mult), [o_.b, q_.b], [o_.b])
        def f(e):
            e.transpose(b[2][:, 0:128], o_[:, 0:128], ident)
            return e.transpose(b[2][:, 128:256], o_[:, 128:256], ident)
        R.op("pe", f, [o_.b, C.b], [b[2].b])
        R.op("act", lambda e: e.activation(y_[:].rearrange("p j t -> p (j t)"), b[2][:, 0:256], AF.Copy), [b[2].b], [y_.b])
        R.dma(G.yT[512:768, t0:t0 + 128].rearrange("(j p) t -> p j t", p=128), y_[:], reads=[y_.b], writes=[G.yT.b], q="pool")

    cnt = 0
    for st_ in range(NT):
        recs = []
        for d in range(2):
            ti = st_ if d == 0 else NT - 1 - st_
            second = (ti >= NT // 2) if d == 0 else (ti < NT // 2)
            recs.append(Rec())
            body(recs[-1], d, ti, d, 2 * d + st_ % 2, second)
        merge(P, recs)
    P.barrier()
    A.reset(m0)


def x_rows(G, l, name, tok0, r0, n):
    if l == 0:
        return (G.ctx_in if name == "ctx" else G.x_in)[r0:r0 + n, :], None
    t = G.xs[l % 2]
    return t[tok0 + r0:tok0 + r0 + n, :], t.b


def phase_c(G, l, streams, hT2):
    P, A = G.P, G.A
    m0 = A.mark()
    rp = G.rowp_sb
    wo = A.alloc("wo", [128, 8, D], BF16)
    xt = [A.alloc("cxt", [128, D]) for _ in range(2)]
    load_weight_bf16(G, wo, lambda k: G.w_out[l, k * 128:(k + 1) * 128, :], D, xt, 8)
    yb = [A.alloc("cyb", [128, 8, 512], BF16) for _ in range(2)]
    t1 = [A.alloc("ct1", [128, D]) for _ in range(2)]
    xn = [A.alloc("cxn", [128, D]) for _ in range(4)]
    st = [A.alloc("cst", [128, 2, 6]) for _ in range(2)]
    mv = [A.alloc("cmv", [128, 2]) for _ in range(2)]
    rstd = [A.alloc("crstd", [128, 1]) for _ in range(2)]
    ident = G.consts[:, K_ID, :]
    po = [[G.psum[0], G.psum[1]], [G.psum[2], G.psum[3]]]
    pT = [G.psum[4], G.psum[5]]
    cb = 0
    ct = 0
    for (name, tok0, L, s) in streams:
        bs = min(512, L)
        for b0 in range(0, L, bs):
            nt = bs // 128
            y_ = yb[cb % 2]
            cb += 1
            P.dma(y_[:, :, 0:bs], G.yT[:, tok0 + b0:tok0 + b0 + bs].rearrange("(k p) t -> p k t", p=128), reads=[G.yT.b], writes=[y_.b])
            recs = []
            for m in range(nt):
                R = Rec()
                recs.append(R)
                q = ct % 2
                ct += 1
                x_, t_, pp = xt[q], t1[q], po[q]
                r0 = b0 + m * 128
                src, sb_ = x_rows(G, l, name, tok0, r0, 128)
                R.dma(x_[:], src, reads=[sb_] if sb_ is not None else [], writes=[x_.b])
                for half in range(2):
                    def f(e, half=half, y_=y_, m=m, pp=pp):
                        for k in range(8):
                            ins = e.matmul(pp[half][:, :], y_[:, k, m * 128:(m + 1) * 128], wo[:, k, half * 512:(half + 1) * 512],
                                           start=(k == 0), stop=(k == 7))
                        return ins
                    R.op("pe", f, [y_.b, wo.b], [pp[half].b])
                    R.op("dve", lambda e, half=half, t_=t_, pp=pp, s=s: e.tensor_tensor(
                        t_[:, half * 512:(half + 1) * 512], pp[half][:, :], G.gb[:, s, 0, half * 512:(half + 1) * 512], ALU.mult),
                        [pp[half].b, G.gb.b], [t_.b])
                R.op("dve", lambda e, t_=t_, x_=x_: e.scalar_tensor_tensor(t_[:], x_[:], DN_ALPHA, t_[:], ALU.mult, ALU.add), [x_.b, t_.b], [t_.b])
                ln_stats(G, t_, 128, st[q], mv[q], rstd[q], P=R)
                R.op("dve", lambda e, t_=t_, q=q: e.tensor_scalar(t_[:], t_[:], mv[q][:, 0:1], rstd[q][:, 0:1], ALU.subtract, ALU.mult),
                     [t_.b, mv[q].b, rstd[q].b], [t_.b])
                R.op("pool", lambda e, t_=t_: e.tensor_tensor(t_[:], t_[:], rp[:, R_LN1W:R_LN1W + D], ALU.mult), [t_.b, rp.b], [t_.b])
                R.op("pool", lambda e, t_=t_: e.tensor_tensor(t_[:], t_[:], rp[:, R_LN1B:R_LN1B + D], ALU.add), [t_.b, rp.b], [t_.b])
                R.dma(G.x1[tok0 + r0:tok0 + r0 + 128, :], t_[:], reads=[t_.b], writes=[G.x1.b], q="pool")
                ln_stats(G, t_, 128, st[q], mv[q], rstd[q], P=R)
                R.op("dve", lambda e, t_=t_, q=q, m=m: e.tensor_scalar(xn[m][:], t_[:], mv[q][:, 0:1], rstd[q][:, 0:1], ALU.subtract, ALU.mult),
                     [t_.b, mv[q].b, rstd[q].b], [xn[m].b])
            for m in range(0, nt, 2):
                merge(P, recs[m:m + 2])
            for k in range(8):
                p = pT[k % 2]
                def f(e, p=p, k=k, nt=nt):
                    for m in range(nt):
                        ins = e.transpose(p[:, m * 128:(m + 1) * 128], xn[m][:, k * 128:(k + 1) * 128], ident)
                    return ins
                P.op("pe", f, [xn[m].b for m in range(nt)] + [G.consts.b], [p.b])
                c0 = tok0 + b0
                P.op("act", lambda e, p=p, k=k, bs=bs, s=s, c0=c0: e.activation(
                    hT2[:, k, c0:c0 + bs], p[:, 0:bs], AF.Identity, bias=G.modc[:, 2, k, s:s + 1], scale=G.modc[:, 3, k, s:s + 1]),
                    [p.b, G.modc.b], [hT2.b])
    P.barrier()
    A.reset(m0)


def phase_d1(G, l, streams, hT2):
    P, A = G.P, G.A
    m0 = A.mark()
    cp = G.colp_sb
    wst = [A.alloc("dwst", [128, 8, 256]) for _ in range(2)]
    wab = [A.alloc("dwab", [128, 8, 256], BF16) for _ in range(2)]
    asb = [A.alloc("dasb", [128, 514]) for _ in range(4)]
    acc = [A.alloc("dacc", [128, 512]) for _ in range(4)]
    hc = [A.alloc("dhc", [128, 512], BF16) for _ in range(4)]
    up = G.ffn_up[l, :, :].rearrange("(k p) c -> p k c", p=128)
    pa = [[G.psum[0], G.psum[1]], [G.psum[2], G.psum[3]]]
    pbs = [G.psum[4], G.psum[5], G.psum[6], G.psum[7]]
    cnt = 0
    for c in range(DFF // 128):
        w_, wb_ = wst[c % 2], wab[c % 2]
        P.dma(w_[:, :, 0:128], up[:, :, c * 128:(c + 1) * 128], writes=[w_.b])
        P.dma(w_[:, :, 128:256], up[:, :, DFF + c * 128:DFF + (c + 1) * 128], writes=[w_.b])
        P.op("pool", lambda e, w_=w_, wb_=wb_: e.tensor_copy(wb_[:], w_[:]), [w_.b], [wb_.b])
        for (name, tok0, L, s) in streams:
            bs = min(512, L)
            for b0 in range(0, L, bs):
                i = cnt % 4
                p0, p1 = pa[cnt % 2]
                pb = pbs[i]
                cnt += 1
                a_, ac_, h_ = asb[i], acc[i], hc[i]
                t0 = tok0 + b0
                lo = max(t0 - 1, tok0)
                hi = min(t0 + bs + 1, tok0 + L)
                jlo, jhi = lo - (t0 - 1), hi - (t0 - 1)
                half = (bs + 2) // 2
                segs = [(jlo, half + 1), (half - 1, jhi)] if bs == 512 else [(jlo, jhi)]
                for si, (j0, j1) in enumerate(segs):
                    pp = p0 if si == 0 else p1
                    def f(e, pp=pp, j0=j0, j1=j1, wb_=wb_, t0=t0):
                        for k in range(8):
                            ins = e.matmul(pp[:, 0:j1 - j0], wb_[:, k, 0:128], hT2[:, k, t0 - 1 + j0:t0 - 1 + j1], start=(k == 0), stop=(k == 7))
                        return ins
                    P.op("pe", f, [wb_.b, hT2.b], [pp.b])
                def f(e, pb=pb, wb_=wb_, t0=t0, bs=bs):
                    for k in range(8):
                        ins = e.matmul(pb[:, 0:bs], wb_[:, k, 128:256], hT2[:, k, t0:t0 + bs], start=(k == 0), stop=(k == 7))
                    return ins
                P.op("pe", f, [wb_.b, hT2.b], [pb.b])
                if jlo > 0:
                    P.op("pool", lambda e, a_=a_: e.memset(a_[:, 0:1], 0.0), [], [a_.b])
                if jhi < bs + 2:
                    P.op("pool", lambda e, a_=a_, bs=bs: e.memset(a_[:, bs + 1:bs + 2], 0.0), [], [a_.b])
                if len(segs) == 2:
                    (a0, a1), (b0_, b1_) = segs
                    P.op("act", lambda e, a_=a_, p0=p0, a0=a0, a1=a1: e.activation(a_[:, a0:a1], p0[:, 0:a1 - a0], AF.Copy), [p0.b], [a_.b])
                    P.op("act", lambda e, a_=a_, p1=p1, a1=a1, b0_=b0_, b1_=b1_: e.activation(
                        a_[:, a1:b1_], p1[:, a1 - b0_:b1_ - b0_], AF.Copy), [p1.b], [a_.b])
                else:
                    (a0, a1), = segs
                    P.op("act", lambda e, a_=a_, p0=p0, a0=a0, a1=a1: e.activation(a_[:, a0:a1], p0[:, 0:a1 - a0], AF.Copy), [p0.b], [a_.b])
                wo_ = C_FFNCW + c * 3
                P.op("dve", lambda e, a_=a_, ac_=ac_, bs=bs, wo_=wo_: e.tensor_scalar(ac_[:, 0:bs], a_[:, 0:bs], cp[:, wo_:wo_ + 1], None, ALU.mult),
                     [a_.b, cp.b], [ac_.b])
                for tap in (1, 2):
                    P.op("dve", lambda e, a_=a_, ac_=ac_, bs=bs, wo_=wo_, tap=tap: e.scalar_tensor_tensor(
                        ac_[:, 0:bs], a_[:, tap:tap + bs], cp[:, wo_ + tap:wo_ + tap + 1], ac_[:, 0:bs], ALU.mult, ALU.add), [a_.b, cp.b, ac_.b], [ac_.b])
                P.op("act", lambda e, ac_=ac_, bs=bs, c=c: e.activation(ac_[:, 0:bs], ac_[:, 0:bs], AF.Silu, bias=cp[:, C_FFNCB + c:C_FFNCB + c + 1]),
                     [ac_.b, cp.b], [ac_.b])
                P.op("dve", lambda e, ac_=ac_, h_=h_, pb=pb, bs=bs: e.tensor_tensor(h_[:, 0:bs], ac_[:, 0:bs], pb[:, 0:bs], ALU.mult), [ac_.b, pb.b], [h_.b])
                P.dma(G.hid[c * 128:(c + 1) * 128, t0:t0 + bs], h_[:, 0:bs], reads=[h_.b], writes=[G.hid.b], q="pool")
    P.barrier()
    A.reset(m0)


def phase_d2(G, l, streams, last):
    P, A = G.P, G.A
    m0 = A.mark()
    rp = G.rowp_sb
    NC_ = DFF // 128
    wd = A.alloc("wd", [128, NC_, D], BF16)
    stage = [A.alloc("wdstage", [128, D]) for _ in range(2)]
    load_weight_bf16(G, wd, lambda k: G.ffn_down[l, k * 128:(k + 1) * 128, :], D, stage, NC_)
    hb = [A.alloc("ehb", [128, NC_, 512], BF16) for _ in range(2)]
    xt = [A.alloc("ext", [128, D]) for _ in range(2)]
    t1 = [A.alloc("et1", [128, D]) for _ in range(2)]
    st = [A.alloc("est", [128, 2, 6]) for _ in range(2)]
    mv = [A.alloc("emv", [128, 2]) for _ in range(2)]
    rstd = [A.alloc("erstd", [128, 1]) for _ in range(2)]
    po = [[G.psum[0], G.psum[1]], [G.psum[2], G.psum[3]]]
    xnext = G.xs[(l + 1) % 2]
    cb = 0
    ct = 0
    for (name, tok0, L, s) in streams:
        bs = min(512, L)
        for b0 in range(0, L, bs):
            nt = bs // 128
            h_ = hb[cb % 2]
            cb += 1
            P.dma(h_[:, :, 0:bs], G.hid[:, tok0 + b0:tok0 + b0 + bs].rearrange("(c p) t -> p c t", p=128), reads=[G.hid.b], writes=[h_.b])
            recs = []
            for m in range(nt):
                R = Rec()
                recs.append(R)
                q = ct % 2
                ct += 1
                x_, t_, pp = xt[q], t1[q], po[q]
                r0 = tok0 + b0 + m * 128
                R.dma(x_[:], G.x1[r0:r0 + 128, :], reads=[G.x1.b], writes=[x_.b])
                for half in range(2):
                    def f(e, half=half, h_=h_, m=m, pp=pp):
                        for c in range(NC_):
                            ins = e.matmul(pp[half][:, :], h_[:, c, m * 128:(m + 1) * 128], wd[:, c, half * 512:(half + 1) * 512],
                                           start=(c == 0), stop=(c == NC_ - 1))
                        return ins
                    R.op("pe", f, [h_.b, wd.b], [pp[half].b])
                    R.op("dve", lambda e, half=half, t_=t_, pp=pp, s=s: e.tensor_tensor(
                        t_[:, half * 512:(half + 1) * 512], pp[half][:, :], G.gb[:, s, 1, half * 512:(half + 1) * 512], ALU.mult),
                        [pp[half].b, G.gb.b], [t_.b])
                R.op("dve", lambda e, t_=t_, x_=x_: e.scalar_tensor_tensor(t_[:], x_[:], DN_ALPHA, t_[:], ALU.mult, ALU.add), [x_.b, t_.b], [t_.b])
                ln_stats(G, t_, 128, st[q], mv[q], rstd[q], P=R)
                R.op("dve", lambda e, t_=t_, q=q: e.tensor_scalar(t_[:], t_[:], mv[q][:, 0:1], rstd[q][:, 0:1], ALU.subtract, ALU.mult),
                     [t_.b, mv[q].b, rstd[q].b], [t_.b])
                R.op("pool", lambda e, t_=t_: e.tensor_tensor(t_[:], t_[:], rp[:, R_LN2W:R_LN2W + D], ALU.mult), [t_.b, rp.b], [t_.b])
                R.op("pool", lambda e, t_=t_: e.tensor_tensor(t_[:], t_[:], rp[:, R_LN2B:R_LN2B + D], ALU.add), [t_.b, rp.b], [t_.b])
                if last:
                    rr = b0 + m * 128
                    R.dma(G.out[rr:rr + 128, :], t_[:], reads=[t_.b], writes=[G.out.b], q="pool")
                else:
                    R.dma(xnext[r0:r0 + 128, :], t_[:], reads=[t_.b], writes=[xnext.b], q="pool")
            for m in range(nt):
                merge(P, recs[m:m + 1])
    P.barrier()
    A.reset(m0)


def _col(v, nchunk):
    return np.ascontiguousarray(v.reshape(nchunk, 128).T)


def prep_inputs(inputs):
    f = lambda a: np.ascontiguousarray(np.asarray(a, dtype=np.float32))
    I = {k: f(v) for k, v in inputs.items()}
    colp = np.zeros((DEPTH, 128, NCOL), np.float32)
    rowp = np.zeros((DEPTH, 1, NROW), np.float32)
    poolw = np.zeros((DEPTH, 128, 2, 128), np.float32)
    gws = np.zeros((DEPTH, 128, 4, 128), np.float32)
    for l in range(DEPTH):
        cw = I["ssd_conv_w"][l]
        colp[l, :, C_SSDCW:C_SSDCW + 28] = cw.T.reshape(4, 128, 7).transpose(1, 0, 2).reshape(128, 28)
        colp[l, :, C_SSDCB:C_SSDCB + 4] = _col(I["ssd_conv_b"][l], 4)
        gw = I["gdn_conv_w"][l]
        colp[l, :, C_GDNCW:C_GDNCW + 42] = gw.T.reshape(6, 128, 7).transpose(1, 0, 2).reshape(128, 42)
        fw = I["ffn_conv_w"][l]
        colp[l, :, C_FFNCW:C_FFNCW + 66] = fw.T.reshape(22, 128, 3).transpose(1, 0, 2).reshape(128, 66)
        colp[l, :, C_FFNCB:C_FFNCB + 22] = _col(I["ffn_conv_b"][l], 22)
        colp[l, :, C_PSCALE:C_PSCALE + 2] = _col(I["pool_scale"][l], 2)
        colp[l, :, C_BMOD:C_BMOD + 48] = I["b_mod"][l].reshape(6, 8, 128).transpose(2, 0, 1).reshape(128, 48)
        r = rowp[l, 0]
        r[R_SSDNW:R_SSDNW + 256] = I["ssd_norm_w"][l]
        r[R_GDNNW:R_GDNNW + 64] = I["gdn_norm_w"][l]
        r[R_GLNW:R_GLNW + 256] = I["gmlp_ln_w"][l]
        r[R_GLNB:R_GLNB + 256] = I["gmlp_ln_b"][l]
        r[R_LN1W:R_LN1W + 1024] = I["ln1_w"][l]
        r[R_LN1B:R_LN1B + 1024] = I["ln1_b"][l]
        r[R_LN2W:R_LN2W + 1024] = I["ln2_w"][l]
        r[R_LN2B:R_LN2B + 1024] = I["ln2_b"][l]
        r[R_SSDD:R_SSDD + 256] = np.repeat(I["ssd_d"][l], 64)
        r[R_GBS:R_GBS + 512] = I["gmlp_bs"][l].reshape(-1)
        r[R_SDTB:R_SDTB + 8] = I["ssd_dt_bias"][l].reshape(-1)
        r[R_SALOG:R_SALOG + 8] = I["ssd_a_log"][l].reshape(-1)
        r[R_GDTB:R_GDTB + 8] = I["gdn_dt_bias"][l].reshape(-1)
        r[R_GALOG:R_GALOG + 8] = I["gdn_a_log"][l].reshape(-1)
        r[R_BG1:R_BG1 + 1024] = I["b_mod"][l][2048:3072]
        r[R_BG2:R_BG2 + 1024] = I["b_mod"][l][5120:6144]
        pw = I["pool_w"][l]
        for g in range(4):
            j, h = g // 2, g % 2
            poolw[l, h * 64:(h + 1) * 64, j, h * 64:(h + 1) * 64] = pw[g]
        gws[l] = I["gmlp_ws"][l].transpose(2, 0, 1)
    consts = make_consts()

    def pinv(RW):
        o = np.zeros((128, 2, RW), np.float32)
        pos = np.arange(RW)
        for jc in range(2):
            for hh in range(2):
                w = (2, 4, 8, 16)[2 * jc + hh]
                lo = np.clip(pos - w // 2, 0, RW)
                hi = np.clip(pos + w - w // 2, 0, RW)
                o[hh * 64:(hh + 1) * 64, jc, :] = 1.0 / (hi - lo).astype(np.float32)
        return o
    shared = dict(consts=consts, w_mod=I["w_mod"], w_in=I["w_in"], w_out=I["w_out"], ffn_up=I["ffn_up"],
                  ffn_down=I["ffn_down"], colp=colp, rowp=rowp, poolw=poolw, gws=gws,
                  pinv_g=pinv(64), pinv_c=pinv(256))
    maps = []
    for core in range(8):
        b = core % 4
        crep = np.zeros((128, 2, 8, 128), np.float32)
        crep[:, 0] = np.repeat(_col(I["c"][b], 8)[:, :, None], 128, axis=2)
        crep[:, 1] = np.repeat(_col(I["c_ctx"], 8)[:, :, None], 128, axis=2)
        m = dict(shared)
        m.update(x_in=I["x"][b], ctx_in=I["ctx"][b], crep=crep)
        maps.append(m)
    return maps


_NC_CACHE = {}


def kernel(**inputs):
    maps = prep_inputs(inputs)
    if "nc" not in _NC_CACHE:
        _NC_CACHE["nc"] = build()
    res = run_bass_kernel_spmd(_NC_CACHE["nc"], maps, core_ids=list(range(8)))
    out = np.stack([np.asarray(res.results[b]["out"], dtype=np.float32) for b in range(4)], axis=0)
    return out
```

```python
import numpy as np
import concourse.bass as bass
import concourse.mybir as mybir
from concourse.bass_utils import run_bass_kernel_spmd

F32 = mybir.dt.float32
BF16 = mybir.dt.bfloat16
ALU = mybir.AluOpType
AF = mybir.ActivationFunctionType
AX = mybir.AxisListType

import os
SSD_STOP = int(os.environ.get('SSD_STOP', '99'))
GDN_ROUNDS = int(os.environ.get('GDN_ROUNDS', '5'))
SEG = 16000
DSEG = 1000
DMAK = 8


class Buf:
    __slots__ = ("w", "r", "name")

    def __init__(self, name=""):
        self.w = None
        self.r = {}
        self.name = name


class Prog:
    ENGS = ["pe", "act", "dve", "pool", "sp"]

    def __init__(self, nc):
        self.nc = nc
        self.ops = {e: [] for e in self.ENGS}
        self.count = {e: 0 for e in self.ENGS}
        self.sems = {}
        self.waited = {e: {} for e in self.ENGS}
        self.dma_n = {e: 0 for e in self.ENGS}
        self.last = {}

    def _sem(self, key):
        if key not in self.sems:
            self.sems[key] = self.nc.alloc_semaphore("s_" + "_".join(map(str, key)))
        return self.sems[key]

    def _need(self, eng, waits, tok):
        if tok is None:
            return
        key, val = tok
        if key[0] == "c" and key[1] == "pe" and eng == "pe":
            return
        if self.waited[eng].get(key, 0) >= val:
            return
        if waits.get(key, 0) < val:
            waits[key] = val

    def op(self, eng, fn, reads=(), writes=(), dma=False, extra=()):
        waits = {}
        for b in reads:
            self._need(eng, waits, b.w)
        for b in writes:
            self._need(eng, waits, b.w)
            for k, v in b.r.items():
                self._need(eng, waits, (k, v))
        for t in extra:
            self._need(eng, waits, t)
        if dma:
            n = self.dma_n[eng]
            self.dma_n[eng] += 1
            s, r = n % DMAK, n // DMAK
            if r >= 1:
                pk = ("d", eng, s, (r - 1) // DSEG)
                self._need(eng, waits, (pk, 16 * (((r - 1) % DSEG) + 1)))
            tok = (("d", eng, s, r // DSEG), 16 * ((r % DSEG) + 1))
            inc = 16
        elif fn is None:
            tok = None
            inc = 0
        else:
            n = self.count[eng]
            self.count[eng] += 1
            tok = (("c", eng, n // SEG), (n % SEG) + 1)
            inc = 1
        for k, v in waits.items():
            self.waited[eng][k] = v
            self._sem(k)
        if tok is not None:
            self._sem(tok[0])
            self.last[tok[0]] = tok[1]
        self.ops[eng].append((list(waits.items()), fn, tok, inc))
        if tok is not None:
            for b in reads:
                b.r[tok[0]] = tok[1]
            for b in writes:
                b.w = tok
                b.r = {}
        return tok

    def dma(self, out, in_, reads=(), writes=(), q="sp", **kw):
        return self.op(q, lambda e: e.dma_start(out=out, in_=in_, **kw), reads, writes, dma=True)

    def barrier(self):
        toks = list(self.last.items())
        for e in self.ENGS:
            self.op(e, None, extra=toks)

    def emit(self):
        nc = self.nc
        with nc.Block() as block:
            def run(name):
                def body(e):
                    for waits, fn, tok, inc in self.ops[name]:
                        for k, v in waits:
                            e.wait_ge(self.sems[k], v)
                        if fn is not None:
                            ins = fn(e)
                            ins.then_inc(self.sems[tok[0]], inc)
                return body
            block.tensor(run("pe"))
            block.scalar(run("act"))
            block.vector(run("dve"))
            block.gpsimd(run("pool"))
            block.sync(run("sp"))


class Rec:
    def __init__(self):
        self.calls = []

    def op(self, eng, fn, reads=(), writes=(), dma=False, extra=()):
        self.calls.append((eng, fn, tuple(reads), tuple(writes), dma, tuple(extra)))

    def dma(self, out, in_, reads=(), writes=(), q="sp", **kw):
        self.op(q, lambda e: e.dma_start(out=out, in_=in_, **kw), reads, writes, dma=True)


def merge(P, recs):
    idx = [0] * len(recs)
    live = True
    while live:
        live = False
        for k, r in enumerate(recs):
            if idx[k] < len(r.calls):
                P.op(*r.calls[idx[k]])
                idx[k] += 1
                live = True


class T:
    def __init__(self, h, name=""):
        self.h = h
        self.b = Buf(name)

    def __getitem__(self, k):
        return self.h[k]


def _dtsize(dt):
    return 2 if dt == BF16 else 4


class Arena:
    def __init__(self, nc, base=16512, limit=229344):
        self.nc, self.base, self.limit, self.top, self.n = nc, base, limit, base, 0

    def alloc(self, name, shape, dt=F32):
        el = 1
        for s in shape[1:]:
            el *= s
        size = (el * _dtsize(dt) + 63) // 64 * 64
        off = self.top
        self.top += size
        assert self.top <= self.limit, (name, self.top)
        self.n += 1
        return T(self.nc.alloc_sbuf_tensor_at(f"{name}{self.n}", list(shape), dt, offset=off), name)

    def mark(self):
        return self.top

    def reset(self, m):
        self.top = m


D = 1024
LC = 256
LL = 4096
TALL = LC + LL
DEPTH = 4
NIN = 2584
DFF = 2816
DN_ALPHA = (2 * DEPTH) ** 0.25
LN_EPS = 1e-6
O_Z, O_XBC, O_DT, O_POOL, O_QKV, O_GATE, O_A, O_B, O_UV = 0, 256, 768, 776, 1032, 1800, 2056, 2064, 2072
FM_GROUPS = [(O_XBC, 512), (O_POOL, 256), (O_QKV, 768), (O_UV, 512)]
NFM = 2048
NTM = 536
C_SSDCW, C_SSDCB, C_GDNCW, C_FFNCW, C_FFNCB, C_PSCALE, C_BMOD = 0, 28, 32, 74, 140, 162, 164
NCOL = 164 + 48
R_SSDNW, R_GDNNW, R_GLNW, R_GLNB, R_LN1W, R_LN1B, R_LN2W, R_LN2B = 0, 256, 320, 576, 832, 1856, 2880, 3904
R_SSDD, R_GBS, R_SDTB, R_SALOG, R_GDTB, R_GALOG, R_BG1, R_BG2 = 4928, 5184, 5696, 5704, 5712, 5720, 5728, 6752
NROW = 7776
K_ID, K_ONES, K_LF, K_LB, K_NMF, K_NMB, K_LF2, K_LB2, K_NI2F, K_NI2B, K_NS2F, K_NS2B, K_BO2, K_SEL0, K_SEL1 = range(15)
NCONST = 15
NEG = -30000.0


def make_consts():
    k = np.arange(128)[:, None]
    m = np.arange(128)[None, :]
    same = (k // 64) == (m // 64)
    c = np.zeros((128, NCONST, 128), np.float32)
    c[:, K_ID] = (k == m)
    c[:, K_ONES] = 1.0
    c[:, K_LF] = (k <= m)
    c[:, K_LB] = (k >= m)
    c[:, K_NMF] = np.where(m >= k, 0.0, NEG)
    c[:, K_NMB] = np.where(m <= k, 0.0, NEG)
    c[:, K_LF2] = (k <= m) & same
    c[:, K_LB2] = (k >= m) & same
    c[:, K_NI2F] = np.where((m >= k) & same, 0.0, NEG)
    c[:, K_NI2B] = np.where((m <= k) & same, 0.0, NEG)
    c[:, K_NS2F] = np.where((k > m) & same, 0.0, NEG)
    c[:, K_NS2B] = np.where((k < m) & same, 0.0, NEG)
    c[:, K_BO2] = same
    c[:, K_SEL0] = (k < 64) * np.ones_like(m)
    c[:, K_SEL1] = (k >= 64) * np.ones_like(m)
    return c


class Ctx:
    pass


def build(nlayers=DEPTH, stop_after=None, dbg=False, mixers=None, only=None):
    nc = bass.Bass("TRN2", target_bir_lowering=False)
    G = Ctx()
    G.nc = nc
    P = Prog(nc)
    G.P = P

    def din(name, shape):
        return nc.dram_tensor(name, list(shape), F32, kind="ExternalInput")

    G.x_in = din("x_in", [LL, D])
    G.ctx_in = din("ctx_in", [LC, D])
    G.crep = din("crep", [128, 2, 8, 128])
    G.consts_d = din("consts", [128, NCONST, 128])
    G.w_mod = din("w_mod", [DEPTH, D, 6 * D])
    G.w_in = din("w_in", [DEPTH, D, NIN])
    G.w_out = din("w_out", [DEPTH, D, D])
    G.ffn_up = din("ffn_up", [DEPTH, D, 2 * DFF])
    G.ffn_down = din("ffn_down", [DEPTH, DFF, D])
    G.colp = din("colp", [DEPTH, 128, NCOL])
    G.rowp = din("rowp", [DEPTH, 1, NROW])
    G.poolw = din("poolw", [DEPTH, 128, 2, 128])
    G.gws = din("gws", [DEPTH, 128, 4, 128])
    G.pinv_g = din("pinv_g", [128, 2, 64])
    G.pinv_c = din("pinv_c", [128, 2, 256])
    G.out = T(nc.dram_tensor("out", [LL, D], F32, kind="ExternalOutput"), "out")
    G.xs = [T(nc.dram_tensor(f"xs{i}", [TALL, D], F32), f"xs{i}") for i in range(2)]
    G.x1 = T(nc.dram_tensor("x1s", [TALL, D], F32), "x1s")
    G.projF = T(nc.dram_tensor("projF", [NFM, TALL], F32), "projF")
    G.projT = T(nc.dram_tensor("projT", [TALL, NTM], F32), "projT")
    G.convF = T(nc.dram_tensor("convF", [1280, TALL], F32), "convF")
    G.yT = T(nc.dram_tensor("yT", [D, TALL], BF16), "yT")
    G.hid = T(nc.dram_tensor("hid", [DFF, TALL], BF16), "hid")
    G.dbg = {}
    if dbg:
        G.dbg["projF"] = T(nc.dram_tensor("d_projF", [NFM, TALL], F32, kind="ExternalOutput"))
        G.dbg["projT"] = T(nc.dram_tensor("d_projT", [TALL, NTM], F32, kind="ExternalOutput"))
        G.dbg["mod"] = T(nc.dram_tensor("d_mod", [128, 64], F32, kind="ExternalOutput"))
        G.dbg["gb"] = T(nc.dram_tensor("d_gb", [128, 4096], F32, kind="ExternalOutput"))
        G.dbg["yT"] = T(nc.dram_tensor("d_yT", [D, TALL], BF16, kind="ExternalOutput"))
        G.dbg["convF"] = T(nc.dram_tensor("d_convF", [1280, TALL], F32, kind="ExternalOutput"))
        G.dbg["x1"] = T(nc.dram_tensor("d_x1", [TALL, D], F32, kind="ExternalOutput"))
        G.dbg["x2"] = T(nc.dram_tensor("d_x2", [TALL, D], F32, kind="ExternalOutput"))
        G.dbg["st"] = T(nc.dram_tensor("d_st", [64, 2 * 4 * 64], F32, kind="ExternalOutput"))

    A = Arena(nc)
    G.A = A
    G.psum = [T(nc.alloc_psum_tensor(f"ps{i}", [128, 512], F32), f"ps{i}") for i in range(8)]
    G.consts = A.alloc("consts", [128, NCONST, 128])
    G.csil = A.alloc("csil", [128, 2, 8, 128])
    G.colp_sb = A.alloc("colp", [128, NCOL])
    G.rowp_sb = A.alloc("rowp", [128, NROW])
    G.modc = A.alloc("modc", [128, 4, 8, 2])
    G.gb = A.alloc("gb", [128, 2, 2, 1024])
    G.eps = A.alloc("eps", [128, 1])
    G.sst = [A.alloc("sst", [64, 4, 64]) for _ in range(2)]
    G.gst = [A.alloc("gst", [64, 4, 64]) for _ in range(2)]
    G.gsb = [A.alloc("gsb", [64, 4, 64], BF16) for _ in range(2)]
    if mixers is not None:
        G.mixers = mixers
    G.dumps_on = dbg
    G.dumps = {}

    P.dma(G.consts[:], G.consts_d[:, :, :], writes=[G.consts.b])
    P.dma(G.csil[:], G.crep[:, :, :, :], writes=[G.csil.b])
    P.op("act", lambda e: e.activation(G.csil[:], G.csil[:], AF.Silu), [G.csil.b], [G.csil.b])
    P.op("dve", lambda e: e.memset(G.eps[:], LN_EPS), [], [G.eps.b])

    streams = [("ctx", 0, LC, 1), ("lat", LC, LL, 0)]
    if only is not None:
        streams = [st_ for st_ in streams if st_[0] in only]
    for l in range(nlayers):
        G.l = l
        xin = G.xs[l % 2]
        mod_phase(G, l)
        if stop_after == "mod":
            break
        phase_a(G, l, streams)
        if stop_after == "A":
            break
        prep_pass(G, l, streams)
        mix = G.mixers if hasattr(G, "mixers") else ("pool", "gmlp", "ssd", "gdn")
        for d_ in range(2):
            P.op("dve", lambda e, d_=d_: e.memset(G.sst[d_][:], 0.0), [], [G.sst[d_].b])
            P.op("dve", lambda e, d_=d_: e.memset(G.gst[d_][:], 0.0), [], [G.gst[d_].b])
            P.op("dve", lambda e, d_=d_: e.memset(G.gsb[d_][:], 0.0), [], [G.gsb[d_].b])
        for st_ in streams:
            if "pool" in mix:
                pool_mixer(G, l, st_)
            if "gmlp" in mix and "ssd" in mix:
                mg = A.mark()
                gt_ = gmlp_setup(G, l)
                ssd_mixer(G, l, st_, co_tile=gt_)
                A.reset(mg)
            else:
                if "gmlp" in mix:
                    gmlp_mixer(G, l, st_)
                if "ssd" in mix:
                    ssd_mixer(G, l, st_)
            if "gdn" in mix:
                gdn_mixer(G, l, st_)
        if stop_after == "mix":
            break
        last = (l == DEPTH - 1)
        cd_streams = [st_ for st_ in streams if not (last and st_[0] == "ctx")]
        mC = A.mark()
        hT2 = A.alloc("hT2", [128, 8, TALL], BF16)
        phase_c(G, l, cd_streams, hT2)
        if stop_after == "C":
            break
        phase_d1(G, l, cd_streams, hT2)
        A.reset(mC)
        if stop_after == "D1":
            break
        phase_d2(G, l, cd_streams, last)

    if dbg:
        P.barrier()
        m0 = A.mark()
        tlo = min(st_[1] for st_ in streams)
        thi = max(st_[1] + st_[2] for st_ in streams)
        TW = thi - tlo
        tmp = A.alloc("dbgtmp", [128, TALL])
        tb = A.alloc("dbgtb", [128, TALL], BF16)
        P.dma(G.dbg["mod"][:, :], G.modc[:].rearrange("p a k s -> p (a k s)"), reads=[G.modc.b], writes=[G.dbg["mod"].b], q="pool")
        P.dma(G.dbg["gb"][:, :], G.gb[:].rearrange("p s g d -> p (s g d)"), reads=[G.gb.b], writes=[G.dbg["gb"].b], q="pool")
        if stop_after == "A":
            for r in range(NFM // 128):
                P.dma(tmp[:, 0:TW], G.projF[r * 128:(r + 1) * 128, tlo:thi], reads=[G.projF.b], writes=[tmp.b])
                P.dma(G.dbg["projF"][r * 128:(r + 1) * 128, tlo:thi], tmp[:, 0:TW], reads=[tmp.b], writes=[G.dbg["projF"].b], q="pool")
            for r in range(tlo // 128, thi // 128):
                P.dma(tmp[:, 0:NTM], G.projT[r * 128:(r + 1) * 128, :], reads=[G.projT.b], writes=[tmp.b])
                P.dma(G.dbg["projT"][r * 128:(r + 1) * 128, :], tmp[:, 0:NTM], reads=[tmp.b], writes=[G.dbg["projT"].b], q="pool")
        if stop_after is None:
            for r in range(tlo // 128, thi // 128):
                P.dma(tmp[:, 0:D], G.x1[r * 128:(r + 1) * 128, :], reads=[G.x1.b], writes=[tmp.b])
                P.dma(G.dbg["x1"][r * 128:(r + 1) * 128, :], tmp[:, 0:D], reads=[tmp.b], writes=[G.dbg["x1"].b], q="pool")
                P.dma(tmp[:, 0:D], G.xs[nlayers % 2][r * 128:(r + 1) * 128, :], reads=[G.xs[nlayers % 2].b], writes=[tmp.b])
                P.dma(G.dbg["x2"][r * 128:(r + 1) * 128, :], tmp[:, 0:D], reads=[tmp.b], writes=[G.dbg["x2"].b], q="pool")
        if stop_after == "mix":
            for d_ in range(2):
                P.dma(G.dbg["st"][:, d_ * 256:(d_ + 1) * 256], G.sst[d_][:].rearrange("p h d -> p (h d)"), reads=[G.sst[d_].b], writes=[G.dbg["st"].b], q="pool")
            rows = dict(ssd=(0, 2), pool=(2, 4), gdn=(4, 6), gmlp=(6, 8))
            for mname in (G.mixers if hasattr(G, "mixers") else rows.keys()):
                for r in range(*rows[mname]):
                    P.dma(tb[:, 0:TW], G.yT[r * 128:(r + 1) * 128, tlo:thi], reads=[G.yT.b], writes=[tb.b])
                    P.dma(G.dbg["yT"][r * 128:(r + 1) * 128, tlo:thi], tb[:, 0:TW], reads=[tb.b], writes=[G.dbg["yT"].b], q="pool")
            for r in range(1280 // 128):
                P.dma(tmp[:, 0:TW], G.convF[r * 128:(r + 1) * 128, tlo:thi], reads=[G.convF.b], writes=[tmp.b])
                P.dma(G.dbg["convF"][r * 128:(r + 1) * 128, tlo:thi], tmp[:, 0:TW], reads=[tmp.b], writes=[G.dbg["convF"].b], q="pool")
        P.op("pool", None, reads=[v.b for v in G.dbg.values()])
        A.reset(m0)
    P.op("pool", None, reads=[G.out.b])
    P.emit()
    return nc


def mod_phase(G, l):
    nc, P, A = G.nc, G.P, G.A
    m0 = A.mark()
    P.dma(G.colp_sb[:], G.colp[l, :, :], writes=[G.colp_sb.b])
    P.dma(G.rowp_sb[:], G.rowp[l, 0:1, :].partition_broadcast(128), writes=[G.rowp_sb.b])
    wm = [A.alloc("wm", [128, 8, 512]) for _ in range(2)]
    pc = G.psum[0]
    pg = [G.psum[1], G.psum[2]]
    wsrc = G.w_mod[l, :, :].rearrange("(k p) c -> p k c", p=128)
    colvec = {0: 0, 1: 1, 3: 2, 4: 3}
    for n in range(12):
        w = wm[n % 2]
        P.dma(w[:], wsrc[:, :, n * 512:(n + 1) * 512], writes=[w.b])
        vec, half = n // 2, n % 2
        if vec in colvec:
            a = colvec[vec]
            for j in range(4):
                kc = half * 4 + j

                def f(e, w=w, j=j, a=a, kc=kc):
                    for k in range(8):
                        ins = e.matmul(pc[:, (a * 8 + kc) * 2:(a * 8 + kc) * 2 + 2], w[:, k, j * 128:(j + 1) * 128],
                                       G.csil[:, :, k, 0], start=(k == 0), stop=(k == 7))
                    return ins
                P.op("pe", f, [w.b, G.csil.b], [pc.b])
        else:
            gi = 0 if vec == 2 else 1
            boff = R_BG1 if gi == 0 else R_BG2
            for s in range(2):
                def f(e, w=w, s=s):
                    for k in range(8):
                        ins = e.matmul(pg[s][:, :], G.csil[:, s, k, :], w[:, k, :], start=(k == 0), stop=(k == 7))
                    return ins
                P.op("pe", f, [w.b, G.csil.b], [pg[s].b])
                P.op("dve", lambda e, s=s, gi=gi, half=half, boff=boff: e.tensor_tensor(
                    G.gb[:, s, gi, half * 512:(half + 1) * 512], pg[s][:, :],
                    G.rowp_sb[:, boff + half * 512: boff + (half + 1) * 512], ALU.add),
                    [pg[s].b, G.rowp_sb.b], [G.gb.b])
    bm = G.colp_sb[:, C_BMOD:C_BMOD + 48].rearrange("p (v k) -> p v k", v=6)
    for vec, a in colvec.items():
        for s in range(2):
            P.op("dve", lambda e, vec=vec, a=a, s=s: e.tensor_tensor(
                G.modc[:, a, :, s], pc[:, a * 16:(a + 1) * 16].rearrange("p (k s) -> p k s", s=2)[:, :, s],
                bm[:, vec, :], ALU.add), [pc.b, G.colp_sb.b], [G.modc.b])
    for a in (1, 3):
        P.op("dve", lambda e, a=a: e.tensor_scalar_add(G.modc[:, a, :, :], G.modc[:, a, :, :], 1.0), [G.modc.b], [G.modc.b])
    P.barrier()
    A.reset(m0)


def ln_stats(G, xt, np_, st, mv, rstd, P=None):
    P = P or G.P
    def f(e):
        e.bn_stats(st[0:np_, 0, :], xt[0:np_, 0:512])
        return e.bn_stats(st[0:np_, 1, :], xt[0:np_, 512:1024])
    P.op("dve", f, [xt.b], [st.b])
    P.op("dve", lambda e: e.bn_aggr(mv[0:np_, :], st[0:np_, :, :].rearrange("p a b -> p (a b)")), [st.b], [mv.b])
    P.op("act", lambda e: e.activation(rstd[0:np_, :], mv[0:np_, 1:2], AF.Sqrt, bias=G.eps[0:np_, :]), [mv.b, G.eps.b], [rstd.b])
    P.op("dve", lambda e: e.reciprocal(rstd[0:np_, :], rstd[0:np_, :]), [rstd.b], [rstd.b])


def load_weight_bf16(G, dst, src_rows, ncols, stage, nk):
    P = G.P
    for k in range(nk):
        s = stage[k % len(stage)]
        P.dma(s[:, 0:ncols], src_rows(k), writes=[s.b])
        P.op("pool", lambda e, s=s, k=k: e.tensor_copy(dst[:, k, 0:ncols], s[:, 0:ncols]), [s.b], [dst.b])


def phase_a(G, l, streams):
    nc, P, A = G.nc, G.P, G.A
    m0 = A.mark()
    wi = A.alloc("wi", [128, 8, NIN], BF16)
    stage = [A.alloc("wstage", [128, NIN]) for _ in range(2)]
    load_weight_bf16(G, wi, lambda k: G.w_in[l, k * 128:(k + 1) * 128, :], NIN, stage, 8)
    xt = [A.alloc("xt", [128, D]) for _ in range(2)]
    xn = [A.alloc("xn", [128, D]) for _ in range(4)]
    st = [A.alloc("st", [128, 2, 6]) for _ in range(2)]
    mv = [A.alloc("mv", [128, 2]) for _ in range(2)]
    rstd = [A.alloc("rstd", [128, 1]) for _ in range(2)]
    hT = [A.alloc("hT", [128, 8, 512], BF16) for _ in range(2)]
    oF = [A.alloc("oF", [128, 512]) for _ in range(3)]
    oT = [A.alloc("oT", [128, NTM]) for _ in range(2)]
    pT = [G.psum[0], G.psum[1]]
    pF = [G.psum[2], G.psum[3]]
    pTa = [G.psum[4], G.psum[5]]
    pTb = [G.psum[6], G.psum[7]]
    ident = G.consts[:, K_ID, :]
    cnt = dict(t=0, b=0, f=0, o=0)
    xsrc = G.xs[l % 2]
    for (name, tok0, L, s) in streams:
        bs = 512 if L >= 512 else L
        for b0 in range(0, L, bs):
            nt = bs // 128
            W = bs
            h = hT[cnt["b"] % 2]
            cnt["b"] += 1
            for m in range(nt):
                t = xt[cnt["t"] % 2]
                q = cnt["t"] % 2
                cnt["t"] += 1
                r0 = b0 + m * 128
                if l == 0:
                    src = (G.ctx_in if name == "ctx" else G.x_in)[r0:r0 + 128, :]
                    P.dma(t[:], src, writes=[t.b])
                else:
                    P.dma(t[:], xsrc[tok0 + r0: tok0 + r0 + 128, :], reads=[xsrc.b], writes=[t.b])
                ln_stats(G, t, 128, st[q], mv[q], rstd[q])
                P.op("dve", lambda e, t=t, q=q, m=m: e.tensor_scalar(xn[m][:], t[:], mv[q][:, 0:1], rstd[q][:, 0:1],
                                                                    ALU.subtract, ALU.mult), [t.b, mv[q].b, rstd[q].b], [xn[m].b])
            for k in range(8):
                p = pT[k % 2]

                def f(e, p=p, k=k, nt=nt):
                    for m in range(nt):
                        ins = e.transpose(p[:, m * 128:(m + 1) * 128], xn[m][:, k * 128:(k + 1) * 128], ident)
                    return ins
                P.op("pe", f, [xn[m].b for m in range(nt)] + [G.consts.b], [p.b])
                P.op("act", lambda e, p=p, k=k, h=h, W=W, s=s: e.activation(
                    h[:, k, 0:W], p[:, 0:W], AF.Identity, bias=G.modc[:, 0, k, s:s + 1], scale=G.modc[:, 1, k, s:s + 1]),
                    [p.b, G.modc.b], [h.b])
            row = 0
            for (c0, n) in FM_GROUPS:
                for j in range(n // 128):
                    p = pF[cnt["f"] % 2]
                    o = oF[cnt["f"] % 3]
                    ev = "act" if cnt["f"] % 2 == 0 else "dve"
                    cnt["f"] += 1
                    cc = c0 + j * 128

                    def f(e, p=p, cc=cc, h=h, W=W):
                        for k in range(8):
                            ins = e.matmul(p[:, 0:W], wi[:, k, cc:cc + 128], h[:, k, 0:W], start=(k == 0), stop=(k == 7))
                        return ins
                    P.op("pe", f, [wi.b, h.b], [p.b])
                    if ev == "act":
                        P.op("act", lambda e, p=p, o=o, W=W: e.activation(o[:, 0:W], p[:, 0:W], AF.Copy), [p.b], [o.b])
                    else:
                        P.op("dve", lambda e, p=p, o=o, W=W: e.tensor_copy(o[:, 0:W], p[:, 0:W]), [p.b], [o.b])
                    P.dma(G.projF[row:row + 128, tok0 + b0: tok0 + b0 + W], o[:, 0:W], reads=[o.b], writes=[G.projF.b],
                          q=("act" if ev == "act" else "pool"))
                    row += 128
            for m in range(nt):
                pa = pTa[cnt["o"] % 2]
                pb = pTb[cnt["o"] % 2]
                o = oT[cnt["o"] % 2]
                cnt["o"] += 1

                def f(e, pa=pa, pb=pb, h=h, m=m):
                    for k in range(8):
                        lw = h[:, k, m * 128:(m + 1) * 128]
                        e.matmul(pa[:, 0:256], lw, wi[:, k, O_Z:O_Z + 256], start=(k == 0), stop=(k == 7))
                        e.matmul(pb[:, 0:272], lw, wi[:, k, O_GATE:O_GATE + 272], start=(k == 0), stop=(k == 7))
                    for k in range(8):
                        lw = h[:, k, m * 128:(m + 1) * 128]
                        ins = e.matmul(pa[:, 256:264], lw, wi[:, k, O_DT:O_DT + 8], start=(k == 0), stop=(k == 7))
                    return ins
                P.op("pe", f, [wi.b, h.b], [pa.b, pb.b])
                P.op("act", lambda e, pa=pa, o=o: e.activation(o[:, 0:264], pa[:, 0:264], AF.Copy), [pa.b], [o.b])
                P.op("dve", lambda e, pb=pb, o=o: e.tensor_copy(o[:, 264:536], pb[:, 0:272]), [pb.b], [o.b])
                r0 = tok0 + b0 + m * 128
                P.dma(G.projT[r0:r0 + 128, :], o[:], reads=[o.b], writes=[G.projT.b], q="pool")
    P.barrier()
    A.reset(m0)


def dbgdump(G, name, t, ap, shape, dt=F32, P=None):
    if not getattr(G, "dumps_on", False) or name in G.dumps:
        return
    o = T(G.nc.dram_tensor("dd_" + name, list(shape), dt, kind="ExternalOutput"))
    G.dumps[name] = o
    nd = len(shape)
    P = P or G.P
    P.dma(o[tuple(slice(None) for _ in range(nd))], ap, reads=[t.b], writes=[o.b], q="pool")
    P.op("pool", None, reads=[o.b])


def bc_ap(ap, dims):
    return bass.AP(ap.tensor, ap.offset, [list(ap.ap[0])] + [list(d) for d in dims])


def prep_pass(G, l, streams):
    P, A = G.P, G.A
    m0 = A.mark()
    xin = [A.alloc("cin", [128, LL + 6]) for _ in range(2)]
    acc = [A.alloc("cacc", [128, LL]) for _ in range(2)]
    sq = [A.alloc("csq", [128, 512]) for _ in range(2)]
    rn = [A.alloc("crn", [128, 512]) for _ in range(2)]
    ps = [G.psum[0], G.psum[1]]
    bo2 = G.consts[:, K_BO2, :]
    chunks = []
    for j in range(4):
        chunks.append((j * 128, j * 128, C_SSDCW + j * 7, C_SSDCB + j, "p"))
    for j in range(6):
        chunks.append((768 + j * 128, 512 + j * 128, C_GDNCW + j * 7, None, "q" if j < 2 else ("k" if j < 4 else "p")))
    cnt = 0
    sc = 0
    for (name, tok0, L, s) in streams:
        for t in xin:
            P.op("dve", lambda e, t=t: e.memset(t[:, 0:3], 0.0), [], [t.b])
            P.op("dve", lambda e, t=t, L=L: e.memset(t[:, L + 3:L + 6], 0.0), [], [t.b])
        for (src, dst, wo, bo, kind) in chunks:
            t = xin[cnt % 2]
            a = acc[cnt % 2]
            cnt += 1
            w = G.colp_sb
            P.dma(t[:, 3:L + 3], G.projF[src:src + 128, tok0:tok0 + L], reads=[G.projF.b], writes=[t.b])
            P.op("dve", lambda e, t=t, a=a, L=L, wo=wo: e.tensor_scalar(a[:, 0:L], t[:, 0:L], w[:, wo:wo + 1], None, ALU.mult),
                 [t.b, w.b], [a.b])
            for tap in range(1, 7):
                P.op("dve", lambda e, t=t, a=a, L=L, wo=wo, tap=tap: e.scalar_tensor_tensor(
                    a[:, 0:L], t[:, tap:tap + L], w[:, wo + tap:wo + tap + 1], a[:, 0:L], ALU.mult, ALU.add),
                    [t.b, w.b, a.b], [a.b])
            if bo is not None:
                P.op("act", lambda e, a=a, L=L, bo=bo: e.activation(a[:, 0:L], a[:, 0:L], AF.Silu, bias=w[:, bo:bo + 1]),
                     [a.b, w.b], [a.b])
            else:
                P.op("act", lambda e, a=a, L=L: e.activation(a[:, 0:L], a[:, 0:L], AF.Silu), [a.b], [a.b])
            if kind in ("q", "k"):
                for sl in range(0, L, 512):
                    Wd = min(512, L - sl)
                    q_, r_, p_ = sq[sc % 2], rn[sc % 2], ps[sc % 2]
                    sc += 1
                    P.op("act", lambda e, a=a, q_=q_, sl=sl, Wd=Wd: e.activation(q_[:, 0:Wd], a[:, sl:sl + Wd], AF.Square), [a.b], [q_.b])
                    P.op("pe", lambda e, q_=q_, p_=p_, Wd=Wd: e.matmul(p_[:, 0:Wd], bo2, q_[:, 0:Wd], start=True, stop=True),
                         [q_.b, G.consts.b], [p_.b])
                    P.op("act", lambda e, r_=r_, p_=p_, Wd=Wd: e.activation(r_[:, 0:Wd], p_[:, 0:Wd], AF.Sqrt, bias=G.eps[:, :]),
                         [p_.b, G.eps.b], [r_.b])
                    P.op("dve", lambda e, r_=r_, Wd=Wd: e.reciprocal(r_[:, 0:Wd], r_[:, 0:Wd]), [r_.b], [r_.b])
                    if kind == "q":
                        P.op("dve", lambda e, a=a, r_=r_, sl=sl, Wd=Wd: e.scalar_tensor_tensor(
                            a[:, sl:sl + Wd], a[:, sl:sl + Wd], 0.125, r_[:, 0:Wd], ALU.mult, ALU.mult), [a.b, r_.b], [a.b])
                    else:
                        P.op("dve", lambda e, a=a, r_=r_, sl=sl, Wd=Wd: e.tensor_tensor(
                            a[:, sl:sl + Wd], a[:, sl:sl + Wd], r_[:, 0:Wd], ALU.mult), [a.b, r_.b], [a.b])
            P.dma(G.convF[dst:dst + 128, tok0:tok0 + L], a[:, 0:L], reads=[a.b], writes=[G.convF.b], q="pool")
    P.barrier()
    A.reset(m0)


def pool_mixer(G, l, stream):
    P, A = G.P, G.A
    name, tok0, L, s = stream
    m0 = A.mark()
    RW = 64 if name == "lat" else L
    NR = L // RW
    PW = RW + 16
    F = NR * PW
    xp = A.alloc("pxp", [128, NR, PW])
    ca = A.alloc("pca", [128, NR, PW])
    cb = A.alloc("pcb", [128, NR, PW])
    tmp = A.alloc("ptmp", [128, NR, RW])
    pooled = A.alloc("ppool", [128, NR, RW], BF16)
    pinv = A.alloc("pinv", [128, 2, RW])
    pwf = A.alloc("pwf", [128, 2, 128])
    pwb = A.alloc("pwb", [128, 2, 128], BF16)
    yo = [A.alloc("pyo", [128, 512], BF16) for _ in range(2)]
    ps = [G.psum[2], G.psum[3]]
    P.dma(pinv[:], (G.pinv_g if name == "lat" else G.pinv_c)[:, :, :], writes=[pinv.b])
    P.dma(pwf[:], G.poolw[l, :, :, :], writes=[pwf.b])
    P.op("pool", lambda e: e.tensor_copy(pwb[:], pwf[:]), [pwf.b], [pwb.b])
    fl = lambda t: t[:].rearrange("p r w -> p (r w)")
    cnt = 0
    for jc in range(2):
        P.op("dve", lambda e: e.memset(xp[:], 0.0), [], [xp.b])
        P.dma(xp[:, :, 8:8 + RW], G.projF[512 + jc * 128:512 + (jc + 1) * 128, tok0:tok0 + L].rearrange("p (r w) -> p r w", w=RW),
              reads=[G.projF.b], writes=[xp.b])
        xf, af, bf = fl(xp), fl(ca), fl(cb)
        P.op("dve", lambda e: e.tensor_tensor(af[:, 0:F - 1], xf[:, 0:F - 1], xf[:, 1:F], ALU.add), [xp.b], [ca.b])
        if jc == 0:
            P.op("dve", lambda e: e.tensor_tensor(bf[64:128, 0:F - 3], af[64:128, 0:F - 3], af[64:128, 2:F - 1], ALU.add), [ca.b], [cb.b])
            srcs = [(0, 64, 2, ca), (64, 128, 4, cb)]
        else:
            P.op("dve", lambda e: e.tensor_tensor(bf[:, 0:F - 3], af[:, 0:F - 3], af[:, 2:F - 1], ALU.add), [ca.b], [cb.b])
            P.op("dve", lambda e: e.tensor_tensor(af[:, 0:F - 7], bf[:, 0:F - 7], bf[:, 4:F - 3], ALU.add), [cb.b, ca.b], [ca.b])
            P.op("dve", lambda e: e.tensor_tensor(bf[64:128, 0:F - 15], af[64:128, 0:F - 15], af[64:128, 8:F - 7], ALU.add), [ca.b, cb.b], [cb.b])
            srcs = [(0, 64, 8, ca), (64, 128, 16, cb)]
        for (lo, hi, w, cw) in srcs:
            o = 8 - w // 2
            P.op("dve", lambda e, lo=lo, hi=hi, cw=cw, o=o, jc=jc: e.tensor_tensor(
                tmp[lo:hi, :, :], cw[lo:hi, :, o:o + RW], bc_ap(pinv[lo:hi, jc, :], [[0, NR], [1, RW]]), ALU.mult),
                [cw.b, pinv.b], [tmp.b])
            P.op("dve", lambda e, lo=lo, hi=hi: e.tensor_tensor(pooled[lo:hi, :, :], tmp[lo:hi, :, :], xp[lo:hi, :, 8:8 + RW], ALU.subtract),
                 [tmp.b, xp.b], [pooled.b])
        pf = pooled[:].rearrange("p r w -> p (r w)")
        for sl in range(0, L, 512):
            Wd = min(512, L - sl)
            p_, y_ = ps[cnt % 2], yo[cnt % 2]
            cnt += 1
            P.op("pe", lambda e, p_=p_, sl=sl, Wd=Wd, jc=jc: e.matmul(p_[:, 0:Wd], pwb[:, jc, :], pf[:, sl:sl + Wd], start=True, stop=True),
                 [pwb.b, pooled.b], [p_.b])
            P.op("act", lambda e, p_=p_, y_=y_, Wd=Wd, jc=jc: e.activation(
                y_[:, 0:Wd], p_[:, 0:Wd], AF.Copy, scale=G.colp_sb[:, C_PSCALE + jc:C_PSCALE + jc + 1]), [p_.b, G.colp_sb.b], [y_.b])
            P.dma(G.yT[256 + jc * 128:256 + (jc + 1) * 128, tok0 + sl:tok0 + sl + Wd], y_[:, 0:Wd], reads=[y_.b], writes=[G.yT.b], q="act")
    P.barrier()
    A.reset(m0)


def gmlp_setup(G, l):
    P, A = G.P, G.A
    NB = 2
    uv = [A.alloc("guv", [128, 4, 128]) for _ in range(NB)]
    gt = [A.alloc("ggt", [128, 4, 128]) for _ in range(NB)]
    vt = [A.alloc("gvt", [128, 256]) for _ in range(NB)]
    vb = [A.alloc("gvb", [128, 256], BF16) for _ in range(NB)]
    st = [A.alloc("gst_", [128, 6]) for _ in range(NB)]
    mv = [A.alloc("gmv", [128, 2]) for _ in range(NB)]
    rs = [A.alloc("grs", [128, 1]) for _ in range(NB)]
    tm = [A.alloc("gtm", [128, 2, 128]) for _ in range(NB)]
    yo = [A.alloc("gyo", [128, 2, 128], BF16) for _ in range(NB)]
    wsf = A.alloc("gwsf", [128, 4, 128])
    wsb = A.alloc("gwsb", [128, 4, 128], BF16)
    P.dma(wsf[:], G.gws[l, :, :, :], writes=[wsf.b])
    P.op("pool", lambda e: e.tensor_copy(wsb[:], wsf[:]), [wsf.b], [wsb.b])
    pT = [G.psum[4], G.psum[5]]
    pq = [G.psum[6], G.psum[7]]
    ident = G.consts[:, K_ID, :]
    rp = G.rowp_sb
    def tile(R, stream, ti):
        name, tok0, L, s = stream
        i = ti % NB
        u, g, v, vbb, p1, p2, tt, y = uv[i], gt[i], vt[i], vb[i], pT[i], pq[i], tm[i], yo[i]
        t0 = tok0 + ti * 128
        R.dma(u[:], G.projF[1536:2048, t0:t0 + 128].rearrange("(j p) t -> p j t", p=128), reads=[G.projF.b], writes=[u.b])
        uf = u[:].rearrange("p j t -> p (j t)")
        gf = g[:].rearrange("p j t -> p (j t)")
        R.op("dve", lambda e, uf=uf, gf=gf: e.tensor_tensor(gf, uf, uf, ALU.mult), [u.b], [g.b])
        R.op("dve", lambda e, gf=gf: e.tensor_scalar(gf, gf, 0.044715, 1.0, ALU.mult, ALU.add), [g.b], [g.b])
        R.op("dve", lambda e, uf=uf, gf=gf: e.tensor_tensor(gf, gf, uf, ALU.mult), [g.b, u.b], [g.b])
        R.op("act", lambda e, gf=gf: e.activation(gf, gf, AF.Sigmoid, scale=1.5957691216057308), [g.b], [g.b])
        R.op("dve", lambda e, uf=uf, gf=gf: e.tensor_tensor(gf, gf, uf, ALU.mult), [g.b, u.b], [g.b])

        def f(e, g=g, p1=p1):
            e.transpose(p1[:, 0:128], g[:, 2, :], ident)
            return e.transpose(p1[:, 128:256], g[:, 3, :], ident)
        R.op("pe", f, [g.b, G.consts.b], [p1.b])
        R.op("act", lambda e, v=v, p1=p1: e.activation(v[:], p1[:, 0:256], AF.Copy), [p1.b], [v.b])
        R.op("dve", lambda e, v=v, i=i: e.bn_stats(st[i][:], v[:]), [v.b], [st[i].b])
        R.op("dve", lambda e, i=i: e.bn_aggr(mv[i][:], st[i][:]), [st[i].b], [mv[i].b])
        R.op("act", lambda e, i=i: e.activation(rs[i][:], mv[i][:, 1:2], AF.Sqrt, bias=G.eps[:, :]), [mv[i].b, G.eps.b], [rs[i].b])
        R.op("dve", lambda e, i=i: e.reciprocal(rs[i][:], rs[i][:]), [rs[i].b], [rs[i].b])
        R.op("dve", lambda e, v=v, i=i: e.tensor_scalar(v[:], v[:], mv[i][:, 0:1], rs[i][:, 0:1], ALU.subtract, ALU.mult),
             [v.b, mv[i].b, rs[i].b], [v.b])
        R.op("dve", lambda e, v=v: e.tensor_tensor(v[:], v[:], rp[:, R_GLNW:R_GLNW + 256], ALU.mult), [v.b, rp.b], [v.b])
        R.op("dve", lambda e, v=v, vbb=vbb: e.tensor_tensor(vbb[:], v[:], rp[:, R_GLNB:R_GLNB + 256], ALU.add), [v.b, rp.b], [vbb.b])

        def f2(e, vbb=vbb, p2=p2):
            e.matmul(p2[:, 0:256], vbb[:, 0:128], wsb[:, 0:2, :].rearrange("p g i -> p (g i)"), start=True, stop=True)
            return e.matmul(p2[:, 256:512], vbb[:, 128:256], wsb[:, 2:4, :].rearrange("p g i -> p (g i)"), start=True, stop=True)
        R.op("pe", f2, [vbb.b, wsb.b], [p2.b])
        for q in range(2):
            for hh in range(2):
                gg = 2 * q + hh
                lo, hi = hh * 64, (hh + 1) * 64
                c0 = q * 256 + hh * 128
                R.op("dve", lambda e, lo=lo, hi=hi, c0=c0, q=q, gg=gg, tt=tt, p2=p2: e.tensor_tensor(
                    tt[lo:hi, q, :], p2[lo:hi, c0:c0 + 128], rp[lo:hi, R_GBS + gg * 128:R_GBS + (gg + 1) * 128], ALU.add),
                    [p2.b, rp.b], [tt.b])
                R.op("dve", lambda e, lo=lo, hi=hi, q=q, tt=tt, y=y, g=g: e.tensor_tensor(
                    y[lo:hi, q, :], tt[lo:hi, q, :], g[lo:hi, q, :], ALU.mult), [tt.b, g.b], [y.b])
        R.dma(G.yT[768:1024, t0:t0 + 128].rearrange("(j p) t -> p j t", p=128), y[:], reads=[y.b], writes=[G.yT.b], q="pool")
    return tile


def gmlp_mixer(G, l, stream):
    P, A = G.P, G.A
    m0 = A.mark()
    tile = gmlp_setup(G, l)
    for ti in range(stream[2] // 128):
        r = Rec()
        tile(r, stream, ti)
        merge(P, [r])
    P.barrier()
    A.reset(m0)


def softplus_small(P, out, x, tmp, reads, extra_w=()):
    P.op("dve", lambda e: e.scalar_tensor_tensor(tmp, x, -1.0, x, ALU.mult, ALU.max), reads, [extra_w[0]])
    P.op("act", lambda e: e.activation(tmp, tmp, AF.Exp, scale=-1.0), [extra_w[0]], [extra_w[0]])
    P.op("act", lambda e: e.activation(tmp, tmp, AF.Ln, bias=1.0), [extra_w[0]], [extra_w[0]])
    P.op("dve", lambda e: e.scalar_tensor_tensor(out, x, 0.0, tmp, ALU.max, ALU.add), reads + [extra_w[0]], [extra_w[1]])


def ssd_mixer(G, l, stream, co_tile=None):
    P, A = G.P, G.A
    name, tok0, L, s = stream
    NT = L // 128
    m0 = A.mark()
    rp = G.rowp_sb
    C = G.consts
    ident = C[:, K_ID, :]
    ones = C[:, K_ONES, :]
    nega = A.alloc("snega", [128, 8])
    negones = A.alloc("snegones", [128, 128])
    yacc = A.alloc("syacc", [128, NT, 256])
    P.op("act", lambda e: e.activation(nega[:], rp[:, R_SALOG:R_SALOG + 8], AF.Exp), [rp.b], [nega.b])
    P.op("dve", lambda e: e.tensor_scalar(nega[:], nega[:], -1.0, None, ALU.mult), [nega.b], [nega.b])
    P.op("dve", lambda e: e.memset(negones[:], -1.0), [], [negones.b])
    NB = 2
    def mk(nm, shape, dt=F32):
        return [A.alloc(nm, shape, dt) for _ in range(NB)]
    zt, cx, dtt, dtm, dta = mk("szt", [128, 264]) + mk("szt", [128, 264]), mk("scx", [128, 2, 128]) + mk("scx", [128, 2, 128]), mk("sdt", [128, 8]), mk("sdtm", [128, 8]), mk("sdta", [128, 8])
    bc4 = mk("sbc4", [64, 4, 128]) + mk("sbc4", [64, 4, 128])
    xtm, btm, cbb = mk("sxtm", [128, 256]), mk("sbtm", [128, 128], BF16), mk("scbb", [64, 4, 128], BF16)
    ML, Dm, Wm = mk("sML", [128, 4, 128]), mk("sDm", [128, 4, 128]), mk("sWm", [128, 4, 128], BF16)
    scl, dtw = mk("sscl", [128, 12]), mk("sdtw", [128, 8])
    xdt, xdw = mk("sxdt", [128, 256], BF16), mk("sxdw", [128, 256], BF16)
    yt, yg, ss, yo = mk("syt", [128, 256]), mk("syg", [128, 256]), mk("sss", [128, 2]), mk("syo", [128, 2, 128], BF16)
    ps = G.psum
    def body(R, d, ti, i, li, second):
        Ltri = C[:, K_LF if d == 0 else K_LB, :]
        nmask = C[:, K_NMF if d == 0 else K_NMB, :]
        t0 = tok0 + ti * 128
        z_, c_, dt_, dm_, da_, x_, b_, cb_ = zt[li], cx[li], dtt[i], dtm[i], dta[i], xtm[i], btm[i], cbb[i]
        bc_ = bc4[li]
        ml_, D_, W_, sc_, dw_, xd_, xw_ = ML[i], Dm[i], Wm[i], scl[i], dtw[i], xdt[i], xdw[i]
        pA, pB = ps[2 * i], ps[2 * i + 1]
        pC = pD = pA
        R.dma(z_[:], G.projT[t0:t0 + 128, 0:264], reads=[G.projT.b], writes=[z_.b])
        R.dma(c_[:], G.convF[0:256, t0:t0 + 128].rearrange("(j p) t -> p j t", p=128), reads=[G.convF.b], writes=[c_.b])
        R.dma(bc_[:], G.convF[256:512, t0:t0 + 128].rearrange("(j p) t -> p j t", p=64), reads=[G.convF.b], writes=[bc_.b])
        R.op("dve", lambda e, z_=z_, dt_=dt_: e.tensor_tensor(dt_[:], z_[:, 256:264], rp[:, R_SDTB:R_SDTB + 8], ALU.add), [z_.b, rp.b], [dt_.b])
        softplus_small(R, dt_[:], dt_[:], dm_[:], [dt_.b], (dm_.b, dt_.b))
        R.op("dve", lambda e, dt_=dt_, da_=da_: e.tensor_tensor(da_[:], dt_[:], nega[:], ALU.mult), [dt_.b, nega.b], [da_.b])
        if SSD_STOP <= 1:
            return
        def f(e, c_=c_, pA=pA, bc_=bc_):
            e.transpose(pA[:, 0:128], c_[:, 0, :], ident)
            e.transpose(pA[:, 128:256], c_[:, 1, :], ident)
            e.transpose(pA[:, 256:320], bc_[:, 0, :], ident[0:64, 0:64])
            return e.transpose(pA[:, 320:384], bc_[:, 1, :], ident[0:64, 0:64])
        R.op("pe", f, [c_.b, bc_.b, C.b], [pA.b])
        R.op("act", lambda e, x_=x_, pA=pA: e.activation(x_[:], pA[:, 0:256], AF.Copy), [pA.b], [x_.b])
        R.op("act", lambda e, b_=b_, pA=pA: e.activation(b_[:], pA[:, 256:384], AF.Copy), [pA.b], [b_.b])
        R.op("pool", lambda e, cb_=cb_, bc_=bc_: e.tensor_copy(cb_[:], bc_[:]), [bc_.b], [cb_.b])
        if SSD_STOP <= 2:
            return
        def f(e, cb_=cb_, pB=pB):
            e.matmul(pB[:, 0:128], cb_[:, 0, :], cb_[:, 2, :], start=True, stop=True)
            return e.matmul(pB[:, 128:256], cb_[:, 1, :], cb_[:, 3, :], start=True, stop=True)
        R.op("pe", f, [cb_.b], [pB.b])
        if SSD_STOP <= 3:
            return
        R.op("dve", lambda e, ml_=ml_, da_=da_, Ltri=Ltri: e.tensor_tensor(
            ml_[:], bc_ap(Ltri, [[0, 4], [1, 128]]), bc_ap(da_[:, d * 4:d * 4 + 4], [[1, 4], [0, 128]]), ALU.mult), [da_.b, C.b], [ml_.b])
        def f(e, ml_=ml_, pC=pC, nmask=nmask):
            e.matmul(pC[:, :], ones, ml_[:].rearrange("p h i -> p (h i)"), start=True, stop=False)
            for h in range(4):
                e.matmul(pC[:, h * 128:(h + 1) * 128], ml_[:, h, :], negones[:], start=False, stop=False)
            return e.matmul(pC[:, :], ident, bc_ap(nmask, [[0, 4], [1, 128]]), start=False, stop=True)
        R.op("pe", f, [ml_.b, C.b, negones.b], [pC.b])
        R.op("act", lambda e, D_=D_, pC=pC: e.activation(D_[:].rearrange("p h i -> p (h i)"), pC[:, :], AF.Exp), [pC.b], [D_.b])
        if SSD_STOP <= 4:
            return
        def f(e, da_=da_, pD=pD, Ltri=Ltri):
            e.matmul(pD[:, 0:4], Ltri, da_[:, d * 4:d * 4 + 4], start=True, stop=True)
            return e.matmul(pD[:, 4:8], ones, da_[:, d * 4:d * 4 + 4], start=True, stop=True)
        R.op("pe", f, [da_.b, C.b], [pD.b])
        R.op("dve", lambda e, sc_=sc_, pD=pD: e.tensor_copy(sc_[:, 0:8], pD[:, 0:8]), [pD.b], [sc_.b])
        R.op("dve", lambda e, sc_=sc_: e.tensor_copy(sc_[:, 8:12], sc_[:, 4:8]), [sc_.b], [sc_.b])
        R.op("dve", lambda e, sc_=sc_: e.tensor_tensor(sc_[:, 4:8], sc_[:, 4:8], sc_[:, 0:4], ALU.subtract), [sc_.b], [sc_.b])
        R.op("act", lambda e, sc_=sc_: e.activation(sc_[:], sc_[:], AF.Exp), [sc_.b], [sc_.b])
        if SSD_STOP <= 5:
            return
        R.op("dve", lambda e, W_=W_, D_=D_, pB=pB: e.tensor_tensor(
            W_[:].rearrange("p (g r) i -> p g r i", g=2), bc_ap(pB[:, 0:256], [[128, 2], [0, 2], [1, 128]]),
            D_[:].rearrange("p (g r) i -> p g r i", g=2), ALU.mult), [pB.b, D_.b], [W_.b])
        if SSD_STOP <= 6:
            return
        R.op("dve", lambda e, dw_=dw_, dt_=dt_, sc_=sc_: e.tensor_tensor(dw_[:, 0:4], dt_[:, d * 4:d * 4 + 4], sc_[:, 4:8], ALU.mult),
             [dt_.b, sc_.b], [dw_.b])
        R.op("dve", lambda e, xd_=xd_, x_=x_, dt_=dt_: e.tensor_tensor(
            xd_[:].rearrange("p (h q) -> p h q", h=4), x_[:].rearrange("p (h q) -> p h q", h=4),
            bc_ap(dt_[:, d * 4:d * 4 + 4], [[1, 4], [0, 64]]), ALU.mult), [x_.b, dt_.b], [xd_.b])
        R.op("dve", lambda e, xw_=xw_, x_=x_, dw_=dw_: e.tensor_tensor(
            xw_[:].rearrange("p (h q) -> p h q", h=4), x_[:].rearrange("p (h q) -> p h q", h=4),
            bc_ap(dw_[:, 0:4], [[1, 4], [0, 64]]), ALU.mult), [x_.b, dw_.b], [xw_.b])
        if SSD_STOP <= 7:
            return
        def f(e, W_=W_, xd_=xd_, pA=pA):
            for h in range(4):
                ins = e.matmul(pA[:, h * 64:(h + 1) * 64], W_[:, h, :], xd_[:, h * 64:(h + 1) * 64], start=True, stop=True)
            return ins
        R.op("pe", f, [W_.b, xd_.b], [pA.b])
        def f(e, bc_=bc_, pB=pB):
            for h in range(4):
                g = h // 2
                ins = e.matmul(pB[:, 256 + h * 64:256 + (h + 1) * 64], bc_[:, 2 + g, :],
                               G.sst[d][:, h, :], start=True, stop=True)
            return ins
        R.op("pe", f, [bc_.b, G.sst[d].b], [pB.b])
        def f(e, b_=b_, xw_=xw_, pD=pD):
            for h in range(4):
                g = h // 2
                ins = e.matmul(pD[0:64, 256 + h * 64:256 + (h + 1) * 64], b_[:, g * 64:(g + 1) * 64], xw_[:, h * 64:(h + 1) * 64], start=True, stop=True)
            return ins
        R.op("pe", f, [b_.b, xw_.b], [pD.b])
        if SSD_STOP <= 8:
            return
        for h in range(4):
            lo, hi = 0, 64
            R.op("dve", lambda e, lo=lo, hi=hi, h=h, sc_=sc_, pD=pD: e.scalar_tensor_tensor(
                G.sst[d][lo:hi, h, :], G.sst[d][lo:hi, h, :], sc_[lo:hi, 8 + h:9 + h], pD[lo:hi, 256 + h * 64:256 + (h + 1) * 64],
                ALU.mult, ALU.add), [G.sst[d].b, sc_.b, pD.b], [G.sst[d].b])
        if SSD_STOP <= 9:
            return
        y_ = yt[i]
        R.op("dve", lambda e, y_=y_, pB=pB, sc_=sc_: e.tensor_tensor(
            y_[:].rearrange("p (h q) -> p h q", h=4), pB[:, 256:512].rearrange("p (h q) -> p h q", h=4),
            bc_ap(sc_[:, 0:4], [[1, 4], [0, 64]]), ALU.mult), [pB.b, sc_.b], [y_.b])
        R.op("dve", lambda e, y_=y_, pA=pA: e.tensor_tensor(y_[:], y_[:], pA[:, 0:256], ALU.add), [y_.b, pA.b], [y_.b])
        if name == "ctx" and ti == 0:
            dbgdump(G, f"dt{d}", dt_, dt_[:], [128, 8], P=R)
            dbgdump(G, f"da{d}", da_, da_[:], [128, 8], P=R)
            dbgdump(G, f"x{d}", x_, x_[:], [128, 256], P=R)
            dbgdump(G, f"D{d}", D_, D_[:].rearrange("p h i -> p (h i)"), [128, 512], P=R)
            dbgdump(G, f"W{d}", W_, W_[:].rearrange("p h i -> p (h i)"), [128, 512], BF16, P=R)
            dbgdump(G, f"sc{d}", sc_, sc_[:], [128, 12], P=R)
            dbgdump(G, f"xd{d}", xd_, xd_[:], [128, 256], BF16, P=R)
            dbgdump(G, f"y{d}", y_, y_[:], [128, 256], P=R)
        if not second:
            R.op("dve", lambda e, x_=x_, ti=ti: e.tensor_tensor(yacc[:, ti, :], x_[:], rp[:, R_SSDD:R_SSDD + 256], ALU.mult),
                 [x_.b, rp.b], [yacc.b])
            R.op("dve", lambda e, y_=y_, ti=ti: e.tensor_tensor(yacc[:, ti, :], yacc[:, ti, :], y_[:], ALU.add), [yacc.b, y_.b], [yacc.b])
        else:
            g_, s_, o_ = yg[i], ss[i], yo[i]
            R.op("dve", lambda e, y_=y_, ti=ti: e.tensor_tensor(y_[:], y_[:], yacc[:, ti, :], ALU.add), [yacc.b, y_.b], [y_.b])
            R.op("act", lambda e, g_=g_, z_=z_: e.activation(g_[:], z_[:, 0:256], AF.Silu), [z_.b], [g_.b])
            R.op("dve", lambda e, g_=g_, y_=y_: e.tensor_tensor(g_[:], g_[:], y_[:], ALU.mult), [g_.b, y_.b], [g_.b])
            R.op("act", lambda e, g_=g_, y_=y_, s_=s_: e.activation(y_[:], g_[:], AF.Square, accum_out=s_[:, 0:1]), [g_.b], [y_.b, s_.b])
            R.op("act", lambda e, s_=s_: e.activation(s_[:, 1:2], s_[:, 0:1], AF.Sqrt, bias=G.eps[:, :], scale=1.0 / 256.0), [s_.b, G.eps.b], [s_.b])
            R.op("dve", lambda e, s_=s_: e.reciprocal(s_[:, 1:2], s_[:, 1:2]), [s_.b], [s_.b])
            R.op("dve", lambda e, g_=g_, s_=s_: e.scalar_tensor_tensor(
                g_[:], g_[:], s_[:, 1:2], rp[:, R_SSDNW:R_SSDNW + 256], ALU.mult, ALU.mult), [g_.b, s_.b, rp.b], [g_.b])
            def f(e, g_=g_, pC=pC):
                e.transpose(pC[:, 0:128], g_[:, 0:128], ident)
                return e.transpose(pC[:, 128:256], g_[:, 128:256], ident)
            R.op("pe", f, [g_.b, C.b], [pC.b])
            R.op("act", lambda e, o_=o_, pC=pC: e.activation(o_[:].rearrange("p j t -> p (j t)"), pC[:, 0:256], AF.Copy), [pC.b], [o_.b])
            R.dma(G.yT[0:256, t0:t0 + 128].rearrange("(j p) t -> p j t", p=128), o_[:], reads=[o_.b], writes=[G.yT.b], q="pool")

    cnt = 0
    for st_ in range(NT):
        recs = []
        for d in range(2):
            ti = st_ if d == 0 else NT - 1 - st_
            second = (ti >= NT // 2) if d == 0 else (ti < NT // 2)
            recs.append(Rec())
            body(recs[-1], d, ti, d, 2 * d + st_ % 2, second)
        if co_tile is not None:
            recs.append(Rec())
            co_tile(recs[-1], stream, st_)
        merge(P, recs)
    P.barrier()
    A.reset(m0)


def gdn_mixer(G, l, stream):
    P, A = G.P, G.A
    name, tok0, L, s = stream
    NT = L // 128
    m0 = A.mark()
    rp, C = G.rowp_sb, G.consts
    ident, ones = C[:, K_ID, :], C[:, K_ONES, :]
    id64 = C[0:64, K_ID, 0:64]
    negag = A.alloc("gnegag", [128, 8])
    negones = A.alloc("gnegones", [128, 128])
    oacc = A.alloc("goacc", [128, NT, 256])
    P.op("act", lambda e: e.activation(negag[:], rp[:, R_GALOG:R_GALOG + 8], AF.Exp), [rp.b], [negag.b])
    P.op("dve", lambda e: e.tensor_scalar(negag[:], negag[:], -1.0, None, ALU.mult), [negag.b], [negag.b])
    P.op("dve", lambda e: e.memset(negones[:], -1.0), [], [negones.b])
    NB = 2

    def mk(nm, shape, dt=F32):
        return [A.alloc(nm, shape, dt) for _ in range(NB)]
    pt, fm, gx, kv = mk("gpt", [128, 272]) + mk("gpt", [128, 272]), mk("gfm", [64, 12, 128]), mk("ggx", [128, 24]), mk("gkv", [128, 512])
    sc, esc, ML = mk("gsc", [128, 16]), mk("gesc", [128, 16]), mk("gML", [128, 4, 128])
    decT, decS, A0, Q0, aT = mk("gdecT", [128, 4, 128]), mk("gdecS", [128, 4, 128]), mk("gA0", [128, 4, 128]), mk("gQ0", [128, 4, 128]), mk("gaT", [128, 4, 128], BF16)
    Pb, Qb, MTb = [mk("gPb", [128, 4, 128]) for _ in range(2)], [mk("gQb", [128, 4, 128]) for _ in range(2)], [mk("gMT", [128, 4, 128]) for _ in range(3)]
    Rv, Rk, bek, usb, wsb = mk("gRv", [128, 4, 64]), mk("gRk", [128, 4, 64]), mk("gbek", [128, 4]), mk("gusb", [128, 256]), mk("gwsb", [64, 512], BF16)
    ektc, kt = [mk("gektc", [128, 4]) for _ in range(2)], [mk("gkt", [128, 4, 64], BF16) for _ in range(2)]
    qbf = mk("gqbf", [64, 4, 128], BF16)
    vnew, oint, ot, sq, ss, yo = mk("gvnew", [128, 256], BF16), mk("goint", [128, 256]), mk("got", [128, 256]), mk("gsq", [128, 256]), mk("gss", [128, 8]), mk("gyo", [128, 2, 128], BF16)
    for i in range(NB):
        P.op("dve", lambda e, i=i: e.memset(vnew[i][:], 0.0), [], [vnew[i].b])
    v4 = lambda t: t[:].rearrange("p (h q) -> p h q", h=4)

    def body(R, d, ti, i, li, second):
        g0_, g1_, g2_, g3_ = G.psum[4 * d:4 * d + 4]
        b = [g0_, g2_, g3_, g2_, g3_, g2_, g3_, g1_]
        bM = g0_
        t0 = tok0 + ti * 128
        Ltri2 = C[:, K_LF2 if d == 0 else K_LB2, :]
        niT = C[:, K_NI2F if d == 0 else K_NI2B, :]
        nsS = C[:, K_NS2F if d == 0 else K_NS2B, :]
        selc = [C[:, K_SEL0, 0:1], C[:, K_SEL1, 0:1]]
        pt_, f_, gx_, kv_, sc_, es_, ml_ = pt[li], fm[i], gx[i], kv[i], sc[i], esc[i], ML[i]
        dT_, dS_, a0_, q0_, at_ = decT[i], decS[i], A0[i], Q0[i], aT[i]
        R.dma(pt_[:], G.projT[t0:t0 + 128, 264:536], reads=[G.projT.b], writes=[pt_.b])
        R.dma(f_[:], G.convF[512:1280, t0:t0 + 128].rearrange("(j p) t -> p j t", p=64), reads=[G.convF.b], writes=[f_.b])
        R.op("dve", lambda e: e.tensor_tensor(gx_[:, 0:8], pt_[:, 256:264], rp[:, R_GDTB:R_GDTB + 8], ALU.add), [pt_.b, rp.b], [gx_.b])
        softplus_small(R, gx_[:, 0:8], gx_[:, 0:8], gx_[:, 16:24], [gx_.b], (gx_.b, gx_.b))
        R.op("dve", lambda e: e.tensor_tensor(gx_[:, 0:8], gx_[:, 0:8], negag[:], ALU.mult), [gx_.b, negag.b], [gx_.b])
        R.op("act", lambda e: e.activation(gx_[:, 8:16], pt_[:, 264:272], AF.Sigmoid), [pt_.b], [gx_.b])
        gd = gx_[:, d * 4:d * 4 + 4]
        bd = gx_[:, 8 + d * 4:8 + d * 4 + 4]
        qb_ = qbf[i]
        R.op("pool", lambda e: e.tensor_copy(qb_[:], f_[:, 0:4, :]), [f_.b], [qb_.b])
        def f(e):
            for h in range(4):
                e.transpose(b[0][:, h * 64:(h + 1) * 64], f_[:, 4 + h, :], id64)
            for h in range(4):
                ins = e.transpose(b[0][:, 256 + h * 64:256 + (h + 1) * 64], f_[:, 8 + h, :], id64)
            return ins
        R.op("pe", f, [f_.b, C.b], [b[0].b])
        R.op("act", lambda e: e.activation(kv_[:], b[0][:, :], AF.Copy), [b[0].b], [kv_.b])
        def f(e):
            e.matmul(b[7][:, 0:4], Ltri2, gd, start=True, stop=True)
            e.matmul(b[7][:, 4:8], C[:, K_BO2, :], gd, start=True, stop=True)
            e.matmul(b[7][:, 8:12], C[:, K_SEL0, :], gd, start=True, stop=True)
            return e.matmul(b[7][:, 12:16], C[:, K_SEL1, :], gd, start=True, stop=True)
        R.op("pe", f, [gx_.b, C.b], [b[7].b])
        R.op("dve", lambda e: e.tensor_copy(sc_[:], b[7][:, 0:16]), [b[7].b], [sc_.b])
        R.op("dve", lambda e: e.tensor_tensor(sc_[:, 4:8], sc_[:, 4:8], sc_[:, 0:4], ALU.subtract), [sc_.b], [sc_.b])
        R.op("act", lambda e: e.activation(es_[:], sc_[:], AF.Exp), [sc_.b], [es_.b])
        R.op("dve", lambda e: e.tensor_tensor(ml_[:], bc_ap(Ltri2, [[0, 4], [1, 128]]), bc_ap(gd, [[1, 4], [0, 128]]), ALU.mult),
             [gx_.b, C.b], [ml_.b])
        mlf = ml_[:].rearrange("p h i -> p (h i)")
        def f(e):
            e.matmul(b[1][:, :], ones, mlf, start=True, stop=False)
            for h in range(4):
                e.matmul(b[1][:, h * 128:(h + 1) * 128], ml_[:, h, :], negones[:], start=False, stop=False)
            return e.matmul(b[1][:, :], ident, bc_ap(niT, [[0, 4], [1, 128]]), start=False, stop=True)
        R.op("pe", f, [ml_.b, C.b, negones.b], [b[1].b])
        def f(e):
            e.matmul(b[2][:, :], negones[:], mlf, start=True, stop=False)
            for h in range(4):
                e.matmul(b[2][:, h * 128:(h + 1) * 128], ml_[:, h, :], ones, start=False, stop=False)
            return e.matmul(b[2][:, :], ident, bc_ap(nsS, [[0, 4], [1, 128]]), start=False, stop=True)
        R.op("pe", f, [ml_.b, C.b, negones.b], [b[2].b])
        fl = lambda t: t[:].rearrange("p h i -> p (h i)")
        R.op("act", lambda e: e.activation(fl(dT_), b[1][:, :], AF.Exp), [b[1].b], [dT_.b])
        R.op("act", lambda e: e.activation(fl(dS_), b[2][:, :], AF.Exp), [b[2].b], [dS_.b])
        def f(e):
            for h in range(4):
                ins = e.matmul(b[3][:, h * 128:(h + 1) * 128], f_[:, 4 + h, :], f_[:, 4 + h, :], start=True, stop=True)
            return ins
        R.op("pe", f, [f_.b], [b[3].b])
        def f(e):
            for h in range(4):
                ins = e.matmul(b[4][:, h * 128:(h + 1) * 128], f_[:, 4 + h, :], f_[:, h, :], start=True, stop=True)
            return ins
        R.op("pe", f, [f_.b], [b[4].b])
        R.op("dve", lambda e: e.tensor_tensor(fl(a0_), b[3][:, :], fl(dS_), ALU.mult), [b[3].b, dS_.b], [a0_.b])
        R.op("dve", lambda e: e.tensor_tensor(a0_[:], a0_[:], bc_ap(bd, [[1, 4], [0, 128]]), ALU.mult), [a0_.b, gx_.b], [a0_.b])
        R.op("dve", lambda e: e.tensor_tensor(fl(at_), b[4][:, :], fl(dT_), ALU.mult), [b[4].b, dT_.b], [at_.b])
        def f(e):
            for h in range(4):
                ins = e.transpose(b[5][:, h * 128:(h + 1) * 128], a0_[:, h, :], ident)
            return ins
        R.op("pe", f, [a0_.b, C.b], [b[5].b])
        R.op("act", lambda e: e.activation(fl(q0_), b[5][:, :], AF.Copy), [b[5].b], [q0_.b])
        mt = MTb[0][i]
        R.op("dve", lambda e, mt=mt: e.tensor_tensor(mt[:], bc_ap(ident, [[0, 4], [1, 128]]), q0_[:], ALU.subtract), [q0_.b, C.b], [mt.b])
        if name == "ctx" and ti == (0 if d == 0 else 1):
            dbgdump(G, f"gA{d}", a0_, fl(a0_), [128, 512], P=R)
            dbgdump(G, f"gQ{d}", q0_, fl(q0_), [128, 512], P=R)
            dbgdump(G, f"gM0{d}", mt, fl(mt), [128, 512], P=R)
            dbgdump(G, f"gdS{d}", dS_, fl(dS_), [128, 512], P=R)
            dbgdump(G, f"ggx{d}", gx_, gx_[:], [128, 24], P=R)
            dbgdump(G, f"gkv{d}", kv_, kv_[:], [128, 512], P=R)
        Pc, Qc = a0_, q0_
        for m in range(GDN_ROUNDS):
            Pn, Qn, mtn = Pb[m % 2][i], Qb[m % 2][i], MTb[(m + 1) % 3][i]
            def f(e, Pc=Pc, Qc=Qc):
                for h in range(4):
                    ins = e.matmul(b[3][:, h * 128:(h + 1) * 128], Qc[:, h, :], Pc[:, h, :], start=True, stop=True)
                return ins
            R.op("pe", f, [Pc.b, Qc.b], [b[3].b])
            def f(e, Pc=Pc, Qc=Qc):
                for h in range(4):
                    ins = e.matmul(b[4][:, h * 128:(h + 1) * 128], Pc[:, h, :], Qc[:, h, :], start=True, stop=True)
                return ins
            R.op("pe", f, [Pc.b, Qc.b], [b[4].b])
            R.op("act", lambda e, Pn=Pn: e.activation(fl(Pn), b[3][:, :], AF.Copy), [b[3].b], [Pn.b])
            R.op("dve", lambda e, Qn=Qn: e.tensor_copy(fl(Qn), b[4][:, :]), [b[4].b], [Qn.b])
            def f(e, Pn=Pn, mt=mt):
                for h in range(4):
                    ins = e.matmul(bM[:, h * 128:(h + 1) * 128], Pn[:, h, :], mt[:, h, :], start=True, stop=True)
                return ins
            R.op("pe", f, [Pn.b, mt.b], [bM.b])
            R.op("dve", lambda e, mt=mt, mtn=mtn: e.tensor_tensor(fl(mtn), fl(mt), bM[:, :], ALU.add), [mt.b, bM.b], [mtn.b])
            Pc, Qc, mt = Pn, Qn, mtn
        rv_, rk_, bk_, u_, w_ = Rv[i], Rk[i], bek[i], usb[i], wsb[i]
        R.op("dve", lambda e: e.tensor_tensor(rv_[:], v4(kv_)[:, 4:8, :] if False else kv_[:, 256:512].rearrange("p (h q) -> p h q", h=4),
                                              bc_ap(bd, [[1, 4], [0, 64]]), ALU.mult), [kv_.b, gx_.b], [rv_.b])
        R.op("dve", lambda e: e.tensor_tensor(bk_[:], bd, es_[:, 0:4], ALU.mult), [gx_.b, es_.b], [bk_.b])
        R.op("dve", lambda e: e.tensor_tensor(rk_[:], kv_[:, 0:256].rearrange("p (h q) -> p h q", h=4),
                                              bc_ap(bk_[:, 0:4], [[1, 4], [0, 64]]), ALU.mult), [kv_.b, bk_.b], [rk_.b])
        def f(e):
            for h in range(4):
                ins = e.matmul(b[7][:, 64 + h * 64:64 + (h + 1) * 64], mt[:, h, :], rv_[:, h, :], start=True, stop=True)
            return ins
        R.op("pe", f, [mt.b, rv_.b], [b[7].b])
        def f(e):
            for h in range(4):
                ins = e.matmul(b[0][0:64, h * 128:(h + 1) * 128], rk_[:, h, :], mt[:, h, :], start=True, stop=True)
            return ins
        R.op("pe", f, [mt.b, rk_.b], [b[0].b])
        R.op("act", lambda e: e.activation(u_[:], b[7][:, 64:320], AF.Copy), [b[7].b], [u_.b])
        R.op("dve", lambda e: e.tensor_copy(w_[:], b[0][0:64, :]), [b[0].b], [w_.b])
        for c in range(2):
            ek, k_ = ektc[c][i], kt[c][i]
            R.op("dve", lambda e, ek=ek, c=c: e.tensor_scalar(ek[:], es_[:, 4:8], selc[c], None, ALU.mult), [es_.b, C.b], [ek.b])
            R.op("dve", lambda e, ek=ek, k_=k_: e.tensor_tensor(k_[:], kv_[:, 0:256].rearrange("p (h q) -> p h q", h=4),
                                                               bc_ap(ek[:, 0:4], [[1, 4], [0, 64]]), ALU.mult), [kv_.b, ek.b], [k_.b])
        vn_, oi_ = vnew[i], oint[i]
        for c in ((0, 1) if d == 0 else (1, 0)):
            lo, hi = c * 64, (c + 1) * 64
            k_ = kt[c][i]
            def f(e):
                for h in range(4):
                    e.matmul(b[5][:, h * 64:(h + 1) * 64], w_[:, h * 128:(h + 1) * 128], G.gsb[d][:, h, :], start=True, stop=True)
                for h in range(4):
                    ins = e.matmul(b[5][:, 256 + h * 64:256 + (h + 1) * 64], qb_[:, h, :], G.gsb[d][:, h, :], start=True, stop=True)
                return ins
            R.op("pe", f, [w_.b, qb_.b, G.gsb[d].b], [b[5].b])
            R.op("dve", lambda e, lo=lo, hi=hi: e.tensor_tensor(vn_[lo:hi, :], u_[lo:hi, :], b[5][lo:hi, 0:256], ALU.subtract), [u_.b, b[5].b], [vn_.b])
            R.op("dve", lambda e, lo=lo, hi=hi: e.tensor_tensor(oi_[lo:hi, :].rearrange("p (h q) -> p h q", h=4),
                                                                b[5][lo:hi, 256:512].rearrange("p (h q) -> p h q", h=4),
                                                                bc_ap(es_[lo:hi, 0:4], [[1, 4], [0, 64]]), ALU.mult), [b[5].b, es_.b], [oi_.b])
            def f(e, k_=k_):
                for h in range(4):
                    ins = e.matmul(b[6][0:64, h * 64:(h + 1) * 64], k_[:, h, :], vn_[:, h * 64:(h + 1) * 64], start=True, stop=True)
                return ins
            R.op("pe", f, [k_.b, vn_.b], [b[6].b])
            for h in range(4):
                R.op("dve", lambda e, h=h, c=c: e.scalar_tensor_tensor(
                    G.gst[d][:, h, :], G.gst[d][:, h, :], es_[0:64, 8 + 4 * c + h:9 + 4 * c + h], b[6][0:64, h * 64:(h + 1) * 64],
                    ALU.mult, ALU.add), [G.gst[d].b, es_.b, b[6].b], [G.gst[d].b])
            R.op("pool", lambda e: e.tensor_copy(G.gsb[d][:], G.gst[d][:]), [G.gst[d].b], [G.gsb[d].b])
        def f(e):
            for h in range(4):
                ins = e.matmul(b[7][:, 64 + h * 64:64 + (h + 1) * 64], at_[:, h, :], vn_[:, h * 64:(h + 1) * 64], start=True, stop=True)
            return ins
        R.op("pe", f, [at_.b, vn_.b], [b[7].b])
        if name == "ctx" and ti == (0 if d == 0 else 1):
            dbgdump(G, f"gMT{d}", mt, fl(mt), [128, 512], P=R)
            dbgdump(G, f"gu{d}", u_, u_[:], [128, 256], P=R)
            dbgdump(G, f"gw{d}", w_, w_[:], [64, 512], BF16, P=R)
            dbgdump(G, f"gvn{d}", vn_, vn_[:], [128, 256], BF16, P=R)
            dbgdump(G, f"gaT{d}", at_, fl(at_), [128, 512], BF16, P=R)
        if not second:
            R.op("dve", lambda e: e.tensor_tensor(oacc[:, ti, :], b[7][:, 64:320], oi_[:], ALU.add), [b[7].b, oi_.b], [oacc.b])
            return
        o_, q_, s_, y_ = ot[i], sq[i], ss[i], yo[i]
        R.op("dve", lambda e: e.tensor_tensor(o_[:], b[7][:, 64:320], oi_[:], ALU.add), [b[7].b, oi_.b], [o_.b])
        R.op("dve", lambda e: e.tensor_tensor(o_[:], o_[:], oacc[:, ti, :], ALU.add), [o_.b, oacc.b], [o_.b])
        R.op("dve", lambda e: e.tensor_tensor(q_[:], o_[:], o_[:], ALU.mult), [o_.b], [q_.b])
        R.op("dve", lambda e: e.reduce_sum(s_[:, 0:4], q_[:].rearrange("p (h q) -> p h q", h=4), AX.X), [q_.b], [s_.b])
        R.op("act", lambda e: e.activation(s_[:, 4:8], s_[:, 0:4], AF.Sqrt, bias=G.eps[:, :], scale=1.0 / 64.0), [s_.b, G.eps.b], [s_.b])
        R.op("dve", lambda e: e.reciprocal(s_[:, 4:8], s_[:, 4:8]), [s_.b], [s_.b])
        R.op("dve", lambda e: e.tensor_tensor(v4(o_), v4(o_), bc_ap(s_[:, 4:8], [[1, 4], [0, 64]]), ALU.mult), [o_.b, s_.b], [o_.b])
        R.op("dve", lambda e: e.tensor_tensor(v4(o_), v4(o_), bc_ap(rp[:, R_GDNNW:R_GDNNW + 64], [[0, 4], [1, 64]]), ALU.mult), [o_.b, rp.b], [o_.b])
        R.op("act", lambda e: e.activation(q_[:], pt_[:, 0:256], AF.Silu), [pt_.b], [q_.b])
        R.op("dve", lambda e: e.tensor_tensor(o_[:], o_[:], q_[:], ALU.mult), [o_.b, q_.b], [o_.b])
        def f(e):
            e.transpose(b[2][:, 0:128], o_[:, 0:128], ident)
            return e.transpose(b[2][:, 128:256], o_[:, 128:256], ident)
        R.op("pe", f, [o_.b, C.b], [b[2].b])
        R.op("act", lambda e: e.activation(y_[:].rearrange("p j t -> p (j t)"), b[2][:, 0:256], AF.Copy), [b[2].b], [y_.b])
        R.dma(G.yT[512:768, t0:t0 + 128].rearrange("(j p) t -> p j t", p=128), y_[:], reads=[y_.b], writes=[G.yT.b], q="pool")

    cnt = 0
    for st_ in range(NT):
        recs = []
        for d in range(2):
            ti = st_ if d == 0 else NT - 1 - st_
            second = (ti >= NT // 2) if d == 0 else (ti < NT // 2)
            recs.append(Rec())
            body(recs[-1], d, ti, d, 2 * d + st_ % 2, second)
        merge(P, recs)
    P.barrier()
    A.reset(m0)


def x_rows(G, l, name, tok0, r0, n):
    if l == 0:
        return (G.ctx_in if name == "ctx" else G.x_in)[r0:r0 + n, :], None
    t = G.xs[l % 2]
    return t[tok0 + r0:tok0 + r0 + n, :], t.b


def phase_c(G, l, streams, hT2):
    P, A = G.P, G.A
    m0 = A.mark()
    rp = G.rowp_sb
    wo = A.alloc("wo", [128, 8, D], BF16)
    xt = [A.alloc("cxt", [128, D]) for _ in range(2)]
    load_weight_bf16(G, wo, lambda k: G.w_out[l, k * 128:(k + 1) * 128, :], D, xt, 8)
    yb = [A.alloc("cyb", [128, 8, 512], BF16) for _ in range(2)]
    t1 = [A.alloc("ct1", [128, D]) for _ in range(2)]
    xn = [A.alloc("cxn", [128, D]) for _ in range(4)]
    st = [A.alloc("cst", [128, 2, 6]) for _ in range(2)]
    mv = [A.alloc("cmv", [128, 2]) for _ in range(2)]
    rstd = [A.alloc("crstd", [128, 1]) for _ in range(2)]
    ident = G.consts[:, K_ID, :]
    po = [[G.psum[0], G.psum[1]], [G.psum[2], G.psum[3]]]
    pT = [G.psum[4], G.psum[5]]
    cb = 0
    ct = 0
    for (name, tok0, L, s) in streams:
        bs = min(512, L)
        for b0 in range(0, L, bs):
            nt = bs // 128
            y_ = yb[cb % 2]
            cb += 1
            P.dma(y_[:, :, 0:bs], G.yT[:, tok0 + b0:tok0 + b0 + bs].rearrange("(k p) t -> p k t", p=128), reads=[G.yT.b], writes=[y_.b])
            recs = []
            for m in range(nt):
                R = Rec()
                recs.append(R)
                q = ct % 2
                ct += 1
                x_, t_, pp = xt[q], t1[q], po[q]
                r0 = b0 + m * 128
                src, sb_ = x_rows(G, l, name, tok0, r0, 128)
                R.dma(x_[:], src, reads=[sb_] if sb_ is not None else [], writes=[x_.b])
                for half in range(2):
                    def f(e, half=half, y_=y_, m=m, pp=pp):
                        for k in range(8):
                            ins = e.matmul(pp[half][:, :], y_[:, k, m * 128:(m + 1) * 128], wo[:, k, half * 512:(half + 1) * 512],
                                           start=(k == 0), stop=(k == 7))
                        return ins
                    R.op("pe", f, [y_.b, wo.b], [pp[half].b])
                    R.op("dve", lambda e, half=half, t_=t_, pp=pp, s=s: e.tensor_tensor(
                        t_[:, half * 512:(half + 1) * 512], pp[half][:, :], G.gb[:, s, 0, half * 512:(half + 1) * 512], ALU.mult),
                        [pp[half].b, G.gb.b], [t_.b])
                R.op("dve", lambda e, t_=t_, x_=x_: e.scalar_tensor_tensor(t_[:], x_[:], DN_ALPHA, t_[:], ALU.mult, ALU.add), [x_.b, t_.b], [t_.b])
                ln_stats(G, t_, 128, st[q], mv[q], rstd[q], P=R)
                R.op("dve", lambda e, t_=t_, q=q: e.tensor_scalar(t_[:], t_[:], mv[q][:, 0:1], rstd[q][:, 0:1], ALU.subtract, ALU.mult),
                     [t_.b, mv[q].b, rstd[q].b], [t_.b])
                R.op("pool", lambda e, t_=t_: e.tensor_tensor(t_[:], t_[:], rp[:, R_LN1W:R_LN1W + D], ALU.mult), [t_.b, rp.b], [t_.b])
                R.op("pool", lambda e, t_=t_: e.tensor_tensor(t_[:], t_[:], rp[:, R_LN1B:R_LN1B + D], ALU.add), [t_.b, rp.b], [t_.b])
                R.dma(G.x1[tok0 + r0:tok0 + r0 + 128, :], t_[:], reads=[t_.b], writes=[G.x1.b], q="pool")
                ln_stats(G, t_, 128, st[q], mv[q], rstd[q], P=R)
                R.op("dve", lambda e, t_=t_, q=q, m=m: e.tensor_scalar(xn[m][:], t_[:], mv[q][:, 0:1], rstd[q][:, 0:1], ALU.subtract, ALU.mult),
                     [t_.b, mv[q].b, rstd[q].b], [xn[m].b])
            for m in range(0, nt, 2):
                merge(P, recs[m:m + 2])
            for k in range(8):
                p = pT[k % 2]
                def f(e, p=p, k=k, nt=nt):
                    for m in range(nt):
                        ins = e.transpose(p[:, m * 128:(m + 1) * 128], xn[m][:, k * 128:(k + 1) * 128], ident)
                    return ins
                P.op("pe", f, [xn[m].b for m in range(nt)] + [G.consts.b], [p.b])
                c0 = tok0 + b0
                P.op("act", lambda e, p=p, k=k, bs=bs, s=s, c0=c0: e.activation(
                    hT2[:, k, c0:c0 + bs], p[:, 0:bs], AF.Identity, bias=G.modc[:, 2, k, s:s + 1], scale=G.modc[:, 3, k, s:s + 1]),
                    [p.b, G.modc.b], [hT2.b])
    P.barrier()
    A.reset(m0)


def phase_d1(G, l, streams, hT2):
    P, A = G.P, G.A
    m0 = A.mark()
    cp = G.colp_sb
    wst = [A.alloc("dwst", [128, 8, 256]) for _ in range(2)]
    wab = [A.alloc("dwab", [128, 8, 256], BF16) for _ in range(2)]
    asb = [A.alloc("dasb", [128, 514]) for _ in range(4)]
    acc = [A.alloc("dacc", [128, 512]) for _ in range(4)]
    hc = [A.alloc("dhc", [128, 512], BF16) for _ in range(4)]
    up = G.ffn_up[l, :, :].rearrange("(k p) c -> p k c", p=128)
    pa = [[G.psum[0], G.psum[1]], [G.psum[2], G.psum[3]]]
    pbs = [G.psum[4], G.psum[5], G.psum[6], G.psum[7]]
    cnt = 0
    for c in range(DFF // 128):
        w_, wb_ = wst[c % 2], wab[c % 2]
        P.dma(w_[:, :, 0:128], up[:, :, c * 128:(c + 1) * 128], writes=[w_.b])
        P.dma(w_[:, :, 128:256], up[:, :, DFF + c * 128:DFF + (c + 1) * 128], writes=[w_.b])
        P.op("pool", lambda e, w_=w_, wb_=wb_: e.tensor_copy(wb_[:], w_[:]), [w_.b], [wb_.b])
        pend = []
        for (name, tok0, L, s) in streams:
            bs = min(512, L)
            for b0 in range(0, L, bs):
                R = Rec()
                pend.append(R)
                i = cnt % 4
                p0, p1 = pa[cnt % 2]
                pb = pbs[i]
                cnt += 1
                a_, ac_, h_ = asb[i], acc[i], hc[i]
                t0 = tok0 + b0
                lo = max(t0 - 1, tok0)
                hi = min(t0 + bs + 1, tok0 + L)
                jlo, jhi = lo - (t0 - 1), hi - (t0 - 1)
                half = (bs + 2) // 2
                segs = [(jlo, half + 1), (half - 1, jhi)] if bs == 512 else [(jlo, jhi)]
                for si, (j0, j1) in enumerate(segs):
                    pp = p0 if si == 0 else p1
                    def f(e, pp=pp, j0=j0, j1=j1, wb_=wb_, t0=t0):
                        for k in range(8):
                            ins = e.matmul(pp[:, 0:j1 - j0], wb_[:, k, 0:128], hT2[:, k, t0 - 1 + j0:t0 - 1 + j1], start=(k == 0), stop=(k == 7))
                        return ins
                    R.op("pe", f, [wb_.b, hT2.b], [pp.b])
                def f(e, pb=pb, wb_=wb_, t0=t0, bs=bs):
                    for k in range(8):
                        ins = e.matmul(pb[:, 0:bs], wb_[:, k, 128:256], hT2[:, k, t0:t0 + bs], start=(k == 0), stop=(k == 7))
                    return ins
                R.op("pe", f, [wb_.b, hT2.b], [pb.b])
                if jlo > 0:
                    R.op("pool", lambda e, a_=a_: e.memset(a_[:, 0:1], 0.0), [], [a_.b])
                if jhi < bs + 2:
                    R.op("pool", lambda e, a_=a_, bs=bs: e.memset(a_[:, bs + 1:bs + 2], 0.0), [], [a_.b])
                if len(segs) == 2:
                    (a0, a1), (b0_, b1_) = segs
                    R.op("act", lambda e, a_=a_, p0=p0, a0=a0, a1=a1: e.activation(a_[:, a0:a1], p0[:, 0:a1 - a0], AF.Copy), [p0.b], [a_.b])
                    R.op("act", lambda e, a_=a_, p1=p1, a1=a1, b0_=b0_, b1_=b1_: e.activation(
                        a_[:, a1:b1_], p1[:, a1 - b0_:b1_ - b0_], AF.Copy), [p1.b], [a_.b])
                else:
                    (a0, a1), = segs
                    R.op("act", lambda e, a_=a_, p0=p0, a0=a0, a1=a1: e.activation(a_[:, a0:a1], p0[:, 0:a1 - a0], AF.Copy), [p0.b], [a_.b])
                wo_ = C_FFNCW + c * 3
                R.op("dve", lambda e, a_=a_, ac_=ac_, bs=bs, wo_=wo_: e.tensor_scalar(ac_[:, 0:bs], a_[:, 0:bs], cp[:, wo_:wo_ + 1], None, ALU.mult),
                     [a_.b, cp.b], [ac_.b])
                for tap in (1, 2):
                    R.op("dve", lambda e, a_=a_, ac_=ac_, bs=bs, wo_=wo_, tap=tap: e.scalar_tensor_tensor(
                        ac_[:, 0:bs], a_[:, tap:tap + bs], cp[:, wo_ + tap:wo_ + tap + 1], ac_[:, 0:bs], ALU.mult, ALU.add), [a_.b, cp.b, ac_.b], [ac_.b])
                R.op("act", lambda e, ac_=ac_, bs=bs, c=c: e.activation(ac_[:, 0:bs], ac_[:, 0:bs], AF.Silu, bias=cp[:, C_FFNCB + c:C_FFNCB + c + 1]),
                     [ac_.b, cp.b], [ac_.b])
                R.op("dve", lambda e, ac_=ac_, h_=h_, pb=pb, bs=bs: e.tensor_tensor(h_[:, 0:bs], ac_[:, 0:bs], pb[:, 0:bs], ALU.mult), [ac_.b, pb.b], [h_.b])
                R.dma(G.hid[c * 128:(c + 1) * 128, t0:t0 + bs], h_[:, 0:bs], reads=[h_.b], writes=[G.hid.b], q="pool")
                if len(pend) == 2:
                    merge(P, pend)
                    pend = []
        if pend:
            merge(P, pend)
    P.barrier()
    A.reset(m0)


def phase_d2(G, l, streams, last):
    P, A = G.P, G.A
    m0 = A.mark()
    rp = G.rowp_sb
    NC_ = DFF // 128
    wd = A.alloc("wd", [128, NC_, D], BF16)
    stage = [A.alloc("wdstage", [128, D]) for _ in range(2)]
    load_weight_bf16(G, wd, lambda k: G.ffn_down[l, k * 128:(k + 1) * 128, :], D, stage, NC_)
    hb = [A.alloc("ehb", [128, NC_, 512], BF16) for _ in range(2)]
    xt = [A.alloc("ext", [128, D]) for _ in range(2)]
    t1 = [A.alloc("et1", [128, D]) for _ in range(2)]
    st = [A.alloc("est", [128, 2, 6]) for _ in range(2)]
    mv = [A.alloc("emv", [128, 2]) for _ in range(2)]
    rstd = [A.alloc("erstd", [128, 1]) for _ in range(2)]
    po = [[G.psum[0], G.psum[1]], [G.psum[2], G.psum[3]]]
    xnext = G.xs[(l + 1) % 2]
    cb = 0
    ct = 0
    for (name, tok0, L, s) in streams:
        bs = min(512, L)
        for b0 in range(0, L, bs):
            nt = bs // 128
            h_ = hb[cb % 2]
            cb += 1
            P.dma(h_[:, :, 0:bs], G.hid[:, tok0 + b0:tok0 + b0 + bs].rearrange("(c p) t -> p c t", p=128), reads=[G.hid.b], writes=[h_.b])
            recs = []
            for m in range(nt):
                R = Rec()
                recs.append(R)
                q = ct % 2
                ct += 1
                x_, t_, pp = xt[q], t1[q], po[q]
                r0 = tok0 + b0 + m * 128
                R.dma(x_[:], G.x1[r0:r0 + 128, :], reads=[G.x1.b], writes=[x_.b])
                for half in range(2):
                    def f(e, half=half, h_=h_, m=m, pp=pp):
                        for c in range(NC_):
                            ins = e.matmul(pp[half][:, :], h_[:, c, m * 128:(m + 1) * 128], wd[:, c, half * 512:(half + 1) * 512],
                                           start=(c == 0), stop=(c == NC_ - 1))
                        return ins
                    R.op("pe", f, [h_.b, wd.b], [pp[half].b])
                    R.op("dve", lambda e, half=half, t_=t_, pp=pp, s=s: e.tensor_tensor(
                        t_[:, half * 512:(half + 1) * 512], pp[half][:, :], G.gb[:, s, 1, half * 512:(half + 1) * 512], ALU.mult),
                        [pp[half].b, G.gb.b], [t_.b])
                R.op("dve", lambda e, t_=t_, x_=x_: e.scalar_tensor_tensor(t_[:], x_[:], DN_ALPHA, t_[:], ALU.mult, ALU.add), [x_.b, t_.b], [t_.b])
                ln_stats(G, t_, 128, st[q], mv[q], rstd[q], P=R)
                R.op("dve", lambda e, t_=t_, q=q: e.tensor_scalar(t_[:], t_[:], mv[q][:, 0:1], rstd[q][:, 0:1], ALU.subtract, ALU.mult),
                     [t_.b, mv[q].b, rstd[q].b], [t_.b])
                R.op("pool", lambda e, t_=t_: e.tensor_tensor(t_[:], t_[:], rp[:, R_LN2W:R_LN2W + D], ALU.mult), [t_.b, rp.b], [t_.b])
                R.op("pool", lambda e, t_=t_: e.tensor_tensor(t_[:], t_[:], rp[:, R_LN2B:R_LN2B + D], ALU.add), [t_.b, rp.b], [t_.b])
                if last:
                    rr = b0 + m * 128
                    R.dma(G.out[rr:rr + 128, :], t_[:], reads=[t_.b], writes=[G.out.b], q="pool")
                else:
                    R.dma(xnext[r0:r0 + 128, :], t_[:], reads=[t_.b], writes=[xnext.b], q="pool")
            for m in range(nt):
                merge(P, recs[m:m + 1])
    P.barrier()
    A.reset(m0)


def _col(v, nchunk):
    return np.ascontiguousarray(v.reshape(nchunk, 128).T)


def prep_inputs(inputs):
    f = lambda a: np.ascontiguousarray(np.asarray(a, dtype=np.float32))
    I = {k: f(v) for k, v in inputs.items()}
    colp = np.zeros((DEPTH, 128, NCOL), np.float32)
    rowp = np.zeros((DEPTH, 1, NROW), np.float32)
    poolw = np.zeros((DEPTH, 128, 2, 128), np.float32)
    gws = np.zeros((DEPTH, 128, 4, 128), np.float32)
    for l in range(DEPTH):
        cw = I["ssd_conv_w"][l]
        colp[l, :, C_SSDCW:C_SSDCW + 28] = cw.T.reshape(4, 128, 7).transpose(1, 0, 2).reshape(128, 28)
        colp[l, :, C_SSDCB:C_SSDCB + 4] = _col(I["ssd_conv_b"][l], 4)
        gw = I["gdn_conv_w"][l]
        colp[l, :, C_GDNCW:C_GDNCW + 42] = gw.T.reshape(6, 128, 7).transpose(1, 0, 2).reshape(128, 42)
        fw = I["ffn_conv_w"][l]
        colp[l, :, C_FFNCW:C_FFNCW + 66] = fw.T.reshape(22, 128, 3).transpose(1, 0, 2).reshape(128, 66)
        colp[l, :, C_FFNCB:C_FFNCB + 22] = _col(I["ffn_conv_b"][l], 22)
        colp[l, :, C_PSCALE:C_PSCALE + 2] = _col(I["pool_scale"][l], 2)
        colp[l, :, C_BMOD:C_BMOD + 48] = I["b_mod"][l].reshape(6, 8, 128).transpose(2, 0, 1).reshape(128, 48)
        r = rowp[l, 0]
        r[R_SSDNW:R_SSDNW + 256] = I["ssd_norm_w"][l]
        r[R_GDNNW:R_GDNNW + 64] = I["gdn_norm_w"][l]
        r[R_GLNW:R_GLNW + 256] = I["gmlp_ln_w"][l]
        r[R_GLNB:R_GLNB + 256] = I["gmlp_ln_b"][l]
        r[R_LN1W:R_LN1W + 1024] = I["ln1_w"][l]
        r[R_LN1B:R_LN1B + 1024] = I["ln1_b"][l]
        r[R_LN2W:R_LN2W + 1024] = I["ln2_w"][l]
        r[R_LN2B:R_LN2B + 1024] = I["ln2_b"][l]
        r[R_SSDD:R_SSDD + 256] = np.repeat(I["ssd_d"][l], 64)
        r[R_GBS:R_GBS + 512] = I["gmlp_bs"][l].reshape(-1)
        r[R_SDTB:R_SDTB + 8] = I["ssd_dt_bias"][l].reshape(-1)
        r[R_SALOG:R_SALOG + 8] = I["ssd_a_log"][l].reshape(-1)
        r[R_GDTB:R_GDTB + 8] = I["gdn_dt_bias"][l].reshape(-1)
        r[R_GALOG:R_GALOG + 8] = I["gdn_a_log"][l].reshape(-1)
        r[R_BG1:R_BG1 + 1024] = I["b_mod"][l][2048:3072]
        r[R_BG2:R_BG2 + 1024] = I["b_mod"][l][5120:6144]
        pw = I["pool_w"][l]
        for g in range(4):
            j, h = g // 2, g % 2
            poolw[l, h * 64:(h + 1) * 64, j, h * 64:(h + 1) * 64] = pw[g]
        gws[l] = I["gmlp_ws"][l].transpose(2, 0, 1)
    consts = make_consts()

    def pinv(RW):
        o = np.zeros((128, 2, RW), np.float32)
        pos = np.arange(RW)
        for jc in range(2):
            for hh in range(2):
                w = (2, 4, 8, 16)[2 * jc + hh]
                lo = np.clip(pos - w // 2, 0, RW)
                hi = np.clip(pos + w - w // 2, 0, RW)
                o[hh * 64:(hh + 1) * 64, jc, :] = 1.0 / (hi - lo).astype(np.float32)
        return o
    shared = dict(consts=consts, w_mod=I["w_mod"], w_in=I["w_in"], w_out=I["w_out"], ffn_up=I["ffn_up"],
                  ffn_down=I["ffn_down"], colp=colp, rowp=rowp, poolw=poolw, gws=gws,
                  pinv_g=pinv(64), pinv_c=pinv(256))
    maps = []
    for core in range(8):
        b = core % 4
        crep = np.zeros((128, 2, 8, 128), np.float32)
        crep[:, 0] = np.repeat(_col(I["c"][b], 8)[:, :, None], 128, axis=2)
        crep[:, 1] = np.repeat(_col(I["c_ctx"], 8)[:, :, None], 128, axis=2)
        m = dict(shared)
        m.update(x_in=I["x"][b], ctx_in=I["ctx"][b], crep=crep)
        maps.append(m)
    return maps


_NC_CACHE = {}


def kernel(**inputs):
    maps = prep_inputs(inputs)
    if "nc" not in _NC_CACHE:
        _NC_CACHE["nc"] = build()
    res = run_bass_kernel_spmd(_NC_CACHE["nc"], maps, core_ids=list(range(8)))
    out = np.stack([np.asarray(res.results[b]["out"], dtype=np.float32) for b in range(4)], axis=0)
    return out
```

```python
import numpy as np
import concourse.bass as bass
import concourse.mybir as mybir
from concourse.bass_utils import run_bass_kernel_spmd

F32 = mybir.dt.float32
BF16 = mybir.dt.bfloat16
ALU = mybir.AluOpType
AF = mybir.ActivationFunctionType
AX = mybir.AxisListType

import os
SSD_STOP = int(os.environ.get('SSD_STOP', '99'))
GDN_ROUNDS = int(os.environ.get('GDN_ROUNDS', '5'))
SEG = 16000
DSEG = 1000
DMAK = 8


class Buf:
    __slots__ = ("w", "r", "name")

    def __init__(self, name=""):
        self.w = None
        self.r = {}
        self.name = name


class Prog:
    ENGS = ["pe", "act", "dve", "pool", "sp"]

    def __init__(self, nc):
        self.nc = nc
        self.ops = {e: [] for e in self.ENGS}
        self.count = {e: 0 for e in self.ENGS}
        self.sems = {}
        self.waited = {e: {} for e in self.ENGS}
        self.dma_n = {e: 0 for e in self.ENGS}
        self.last = {}

    def _sem(self, key):
        if key not in self.sems:
            self.sems[key] = self.nc.alloc_semaphore("s_" + "_".join(map(str, key)))
        return self.sems[key]

    def _need(self, eng, waits, tok):
        if tok is None:
            return
        key, val = tok
        if key[0] == "c" and key[1] == "pe" and eng == "pe":
            return
        if self.waited[eng].get(key, 0) >= val:
            return
        if waits.get(key, 0) < val:
            waits[key] = val

    def op(self, eng, fn, reads=(), writes=(), dma=False, extra=()):
        waits = {}
        for b in reads:
            self._need(eng, waits, b.w)
        for b in writes:
            self._need(eng, waits, b.w)
            for k, v in b.r.items():
                self._need(eng, waits, (k, v))
        for t in extra:
            self._need(eng, waits, t)
        if dma:
            n = self.dma_n[eng]
            self.dma_n[eng] += 1
            s, r = n % DMAK, n // DMAK
            if r >= 1:
                pk = ("d", eng, s, (r - 1) // DSEG)
                self._need(eng, waits, (pk, 16 * (((r - 1) % DSEG) + 1)))
            tok = (("d", eng, s, r // DSEG), 16 * ((r % DSEG) + 1))
            inc = 16
        elif fn is None:
            tok = None
            inc = 0
        else:
            n = self.count[eng]
            self.count[eng] += 1
            tok = (("c", eng, n // SEG), (n % SEG) + 1)
            inc = 1
        for k, v in waits.items():
            self.waited[eng][k] = v
            self._sem(k)
        if tok is not None:
            self._sem(tok[0])
            self.last[tok[0]] = tok[1]
        self.ops[eng].append((list(waits.items()), fn, tok, inc))
        if tok is not None:
            for b in reads:
                b.r[tok[0]] = tok[1]
            for b in writes:
                b.w = tok
                b.r = {}
        return tok

    def dma(self, out, in_, reads=(), writes=(), q="sp", **kw):
        return self.op(q, lambda e: e.dma_start(out=out, in_=in_, **kw), reads, writes, dma=True)

    def barrier(self):
        toks = list(self.last.items())
        for e in self.ENGS:
            self.op(e, None, extra=toks)

    def emit(self):
        nc = self.nc
        with nc.Block() as block:
            def run(name):
                def body(e):
                    for waits, fn, tok, inc in self.ops[name]:
                        for k, v in waits:
                            e.wait_ge(self.sems[k], v)
                        if fn is not None:
                            ins = fn(e)
                            ins.then_inc(self.sems[tok[0]], inc)
                return body
            block.tensor(run("pe"))
            block.scalar(run("act"))
            block.vector(run("dve"))
            block.gpsimd(run("pool"))
            block.sync(run("sp"))


class Rec:
    def __init__(self):
        self.calls = []

    def op(self, eng, fn, reads=(), writes=(), dma=False, extra=()):
        self.calls.append((eng, fn, tuple(reads), tuple(writes), dma, tuple(extra)))

    def dma(self, out, in_, reads=(), writes=(), q="sp", **kw):
        self.op(q, lambda e: e.dma_start(out=out, in_=in_, **kw), reads, writes, dma=True)


def merge(P, recs):
    idx = [0] * len(recs)
    live = True
    while live:
        live = False
        for k, r in enumerate(recs):
            if idx[k] < len(r.calls):
                P.op(*r.calls[idx[k]])
                idx[k] += 1
                live = True


class T:
    def __init__(self, h, name=""):
        self.h = h
        self.b = Buf(name)

    def __getitem__(self, k):
        return self.h[k]


def _dtsize(dt):
    return 2 if dt == BF16 else 4


class Arena:
    def __init__(self, nc, base=16512, limit=229344):
        self.nc, self.base, self.limit, self.top, self.n = nc, base, limit, base, 0

    def alloc(self, name, shape, dt=F32):
        el = 1
        for s in shape[1:]:
            el *= s
        size = (el * _dtsize(dt) + 63) // 64 * 64
        off = self.top
        self.top += size
        assert self.top <= self.limit, (name, self.top)
        self.n += 1
        return T(self.nc.alloc_sbuf_tensor_at(f"{name}{self.n}", list(shape), dt, offset=off), name)

    def mark(self):
        return self.top

    def reset(self, m):
        self.top = m


D = 1024
LC = 256
LL = 4096
TALL = LC + LL
DEPTH = 4
NIN = 2584
DFF = 2816
DN_ALPHA = (2 * DEPTH) ** 0.25
LN_EPS = 1e-6
O_Z, O_XBC, O_DT, O_POOL, O_QKV, O_GATE, O_A, O_B, O_UV = 0, 256, 768, 776, 1032, 1800, 2056, 2064, 2072
FM_GROUPS = [(O_XBC, 512), (O_POOL, 256), (O_QKV, 768), (O_UV, 512)]
NFM = 2048
NTM = 536
C_SSDCW, C_SSDCB, C_GDNCW, C_FFNCW, C_FFNCB, C_PSCALE, C_BMOD = 0, 28, 32, 74, 140, 162, 164
NCOL = 164 + 48
R_SSDNW, R_GDNNW, R_GLNW, R_GLNB, R_LN1W, R_LN1B, R_LN2W, R_LN2B = 0, 256, 320, 576, 832, 1856, 2880, 3904
R_SSDD, R_GBS, R_SDTB, R_SALOG, R_GDTB, R_GALOG, R_BG1, R_BG2 = 4928, 5184, 5696, 5704, 5712, 5720, 5728, 6752
NROW = 7776
K_ID, K_ONES, K_LF, K_LB, K_NMF, K_NMB, K_LF2, K_LB2, K_NI2F, K_NI2B, K_NS2F, K_NS2B, K_BO2, K_SEL0, K_SEL1 = range(15)
NCONST = 15
NEG = -30000.0


def make_consts():
    k = np.arange(128)[:, None]
    m = np.arange(128)[None, :]
    same = (k // 64) == (m // 64)
    c = np.zeros((128, NCONST, 128), np.float32)
    c[:, K_ID] = (k == m)
    c[:, K_ONES] = 1.0
    c[:, K_LF] = (k <= m)
    c[:, K_LB] = (k >= m)
    c[:, K_NMF] = np.where(m >= k, 0.0, NEG)
    c[:, K_NMB] = np.where(m <= k, 0.0, NEG)
    c[:, K_LF2] = (k <= m) & same
    c[:, K_LB2] = (k >= m) & same
    c[:, K_NI2F] = np.where((m >= k) & same, 0.0, NEG)
    c[:, K_NI2B] = np.where((m <= k) & same, 0.0, NEG)
    c[:, K_NS2F] = np.where((k > m) & same, 0.0, NEG)
    c[:, K_NS2B] = np.where((k < m) & same, 0.0, NEG)
    c[:, K_BO2] = same
    c[:, K_SEL0] = (k < 64) * np.ones_like(m)
    c[:, K_SEL1] = (k >= 64) * np.ones_like(m)
    return c


class Ctx:
    pass


def build(nlayers=DEPTH, stop_after=None, dbg=False, mixers=None, only=None):
    nc = bass.Bass("TRN2", target_bir_lowering=False)
    G = Ctx()
    G.nc = nc
    P = Prog(nc)
    G.P = P

    def din(name, shape):
        return nc.dram_tensor(name, list(shape), F32, kind="ExternalInput")

    G.x_in = din("x_in", [LL, D])
    G.ctx_in = din("ctx_in", [LC, D])
    G.crep = din("crep", [128, 2, 8, 128])
    G.consts_d = din("consts", [128, NCONST, 128])
    G.w_mod = din("w_mod", [DEPTH, D, 6 * D])
    G.w_in = din("w_in", [DEPTH, D, NIN])
    G.w_out = din("w_out", [DEPTH, D, D])
    G.ffn_up = din("ffn_up", [DEPTH, D, 2 * DFF])
    G.ffn_down = din("ffn_down", [DEPTH, DFF, D])
    G.colp = din("colp", [DEPTH, 128, NCOL])
    G.rowp = din("rowp", [DEPTH, 1, NROW])
    G.poolw = din("poolw", [DEPTH, 128, 2, 128])
    G.gws = din("gws", [DEPTH, 128, 4, 128])
    G.pinv_g = din("pinv_g", [128, 2, 64])
    G.pinv_c = din("pinv_c", [128, 2, 256])
    G.out = T(nc.dram_tensor("out", [LL, D], F32, kind="ExternalOutput"), "out")
    G.xs = [T(nc.dram_tensor(f"xs{i}", [TALL, D], F32), f"xs{i}") for i in range(2)]
    G.x1 = T(nc.dram_tensor("x1s", [TALL, D], F32), "x1s")
    G.projF = T(nc.dram_tensor("projF", [NFM, TALL], F32), "projF")
    G.projT = T(nc.dram_tensor("projT", [TALL, NTM], F32), "projT")
    G.convF = T(nc.dram_tensor("convF", [1280, TALL], F32), "convF")
    G.yT = T(nc.dram_tensor("yT", [D, TALL], BF16), "yT")
    G.hid = T(nc.dram_tensor("hid", [DFF, TALL], BF16), "hid")
    G.dbg = {}
    if dbg:
        G.dbg["projF"] = T(nc.dram_tensor("d_projF", [NFM, TALL], F32, kind="ExternalOutput"))
        G.dbg["projT"] = T(nc.dram_tensor("d_projT", [TALL, NTM], F32, kind="ExternalOutput"))
        G.dbg["mod"] = T(nc.dram_tensor("d_mod", [128, 64], F32, kind="ExternalOutput"))
        G.dbg["gb"] = T(nc.dram_tensor("d_gb", [128, 4096], F32, kind="ExternalOutput"))
        G.dbg["yT"] = T(nc.dram_tensor("d_yT", [D, TALL], BF16, kind="ExternalOutput"))
        G.dbg["convF"] = T(nc.dram_tensor("d_convF", [1280, TALL], F32, kind="ExternalOutput"))
        G.dbg["x1"] = T(nc.dram_tensor("d_x1", [TALL, D], F32, kind="ExternalOutput"))
        G.dbg["x2"] = T(nc.dram_tensor("d_x2", [TALL, D], F32, kind="ExternalOutput"))
        G.dbg["st"] = T(nc.dram_tensor("d_st", [64, 2 * 4 * 64], F32, kind="ExternalOutput"))

    A = Arena(nc)
    G.A = A
    G.psum = [T(nc.alloc_psum_tensor(f"ps{i}", [128, 512], F32), f"ps{i}") for i in range(8)]
    G.consts = A.alloc("consts", [128, NCONST, 128])
    G.csil = A.alloc("csil", [128, 2, 8, 128])
    G.colp_sb = A.alloc("colp", [128, NCOL])
    G.rowp_sb = A.alloc("rowp", [128, NROW])
    G.modc = A.alloc("modc", [128, 4, 8, 2])
    G.gb = A.alloc("gb", [128, 2, 2, 1024])
    G.eps = A.alloc("eps", [128, 1])
    G.sst = [A.alloc("sst", [64, 4, 64]) for _ in range(2)]
    G.gst = [A.alloc("gst", [64, 4, 64]) for _ in range(2)]
    G.gsb = [A.alloc("gsb", [64, 4, 64], BF16) for _ in range(2)]
    if mixers is not None:
        G.mixers = mixers
    G.dumps_on = dbg
    G.dumps = {}

    P.dma(G.consts[:], G.consts_d[:, :, :], writes=[G.consts.b])
    P.dma(G.csil[:], G.crep[:, :, :, :], writes=[G.csil.b])
    P.op("act", lambda e: e.activation(G.csil[:], G.csil[:], AF.Silu), [G.csil.b], [G.csil.b])
    P.op("dve", lambda e: e.memset(G.eps[:], LN_EPS), [], [G.eps.b])

    streams = [("ctx", 0, LC, 1), ("lat", LC, LL, 0)]
    if only is not None:
        streams = [st_ for st_ in streams if st_[0] in only]
    for l in range(nlayers):
        G.l = l
        xin = G.xs[l % 2]
        mod_phase(G, l)
        if stop_after == "mod":
            break
        phase_a(G, l, streams)
        if stop_after == "A":
            break
        prep_pass(G, l, streams)
        mix = G.mixers if hasattr(G, "mixers") else ("pool", "gmlp", "ssd", "gdn")
        for d_ in range(2):
            P.op("dve", lambda e, d_=d_: e.memset(G.sst[d_][:], 0.0), [], [G.sst[d_].b])
            P.op("dve", lambda e, d_=d_: e.memset(G.gst[d_][:], 0.0), [], [G.gst[d_].b])
            P.op("dve", lambda e, d_=d_: e.memset(G.gsb[d_][:], 0.0), [], [G.gsb[d_].b])
        for st_ in streams:
            if "pool" in mix:
                pool_mixer(G, l, st_)
            if "gmlp" in mix and "ssd" in mix:
                mg = A.mark()
                gt_ = gmlp_setup(G, l)
                ssd_mixer(G, l, st_, co_tile=gt_)
                A.reset(mg)
            else:
                if "gmlp" in mix:
                    gmlp_mixer(G, l, st_)
                if "ssd" in mix:
                    ssd_mixer(G, l, st_)
            if "gdn" in mix:
                gdn_mixer(G, l, st_)
        if stop_after == "mix":
            break
        last = (l == DEPTH - 1)
        cd_streams = [st_ for st_ in streams if not (last and st_[0] == "ctx")]
        mC = A.mark()
        hT2 = A.alloc("hT2", [128, 8, TALL], BF16)
        phase_c(G, l, cd_streams, hT2)
        if stop_after == "C":
            break
        phase_d1(G, l, cd_streams, hT2)
        A.reset(mC)
        if stop_after == "D1":
            break
        phase_d2(G, l, cd_streams, last)

    if dbg:
        P.barrier()
        m0 = A.mark()
        tlo = min(st_[1] for st_ in streams)
        thi = max(st_[1] + st_[2] for st_ in streams)
        TW = thi - tlo
        tmp = A.alloc("dbgtmp", [128, TALL])
        tb = A.alloc("dbgtb", [128, TALL], BF16)
        P.dma(G.dbg["mod"][:, :], G.modc[:].rearrange("p a k s -> p (a k s)"), reads=[G.modc.b], writes=[G.dbg["mod"].b], q="pool")
        P.dma(G.dbg["gb"][:, :], G.gb[:].rearrange("p s g d -> p (s g d)"), reads=[G.gb.b], writes=[G.dbg["gb"].b], q="pool")
        if stop_after == "A":
            for r in range(NFM // 128):
                P.dma(tmp[:, 0:TW], G.projF[r * 128:(r + 1) * 128, tlo:thi], reads=[G.projF.b], writes=[tmp.b])
                P.dma(G.dbg["projF"][r * 128:(r + 1) * 128, tlo:thi], tmp[:, 0:TW], reads=[tmp.b], writes=[G.dbg["projF"].b], q="pool")
            for r in range(tlo // 128, thi // 128):
                P.dma(tmp[:, 0:NTM], G.projT[r * 128:(r + 1) * 128, :], reads=[G.projT.b], writes=[tmp.b])
                P.dma(G.dbg["projT"][r * 128:(r + 1) * 128, :], tmp[:, 0:NTM], reads=[tmp.b], writes=[G.dbg["projT"].b], q="pool")
        if stop_after is None:
            for r in range(tlo // 128, thi // 128):
                P.dma(tmp[:, 0:D], G.x1[r * 128:(r + 1) * 128, :], reads=[G.x1.b], writes=[tmp.b])
                P.dma(G.dbg["x1"][r * 128:(r + 1) * 128, :], tmp[:, 0:D], reads=[tmp.b], writes=[G.dbg["x1"].b], q="pool")
                P.dma(tmp[:, 0:D], G.xs[nlayers % 2][r * 128:(r + 1) * 128, :], reads=[G.xs[nlayers % 2].b], writes=[tmp.b])
                P.dma(G.dbg["x2"][r * 128:(r + 1) * 128, :], tmp[:, 0:D], reads=[tmp.b], writes=[G.dbg["x2"].b], q="pool")
        if stop_after == "mix":
            for d_ in range(2):
                P.dma(G.dbg["st"][:, d_ * 256:(d_ + 1) * 256], G.sst[d_][:].rearrange("p h d -> p (h d)"), reads=[G.sst[d_].b], writes=[G.dbg["st"].b], q="pool")
            rows = dict(ssd=(0, 2), pool=(2, 4), gdn=(4, 6), gmlp=(6, 8))
            for mname in (G.mixers if hasattr(G, "mixers") else rows.keys()):
                for r in range(*rows[mname]):
                    P.dma(tb[:, 0:TW], G.yT[r * 128:(r + 1) * 128, tlo:thi], reads=[G.yT.b], writes=[tb.b])
                    P.dma(G.dbg["yT"][r * 128:(r + 1) * 128, tlo:thi], tb[:, 0:TW], reads=[tb.b], writes=[G.dbg["yT"].b], q="pool")
            for r in range(1280 // 128):
                P.dma(tmp[:, 0:TW], G.convF[r * 128:(r + 1) * 128, tlo:thi], reads=[G.convF.b], writes=[tmp.b])
                P.dma(G.dbg["convF"][r * 128:(r + 1) * 128, tlo:thi], tmp[:, 0:TW], reads=[tmp.b], writes=[G.dbg["convF"].b], q="pool")
        P.op("pool", None, reads=[v.b for v in G.dbg.values()])
        A.reset(m0)
    P.op("pool", None, reads=[G.out.b])
    P.emit()
    return nc


def mod_phase(G, l):
    nc, P, A = G.nc, G.P, G.A
    m0 = A.mark()
    P.dma(G.colp_sb[:], G.colp[l, :, :], writes=[G.colp_sb.b])
    P.dma(G.rowp_sb[:], G.rowp[l, 0:1, :].partition_broadcast(128), writes=[G.rowp_sb.b])
    wm = [A.alloc("wm", [128, 8, 512]) for _ in range(2)]
    pc = G.psum[0]
    pg = [G.psum[1], G.psum[2]]
    wsrc = G.w_mod[l, :, :].rearrange("(k p) c -> p k c", p=128)
    colvec = {0: 0, 1: 1, 3: 2, 4: 3}
    for n in range(12):
        w = wm[n % 2]
        P.dma(w[:], wsrc[:, :, n * 512:(n + 1) * 512], writes=[w.b])
        vec, half = n // 2, n % 2
        if vec in colvec:
            a = colvec[vec]
            for j in range(4):
                kc = half * 4 + j

                def f(e, w=w, j=j, a=a, kc=kc):
                    for k in range(8):
                        ins = e.matmul(pc[:, (a * 8 + kc) * 2:(a * 8 + kc) * 2 + 2], w[:, k, j * 128:(j + 1) * 128],
                                       G.csil[:, :, k, 0], start=(k == 0), stop=(k == 7))
                    return ins
                P.op("pe", f, [w.b, G.csil.b], [pc.b])
        else:
            gi = 0 if vec == 2 else 1
            boff = R_BG1 if gi == 0 else R_BG2
            for s in range(2):
                def f(e, w=w, s=s):
                    for k in range(8):
                        ins = e.matmul(pg[s][:, :], G.csil[:, s, k, :], w[:, k, :], start=(k == 0), stop=(k == 7))
                    return ins
                P.op("pe", f, [w.b, G.csil.b], [pg[s].b])
                P.op("dve", lambda e, s=s, gi=gi, half=half, boff=boff: e.tensor_tensor(
                    G.gb[:, s, gi, half * 512:(half + 1) * 512], pg[s][:, :],
                    G.rowp_sb[:, boff + half * 512: boff + (half + 1) * 512], ALU.add),
                    [pg[s].b, G.rowp_sb.b], [G.gb.b])
    bm = G.colp_sb[:, C_BMOD:C_BMOD + 48].rearrange("p (v k) -> p v k", v=6)
    for vec, a in colvec.items():
        for s in range(2):
            P.op("dve", lambda e, vec=vec, a=a, s=s: e.tensor_tensor(
                G.modc[:, a, :, s], pc[:, a * 16:(a + 1) * 16].rearrange("p (k s) -> p k s", s=2)[:, :, s],
                bm[:, vec, :], ALU.add), [pc.b, G.colp_sb.b], [G.modc.b])
    for a in (1, 3):
        P.op("dve", lambda e, a=a: e.tensor_scalar_add(G.modc[:, a, :, :], G.modc[:, a, :, :], 1.0), [G.modc.b], [G.modc.b])
    P.barrier()
    A.reset(m0)


def ln_stats(G, xt, np_, st, mv, rstd, P=None):
    P = P or G.P
    def f(e):
        e.bn_stats(st[0:np_, 0, :], xt[0:np_, 0:512])
        return e.bn_stats(st[0:np_, 1, :], xt[0:np_, 512:1024])
    P.op("dve", f, [xt.b], [st.b])
    P.op("dve", lambda e: e.bn_aggr(mv[0:np_, :], st[0:np_, :, :].rearrange("p a b -> p (a b)")), [st.b], [mv.b])
    P.op("act", lambda e: e.activation(rstd[0:np_, :], mv[0:np_, 1:2], AF.Sqrt, bias=G.eps[0:np_, :]), [mv.b, G.eps.b], [rstd.b])
    P.op("dve", lambda e: e.reciprocal(rstd[0:np_, :], rstd[0:np_, :]), [rstd.b], [rstd.b])


def load_weight_bf16(G, dst, src_rows, ncols, stage, nk):
    P = G.P
    for k in range(nk):
        s = stage[k % len(stage)]
        P.dma(s[:, 0:ncols], src_rows(k), writes=[s.b])
        P.op("pool", lambda e, s=s, k=k: e.tensor_copy(dst[:, k, 0:ncols], s[:, 0:ncols]), [s.b], [dst.b])


def phase_a(G, l, streams):
    nc, P, A = G.nc, G.P, G.A
    m0 = A.mark()
    wi = A.alloc("wi", [128, 8, NIN], BF16)
    stage = [A.alloc("wstage", [128, NIN]) for _ in range(2)]
    load_weight_bf16(G, wi, lambda k: G.w_in[l, k * 128:(k + 1) * 128, :], NIN, stage, 8)
    xt = [A.alloc("xt", [128, D]) for _ in range(2)]
    xn = [A.alloc("xn", [128, D]) for _ in range(4)]
    st = [A.alloc("st", [128, 2, 6]) for _ in range(2)]
    mv = [A.alloc("mv", [128, 2]) for _ in range(2)]
    rstd = [A.alloc("rstd", [128, 1]) for _ in range(2)]
    hT = [A.alloc("hT", [128, 8, 512], BF16) for _ in range(2)]
    oF = [A.alloc("oF", [128, 512]) for _ in range(3)]
    oT = [A.alloc("oT", [128, NTM]) for _ in range(2)]
    pT = [G.psum[0], G.psum[1]]
    pF = [G.psum[2], G.psum[3]]
    pTa = [G.psum[4], G.psum[5]]
    pTb = [G.psum[6], G.psum[7]]
    ident = G.consts[:, K_ID, :]
    cnt = dict(t=0, b=0, f=0, o=0)
    xsrc = G.xs[l % 2]
    blocks = []
    for (name, tok0, L, s) in streams:
        bs = 512 if L >= 512 else L
        for b0 in range(0, L, bs):
            blocks.append((name, tok0, L, s, b0, bs))

    def front(R, blk, h):
        name, tok0, L, s, b0, bs = blk
        nt, W = bs // 128, bs
        for m in range(nt):
            t = xt[cnt["t"] % 2]
            q = cnt["t"] % 2
            cnt["t"] += 1
            r0 = b0 + m * 128
            if l == 0:
                src = (G.ctx_in if name == "ctx" else G.x_in)[r0:r0 + 128, :]
                R.dma(t[:], src, writes=[t.b])
            else:
                R.dma(t[:], xsrc[tok0 + r0: tok0 + r0 + 128, :], reads=[xsrc.b], writes=[t.b])
            ln_stats(G, t, 128, st[q], mv[q], rstd[q], P=R)
            R.op("dve", lambda e, t=t, q=q, m=m: e.tensor_scalar(xn[m][:], t[:], mv[q][:, 0:1], rstd[q][:, 0:1],
                                                                ALU.subtract, ALU.mult), [t.b, mv[q].b, rstd[q].b], [xn[m].b])
        for k in range(8):
            p = pT[k % 2]

            def f(e, p=p, k=k, nt=nt):
                for m in range(nt):
                    ins = e.transpose(p[:, m * 128:(m + 1) * 128], xn[m][:, k * 128:(k + 1) * 128], ident)
                return ins
            R.op("pe", f, [xn[m].b for m in range(nt)] + [G.consts.b], [p.b])
            R.op("act", lambda e, p=p, k=k, h=h, W=W, s=s: e.activation(
                h[:, k, 0:W], p[:, 0:W], AF.Identity, bias=G.modc[:, 0, k, s:s + 1], scale=G.modc[:, 1, k, s:s + 1]),
                [p.b, G.modc.b], [h.b])

    def main(R, blk, h):
        name, tok0, L, s, b0, bs = blk
        nt, W = bs // 128, bs
        row = 0
        for (c0, n) in FM_GROUPS:
            for j in range(n // 128):
                p = pF[cnt["f"] % 2]
                o = oF[cnt["f"] % 3]
                ev = "act" if cnt["f"] % 2 == 0 else "dve"
                cnt["f"] += 1
                cc = c0 + j * 128

                def f(e, p=p, cc=cc, h=h, W=W):
                    for k in range(8):
                        ins = e.matmul(p[:, 0:W], wi[:, k, cc:cc + 128], h[:, k, 0:W], start=(k == 0), stop=(k == 7))
                    return ins
                R.op("pe", f, [wi.b, h.b], [p.b])
                if ev == "act":
                    R.op("act", lambda e, p=p, o=o, W=W: e.activation(o[:, 0:W], p[:, 0:W], AF.Copy), [p.b], [o.b])
                else:
                    R.op("dve", lambda e, p=p, o=o, W=W: e.tensor_copy(o[:, 0:W], p[:, 0:W]), [p.b], [o.b])
                R.dma(G.projF[row:row + 128, tok0 + b0: tok0 + b0 + W], o[:, 0:W], reads=[o.b], writes=[G.projF.b],
                      q=("act" if ev == "act" else "pool"))
                row += 128
        for m in range(nt):
            pa = pTa[cnt["o"] % 2]
            pb = pTb[cnt["o"] % 2]
            o = oT[cnt["o"] % 2]
            cnt["o"] += 1

            def f(e, pa=pa, pb=pb, h=h, m=m):
                for k in range(8):
                    lw = h[:, k, m * 128:(m + 1) * 128]
                    e.matmul(pa[:, 0:256], lw, wi[:, k, O_Z:O_Z + 256], start=(k == 0), stop=(k == 7))
                    e.matmul(pb[:, 0:272], lw, wi[:, k, O_GATE:O_GATE + 272], start=(k == 0), stop=(k == 7))
                for k in range(8):
                    lw = h[:, k, m * 128:(m + 1) * 128]
                    ins = e.matmul(pa[:, 256:264], lw, wi[:, k, O_DT:O_DT + 8], start=(k == 0), stop=(k == 7))
                return ins
            R.op("pe", f, [wi.b, h.b], [pa.b, pb.b])
            R.op("act", lambda e, pa=pa, o=o: e.activation(o[:, 0:264], pa[:, 0:264], AF.Copy), [pa.b], [o.b])
            R.op("dve", lambda e, pb=pb, o=o: e.tensor_copy(o[:, 264:536], pb[:, 0:272]), [pb.b], [o.b])
            r0 = tok0 + b0 + m * 128
            R.dma(G.projT[r0:r0 + 128, :], o[:], reads=[o.b], writes=[G.projT.b], q="pool")

    r0_ = Rec()
    front(r0_, blocks[0], hT[0])
    merge(P, [r0_])
    for bi, blk in enumerate(blocks):
        recs = [Rec()]
        main(recs[0], blk, hT[bi % 2])
        if bi + 1 < len(blocks):
            recs.append(Rec())
            front(recs[1], blocks[bi + 1], hT[(bi + 1) % 2])
        merge(P, recs)
    P.barrier()
    A.reset(m0)


def dbgdump(G, name, t, ap, shape, dt=F32, P=None):
    if not getattr(G, "dumps_on", False) or name in G.dumps:
        return
    o = T(G.nc.dram_tensor("dd_" + name, list(shape), dt, kind="ExternalOutput"))
    G.dumps[name] = o
    nd = len(shape)
    P = P or G.P
    P.dma(o[tuple(slice(None) for _ in range(nd))], ap, reads=[t.b], writes=[o.b], q="pool")
    P.op("pool", None, reads=[o.b])


def bc_ap(ap, dims):
    return bass.AP(ap.tensor, ap.offset, [list(ap.ap[0])] + [list(d) for d in dims])


def prep_pass(G, l, streams):
    P, A = G.P, G.A
    m0 = A.mark()
    xin = [A.alloc("cin", [128, LL + 6]) for _ in range(2)]
    acc = [A.alloc("cacc", [128, LL]) for _ in range(2)]
    sq = [A.alloc("csq", [128, 512]) for _ in range(2)]
    rn = [A.alloc("crn", [128, 512]) for _ in range(2)]
    ps = [G.psum[0], G.psum[1]]
    bo2 = G.consts[:, K_BO2, :]
    chunks = []
    for j in range(4):
        chunks.append((j * 128, j * 128, C_SSDCW + j * 7, C_SSDCB + j, "p"))
    for j in range(6):
        chunks.append((768 + j * 128, 512 + j * 128, C_GDNCW + j * 7, None, "q" if j < 2 else ("k" if j < 4 else "p")))
    cnt = 0
    sc = 0
    for (name, tok0, L, s) in streams:
        for t in xin:
            P.op("dve", lambda e, t=t: e.memset(t[:, 0:3], 0.0), [], [t.b])
            P.op("dve", lambda e, t=t, L=L: e.memset(t[:, L + 3:L + 6], 0.0), [], [t.b])
        for (src, dst, wo, bo, kind) in chunks:
            t = xin[cnt % 2]
            a = acc[cnt % 2]
            cnt += 1
            w = G.colp_sb
            P.dma(t[:, 3:L + 3], G.projF[src:src + 128, tok0:tok0 + L], reads=[G.projF.b], writes=[t.b])
            P.op("dve", lambda e, t=t, a=a, L=L, wo=wo: e.tensor_scalar(a[:, 0:L], t[:, 0:L], w[:, wo:wo + 1], None, ALU.mult),
                 [t.b, w.b], [a.b])
            for tap in range(1, 7):
                P.op("dve", lambda e, t=t, a=a, L=L, wo=wo, tap=tap: e.scalar_tensor_tensor(
                    a[:, 0:L], t[:, tap:tap + L], w[:, wo + tap:wo + tap + 1], a[:, 0:L], ALU.mult, ALU.add),
                    [t.b, w.b, a.b], [a.b])
            if bo is not None:
                P.op("act", lambda e, a=a, L=L, bo=bo: e.activation(a[:, 0:L], a[:, 0:L], AF.Silu, bias=w[:, bo:bo + 1]),
                     [a.b, w.b], [a.b])
            else:
                P.op("act", lambda e, a=a, L=L: e.activation(a[:, 0:L], a[:, 0:L], AF.Silu), [a.b], [a.b])
            if kind in ("q", "k"):
                for sl in range(0, L, 512):
                    Wd = min(512, L - sl)
                    q_, r_, p_ = sq[sc % 2], rn[sc % 2], ps[sc % 2]
                    sc += 1
                    P.op("act", lambda e, a=a, q_=q_, sl=sl, Wd=Wd: e.activation(q_[:, 0:Wd], a[:, sl:sl + Wd], AF.Square), [a.b], [q_.b])
                    P.op("pe", lambda e, q_=q_, p_=p_, Wd=Wd: e.matmul(p_[:, 0:Wd], bo2, q_[:, 0:Wd], start=True, stop=True),
                         [q_.b, G.consts.b], [p_.b])
                    P.op("act", lambda e, r_=r_, p_=p_, Wd=Wd: e.activation(r_[:, 0:Wd], p_[:, 0:Wd], AF.Sqrt, bias=G.eps[:, :]),
                         [p_.b, G.eps.b], [r_.b])
                    P.op("dve", lambda e, r_=r_, Wd=Wd: e.reciprocal(r_[:, 0:Wd], r_[:, 0:Wd]), [r_.b], [r_.b])
                    if kind == "q":
                        P.op("dve", lambda e, a=a, r_=r_, sl=sl, Wd=Wd: e.scalar_tensor_tensor(
                            a[:, sl:sl + Wd], a[:, sl:sl + Wd], 0.125, r_[:, 0:Wd], ALU.mult, ALU.mult), [a.b, r_.b], [a.b])
                    else:
                        P.op("dve", lambda e, a=a, r_=r_, sl=sl, Wd=Wd: e.tensor_tensor(
                            a[:, sl:sl + Wd], a[:, sl:sl + Wd], r_[:, 0:Wd], ALU.mult), [a.b, r_.b], [a.b])
            P.dma(G.convF[dst:dst + 128, tok0:tok0 + L], a[:, 0:L], reads=[a.b], writes=[G.convF.b], q="pool")
    P.barrier()
    A.reset(m0)


def pool_mixer(G, l, stream):
    P, A = G.P, G.A
    name, tok0, L, s = stream
    m0 = A.mark()
    RW = 64 if name == "lat" else L
    NR = L // RW
    PW = RW + 16
    F = NR * PW
    xp = A.alloc("pxp", [128, NR, PW])
    ca = A.alloc("pca", [128, NR, PW])
    cb = A.alloc("pcb", [128, NR, PW])
    tmp = A.alloc("ptmp", [128, NR, RW])
    pooled = A.alloc("ppool", [128, NR, RW], BF16)
    pinv = A.alloc("pinv", [128, 2, RW])
    pwf = A.alloc("pwf", [128, 2, 128])
    pwb = A.alloc("pwb", [128, 2, 128], BF16)
    yo = [A.alloc("pyo", [128, 512], BF16) for _ in range(2)]
    ps = [G.psum[2], G.psum[3]]
    P.dma(pinv[:], (G.pinv_g if name == "lat" else G.pinv_c)[:, :, :], writes=[pinv.b])
    P.dma(pwf[:], G.poolw[l, :, :, :], writes=[pwf.b])
    P.op("pool", lambda e: e.tensor_copy(pwb[:], pwf[:]), [pwf.b], [pwb.b])
    fl = lambda t: t[:].rearrange("p r w -> p (r w)")
    cnt = 0
    for jc in range(2):
        P.op("dve", lambda e: e.memset(xp[:], 0.0), [], [xp.b])
        P.dma(xp[:, :, 8:8 + RW], G.projF[512 + jc * 128:512 + (jc + 1) * 128, tok0:tok0 + L].rearrange("p (r w) -> p r w", w=RW),
              reads=[G.projF.b], writes=[xp.b])
        xf, af, bf = fl(xp), fl(ca), fl(cb)
        P.op("dve", lambda e: e.tensor_tensor(af[:, 0:F - 1], xf[:, 0:F - 1], xf[:, 1:F], ALU.add), [xp.b], [ca.b])
        if jc == 0:
            P.op("dve", lambda e: e.tensor_tensor(bf[64:128, 0:F - 3], af[64:128, 0:F - 3], af[64:128, 2:F - 1], ALU.add), [ca.b], [cb.b])
            srcs = [(0, 64, 2, ca), (64, 128, 4, cb)]
        else:
            P.op("dve", lambda e: e.tensor_tensor(bf[:, 0:F - 3], af[:, 0:F - 3], af[:, 2:F - 1], ALU.add), [ca.b], [cb.b])
            P.op("dve", lambda e: e.tensor_tensor(af[:, 0:F - 7], bf[:, 0:F - 7], bf[:, 4:F - 3], ALU.add), [cb.b, ca.b], [ca.b])
            P.op("dve", lambda e: e.tensor_tensor(bf[64:128, 0:F - 15], af[64:128, 0:F - 15], af[64:128, 8:F - 7], ALU.add), [ca.b, cb.b], [cb.b])
            srcs = [(0, 64, 8, ca), (64, 128, 16, cb)]
        for (lo, hi, w, cw) in srcs:
            o = 8 - w // 2
            P.op("dve", lambda e, lo=lo, hi=hi, cw=cw, o=o, jc=jc: e.tensor_tensor(
                tmp[lo:hi, :, :], cw[lo:hi, :, o:o + RW], bc_ap(pinv[lo:hi, jc, :], [[0, NR], [1, RW]]), ALU.mult),
                [cw.b, pinv.b], [tmp.b])
            P.op("dve", lambda e, lo=lo, hi=hi: e.tensor_tensor(pooled[lo:hi, :, :], tmp[lo:hi, :, :], xp[lo:hi, :, 8:8 + RW], ALU.subtract),
                 [tmp.b, xp.b], [pooled.b])
        pf = pooled[:].rearrange("p r w -> p (r w)")
        for sl in range(0, L, 512):
            Wd = min(512, L - sl)
            p_, y_ = ps[cnt % 2], yo[cnt % 2]
            cnt += 1
            P.op("pe", lambda e, p_=p_, sl=sl, Wd=Wd, jc=jc: e.matmul(p_[:, 0:Wd], pwb[:, jc, :], pf[:, sl:sl + Wd], start=True, stop=True),
                 [pwb.b, pooled.b], [p_.b])
            P.op("act", lambda e, p_=p_, y_=y_, Wd=Wd, jc=jc: e.activation(
                y_[:, 0:Wd], p_[:, 0:Wd], AF.Copy, scale=G.colp_sb[:, C_PSCALE + jc:C_PSCALE + jc + 1]), [p_.b, G.colp_sb.b], [y_.b])
            P.dma(G.yT[256 + jc * 128:256 + (jc + 1) * 128, tok0 + sl:tok0 + sl + Wd], y_[:, 0:Wd], reads=[y_.b], writes=[G.yT.b], q="act")
    P.barrier()
    A.reset(m0)


def gmlp_setup(G, l):
    P, A = G.P, G.A
    NB = 2
    uv = [A.alloc("guv", [128, 4, 128]) for _ in range(NB)]
    gt = [A.alloc("ggt", [128, 4, 128]) for _ in range(NB)]
    vt = [A.alloc("gvt", [128, 256]) for _ in range(NB)]
    vb = [A.alloc("gvb", [128, 256], BF16) for _ in range(NB)]
    st = [A.alloc("gst_", [128, 6]) for _ in range(NB)]
    mv = [A.alloc("gmv", [128, 2]) for _ in range(NB)]
    rs = [A.alloc("grs", [128, 1]) for _ in range(NB)]
    tm = [A.alloc("gtm", [128, 2, 128]) for _ in range(NB)]
    yo = [A.alloc("gyo", [128, 2, 128], BF16) for _ in range(NB)]
    wsf = A.alloc("gwsf", [128, 4, 128])
    wsb = A.alloc("gwsb", [128, 4, 128], BF16)
    P.dma(wsf[:], G.gws[l, :, :, :], writes=[wsf.b])
    P.op("pool", lambda e: e.tensor_copy(wsb[:], wsf[:]), [wsf.b], [wsb.b])
    pT = [G.psum[4], G.psum[5]]
    pq = [G.psum[6], G.psum[7]]
    ident = G.consts[:, K_ID, :]
    rp = G.rowp_sb
    def tile(R, stream, ti):
        name, tok0, L, s = stream
        i = ti % NB
        u, g, v, vbb, p1, p2, tt, y = uv[i], gt[i], vt[i], vb[i], pT[i], pq[i], tm[i], yo[i]
        t0 = tok0 + ti * 128
        R.dma(u[:], G.projF[1536:2048, t0:t0 + 128].rearrange("(j p) t -> p j t", p=128), reads=[G.projF.b], writes=[u.b])
        uf = u[:].rearrange("p j t -> p (j t)")
        gf = g[:].rearrange("p j t -> p (j t)")
        R.op("dve", lambda e, uf=uf, gf=gf: e.tensor_tensor(gf, uf, uf, ALU.mult), [u.b], [g.b])
        R.op("dve", lambda e, gf=gf: e.tensor_scalar(gf, gf, 0.044715, 1.0, ALU.mult, ALU.add), [g.b], [g.b])
        R.op("dve", lambda e, uf=uf, gf=gf: e.tensor_tensor(gf, gf, uf, ALU.mult), [g.b, u.b], [g.b])
        R.op("act", lambda e, gf=gf: e.activation(gf, gf, AF.Sigmoid, scale=1.5957691216057308), [g.b], [g.b])
        R.op("dve", lambda e, uf=uf, gf=gf: e.tensor_tensor(gf, gf, uf, ALU.mult), [g.b, u.b], [g.b])

        def f(e, g=g, p1=p1):
            e.transpose(p1[:, 0:128], g[:, 2, :], ident)
            return e.transpose(p1[:, 128:256], g[:, 3, :], ident)
        R.op("pe", f, [g.b, G.consts.b], [p1.b])
        R.op("act", lambda e, v=v, p1=p1: e.activation(v[:], p1[:, 0:256], AF.Copy), [p1.b], [v.b])
        R.op("dve", lambda e, v=v, i=i: e.bn_stats(st[i][:], v[:]), [v.b], [st[i].b])
        R.op("dve", lambda e, i=i: e.bn_aggr(mv[i][:], st[i][:]), [st[i].b], [mv[i].b])
        R.op("act", lambda e, i=i: e.activation(rs[i][:], mv[i][:, 1:2], AF.Sqrt, bias=G.eps[:, :]), [mv[i].b, G.eps.b], [rs[i].b])
        R.op("dve", lambda e, i=i: e.reciprocal(rs[i][:], rs[i][:]), [rs[i].b], [rs[i].b])
        R.op("dve", lambda e, v=v, i=i: e.tensor_scalar(v[:], v[:], mv[i][:, 0:1], rs[i][:, 0:1], ALU.subtract, ALU.mult),
             [v.b, mv[i].b, rs[i].b], [v.b])
        R.op("dve", lambda e, v=v: e.tensor_tensor(v[:], v[:], rp[:, R_GLNW:R_GLNW + 256], ALU.mult), [v.b, rp.b], [v.b])
        R.op("dve", lambda e, v=v, vbb=vbb: e.tensor_tensor(vbb[:], v[:], rp[:, R_GLNB:R_GLNB + 256], ALU.add), [v.b, rp.b], [vbb.b])

        def f2(e, vbb=vbb, p2=p2):
            e.matmul(p2[:, 0:256], vbb[:, 0:128], wsb[:, 0:2, :].rearrange("p g i -> p (g i)"), start=True, stop=True)
            return e.matmul(p2[:, 256:512], vbb[:, 128:256], wsb[:, 2:4, :].rearrange("p g i -> p (g i)"), start=True, stop=True)
        R.op("pe", f2, [vbb.b, wsb.b], [p2.b])
        for q in range(2):
            for hh in range(2):
                gg = 2 * q + hh
                lo, hi = hh * 64, (hh + 1) * 64
                c0 = q * 256 + hh * 128
                R.op("dve", lambda e, lo=lo, hi=hi, c0=c0, q=q, gg=gg, tt=tt, p2=p2: e.tensor_tensor(
                    tt[lo:hi, q, :], p2[lo:hi, c0:c0 + 128], rp[lo:hi, R_GBS + gg * 128:R_GBS + (gg + 1) * 128], ALU.add),
                    [p2.b, rp.b], [tt.b])
                R.op("dve", lambda e, lo=lo, hi=hi, q=q, tt=tt, y=y, g=g: e.tensor_tensor(
                    y[lo:hi, q, :], tt[lo:hi, q, :], g[lo:hi, q, :], ALU.mult), [tt.b, g.b], [y.b])
        R.dma(G.yT[768:1024, t0:t0 + 128].rearrange("(j p) t -> p j t", p=128), y[:], reads=[y.b], writes=[G.yT.b], q="pool")
    return tile


def gmlp_mixer(G, l, stream):
    P, A = G.P, G.A
    m0 = A.mark()
    tile = gmlp_setup(G, l)
    for ti in range(stream[2] // 128):
        r = Rec()
        tile(r, stream, ti)
        merge(P, [r])
    P.barrier()
    A.reset(m0)


def softplus_small(P, out, x, tmp, reads, extra_w=()):
    P.op("dve", lambda e: e.scalar_tensor_tensor(tmp, x, -1.0, x, ALU.mult, ALU.max), reads, [extra_w[0]])
    P.op("act", lambda e: e.activation(tmp, tmp, AF.Exp, scale=-1.0), [extra_w[0]], [extra_w[0]])
    P.op("act", lambda e: e.activation(tmp, tmp, AF.Ln, bias=1.0), [extra_w[0]], [extra_w[0]])
    P.op("dve", lambda e: e.scalar_tensor_tensor(out, x, 0.0, tmp, ALU.max, ALU.add), reads + [extra_w[0]], [extra_w[1]])


def ssd_mixer(G, l, stream, co_tile=None):
    P, A = G.P, G.A
    name, tok0, L, s = stream
    NT = L // 128
    m0 = A.mark()
    rp = G.rowp_sb
    C = G.consts
    ident = C[:, K_ID, :]
    ones = C[:, K_ONES, :]
    nega = A.alloc("snega", [128, 8])
    negones = A.alloc("snegones", [128, 128])
    yacc = A.alloc("syacc", [128, NT, 256])
    P.op("act", lambda e: e.activation(nega[:], rp[:, R_SALOG:R_SALOG + 8], AF.Exp), [rp.b], [nega.b])
    P.op("dve", lambda e: e.tensor_scalar(nega[:], nega[:], -1.0, None, ALU.mult), [nega.b], [nega.b])
    P.op("dve", lambda e: e.memset(negones[:], -1.0), [], [negones.b])
    NB = 2
    def mk(nm, shape, dt=F32):
        return [A.alloc(nm, shape, dt) for _ in range(NB)]
    zt, cx, dtt, dtm, dta = mk("szt", [128, 264]) + mk("szt", [128, 264]), mk("scx", [128, 2, 128]) + mk("scx", [128, 2, 128]), mk("sdt", [128, 8]), mk("sdtm", [128, 8]), mk("sdta", [128, 8])
    bc4 = mk("sbc4", [64, 4, 128]) + mk("sbc4", [64, 4, 128])
    xtm, btm, cbb = mk("sxtm", [128, 256]), mk("sbtm", [128, 128], BF16), mk("scbb", [64, 4, 128], BF16)
    ML, Dm, Wm = mk("sML", [128, 4, 128]), mk("sDm", [128, 4, 128]), mk("sWm", [128, 4, 128], BF16)
    scl, dtw = mk("sscl", [128, 12]), mk("sdtw", [128, 8])
    xdt, xdw = mk("sxdt", [128, 256], BF16), mk("sxdw", [128, 256], BF16)
    yt, yg, ss, yo = mk("syt", [128, 256]), mk("syg", [128, 256]), mk("sss", [128, 2]), mk("syo", [128, 2, 128], BF16)
    ps = G.psum
    def body(R, d, ti, i, li, second):
        Ltri = C[:, K_LF if d == 0 else K_LB, :]
        nmask = C[:, K_NMF if d == 0 else K_NMB, :]
        t0 = tok0 + ti * 128
        z_, c_, dt_, dm_, da_, x_, b_, cb_ = zt[li], cx[li], dtt[i], dtm[i], dta[i], xtm[i], btm[i], cbb[i]
        bc_ = bc4[li]
        ml_, D_, W_, sc_, dw_, xd_, xw_ = ML[i], Dm[i], Wm[i], scl[i], dtw[i], xdt[i], xdw[i]
        pA, pB = ps[2 * i], ps[2 * i + 1]
        pC = pD = pA
        R.dma(z_[:], G.projT[t0:t0 + 128, 0:264], reads=[G.projT.b], writes=[z_.b])
        R.dma(c_[:], G.convF[0:256, t0:t0 + 128].rearrange("(j p) t -> p j t", p=128), reads=[G.convF.b], writes=[c_.b])
        R.dma(bc_[:], G.convF[256:512, t0:t0 + 128].rearrange("(j p) t -> p j t", p=64), reads=[G.convF.b], writes=[bc_.b])
        R.op("dve", lambda e, z_=z_, dt_=dt_: e.tensor_tensor(dt_[:], z_[:, 256:264], rp[:, R_SDTB:R_SDTB + 8], ALU.add), [z_.b, rp.b], [dt_.b])
        softplus_small(R, dt_[:], dt_[:], dm_[:], [dt_.b], (dm_.b, dt_.b))
        R.op("dve", lambda e, dt_=dt_, da_=da_: e.tensor_tensor(da_[:], dt_[:], nega[:], ALU.mult), [dt_.b, nega.b], [da_.b])
        if SSD_STOP <= 1:
            return
        def f(e, c_=c_, pA=pA, bc_=bc_):
            e.transpose(pA[:, 0:128], c_[:, 0, :], ident)
            e.transpose(pA[:, 128:256], c_[:, 1, :], ident)
            e.transpose(pA[:, 256:320], bc_[:, 0, :], ident[0:64, 0:64])
            return e.transpose(pA[:, 320:384], bc_[:, 1, :], ident[0:64, 0:64])
        R.op("pe", f, [c_.b, bc_.b, C.b], [pA.b])
        R.op("act", lambda e, x_=x_, pA=pA: e.activation(x_[:], pA[:, 0:256], AF.Copy), [pA.b], [x_.b])
        R.op("act", lambda e, b_=b_, pA=pA: e.activation(b_[:], pA[:, 256:384], AF.Copy), [pA.b], [b_.b])
        R.op("pool", lambda e, cb_=cb_, bc_=bc_: e.tensor_copy(cb_[:], bc_[:]), [bc_.b], [cb_.b])
        if SSD_STOP <= 2:
            return
        def f(e, cb_=cb_, pB=pB):
            e.matmul(pB[:, 0:128], cb_[:, 0, :], cb_[:, 2, :], start=True, stop=True)
            return e.matmul(pB[:, 128:256], cb_[:, 1, :], cb_[:, 3, :], start=True, stop=True)
        R.op("pe", f, [cb_.b], [pB.b])
        if SSD_STOP <= 3:
            return
        R.op("dve", lambda e, ml_=ml_, da_=da_, Ltri=Ltri: e.tensor_tensor(
            ml_[:], bc_ap(Ltri, [[0, 4], [1, 128]]), bc_ap(da_[:, d * 4:d * 4 + 4], [[1, 4], [0, 128]]), ALU.mult), [da_.b, C.b], [ml_.b])
        def f(e, ml_=ml_, pC=pC, nmask=nmask):
            e.matmul(pC[:, :], ones, ml_[:].rearrange("p h i -> p (h i)"), start=True, stop=False)
            for h in range(4):
                e.matmul(pC[:, h * 128:(h + 1) * 128], ml_[:, h, :], negones[:], start=False, stop=False)
            return e.matmul(pC[:, :], ident, bc_ap(nmask, [[0, 4], [1, 128]]), start=False, stop=True)
        R.op("pe", f, [ml_.b, C.b, negones.b], [pC.b])
        R.op("act", lambda e, D_=D_, pC=pC: e.activation(D_[:].rearrange("p h i -> p (h i)"), pC[:, :], AF.Exp), [pC.b], [D_.b])
        if SSD_STOP <= 4:
            return
        def f(e, da_=da_, pD=pD, Ltri=Ltri):
            e.matmul(pD[:, 0:4], Ltri, da_[:, d * 4:d * 4 + 4], start=True, stop=True)
            return e.matmul(pD[:, 4:8], ones, da_[:, d * 4:d * 4 + 4], start=True, stop=True)
        R.op("pe", f, [da_.b, C.b], [pD.b])
        R.op("dve", lambda e, sc_=sc_, pD=pD: e.tensor_copy(sc_[:, 0:8], pD[:, 0:8]), [pD.b], [sc_.b])
        R.op("dve", lambda e, sc_=sc_: e.tensor_copy(sc_[:, 8:12], sc_[:, 4:8]), [sc_.b], [sc_.b])
        R.op("dve", lambda e, sc_=sc_: e.tensor_tensor(sc_[:, 4:8], sc_[:, 4:8], sc_[:, 0:4], ALU.subtract), [sc_.b], [sc_.b])
        R.op("act", lambda e, sc_=sc_: e.activation(sc_[:], sc_[:], AF.Exp), [sc_.b], [sc_.b])
        if SSD_STOP <= 5:
            return
        R.op("dve", lambda e, W_=W_, D_=D_, pB=pB: e.tensor_tensor(
            W_[:].rearrange("p (g r) i -> p g r i", g=2), bc_ap(pB[:, 0:256], [[128, 2], [0, 2], [1, 128]]),
            D_[:].rearrange("p (g r) i -> p g r i", g=2), ALU.mult), [pB.b, D_.b], [W_.b])
        if SSD_STOP <= 6:
            return
        R.op("dve", lambda e, dw_=dw_, dt_=dt_, sc_=sc_: e.tensor_tensor(dw_[:, 0:4], dt_[:, d * 4:d * 4 + 4], sc_[:, 4:8], ALU.mult),
             [dt_.b, sc_.b], [dw_.b])
        R.op("dve", lambda e, xd_=xd_, x_=x_, dt_=dt_: e.tensor_tensor(
            xd_[:].rearrange("p (h q) -> p h q", h=4), x_[:].rearrange("p (h q) -> p h q", h=4),
            bc_ap(dt_[:, d * 4:d * 4 + 4], [[1, 4], [0, 64]]), ALU.mult), [x_.b, dt_.b], [xd_.b])
        R.op("dve", lambda e, xw_=xw_, x_=x_, dw_=dw_: e.tensor_tensor(
            xw_[:].rearrange("p (h q) -> p h q", h=4), x_[:].rearrange("p (h q) -> p h q", h=4),
            bc_ap(dw_[:, 0:4], [[1, 4], [0, 64]]), ALU.mult), [x_.b, dw_.b], [xw_.b])
        if SSD_STOP <= 7:
            return
        def f(e, W_=W_, xd_=xd_, pA=pA):
            for h in range(4):
                ins = e.matmul(pA[:, h * 64:(h + 1) * 64], W_[:, h, :], xd_[:, h * 64:(h + 1) * 64], start=True, stop=True)
            return ins
        R.op("pe", f, [W_.b, xd_.b], [pA.b])
        def f(e, bc_=bc_, pB=pB):
            for h in range(4):
                g = h // 2
                ins = e.matmul(pB[:, 256 + h * 64:256 + (h + 1) * 64], bc_[:, 2 + g, :],
                               G.sst[d][:, h, :], start=True, stop=True)
            return ins
        R.op("pe", f, [bc_.b, G.sst[d].b], [pB.b])
        def f(e, b_=b_, xw_=xw_, pD=pD):
            for h in range(4):
                g = h // 2
                ins = e.matmul(pD[0:64, 256 + h * 64:256 + (h + 1) * 64], b_[:, g * 64:(g + 1) * 64], xw_[:, h * 64:(h + 1) * 64], start=True, stop=True)
            return ins
        R.op("pe", f, [b_.b, xw_.b], [pD.b])
        if SSD_STOP <= 8:
            return
        for h in range(4):
            lo, hi = 0, 64
            R.op("dve", lambda e, lo=lo, hi=hi, h=h, sc_=sc_, pD=pD: e.scalar_tensor_tensor(
                G.sst[d][lo:hi, h, :], G.sst[d][lo:hi, h, :], sc_[lo:hi, 8 + h:9 + h], pD[lo:hi, 256 + h * 64:256 + (h + 1) * 64],
                ALU.mult, ALU.add), [G.sst[d].b, sc_.b, pD.b], [G.sst[d].b])
        if SSD_STOP <= 9:
            return
        y_ = yt[i]
        R.op("dve", lambda e, y_=y_, pB=pB, sc_=sc_: e.tensor_tensor(
            y_[:].rearrange("p (h q) -> p h q", h=4), pB[:, 256:512].rearrange("p (h q) -> p h q", h=4),
            bc_ap(sc_[:, 0:4], [[1, 4], [0, 64]]), ALU.mult), [pB.b, sc_.b], [y_.b])
        R.op("dve", lambda e, y_=y_, pA=pA: e.tensor_tensor(y_[:], y_[:], pA[:, 0:256], ALU.add), [y_.b, pA.b], [y_.b])
        if name == "ctx" and ti == 0:
            dbgdump(G, f"dt{d}", dt_, dt_[:], [128, 8], P=R)
            dbgdump(G, f"da{d}", da_, da_[:], [128, 8], P=R)
            dbgdump(G, f"x{d}", x_, x_[:], [128, 256], P=R)
            dbgdump(G, f"D{d}", D_, D_[:].rearrange("p h i -> p (h i)"), [128, 512], P=R)
            dbgdump(G, f"W{d}", W_, W_[:].rearrange("p h i -> p (h i)"), [128, 512], BF16, P=R)
            dbgdump(G, f"sc{d}", sc_, sc_[:], [128, 12], P=R)
            dbgdump(G, f"xd{d}", xd_, xd_[:], [128, 256], BF16, P=R)
            dbgdump(G, f"y{d}", y_, y_[:], [128, 256], P=R)
        if not second:
            R.op("dve", lambda e, x_=x_, ti=ti: e.tensor_tensor(yacc[:, ti, :], x_[:], rp[:, R_SSDD:R_SSDD + 256], ALU.mult),
                 [x_.b, rp.b], [yacc.b])
            R.op("dve", lambda e, y_=y_, ti=ti: e.tensor_tensor(yacc[:, ti, :], yacc[:, ti, :], y_[:], ALU.add), [yacc.b, y_.b], [yacc.b])
        else:
            g_, s_, o_ = yg[i], ss[i], yo[i]
            R.op("dve", lambda e, y_=y_, ti=ti: e.tensor_tensor(y_[:], y_[:], yacc[:, ti, :], ALU.add), [yacc.b, y_.b], [y_.b])
            R.op("act", lambda e, g_=g_, z_=z_: e.activation(g_[:], z_[:, 0:256], AF.Silu), [z_.b], [g_.b])
            R.op("dve", lambda e, g_=g_, y_=y_: e.tensor_tensor(g_[:], g_[:], y_[:], ALU.mult), [g_.b, y_.b], [g_.b])
            R.op("act", lambda e, g_=g_, y_=y_, s_=s_: e.activation(y_[:], g_[:], AF.Square, accum_out=s_[:, 0:1]), [g_.b], [y_.b, s_.b])
            R.op("act", lambda e, s_=s_: e.activation(s_[:, 1:2], s_[:, 0:1], AF.Sqrt, bias=G.eps[:, :], scale=1.0 / 256.0), [s_.b, G.eps.b], [s_.b])
            R.op("dve", lambda e, s_=s_: e.reciprocal(s_[:, 1:2], s_[:, 1:2]), [s_.b], [s_.b])
            R.op("dve", lambda e, g_=g_, s_=s_: e.scalar_tensor_tensor(
                g_[:], g_[:], s_[:, 1:2], rp[:, R_SSDNW:R_SSDNW + 256], ALU.mult, ALU.mult), [g_.b, s_.b, rp.b], [g_.b])
            def f(e, g_=g_, pC=pC):
                e.transpose(pC[:, 0:128], g_[:, 0:128], ident)
                return e.transpose(pC[:, 128:256], g_[:, 128:256], ident)
            R.op("pe", f, [g_.b, C.b], [pC.b])
            R.op("act", lambda e, o_=o_, pC=pC: e.activation(o_[:].rearrange("p j t -> p (j t)"), pC[:, 0:256], AF.Copy), [pC.b], [o_.b])
            R.dma(G.yT[0:256, t0:t0 + 128].rearrange("(j p) t -> p j t", p=128), o_[:], reads=[o_.b], writes=[G.yT.b], q="pool")

    cnt = 0
    for st_ in range(NT):
        recs = []
        for d in range(2):
            ti = st_ if d == 0 else NT - 1 - st_
            second = (ti >= NT // 2) if d == 0 else (ti < NT // 2)
            recs.append(Rec())
            body(recs[-1], d, ti, d, 2 * d + st_ % 2, second)
        if co_tile is not None:
            recs.append(Rec())
            co_tile(recs[-1], stream, st_)
        merge(P, recs)
    P.barrier()
    A.reset(m0)


def gdn_mixer(G, l, stream):
    P, A = G.P, G.A
    name, tok0, L, s = stream
    NT = L // 128
    m0 = A.mark()
    rp, C = G.rowp_sb, G.consts
    ident, ones = C[:, K_ID, :], C[:, K_ONES, :]
    id64 = C[0:64, K_ID, 0:64]
    negag = A.alloc("gnegag", [128, 8])
    negones = A.alloc("gnegones", [128, 128])
    oacc = A.alloc("goacc", [128, NT, 256])
    P.op("act", lambda e: e.activation(negag[:], rp[:, R_GALOG:R_GALOG + 8], AF.Exp), [rp.b], [negag.b])
    P.op("dve", lambda e: e.tensor_scalar(negag[:], negag[:], -1.0, None, ALU.mult), [negag.b], [negag.b])
    P.op("dve", lambda e: e.memset(negones[:], -1.0), [], [negones.b])
    NB = 2

    def mk(nm, shape, dt=F32):
        return [A.alloc(nm, shape, dt) for _ in range(NB)]
    pt, fm, gx, kv = mk("gpt", [128, 272]) + mk("gpt", [128, 272]), mk("gfm", [64, 12, 128]), mk("ggx", [128, 24]), mk("gkv", [128, 512])
    sc, esc, ML = mk("gsc", [128, 16]), mk("gesc", [128, 16]), mk("gML", [128, 4, 128])
    decT, decS, A0, Q0, aT = mk("gdecT", [128, 4, 128]), mk("gdecS", [128, 4, 128]), mk("gA0", [128, 4, 128]), mk("gQ0", [128, 4, 128]), mk("gaT", [128, 4, 128], BF16)
    Pb, Qb, MTb = [mk("gPb", [128, 4, 128]) for _ in range(2)], [mk("gQb", [128, 4, 128]) for _ in range(2)], [mk("gMT", [128, 4, 128]) for _ in range(3)]
    Rv, Rk, bek, usb, wsb = mk("gRv", [128, 4, 64]), mk("gRk", [128, 4, 64]), mk("gbek", [128, 4]), mk("gusb", [128, 256]), mk("gwsb", [64, 512], BF16)
    ektc, kt = [mk("gektc", [128, 4]) for _ in range(2)], [mk("gkt", [128, 4, 64], BF16) for _ in range(2)]
    qbf = mk("gqbf", [64, 4, 128], BF16)
    vnew, oint, ot, sq, ss, yo = mk("gvnew", [128, 256], BF16), mk("goint", [128, 256]), mk("got", [128, 256]), mk("gsq", [128, 256]), mk("gss", [128, 8]), mk("gyo", [128, 2, 128], BF16)
    for i in range(NB):
        P.op("dve", lambda e, i=i: e.memset(vnew[i][:], 0.0), [], [vnew[i].b])
    v4 = lambda t: t[:].rearrange("p (h q) -> p h q", h=4)

    def body(R, d, ti, i, li, second):
        g0_, g1_, g2_, g3_ = G.psum[4 * d:4 * d + 4]
        b = [g0_, g2_, g3_, g2_, g3_, g2_, g3_, g1_]
        bM = g0_
        t0 = tok0 + ti * 128
        Ltri2 = C[:, K_LF2 if d == 0 else K_LB2, :]
        niT = C[:, K_NI2F if d == 0 else K_NI2B, :]
        nsS = C[:, K_NS2F if d == 0 else K_NS2B, :]
        selc = [C[:, K_SEL0, 0:1], C[:, K_SEL1, 0:1]]
        pt_, f_, gx_, kv_, sc_, es_, ml_ = pt[li], fm[i], gx[i], kv[i], sc[i], esc[i], ML[i]
        dT_, dS_, a0_, q0_, at_ = decT[i], decS[i], A0[i], Q0[i], aT[i]
        R.dma(pt_[:], G.projT[t0:t0 + 128, 264:536], reads=[G.projT.b], writes=[pt_.b])
        R.dma(f_[:], G.convF[512:1280, t0:t0 + 128].rearrange("(j p) t -> p j t", p=64), reads=[G.convF.b], writes=[f_.b])
        R.op("dve", lambda e: e.tensor_tensor(gx_[:, 0:8], pt_[:, 256:264], rp[:, R_GDTB:R_GDTB + 8], ALU.add), [pt_.b, rp.b], [gx_.b])
        softplus_small(R, gx_[:, 0:8], gx_[:, 0:8], gx_[:, 16:24], [gx_.b], (gx_.b, gx_.b))
        R.op("dve", lambda e: e.tensor_tensor(gx_[:, 0:8], gx_[:, 0:8], negag[:], ALU.mult), [gx_.b, negag.b], [gx_.b])
        R.op("act", lambda e: e.activation(gx_[:, 8:16], pt_[:, 264:272], AF.Sigmoid), [pt_.b], [gx_.b])
        gd = gx_[:, d * 4:d * 4 + 4]
        bd = gx_[:, 8 + d * 4:8 + d * 4 + 4]
        qb_ = qbf[i]
        R.op("pool", lambda e: e.tensor_copy(qb_[:], f_[:, 0:4, :]), [f_.b], [qb_.b])
        def f(e):
            for h in range(4):
                e.transpose(b[0][:, h * 64:(h + 1) * 64], f_[:, 4 + h, :], id64)
            for h in range(4):
                ins = e.transpose(b[0][:, 256 + h * 64:256 + (h + 1) * 64], f_[:, 8 + h, :], id64)
            return ins
        R.op("pe", f, [f_.b, C.b], [b[0].b])
        R.op("act", lambda e: e.activation(kv_[:], b[0][:, :], AF.Copy), [b[0].b], [kv_.b])
        def f(e):
            e.matmul(b[7][:, 0:4], Ltri2, gd, start=True, stop=True)
            e.matmul(b[7][:, 4:8], C[:, K_BO2, :], gd, start=True, stop=True)
            e.matmul(b[7][:, 8:12], C[:, K_SEL0, :], gd, start=True, stop=True)
            return e.matmul(b[7][:, 12:16], C[:, K_SEL1, :], gd, start=True, stop=True)
        R.op("pe", f, [gx_.b, C.b], [b[7].b])
        R.op("dve", lambda e: e.tensor_copy(sc_[:], b[7][:, 0:16]), [b[7].b], [sc_.b])
        R.op("dve", lambda e: e.tensor_tensor(sc_[:, 4:8], sc_[:, 4:8], sc_[:, 0:4], ALU.subtract), [sc_.b], [sc_.b])
        R.op("act", lambda e: e.activation(es_[:], sc_[:], AF.Exp), [sc_.b], [es_.b])
        R.op("dve", lambda e: e.tensor_tensor(ml_[:], bc_ap(Ltri2, [[0, 4], [1, 128]]), bc_ap(gd, [[1, 4], [0, 128]]), ALU.mult),
             [gx_.b, C.b], [ml_.b])
        mlf = ml_[:].rearrange("p h i -> p (h i)")
        def f(e):
            e.matmul(b[1][:, :], ones, mlf, start=True, stop=False)
            for h in range(4):
                e.matmul(b[1][:, h * 128:(h + 1) * 128], ml_[:, h, :], negones[:], start=False, stop=False)
            return e.matmul(b[1][:, :], ident, bc_ap(niT, [[0, 4], [1, 128]]), start=False, stop=True)
        R.op("pe", f, [ml_.b, C.b, negones.b], [b[1].b])
        def f(e):
            e.matmul(b[2][:, :], negones[:], mlf, start=True, stop=False)
            for h in range(4):
                e.matmul(b[2][:, h * 128:(h + 1) * 128], ml_[:, h, :], ones, start=False, stop=False)
            return e.matmul(b[2][:, :], ident, bc_ap(nsS, [[0, 4], [1, 128]]), start=False, stop=True)
        R.op("pe", f, [ml_.b, C.b, negones.b], [b[2].b])
        fl = lambda t: t[:].rearrange("p h i -> p (h i)")
        R.op("act", lambda e: e.activation(fl(dT_), b[1][:, :], AF.Exp), [b[1].b], [dT_.b])
        R.op("act", lambda e: e.activation(fl(dS_), b[2][:, :], AF.Exp), [b[2].b], [dS_.b])
        def f(e):
            for h in range(4):
                ins = e.matmul(b[3][:, h * 128:(h + 1) * 128], f_[:, 4 + h, :], f_[:, 4 + h, :], start=True, stop=True)
            return ins
        R.op("pe", f, [f_.b], [b[3].b])
        def f(e):
            for h in range(4):
                ins = e.matmul(b[4][:, h * 128:(h + 1) * 128], f_[:, 4 + h, :], f_[:, h, :], start=True, stop=True)
            return ins
        R.op("pe", f, [f_.b], [b[4].b])
        R.op("dve", lambda e: e.tensor_tensor(fl(a0_), b[3][:, :], fl(dS_), ALU.mult), [b[3].b, dS_.b], [a0_.b])
        R.op("dve", lambda e: e.tensor_tensor(a0_[:], a0_[:], bc_ap(bd, [[1, 4], [0, 128]]), ALU.mult), [a0_.b, gx_.b], [a0_.b])
        R.op("dve", lambda e: e.tensor_tensor(fl(at_), b[4][:, :], fl(dT_), ALU.mult), [b[4].b, dT_.b], [at_.b])
        def f(e):
            for h in range(4):
                ins = e.transpose(b[5][:, h * 128:(h + 1) * 128], a0_[:, h, :], ident)
            return ins
        R.op("pe", f, [a0_.b, C.b], [b[5].b])
        R.op("act", lambda e: e.activation(fl(q0_), b[5][:, :], AF.Copy), [b[5].b], [q0_.b])
        mt = MTb[0][i]
        R.op("dve", lambda e, mt=mt: e.tensor_tensor(mt[:], bc_ap(ident, [[0, 4], [1, 128]]), q0_[:], ALU.subtract), [q0_.b, C.b], [mt.b])
        if name == "ctx" and ti == (0 if d == 0 else 1):
            dbgdump(G, f"gA{d}", a0_, fl(a0_), [128, 512], P=R)
            dbgdump(G, f"gQ{d}", q0_, fl(q0_), [128, 512], P=R)
            dbgdump(G, f"gM0{d}", mt, fl(mt), [128, 512], P=R)
            dbgdump(G, f"gdS{d}", dS_, fl(dS_), [128, 512], P=R)
            dbgdump(G, f"ggx{d}", gx_, gx_[:], [128, 24], P=R)
            dbgdump(G, f"gkv{d}", kv_, kv_[:], [128, 512], P=R)
        Pc, Qc = a0_, q0_
        for m in range(GDN_ROUNDS):
            Pn, Qn, mtn = Pb[m % 2][i], Qb[m % 2][i], MTb[(m + 1) % 3][i]
            def f(e, Pc=Pc, Qc=Qc):
                for h in range(4):
                    ins = e.matmul(b[3][:, h * 128:(h + 1) * 128], Qc[:, h, :], Pc[:, h, :], start=True, stop=True)
                return ins
            R.op("pe", f, [Pc.b, Qc.b], [b[3].b])
            def f(e, Pc=Pc, Qc=Qc):
                for h in range(4):
                    ins = e.matmul(b[4][:, h * 128:(h + 1) * 128], Pc[:, h, :], Qc[:, h, :], start=True, stop=True)
                return ins
            R.op("pe", f, [Pc.b, Qc.b], [b[4].b])
            R.op("act", lambda e, Pn=Pn: e.activation(fl(Pn), b[3][:, :], AF.Copy), [b[3].b], [Pn.b])
            R.op("dve", lambda e, Qn=Qn: e.tensor_copy(fl(Qn), b[4][:, :]), [b[4].b], [Qn.b])
            def f(e, Pn=Pn, mt=mt):
                for h in range(4):
                    ins = e.matmul(bM[:, h * 128:(h + 1) * 128], Pn[:, h, :], mt[:, h, :], start=True, stop=True)
                return ins
            R.op("pe", f, [Pn.b, mt.b], [bM.b])
            R.op("dve", lambda e, mt=mt, mtn=mtn: e.tensor_tensor(fl(mtn), fl(mt), bM[:, :], ALU.add), [mt.b, bM.b], [mtn.b])
            Pc, Qc, mt = Pn, Qn, mtn
        rv_, rk_, bk_, u_, w_ = Rv[i], Rk[i], bek[i], usb[i], wsb[i]
        R.op("dve", lambda e: e.tensor_tensor(rv_[:], v4(kv_)[:, 4:8, :] if False else kv_[:, 256:512].rearrange("p (h q) -> p h q", h=4),
                                              bc_ap(bd, [[1, 4], [0, 64]]), ALU.mult), [kv_.b, gx_.b], [rv_.b])
        R.op("dve", lambda e: e.tensor_tensor(bk_[:], bd, es_[:, 0:4], ALU.mult), [gx_.b, es_.b], [bk_.b])
        R.op("dve", lambda e: e.tensor_tensor(rk_[:], kv_[:, 0:256].rearrange("p (h q) -> p h q", h=4),
                                              bc_ap(bk_[:, 0:4], [[1, 4], [0, 64]]), ALU.mult), [kv_.b, bk_.b], [rk_.b])
        def f(e):
            for h in range(4):
                ins = e.matmul(b[7][:, 64 + h * 64:64 + (h + 1) * 64], mt[:, h, :], rv_[:, h, :], start=True, stop=True)
            return ins
        R.op("pe", f, [mt.b, rv_.b], [b[7].b])
        def f(e):
            for h in range(4):
                ins = e.matmul(b[0][0:64, h * 128:(h + 1) * 128], rk_[:, h, :], mt[:, h, :], start=True, stop=True)
            return ins
        R.op("pe", f, [mt.b, rk_.b], [b[0].b])
        R.op("act", lambda e: e.activation(u_[:], b[7][:, 64:320], AF.Copy), [b[7].b], [u_.b])
        R.op("dve", lambda e: e.tensor_copy(w_[:], b[0][0:64, :]), [b[0].b], [w_.b])
        for c in range(2):
            ek, k_ = ektc[c][i], kt[c][i]
            R.op("dve", lambda e, ek=ek, c=c: e.tensor_scalar(ek[:], es_[:, 4:8], selc[c], None, ALU.mult), [es_.b, C.b], [ek.b])
            R.op("dve", lambda e, ek=ek, k_=k_: e.tensor_tensor(k_[:], kv_[:, 0:256].rearrange("p (h q) -> p h q", h=4),
                                                               bc_ap(ek[:, 0:4], [[1, 4], [0, 64]]), ALU.mult), [kv_.b, ek.b], [k_.b])
        vn_, oi_ = vnew[i], oint[i]
        for c in ((0, 1) if d == 0 else (1, 0)):
            lo, hi = c * 64, (c + 1) * 64
            k_ = kt[c][i]
            def f(e):
                for h in range(4):
                    e.matmul(b[5][:, h * 64:(h + 1) * 64], w_[:, h * 128:(h + 1) * 128], G.gsb[d][:, h, :], start=True, stop=True)
                for h in range(4):
                    ins = e.matmul(b[5][:, 256 + h * 64:256 + (h + 1) * 64], qb_[:, h, :], G.gsb[d][:, h, :], start=True, stop=True)
                return ins
            R.op("pe", f, [w_.b, qb_.b, G.gsb[d].b], [b[5].b])
            R.op("dve", lambda e, lo=lo, hi=hi: e.tensor_tensor(vn_[lo:hi, :], u_[lo:hi, :], b[5][lo:hi, 0:256], ALU.subtract), [u_.b, b[5].b], [vn_.b])
            R.op("dve", lambda e, lo=lo, hi=hi: e.tensor_tensor(oi_[lo:hi, :].rearrange("p (h q) -> p h q", h=4),
                                                                b[5][lo:hi, 256:512].rearrange("p (h q) -> p h q", h=4),
                                                                bc_ap(es_[lo:hi, 0:4], [[1, 4], [0, 64]]), ALU.mult), [b[5].b, es_.b], [oi_.b])
            def f(e, k_=k_):
                for h in range(4):
                    ins = e.matmul(b[6][0:64, h * 64:(h + 1) * 64], k_[:, h, :], vn_[:, h * 64:(h + 1) * 64], start=True, stop=True)
                return ins
            R.op("pe", f, [k_.b, vn_.b], [b[6].b])
            for h in range(4):
                R.op("dve", lambda e, h=h, c=c: e.scalar_tensor_tensor(
                    G.gst[d][:, h, :], G.gst[d][:, h, :], es_[0:64, 8 + 4 * c + h:9 + 4 * c + h], b[6][0:64, h * 64:(h + 1) * 64],
                    ALU.mult, ALU.add), [G.gst[d].b, es_.b, b[6].b], [G.gst[d].b])
            R.op("pool", lambda e: e.tensor_copy(G.gsb[d][:], G.gst[d][:]), [G.gst[d].b], [G.gsb[d].b])
        def f(e):
            for h in range(4):
                ins = e.matmul(b[7][:, 64 + h * 64:64 + (h + 1) * 64], at_[:, h, :], vn_[:, h * 64:(h + 1) * 64], start=True, stop=True)
            return ins
        R.op("pe", f, [at_.b, vn_.b], [b[7].b])
        if name == "ctx" and ti == (0 if d == 0 else 1):
            dbgdump(G, f"gMT{d}", mt, fl(mt), [128, 512], P=R)
            dbgdump(G, f"gu{d}", u_, u_[:], [128, 256], P=R)
            dbgdump(G, f"gw{d}", w_, w_[:], [64, 512], BF16, P=R)
            dbgdump(G, f"gvn{d}", vn_, vn_[:], [128, 256], BF16, P=R)
            dbgdump(G, f"gaT{d}", at_, fl(at_), [128, 512], BF16, P=R)
        if not second:
            R.op("dve", lambda e: e.tensor_tensor(oacc[:, ti, :], b[7][:, 64:320], oi_[:], ALU.add), [b[7].b, oi_.b], [oacc.b])
            return
        o_, q_, s_, y_ = ot[i], sq[i], ss[i], yo[i]
        R.op("dve", lambda e: e.tensor_tensor(o_[:], b[7][:, 64:320], oi_[:], ALU.add), [b[7].b, oi_.b], [o_.b])
        R.op("dve", lambda e: e.tensor_tensor(o_[:], o_[:], oacc[:, ti, :], ALU.add), [o_.b, oacc.b], [o_.b])
        R.op("dve", lambda e: e.tensor_tensor(q_[:], o_[:], o_[:], ALU.mult), [o_.b], [q_.b])
        R.op("dve", lambda e: e.reduce_sum(s_[:, 0:4], q_[:].rearrange("p (h q) -> p h q", h=4), AX.X), [q_.b], [s_.b])
        R.op("act", lambda e: e.activation(s_[:, 4:8], s_[:, 0:4], AF.Sqrt, bias=G.eps[:, :], scale=1.0 / 64.0), [s_.b, G.eps.b], [s_.b])
        R.op("dve", lambda e: e.reciprocal(s_[:, 4:8], s_[:, 4:8]), [s_.b], [s_.b])
        R.op("dve", lambda e: e.tensor_tensor(v4(o_), v4(o_), bc_ap(s_[:, 4:8], [[1, 4], [0, 64]]), ALU.mult), [o_.b, s_.b], [o_.b])
        R.op("dve", lambda e: e.tensor_tensor(v4(o_), v4(o_), bc_ap(rp[:, R_GDNNW:R_GDNNW + 64], [[0, 4], [1, 64]]), ALU.mult), [o_.b, rp.b], [o_.b])
        R.op("act", lambda e: e.activation(q_[:], pt_[:, 0:256], AF.Silu), [pt_.b], [q_.b])
        R.op("dve", lambda e: e.tensor_tensor(o_[:], o_[:], q_[:], ALU.mult), [o_.b, q_.b], [o_.b])
        def f(e):
            e.transpose(b[2][:, 0:128], o_[:, 0:128], ident)
            return e.transpose(b[2][:, 128:256], o_[:, 128:256], ident)
        R.op("pe", f, [o_.b, C.b], [b[2].b])
        R.op("act", lambda e: e.activation(y_[:].rearrange("p j t -> p (j t)"), b[2][:, 0:256], AF.Copy), [b[2].b], [y_.b])
        R.dma(G.yT[512:768, t0:t0 + 128].rearrange("(j p) t -> p j t", p=128), y_[:], reads=[y_.b], writes=[G.yT.b], q="pool")

    cnt = 0
    for st_ in range(NT):
        recs = []
        for d in range(2):
            ti = st_ if d == 0 else NT - 1 - st_
            second = (ti >= NT // 2) if d == 0 else (ti < NT // 2)
            recs.append(Rec())
            body(recs[-1], d, ti, d, 2 * d + st_ % 2, second)
        merge(P, recs)
    P.barrier()
    A.reset(m0)


def x_rows(G, l, name, tok0, r0, n):
    if l == 0:
        return (G.ctx_in if name == "ctx" else G.x_in)[r0:r0 + n, :], None
    t = G.xs[l % 2]
    return t[tok0 + r0:tok0 + r0 + n, :], t.b


def phase_c(G, l, streams, hT2):
    P, A = G.P, G.A
    m0 = A.mark()
    rp = G.rowp_sb
    wo = A.alloc("wo", [128, 8, D], BF16)
    xt = [A.alloc("cxt", [128, D]) for _ in range(2)]
    load_weight_bf16(G, wo, lambda k: G.w_out[l, k * 128:(k + 1) * 128, :], D, xt, 8)
    yb = [A.alloc("cyb", [128, 8, 512], BF16) for _ in range(2)]
    t1 = [A.alloc("ct1", [128, D]) for _ in range(2)]
    xn = [A.alloc("cxn", [128, D]) for _ in range(4)]
    st = [A.alloc("cst", [128, 2, 6]) for _ in range(2)]
    mv = [A.alloc("cmv", [128, 2]) for _ in range(2)]
    rstd = [A.alloc("crstd", [128, 1]) for _ in range(2)]
    ident = G.consts[:, K_ID, :]
    po = [[G.psum[0], G.psum[1]], [G.psum[2], G.psum[3]]]
    pT = [G.psum[4], G.psum[5]]
    cb = 0
    ct = 0
    for (name, tok0, L, s) in streams:
        bs = min(512, L)
        for b0 in range(0, L, bs):
            nt = bs // 128
            y_ = yb[cb % 2]
            cb += 1
            P.dma(y_[:, :, 0:bs], G.yT[:, tok0 + b0:tok0 + b0 + bs].rearrange("(k p) t -> p k t", p=128), reads=[G.yT.b], writes=[y_.b])
            recs = []
            for m in range(nt):
                R = Rec()
                recs.append(R)
                q = ct % 2
                ct += 1
                x_, t_, pp = xt[q], t1[q], po[q]
                r0 = b0 + m * 128
                src, sb_ = x_rows(G, l, name, tok0, r0, 128)
                R.dma(x_[:], src, reads=[sb_] if sb_ is not None else [], writes=[x_.b])
                for half in range(2):
                    def f(e, half=half, y_=y_, m=m, pp=pp):
                        for k in range(8):
                            ins = e.matmul(pp[half][:, :], y_[:, k, m * 128:(m + 1) * 128], wo[:, k, half * 512:(half + 1) * 512],
                                           start=(k == 0), stop=(k == 7))
                        return ins
                    R.op("pe", f, [y_.b, wo.b], [pp[half].b])
                    R.op("dve", lambda e, half=half, t_=t_, pp=pp, s=s: e.tensor_tensor(
                        t_[:, half * 512:(half + 1) * 512], pp[half][:, :], G.gb[:, s, 0, half * 512:(half + 1) * 512], ALU.mult),
                        [pp[half].b, G.gb.b], [t_.b])
                R.op("dve", lambda e, t_=t_, x_=x_: e.scalar_tensor_tensor(t_[:], x_[:], DN_ALPHA, t_[:], ALU.mult, ALU.add), [x_.b, t_.b], [t_.b])
                ln_stats(G, t_, 128, st[q], mv[q], rstd[q], P=R)
                R.op("dve", lambda e, t_=t_, q=q: e.tensor_scalar(t_[:], t_[:], mv[q][:, 0:1], rstd[q][:, 0:1], ALU.subtract, ALU.mult),
                     [t_.b, mv[q].b, rstd[q].b], [t_.b])
                R.op("pool", lambda e, t_=t_: e.tensor_tensor(t_[:], t_[:], rp[:, R_LN1W:R_LN1W + D], ALU.mult), [t_.b, rp.b], [t_.b])
                R.op("pool", lambda e, t_=t_: e.tensor_tensor(t_[:], t_[:], rp[:, R_LN1B:R_LN1B + D], ALU.add), [t_.b, rp.b], [t_.b])
                R.dma(G.x1[tok0 + r0:tok0 + r0 + 128, :], t_[:], reads=[t_.b], writes=[G.x1.b], q="pool")
                ln_stats(G, t_, 128, st[q], mv[q], rstd[q], P=R)
                R.op("dve", lambda e, t_=t_, q=q, m=m: e.tensor_scalar(xn[m][:], t_[:], mv[q][:, 0:1], rstd[q][:, 0:1], ALU.subtract, ALU.mult),
                     [t_.b, mv[q].b, rstd[q].b], [xn[m].b])
            for m in range(0, nt, 2):
                merge(P, recs[m:m + 2])
            for k in range(8):
                p = pT[k % 2]
                def f(e, p=p, k=k, nt=nt):
                    for m in range(nt):
                        ins = e.transpose(p[:, m * 128:(m + 1) * 128], xn[m][:, k * 128:(k + 1) * 128], ident)
                    return ins
                P.op("pe", f, [xn[m].b for m in range(nt)] + [G.consts.b], [p.b])
                c0 = tok0 + b0
                P.op("act", lambda e, p=p, k=k, bs=bs, s=s, c0=c0: e.activation(
                    hT2[:, k, c0:c0 + bs], p[:, 0:bs], AF.Identity, bias=G.modc[:, 2, k, s:s + 1], scale=G.modc[:, 3, k, s:s + 1]),
                    [p.b, G.modc.b], [hT2.b])
    P.barrier()
    A.reset(m0)


def phase_d1(G, l, streams, hT2):
    P, A = G.P, G.A
    m0 = A.mark()
    cp = G.colp_sb
    wst = [A.alloc("dwst", [128, 8, 256]) for _ in range(2)]
    wab = [A.alloc("dwab", [128, 8, 256], BF16) for _ in range(2)]
    asb = [A.alloc("dasb", [128, 514]) for _ in range(4)]
    acc = [A.alloc("dacc", [128, 512]) for _ in range(4)]
    hc = [A.alloc("dhc", [128, 512], BF16) for _ in range(4)]
    up = G.ffn_up[l, :, :].rearrange("(k p) c -> p k c", p=128)
    pa = [[G.psum[0], G.psum[1]], [G.psum[2], G.psum[3]]]
    pbs = [G.psum[4], G.psum[5], G.psum[6], G.psum[7]]
    cnt = 0
    for c in range(DFF // 128):
        w_, wb_ = wst[c % 2], wab[c % 2]
        P.dma(w_[:, :, 0:128], up[:, :, c * 128:(c + 1) * 128], writes=[w_.b])
        P.dma(w_[:, :, 128:256], up[:, :, DFF + c * 128:DFF + (c + 1) * 128], writes=[w_.b])
        P.op("pool", lambda e, w_=w_, wb_=wb_: e.tensor_copy(wb_[:], w_[:]), [w_.b], [wb_.b])
        pend = []
        for (name, tok0, L, s) in streams:
            bs = min(512, L)
            for b0 in range(0, L, bs):
                R = Rec()
                pend.append(R)
                i = cnt % 4
                p0, p1 = pa[cnt % 2]
                pb = pbs[i]
                cnt += 1
                a_, ac_, h_ = asb[i], acc[i], hc[i]
                t0 = tok0 + b0
                lo = max(t0 - 1, tok0)
                hi = min(t0 + bs + 1, tok0 + L)
                jlo, jhi = lo - (t0 - 1), hi - (t0 - 1)
                half = (bs + 2) // 2
                segs = [(jlo, half + 1), (half - 1, jhi)] if bs == 512 else [(jlo, jhi)]
                for si, (j0, j1) in enumerate(segs):
                    pp = p0 if si == 0 else p1
                    def f(e, pp=pp, j0=j0, j1=j1, wb_=wb_, t0=t0):
                        for k in range(8):
                            ins = e.matmul(pp[:, 0:j1 - j0], wb_[:, k, 0:128], hT2[:, k, t0 - 1 + j0:t0 - 1 + j1], start=(k == 0), stop=(k == 7))
                        return ins
                    R.op("pe", f, [wb_.b, hT2.b], [pp.b])
                def f(e, pb=pb, wb_=wb_, t0=t0, bs=bs):
                    for k in range(8):
                        ins = e.matmul(pb[:, 0:bs], wb_[:, k, 128:256], hT2[:, k, t0:t0 + bs], start=(k == 0), stop=(k == 7))
                    return ins
                R.op("pe", f, [wb_.b, hT2.b], [pb.b])
                if jlo > 0:
                    R.op("pool", lambda e, a_=a_: e.memset(a_[:, 0:1], 0.0), [], [a_.b])
                if jhi < bs + 2:
                    R.op("pool", lambda e, a_=a_, bs=bs: e.memset(a_[:, bs + 1:bs + 2], 0.0), [], [a_.b])
                if len(segs) == 2:
                    (a0, a1), (b0_, b1_) = segs
                    R.op("act", lambda e, a_=a_, p0=p0, a0=a0, a1=a1: e.activation(a_[:, a0:a1], p0[:, 0:a1 - a0], AF.Copy), [p0.b], [a_.b])
                    R.op("act", lambda e, a_=a_, p1=p1, a1=a1, b0_=b0_, b1_=b1_: e.activation(
                        a_[:, a1:b1_], p1[:, a1 - b0_:b1_ - b0_], AF.Copy), [p1.b], [a_.b])
                else:
                    (a0, a1), = segs
                    R.op("act", lambda e, a_=a_, p0=p0, a0=a0, a1=a1: e.activation(a_[:, a0:a1], p0[:, 0:a1 - a0], AF.Copy), [p0.b], [a_.b])
                wo_ = C_FFNCW + c * 3
                R.op("dve", lambda e, a_=a_, ac_=ac_, bs=bs, wo_=wo_: e.tensor_scalar(ac_[:, 0:bs], a_[:, 0:bs], cp[:, wo_:wo_ + 1], None, ALU.mult),
                     [a_.b, cp.b], [ac_.b])
                for tap in (1, 2):
                    R.op("dve", lambda e, a_=a_, ac_=ac_, bs=bs, wo_=wo_, tap=tap: e.scalar_tensor_tensor(
                        ac_[:, 0:bs], a_[:, tap:tap + bs], cp[:, wo_ + tap:wo_ + tap + 1], ac_[:, 0:bs], ALU.mult, ALU.add), [a_.b, cp.b, ac_.b], [ac_.b])
                R.op("act", lambda e, ac_=ac_, bs=bs, c=c: e.activation(ac_[:, 0:bs], ac_[:, 0:bs], AF.Silu, bias=cp[:, C_FFNCB + c:C_FFNCB + c + 1]),
                     [ac_.b, cp.b], [ac_.b])
                R.op("dve", lambda e, ac_=ac_, h_=h_, pb=pb, bs=bs: e.tensor_tensor(h_[:, 0:bs], ac_[:, 0:bs], pb[:, 0:bs], ALU.mult), [ac_.b, pb.b], [h_.b])
                R.dma(G.hid[c * 128:(c + 1) * 128, t0:t0 + bs], h_[:, 0:bs], reads=[h_.b], writes=[G.hid.b], q="pool")
                if len(pend) == 2:
                    merge(P, pend)
                    pend = []
        if pend:
            merge(P, pend)
    P.barrier()
    A.reset(m0)


def phase_d2(G, l, streams, last):
    P, A = G.P, G.A
    m0 = A.mark()
    rp = G.rowp_sb
    NC_ = DFF // 128
    wd = A.alloc("wd", [128, NC_, D], BF16)
    stage = [A.alloc("wdstage", [128, D]) for _ in range(2)]
    load_weight_bf16(G, wd, lambda k: G.ffn_down[l, k * 128:(k + 1) * 128, :], D, stage, NC_)
    hb = [A.alloc("ehb", [128, NC_, 512], BF16) for _ in range(2)]
    xt = [A.alloc("ext", [128, D]) for _ in range(2)]
    t1 = [A.alloc("et1", [128, D]) for _ in range(2)]
    st = [A.alloc("est", [128, 2, 6]) for _ in range(2)]
    mv = [A.alloc("emv", [128, 2]) for _ in range(2)]
    rstd = [A.alloc("erstd", [128, 1]) for _ in range(2)]
    po = [[G.psum[0], G.psum[1]], [G.psum[2], G.psum[3]]]
    xnext = G.xs[(l + 1) % 2]
    cb = 0
    ct = 0
    for (name, tok0, L, s) in streams:
        bs = min(512, L)
        for b0 in range(0, L, bs):
            nt = bs // 128
            h_ = hb[cb % 2]
            cb += 1
            P.dma(h_[:, :, 0:bs], G.hid[:, tok0 + b0:tok0 + b0 + bs].rearrange("(c p) t -> p c t", p=128), reads=[G.hid.b], writes=[h_.b])
            recs = []
            for m in range(nt):
                R = Rec()
                recs.append(R)
                q = ct % 2
                ct += 1
                x_, t_, pp = xt[q], t1[q], po[q]
                r0 = tok0 + b0 + m * 128
                R.dma(x_[:], G.x1[r0:r0 + 128, :], reads=[G.x1.b], writes=[x_.b])
                for half in range(2):
                    def f(e, half=half, h_=h_, m=m, pp=pp):
                        for c in range(NC_):
                            ins = e.matmul(pp[half][:, :], h_[:, c, m * 128:(m + 1) * 128], wd[:, c, half * 512:(half + 1) * 512],
                                           start=(c == 0), stop=(c == NC_ - 1))
                        return ins
                    R.op("pe", f, [h_.b, wd.b], [pp[half].b])
                    R.op("dve", lambda e, half=half, t_=t_, pp=pp, s=s: e.tensor_tensor(
                        t_[:, half * 512:(half + 1) * 512], pp[half][:, :], G.gb[:, s, 1, half * 512:(half + 1) * 512], ALU.mult),
                        [pp[half].b, G.gb.b], [t_.b])
                R.op("dve", lambda e, t_=t_, x_=x_: e.scalar_tensor_tensor(t_[:], x_[:], DN_ALPHA, t_[:], ALU.mult, ALU.add), [x_.b, t_.b], [t_.b])
                ln_stats(G, t_, 128, st[q], mv[q], rstd[q], P=R)
                R.op("dve", lambda e, t_=t_, q=q: e.tensor_scalar(t_[:], t_[:], mv[q][:, 0:1], rstd[q][:, 0:1], ALU.subtract, ALU.mult),
                     [t_.b, mv[q].b, rstd[q].b], [t_.b])
                R.op("pool", lambda e, t_=t_: e.tensor_tensor(t_[:], t_[:], rp[:, R_LN2W:R_LN2W + D], ALU.mult), [t_.b, rp.b], [t_.b])
                R.op("pool", lambda e, t_=t_: e.tensor_tensor(t_[:], t_[:], rp[:, R_LN2B:R_LN2B + D], ALU.add), [t_.b, rp.b], [t_.b])
                if last:
                    rr = b0 + m * 128
                    R.dma(G.out[rr:rr + 128, :], t_[:], reads=[t_.b], writes=[G.out.b], q="pool")
                else:
                    R.dma(xnext[r0:r0 + 128, :], t_[:], reads=[t_.b], writes=[xnext.b], q="pool")
            for m in range(nt):
                merge(P, recs[m:m + 1])
    P.barrier()
    A.reset(m0)


def _col(v, nchunk):
    return np.ascontiguousarray(v.reshape(nchunk, 128).T)


def prep_inputs(inputs):
    f = lambda a: np.ascontiguousarray(np.asarray(a, dtype=np.float32))
    I = {k: f(v) for k, v in inputs.items()}
    colp = np.zeros((DEPTH, 128, NCOL), np.float32)
    rowp = np.zeros((DEPTH, 1, NROW), np.float32)
    poolw = np.zeros((DEPTH, 128, 2, 128), np.float32)
    gws = np.zeros((DEPTH, 128, 4, 128), np.float32)
    for l in range(DEPTH):
        cw = I["ssd_conv_w"][l]
        colp[l, :, C_SSDCW:C_SSDCW + 28] = cw.T.reshape(4, 128, 7).transpose(1, 0, 2).reshape(128, 28)
        colp[l, :, C_SSDCB:C_SSDCB + 4] = _col(I["ssd_conv_b"][l], 4)
        gw = I["gdn_conv_w"][l]
        colp[l, :, C_GDNCW:C_GDNCW + 42] = gw.T.reshape(6, 128, 7).transpose(1, 0, 2).reshape(128, 42)
        fw = I["ffn_conv_w"][l]
        colp[l, :, C_FFNCW:C_FFNCW + 66] = fw.T.reshape(22, 128, 3).transpose(1, 0, 2).reshape(128, 66)
        colp[l, :, C_FFNCB:C_FFNCB + 22] = _col(I["ffn_conv_b"][l], 22)
        colp[l, :, C_PSCALE:C_PSCALE + 2] = _col(I["pool_scale"][l], 2)
        colp[l, :, C_BMOD:C_BMOD + 48] = I["b_mod"][l].reshape(6, 8, 128).transpose(2, 0, 1).reshape(128, 48)
        r = rowp[l, 0]
        r[R_SSDNW:R_SSDNW + 256] = I["ssd_norm_w"][l]
        r[R_GDNNW:R_GDNNW + 64] = I["gdn_norm_w"][l]
        r[R_GLNW:R_GLNW + 256] = I["gmlp_ln_w"][l]
        r[R_GLNB:R_GLNB + 256] = I["gmlp_ln_b"][l]
        r[R_LN1W:R_LN1W + 1024] = I["ln1_w"][l]
        r[R_LN1B:R_LN1B + 1024] = I["ln1_b"][l]
        r[R_LN2W:R_LN2W + 1024] = I["ln2_w"][l]
        r[R_LN2B:R_LN2B + 1024] = I["ln2_b"][l]
        r[R_SSDD:R_SSDD + 256] = np.repeat(I["ssd_d"][l], 64)
        r[R_GBS:R_GBS + 512] = I["gmlp_bs"][l].reshape(-1)
        r[R_SDTB:R_SDTB + 8] = I["ssd_dt_bias"][l].reshape(-1)
        r[R_SALOG:R_SALOG + 8] = I["ssd_a_log"][l].reshape(-1)
        r[R_GDTB:R_GDTB + 8] = I["gdn_dt_bias"][l].reshape(-1)
        r[R_GALOG:R_GALOG + 8] = I["gdn_a_log"][l].reshape(-1)
        r[R_BG1:R_BG1 + 1024] = I["b_mod"][l][2048:3072]
        r[R_BG2:R_BG2 + 1024] = I["b_mod"][l][5120:6144]
        pw = I["pool_w"][l]
        for g in range(4):
            j, h = g // 2, g % 2
            poolw[l, h * 64:(h + 1) * 64, j, h * 64:(h + 1) * 64] = pw[g]
        gws[l] = I["gmlp_ws"][l].transpose(2, 0, 1)
    consts = make_consts()

    def pinv(RW):
        o = np.zeros((128, 2, RW), np.float32)
        pos = np.arange(RW)
        for jc in range(2):
            for hh in range(2):
                w = (2, 4, 8, 16)[2 * jc + hh]
                lo = np.clip(pos - w // 2, 0, RW)
                hi = np.clip(pos + w - w // 2, 0, RW)
                o[hh * 64:(hh + 1) * 64, jc, :] = 1.0 / (hi - lo).astype(np.float32)
        return o
    shared = dict(consts=consts, w_mod=I["w_mod"], w_in=I["w_in"], w_out=I["w_out"], ffn_up=I["ffn_up"],
                  ffn_down=I["ffn_down"], colp=colp, rowp=rowp, poolw=poolw, gws=gws,
                  pinv_g=pinv(64), pinv_c=pinv(256))
    maps = []
    for core in range(8):
        b = core % 4
        crep = np.zeros((128, 2, 8, 128), np.float32)
        crep[:, 0] = np.repeat(_col(I["c"][b], 8)[:, :, None], 128, axis=2)
        crep[:, 1] = np.repeat(_col(I["c_ctx"], 8)[:, :, None], 128, axis=2)
        m = dict(shared)
        m.update(x_in=I["x"][b], ctx_in=I["ctx"][b], crep=crep)
        maps.append(m)
    return maps


_NC_CACHE = {}


def kernel(**inputs):
    maps = prep_inputs(inputs)
    if "nc" not in _NC_CACHE:
        _NC_CACHE["nc"] = build()
    res = run_bass_kernel_spmd(_NC_CACHE["nc"], maps, core_ids=list(range(8)))
    out = np.stack([np.asarray(res.results[b]["out"], dtype=np.float32) for b in range(4)], axis=0)
    return out
```
